# Optimizing a Trainium2 kernel written in Bass

```python
import jax, jax.numpy as jnp
from jax import lax
import numpy as np

D_MODEL = 1024
BATCH = 8
SEQ = 2048
DEPTH = 1
DEC_BATCH = 128
DEC_SEQ = 1
PAST_LEN = 16384
PAGE_SIZE = 128

RWKV_WIDTH = D_MODEL // 2
RWKV_HEAD = 64
RWKV_HEADS = RWKV_WIDTH // RWKV_HEAD
RWKV_DECAY_LORA = 64
RWKV_A_LORA = 64
RWKV_GATE_LORA = 128
RWKV_PROJ = 3 * RWKV_WIDTH + RWKV_DECAY_LORA + RWKV_A_LORA + RWKV_GATE_LORA
RWKV_SPLITS = (RWKV_WIDTH, 2 * RWKV_WIDTH, 3 * RWKV_WIDTH,
               3 * RWKV_WIDTH + RWKV_DECAY_LORA, 3 * RWKV_WIDTH + RWKV_DECAY_LORA + RWKV_A_LORA)
RWKV_GN_EPS = 64e-5
L2_EPS = 1e-12
HGRN_WIDTH = D_MODEL // 2
HGRN_HEADS = 4
HGRN_HEAD = HGRN_WIDTH // HGRN_HEADS
HGRN_PROJ = 4 * HGRN_WIDTH
HGRN_CHUNK = 64
GATE_PROJ = 2 * D_MODEL
IN_PROJ = RWKV_PROJ + HGRN_PROJ + GATE_PROJ
D_FF = ((8 * D_MODEL + 3 * 256 - 1) // (3 * 256)) * 256
RMS_EPS = 1e-6

kernel_name = 'rwkv7_hgrn2_gated_parallel_decoder_step'


def _rms_norm(x, g):
    xf = x.astype(jnp.float32)
    y = xf * lax.rsqrt(jnp.mean(xf * xf, axis=-1, keepdims=True) + RMS_EPS)
    return (y * g.astype(jnp.float32)).astype(x.dtype)


def _rwkv7_mixer(p, shift_prev, wkv0, mu, w0, w2, a0, a2, g2, k_k, k_a, r_k, ln_g, ln_b):
    B, T, _ = p.shape
    f32 = jnp.float32
    pf = p.astype(f32)
    p_prev = jnp.concatenate([shift_prev[:, None, :].astype(f32), pf[:, :-1]], axis=1)
    ps = pf + mu.astype(f32) * (p_prev - pf)
    r, k, v, wd, ad, gd = jnp.split(ps, list(RWKV_SPLITS), axis=-1)
    w = -jax.nn.softplus(-(w0 + jnp.tanh(wd) @ w2)) - 0.5
    decay = jnp.exp(-jnp.exp(w))
    a = jax.nn.sigmoid(a0 + ad @ a2)
    g = jax.nn.sigmoid(gd) @ g2

    def heads(t):
        return t.reshape(B, T, RWKV_HEADS, RWKV_HEAD)

    kk = heads(k * k_k)
    kk = kk / jnp.maximum(jnp.linalg.norm(kk, axis=-1, keepdims=True), L2_EPS)
    k = k * (1.0 + (a - 1.0) * k_a)
    r_h, k_h, v_h, w_h, a_h = heads(r), heads(k), heads(v), heads(decay), heads(a)
    b_h = kk * a_h

    def step(S, inp):
        r_t, w_t, k_t, v_t, aa_t, bb_t = inp
        sa = jnp.einsum('bhvk,bhk->bhv', S, aa_t)
        S = (S * w_t[:, :, None, :] + sa[..., None] * bb_t[:, :, None, :]
             + v_t[..., None] * k_t[:, :, None, :])
        y = jnp.einsum('bhvk,bhk->bhv', S, r_t)
        return S, y

    seq = tuple(jnp.moveaxis(t, 1, 0) for t in (r_h, w_h, k_h, v_h, -kk, b_h))
    S_T, y = lax.scan(step, wkv0.astype(f32), seq)
    y = jnp.moveaxis(y, 0, 1)
    mean = jnp.mean(y, axis=-1, keepdims=True)
    var = jnp.mean(jnp.square(y - mean), axis=-1, keepdims=True)
    y = ((y - mean) * lax.rsqrt(var + RWKV_GN_EPS)).reshape(B, T, RWKV_WIDTH) * ln_g + ln_b
    bonus = jnp.sum(r_h * k_h * r_k, axis=-1, keepdims=True) * v_h
    out = (y + bonus.reshape(B, T, RWKV_WIDTH)) * g
    return out.astype(p.dtype), S_T.astype(wkv0.dtype), p[:, -1]


def _gla_chunkwise(q, k, v, log_f, S0):
    B, T, H, K = q.shape
    V = v.shape[-1]
    C = min(HGRN_CHUNK, T)
    nc = -(-T // C)
    pad = nc * C - T

    def blocks(t):
        t = jnp.pad(t, ((0, 0), (0, pad), (0, 0), (0, 0)))
        return t.reshape(B, nc, C, H, t.shape[-1]).transpose(1, 0, 3, 2, 4)

    causal = jnp.tril(jnp.ones((C, C), dtype=bool))

    def step(S, inp):
        qc, kc, vc, gc = inp
        b = jnp.cumsum(gc, axis=2)
        inter = jnp.einsum('bhtk,bhkv->bhtv', qc * jnp.exp(b), S)
        diff = jnp.where(causal[:, :, None], b[:, :, :, None, :] - b[:, :, None, :, :], -jnp.inf)
        scores = jnp.einsum('bhtk,bhtsk,bhsk->bhts', qc, jnp.exp(diff), kc)
        intra = jnp.einsum('bhts,bhsv->bhtv', scores, vc)
        b_last = b[:, :, -1:, :]
        S = (jnp.exp(b_last[:, :, 0, :])[..., None] * S
             + jnp.einsum('bhsk,bhsv->bhkv', kc * jnp.exp(b_last - b), vc))
        return S, inter + intra

    S_T, o = lax.scan(step, S0, tuple(blocks(t) for t in (q, k, v, log_f)))
    o = o.transpose(1, 0, 3, 2, 4).reshape(B, nc * C, H, V)[:, :T]
    return o, S_T


def _hgrn2_mixer(p, S0, lb, norm_g):
    B, T, _ = p.shape
    f32 = jnp.float32
    q, f_logit, i, og = jnp.split(p.astype(f32), 4, axis=-1)
    f = lb + (1.0 - lb) * jax.nn.sigmoid(f_logit)
    log_f = jnp.log(f)
    k = 1.0 - f

    def heads(t):
        return t.reshape(B, T, HGRN_HEADS, HGRN_HEAD)

    o, S_T = _gla_chunkwise(heads(jax.nn.silu(q)), heads(k), heads(i), heads(log_f), S0.astype(f32))
    o = o * lax.rsqrt(jnp.mean(o * o, axis=-1, keepdims=True) + RMS_EPS)
    o = o.reshape(B, T, HGRN_WIDTH) * norm_g * jax.nn.sigmoid(og)
    return o.astype(p.dtype), S_T.astype(S0.dtype)


def _trunk(x, wkv0, shift0, hgrn0, prm):
    lb_all = jnp.cumsum(jax.nn.softmax(prm['hgrn_lb'].astype(jnp.float32), axis=0), axis=0)
    new_wkv, new_shift, new_hgrn = [], [], []
    for l in range(DEPTH):
        h = _rms_norm(x, prm['norm_mix_g'][l])
        proj = h @ prm['w_in'][l]
        p_rwkv, p_hgrn, p_gate = jnp.split(proj, [RWKV_PROJ, RWKV_PROJ + HGRN_PROJ], axis=-1)
        o_a, wkv_l, shift_l = _rwkv7_mixer(
            p_rwkv, shift0[l], wkv0[l], prm['rwkv_mu'][l], prm['rwkv_w0'][l], prm['rwkv_w2'][l],
            prm['rwkv_a0'][l], prm['rwkv_a2'][l], prm['rwkv_g2'][l], prm['rwkv_k_k'][l],
            prm['rwkv_k_a'][l], prm['rwkv_r_k'][l], prm['rwkv_ln_g'][l], prm['rwkv_ln_b'][l])
        o_b, hgrn_l = _hgrn2_mixer(p_hgrn, hgrn0[l], lb_all[l], prm['hgrn_norm_g'][l])
        gate_a, gate_b = jnp.split(jax.nn.sigmoid(p_gate), 2, axis=-1)
        merged = gate_a * (o_a @ prm['w_up_a'][l]) + gate_b * (o_b @ prm['w_up_b'][l])
        x = x + merged @ prm['w_out'][l]
        h = _rms_norm(x, prm['norm_ffn_g'][l])
        x = x + (jax.nn.silu(h @ prm['w_ffn_gate'][l]) * (h @ prm['w_ffn_up'][l])) @ prm['w_ffn_down'][l]
        new_wkv.append(wkv_l)
        new_shift.append(shift_l)
        new_hgrn.append(hgrn_l)
    y = _rms_norm(x, prm['norm_final_g'])
    return y, jnp.stack(new_wkv), jnp.stack(new_shift), jnp.stack(new_hgrn)


def setup_inputs(seed: int = 0) -> dict:
    key = jax.random.key(seed)
    ks = jax.random.split(key, 32)
    f32 = jnp.float32

    def nrm(k, shape, scale):
        return jax.random.normal(k, shape, f32) * scale

    L = DEPTH
    return {
        'x_prompt': nrm(ks[0], (BATCH, SEQ, D_MODEL), 1.0),
        'x_sample': nrm(ks[1], (DEC_BATCH, DEC_SEQ, D_MODEL), 1.0),
        'state_rwkv_wkv': nrm(ks[2], (L, DEC_BATCH, RWKV_HEADS, RWKV_HEAD, RWKV_HEAD), 0.5),
        'state_rwkv_shift': nrm(ks[3], (L, DEC_BATCH, RWKV_PROJ), 1.0),
        'state_hgrn': nrm(ks[4], (L, DEC_BATCH, HGRN_HEADS, HGRN_HEAD, HGRN_HEAD), 0.5),
        'norm_mix_g': 1.0 + nrm(ks[5], (L, D_MODEL), 0.02),
        'w_in': nrm(ks[6], (L, D_MODEL, IN_PROJ), D_MODEL ** -0.5),
        'rwkv_mu': jax.random.uniform(ks[7], (L, RWKV_PROJ), f32),
        'rwkv_w0': -1.0 + nrm(ks[8], (L, RWKV_WIDTH), 0.3),
        'rwkv_w2': nrm(ks[9], (L, RWKV_DECAY_LORA, RWKV_WIDTH), 0.1),
        'rwkv_a0': nrm(ks[10], (L, RWKV_WIDTH), 0.1),
        'rwkv_a2': nrm(ks[11], (L, RWKV_A_LORA, RWKV_WIDTH), RWKV_A_LORA ** -0.5),
        'rwkv_g2': nrm(ks[12], (L, RWKV_GATE_LORA, RWKV_WIDTH), RWKV_GATE_LORA ** -0.5),
        'rwkv_k_k': 0.85 + nrm(ks[13], (L, RWKV_WIDTH), 0.02),
        'rwkv_k_a': 1.0 + nrm(ks[14], (L, RWKV_WIDTH), 0.02),
        'rwkv_r_k': nrm(ks[15], (L, RWKV_HEADS, RWKV_HEAD), 0.1),
        'rwkv_ln_g': 1.0 + nrm(ks[16], (L, RWKV_WIDTH), 0.02),
        'rwkv_ln_b': nrm(ks[17], (L, RWKV_WIDTH), 0.02),
        'w_up_a': nrm(ks[18], (L, RWKV_WIDTH, D_MODEL), RWKV_WIDTH ** -0.5),
        'hgrn_lb': nrm(ks[19], (L + 1, HGRN_WIDTH), 0.1),
        'hgrn_norm_g': 1.0 + nrm(ks[20], (L, HGRN_WIDTH), 0.02),
        'w_up_b': nrm(ks[21], (L, HGRN_WIDTH, D_MODEL), HGRN_WIDTH ** -0.5),
        'w_out': nrm(ks[22], (L, D_MODEL, D_MODEL), D_MODEL ** -0.5),
        'norm_ffn_g': 1.0 + nrm(ks[23], (L, D_MODEL), 0.02),
        'w_ffn_gate': nrm(ks[24], (L, D_MODEL, D_FF), D_MODEL ** -0.5),
        'w_ffn_up': nrm(ks[25], (L, D_MODEL, D_FF), D_MODEL ** -0.5),
        'w_ffn_down': nrm(ks[26], (L, D_FF, D_MODEL), D_FF ** -0.5),
        'norm_final_g': 1.0 + nrm(ks[27], (D_MODEL,), 0.02),
    }


def reference(x_prompt, x_sample, state_rwkv_wkv, state_rwkv_shift, state_hgrn,
              norm_mix_g, w_in, rwkv_mu, rwkv_w0, rwkv_w2, rwkv_a0, rwkv_a2, rwkv_g2,
              rwkv_k_k, rwkv_k_a, rwkv_r_k, rwkv_ln_g, rwkv_ln_b, w_up_a, hgrn_lb,
              hgrn_norm_g, w_up_b, w_out, norm_ffn_g, w_ffn_gate, w_ffn_up, w_ffn_down,
              norm_final_g):
    prm = dict(norm_mix_g=norm_mix_g, w_in=w_in, rwkv_mu=rwkv_mu, rwkv_w0=rwkv_w0,
               rwkv_w2=rwkv_w2, rwkv_a0=rwkv_a0, rwkv_a2=rwkv_a2, rwkv_g2=rwkv_g2,
               rwkv_k_k=rwkv_k_k, rwkv_k_a=rwkv_k_a, rwkv_r_k=rwkv_r_k, rwkv_ln_g=rwkv_ln_g,
               rwkv_ln_b=rwkv_ln_b, w_up_a=w_up_a, hgrn_lb=hgrn_lb, hgrn_norm_g=hgrn_norm_g,
               w_up_b=w_up_b, w_out=w_out, norm_ffn_g=norm_ffn_g, w_ffn_gate=w_ffn_gate,
               w_ffn_up=w_ffn_up, w_ffn_down=w_ffn_down, norm_final_g=norm_final_g)
    bp = x_prompt.shape[0]
    dt = x_prompt.dtype
    wkv_zero = jnp.zeros((DEPTH, bp, RWKV_HEADS, RWKV_HEAD, RWKV_HEAD), dt)
    shift_zero = jnp.zeros((DEPTH, bp, RWKV_PROJ), dt)
    hgrn_zero = jnp.zeros((DEPTH, bp, HGRN_HEADS, HGRN_HEAD, HGRN_HEAD), dt)
    y_prompt, wkv_p, shift_p, hgrn_p = _trunk(x_prompt, wkv_zero, shift_zero, hgrn_zero, prm)
    y_sample, wkv_s, shift_s, hgrn_s = _trunk(x_sample, state_rwkv_wkv, state_rwkv_shift, state_hgrn, prm)
    return (y_prompt, y_sample, wkv_p, shift_p, hgrn_p, wkv_s, shift_s, hgrn_s)
```

```python
from contextlib import ExitStack
import numpy as np
import concourse.bass as bass
import concourse.mybir as mybir
from concourse.bass_utils import run_bass_kernel_spmd

F32 = mybir.dt.float32
BF16 = mybir.dt.bfloat16
AF = mybir.ActivationFunctionType
ALU = mybir.AluOpType
AX = mybir.AxisListType

COMPUTE = ("pe", "act", "dve", "pool")


class Op:
    __slots__ = ("idx", "eng", "fn", "call", "is_dma", "deps", "val", "marked", "slot", "acc", "dur", "xfer", "seq", "mode")

    def __init__(self, idx, eng, fn, call, is_dma):
        self.idx = idx
        self.eng = eng
        self.fn = fn
        self.call = call
        self.is_dma = is_dma
        self.deps = ()
        self.val = None
        self.marked = False
        self.slot = None
        self.acc = None
        self.dur = 100.0
        self.xfer = 0.0
        self.seq = 0
        self.mode = None


class _Rec:
    def __init__(self):
        self.call = None

    def __getattr__(self, name):
        def f(*a, **k):
            self.call = (name, a, k)
            return self
        return f


_WRITE_KW = ("out", "accum_out", "ap")
_ESZ = {}


def _esz(dt):
    if dt not in _ESZ:
        _ESZ[dt] = 4 if dt == F32 else 2
    return _ESZ[dt]


def _ap_intervals(ap):
    sp = str(ap.space)
    esz = _esz(ap.dtype)
    dims = list(ap.ap)
    if sp == "DRAM":
        base = ap.offset
        p0, p1 = 0, 1
        fd = dims
    else:
        pstride, pcnt = dims[0]
        p0 = ap.offset // pstride
        base = ap.offset % pstride
        p1 = p0 + pcnt
        fd = dims[1:]
    fd = sorted([(st_, c) for (st_, c) in fd if c > 1 and st_ != 0])
    run = 1
    k = 0
    while k < len(fd) and fd[k][0] == run:
        run *= fd[k][1]
        k += 1
    outer = fd[k:]
    n_outer = 1
    for st_, c in outer:
        n_outer *= c
    if n_outer <= 32:
        offs = [0]
        for st_, c in outer:
            offs = [o + i * st_ for o in offs for i in range(c)]
        iv = [((base + o) * esz, (base + o + run) * esz) for o in offs]
    else:
        ext = sum((c - 1) * st_ for st_, c in outer) + run
        iv = [(base * esz, (base + ext) * esz)]
    return sp, ap.name, p0, p1, iv


def _call_accesses(call):
    name, a, k = call
    items = []
    if name == "matmul":
        items.append((a[0] if a else k.get("out"), True))
        for kw in ("lhsT", "rhs"):
            items.append((k[kw], False))
    elif name == "memset":
        items.append((a[0] if a else k.get("ap"), True))
    else:
        for kw, v in k.items():
            if hasattr(v, "ap") and hasattr(v, "space"):
                items.append((v, kw in _WRITE_KW))
        for v in a:
            if hasattr(v, "ap") and hasattr(v, "space"):
                items.append((v, False))
    acc = []
    for ap, is_w in items:
        if ap is None:
            continue
        sp, nm, p0, p1, iv = _ap_intervals(ap)
        if sp == "PSUM":
            banks = set()
            for b0, b1 in iv:
                for bk in range(b0 // 2048, (b1 - 1) // 2048 + 1):
                    banks.add(bk)
            for bk in banks:
                acc.append((("ps", bk), 0, 128, 0, 1, is_w))
        else:
            for b0, b1 in iv:
                acc.append(((sp, nm), p0, p1, b0, b1, is_w))
    return acc


def _free_elems(ap):
    n = 1
    for s_ in ap.shape[1:]:
        n *= s_
    return n


_WI = {"fp32": 1.0, "dve": 1.0, "act": 1.0, "pe_small": 1.0, "pe_big": 1.0, "lat": 1.0, "pool": 1.0}


def _est(op):
    _est0(op)
    name = op.call[0]
    if op.is_dma:
        return
    if name == "matmul":
        k = op.call[2]
        if k["lhsT"].dtype == F32:
            op.dur *= _WI["fp32"]
        elif _free_elems(k["rhs"]) >= 512:
            op.dur *= _WI["pe_big"]
        else:
            op.dur *= _WI["pe_small"]
    elif op.eng in ("dve", "act", "pool"):
        op.dur *= _WI[op.eng]


def _est0(op):
    name, a, k = op.call
    if op.is_dma:
        ap = k.get("out")
        nbytes = 1
        for s_ in ap.shape:
            nbytes *= s_
        nbytes *= _esz(ap.dtype)
        op.dur = 1200.0 if op.eng == "pool" else 150.0
        op.xfer = 2000.0 + nbytes / 150.0
        return
    def _rnd(v):
        return 32 if v <= 32 else (64 if v <= 64 else 128)
    if name == "matmul":
        n = _free_elems(k["rhs"])
        f = 4.0 if k["lhsT"].dtype == F32 else 1.0
        op.dur = (16.0 + max(n, 48) * 0.42) * f
        op.mode = (_rnd(k["lhsT"].shape[0]), _rnd(_free_elems(k["lhsT"])), f)
    elif name == "transpose":
        op.dur = 70.0
        op.mode = ("T", _rnd(k["in_"].shape[0]), _rnd(_free_elems(k["in_"])))
    elif op.eng == "act":
        op.dur = 120.0 + _free_elems(k["out"]) * 0.85
        fn_ = k.get("func")
        if fn_ in (AF.Exp, AF.Ln):
            op.mode = "A"
        elif fn_ in (AF.Sigmoid, AF.Tanh):
            op.mode = "B"
        elif fn_ == AF.Silu:
            op.mode = "C"
    else:
        o = k.get("out") if "out" in k else (a[0] if a else None)
        n = _free_elems(o) if o is not None else 64
        if op.eng == "pool":
            op.dur = 100.0 + n * 2.2
        else:
            op.dur = 70.0 + n * 1.17


class Sched:
    def __init__(self, nc, dma_slots=None, reorder=True):
        self.nc = nc
        self.ops = []
        self.dma_slots = dma_slots or {"sp": 8, "act": 2, "pool": 6, "dve": 2, "pe": 2}
        self.reorder = reorder
        self.tag = ""
        self.tags = []

    def _mk(self, eng, fn, is_dma):
        rec = _Rec()
        fn(rec)
        name, a, k = rec.call
        op = Op(len(self.ops), eng, (lambda e: getattr(e, name)(*a, **k)), (name, a, k), is_dma)
        self.ops.append(op)
        self.tags.append(self.tag)
        return op

    def add(self, eng, fn, reads=(), writes=()):
        return self._mk(eng, fn, False)

    def dma(self, eng, fn, reads=(), writes=()):
        return self._mk(eng, fn, True)

    def barrier(self):
        pass

    def _build_dag(self):
        wlog = {}
        rlog = {}
        for op in self.ops:
            acc = _call_accesses(op.call)
            _est(op)
            deps = set()
            for key, p0, p1, b0, b1, is_w in acc:
                for (q0, q1, c0, c1, oi, oe) in wlog.get(key, ()):
                    if q0 < p1 and p0 < q1 and c0 < b1 and b0 < c1:
                        deps.add(oi)
                if is_w or key[0] == "ps":
                    for (q0, q1, c0, c1, oi, oe) in rlog.get(key, ()):
                        if q0 < p1 and p0 < q1 and c0 < b1 and b0 < c1:
                            if is_w or oe != op.eng:
                                deps.add(oi)
            for key, p0, p1, b0, b1, is_w in acc:
                if is_w:
                    for lg in (wlog, rlog):
                        l_ = lg.get(key)
                        if l_:
                            lg[key] = [e_ for e_ in l_ if not (p0 <= e_[0] and e_[1] <= p1 and b0 <= e_[2] and e_[3] <= b1)]
                    wlog.setdefault(key, []).append((p0, p1, b0, b1, op.idx, op.eng))
                else:
                    rlog.setdefault(key, []).append((p0, p1, b0, b1, op.idx, op.eng))
            deps.discard(op.idx)
            op.deps = tuple(sorted(deps))

    def _list_schedule(self):
        import heapq
        ops = self.ops
        n = len(ops)
        if not self.reorder:
            return list(range(n))
        npred = [len(o.deps) for o in ops]
        succ = [[] for _ in range(n)]
        for o in ops:
            for d in o.deps:
                succ[d].append(o.idx)
        ready_t = [0.0] * n
        fin = [0.0] * n
        import os as _os2
        PRI = _os2.environ.get("KPRI", "5")
        prio = list(range(n))
        if PRI != "idx":
            cp = [0.0] * n
            for i_ in range(n - 1, -1, -1):
                m = 0.0
                for s_ in succ[i_]:
                    if cp[s_] > m:
                        m = cp[s_]
                cp[i_] = m + ops[i_].dur + ops[i_].xfer + 200.0
            w_ = float(PRI)
            rank = sorted(range(n), key=lambda i_: (i_ - w_ * cp[i_] / 100.0))
            for r_, i_ in enumerate(rank):
                prio[i_] = r_
        engs = ("pe", "act", "dve", "pool", "sp")
        future = {e: [] for e in engs}
        avail = {e: [] for e in engs}
        free = {e: 0.0 for e in engs}
        for o in ops:
            if npred[o.idx] == 0:
                heapq.heappush(future[o.eng], (0.0, o.idx))
        order = []
        self.start_t = {}
        pe_mode = [None]
        act_set = [None]
        WINDOW = 3000
        done_upto = 0
        sched = [False] * n
        while len(order) < n:
            best = None
            for e in engs:
                fu, av = future[e], avail[e]
                while fu and fu[0][0] <= free[e]:
                    t_, i_ = heapq.heappop(fu)
                    heapq.heappush(av, (prio[i_], i_))
                if av:
                    pick = av[0]
                    if e == "pe" and len(av) > 1 and ops[pick[1]].mode != pe_mode[0]:
                        best_same = None
                        for i2 in av:
                            if ops[i2[1]].mode == pe_mode[0] and (best_same is None or i2 < best_same):
                                best_same = i2
                        if best_same is not None and best_same[0] - pick[0] < 400:
                            pick = best_same
                    if e == "act" and len(av) > 1 and ops[pick[1]].mode not in (None, act_set[0]):
                        best_same = None
                        for i2 in av:
                            if ops[i2[1]].mode in (None, act_set[0]) and (best_same is None or i2 < best_same):
                                best_same = i2
                        if best_same is not None and best_same[0] - pick[0] < 300:
                            pick = best_same
                    cand = (free[e], pick[0], e, True, pick)
                elif fu:
                    cand = (fu[0][0], prio[fu[0][1]], e, False, fu[0])
                else:
                    continue
                if best is None or cand[:2] < best[:2]:
                    best = cand
            start, _pr, e, from_av, item = best
            if from_av:
                i_ = item[1]
                if avail[e][0] == item:
                    heapq.heappop(avail[e])
                else:
                    avail[e].remove(item)
                    heapq.heapify(avail[e])
            else:
                i_ = item[1]
                heapq.heappop(future[e])
            o = ops[i_]
            if e == "pe":
                if o.mode != pe_mode[0]:
                    start += 300.0
                pe_mode[0] = o.mode
            elif e == "act" and o.mode is not None:
                if o.mode != act_set[0]:
                    start += 1300.0
                act_set[0] = o.mode
            free[e] = start + o.dur
            self.start_t[i_] = start
            fin[i_] = start + o.dur + o.xfer
            order.append(i_)
            for s_ in succ[i_]:
                lat = (40.0 if (ops[s_].eng == e and e == "pe" and not o.is_dma) else (150.0 if ops[s_].eng == e else 400.0)) * _WI["lat"]
                if fin[i_] + lat > ready_t[s_]:
                    ready_t[s_] = fin[i_] + lat
                npred[s_] -= 1
                if npred[s_] == 0:
                    heapq.heappush(future[ops[s_].eng], (ready_t[s_], s_))
        self.est_makespan = max(fin) if fin else 0.0
        return order

    def finalize(self):
        self._build_dag()
        order = self._list_schedule()
        self.order = order
        ops = self.ops
        dma_count = {}
        slot_last = {}
        extra = {}
        seqc = {}
        for i_ in order:
            op = ops[i_]
            seqc[op.eng] = seqc.get(op.eng, 0) + 1
            op.seq = seqc[op.eng]
            if op.is_dma:
                n_ = dma_count.get(op.eng, 0)
                dma_count[op.eng] = n_ + 1
                k = self.dma_slots[op.eng]
                op.slot = (op.eng, n_ % k)
                op.val = 16 * (n_ // k + 1)
                prev = slot_last.get(op.slot)
                if prev is not None:
                    extra[i_] = prev
                slot_last[op.slot] = i_
        known = {}
        vc = {}
        self.waits = {}
        for i_ in order:
            op = ops[i_]
            K = known.setdefault(op.eng, {})
            deps = list(op.deps)
            if i_ in extra:
                deps.append(extra[i_])
            cand = []
            for d in deps:
                p = ops[d]
                if (not p.is_dma) and p.eng == "pe" and op.eng == "pe" and not op.is_dma:
                    continue
                cand.append(d)
            cand.sort(key=lambda d: -ops[d].seq)
            w = []
            for d in cand:
                p = ops[d]
                key = ("dma", p.slot) if p.is_dma else ("eng", p.eng)
                val = p.val if p.is_dma else p.seq
                if K.get(key, -1) >= val:
                    continue
                w.append(d)
                p.marked = True
                for kk, vv in vc[d].items():
                    if K.get(kk, -1) < vv:
                        K[kk] = vv
            self.waits[i_] = w
            v = dict(K)
            if op.is_dma:
                v[("dma", op.slot)] = op.val
            else:
                if v.get(("eng", op.eng), -1) < op.seq and op.eng == "pe":
                    pass
                v[("eng", op.eng)] = max(v.get(("eng", op.eng), -1), op.seq)
            vc[i_] = v
        cnt = {}
        for i_ in order:
            op = ops[i_]
            if not op.is_dma and op.marked:
                cnt[op.eng] = cnt.get(op.eng, 0) + 1
                op.val = cnt[op.eng]
        self.counts = cnt

    def emit(self, stack):
        nc = self.nc
        sems = {}
        for e in COMPUTE:
            sems[("eng", e)] = stack.enter_context(nc.semaphore("s_" + e))
        for e, k in self.dma_slots.items():
            for i in range(k):
                sems[("dma", (e, i))] = stack.enter_context(nc.semaphore("d_%s%d" % (e, i)))
        block = stack.enter_context(nc.Block())
        ops = self.ops
        order = self.order
        waits = self.waits

        def run(engname, eng):
            for i_ in order:
                op = ops[i_]
                if op.eng != engname:
                    continue
                for d in waits[i_]:
                    p = ops[d]
                    s = sems[("dma", p.slot)] if p.is_dma else sems[("eng", p.eng)]
                    eng.wait_ge(s, p.val)
                ins = op.fn(eng)
                if op.is_dma:
                    ins.then_inc(sems[("dma", op.slot)], 16)
                elif op.marked:
                    ins.then_inc(sems[("eng", op.eng)], 1)
            last = {}
            for i_ in order:
                op = ops[i_]
                if op.is_dma and op.eng == engname:
                    last[op.slot] = op.val
            for slot, v in last.items():
                eng.wait_ge(sems[("dma", slot)], v)

        @block.sync
        def _(e):
            run("sp", e)

        @block.tensor
        def _(e):
            run("pe", e)

        @block.scalar
        def _(e):
            run("act", e)

        @block.vector
        def _(e):
            run("dve", e)

        @block.gpsimd
        def _(e):
            run("pool", e)


D = 1024
T = 2048
NS = 16
HALF = 1024
TW = HALF + NS
RW = 512
RPROJ = 1792
HW_ = 512
DFF = 2816
INP = 5888
CDEC = -0.6065306597126334
GN_EPS = 64e-5
RMS_EPS = 1e-6

PF = {}
_c = 0
for _n, _w in (("mu", 14), ("w0", 4), ("a0", 4), ("k_k", 4), ("k_a", 4), ("r_k", 4), ("ln_g", 4), ("ln_b", 4),
               ("lb0", 4), ("lb1", 4), ("hng", 4)):
    PF[_n] = _c
    _c += _w
PF_N = _c
DV = {"omu": 0, "omka": 14, "lb": 18, "omlb": 22}
DV_N = 29

CO = {"ident": 0, "blk64": 128, "ones": 256, "maskNM": 384, "maskL": 512, "maskI": 576, "cmask": 640}
CO_N = 640 + 1024 + 256


def make_consts():
    c = np.zeros((128, CO_N), np.float32)
    p = np.arange(128)
    c[:, 0:128] = np.eye(128, dtype=np.float32)
    c[:, 128:256] = (p[:, None] // 64 == p[None, :] // 64).astype(np.float32)
    c[:, 256:384] = 1.0
    s = (p % 64)[:, None]
    t = np.arange(64)[None, :]
    c[:, 384:448] = (s < t)
    c[:, 448:512] = (s <= t)
    c[:, 512:576] = (s > t)
    c[:, 576:640] = (s == t)
    tt = np.arange(1024)
    c[:, 640:1664] = (tt % 64 != 0).astype(np.float32)[None, :]
    c[:, 1664:1920] = np.eye(16, dtype=np.float32).reshape(-1)[None, :]
    return c


class Arena:
    def __init__(self, tile, nwords):
        self.t = tile
        self.n = nwords
        self.top = 0

    def alloc(self, dtype, shape):
        free = 1
        for s in shape[1:]:
            free *= s
        words = free if dtype == F32 else (free + 1) // 2
        words = (words + 7) // 8 * 8
        off = self.top
        self.top += words
        assert self.top <= self.n, ("arena overflow", self.top, self.n)
        v = self.t[0:shape[0], off:off + words]
        if dtype == BF16:
            v = v.bitcast(BF16)
        v = v[:, 0:free]
        if len(shape) > 2:
            names = ["d%d" % i for i in range(len(shape) - 1)]
            pat = "p (" + " ".join(names) + ") -> p " + " ".join(names)
            kw = {names[i]: shape[i + 1] for i in range(len(names))}
            v = v.rearrange(pat, **kw)
        return v


ARENA_WORDS = 32560


def build_program(dbg=None, passes=(0, 1), stop_after=None):
    nc = bass.Bass("TRN2", target_bir_lowering=False)

    def din(name, shape):
        return nc.dram_tensor(name, list(shape), F32, kind="ExternalInput").ap()

    def dout(name, shape):
        return nc.dram_tensor(name, list(shape), F32, kind="ExternalOutput").ap()

    x_p = din("x_p", [T, D])
    x_s = din("x_s", [NS, D])
    wkv_s = din("wkv_s", [NS * 8, 4096])
    shift_s = din("shift_s", [NS, RPROJ])
    hgrn_s = din("hgrn_s", [NS, 4, 128, 128])
    w_in = din("w_in", [D, INP])
    w2a2 = din("w2a2", [128, 512])
    g2 = din("g2", [128, 512])
    w_up_a = din("w_up_a", [RW, D])
    w_up_b = din("w_up_b", [HW_, D])
    w_out = din("w_out", [D, D])
    w_fg = din("w_fg", [D, DFF])
    w_fu = din("w_fu", [D, DFF])
    w_fd = din("w_fd", [DFF, D])
    pfm_d = din("pfm", [128, PF_N])
    consts_d = din("consts", [128, CO_N])
    g_mix = din("g_mix", [D])
    g_ffn = din("g_ffn", [D])
    g_fin = din("g_fin", [D])

    y_p = dout("y_p", [T, D])
    y_s = dout("y_s", [NS, D])
    o_wkv_p = dout("o_wkv_p", [8, 64, 64])
    o_shift_p = dout("o_shift_p", [RPROJ])
    o_hgrn_p = dout("o_hgrn_p", [4, 128, 128])
    o_wkv_s = dout("o_wkv_s", [NS * 8, 4096])
    o_shift_s = dout("o_shift_s", [NS, RPROJ])
    o_hgrn_s = dout("o_hgrn_s", [NS, 4, 128, 128])
    dbg_out = {}
    if dbg:
        for k, shp in dbg.items():
            dbg_out[k] = dout("dbg_" + k, shp)
    scr_a = nc.dram_tensor("scr_a", [NS, 6, 512], F32).ap()
    scr_y = nc.dram_tensor("scr_y", [NS, 512], F32).ap()

    st = ExitStack()
    with st:
        def sb(name, shape, dt):
            return st.enter_context(nc.sbuf_tensor(name, list(shape), dt))

        arena_t = sb("arena", [128, ARENA_WORDS], F32)
        hT = sb("hT", [128, 8, TW], BF16)
        oa = sb("oa", [128, 4, TW], BF16)
        ob = sb("ob", [128, 4, TW], BF16)
        NWB = 4
        wbuf = [sb("wbuf%d" % i, [128, 8, 512], BF16) for i in range(NWB)]
        consts = sb("consts_sb", [128, 384], F32)
        ident_bf = sb("ident_bf", [128, 128], BF16)
        masks_bf = sb("masks_bf", [128, 256], BF16)
        pfm = sb("pfm_sb", [128, PF_N], F32)
        dv = sb("dv", [128, DV_N], F32)
        lw_w = sb("lw_w", [128, 512], BF16)
        g2_w = sb("g2_w", [128, 512], BF16)
        carry = sb("carry", [128, 14], F32)
        shiftS = sb("shiftS", [128, 14, NS], F32)
        sprevT = sb("sprevT", [128, 14, NS], F32)
        H0f = sb("H0f", [128, 4, 64], F32)
        H0bd = sb("H0bd", [128, 4, 128], BF16)
        S0f = sb("S0f", [128, 4, 128], F32)
        S0b = sb("S0b", [128, 4, 128], BF16)
        WC = sb("WC", [128, 4, 16], F32)
        DC = sb("DC", [128, 4, 16], F32)
        stat = sb("stat", [128, 64], F32)
        ftmp = sb("ftmp", [128, 2, 512], F32)
        ps = st.enter_context(nc.psum_tensor("ps", [128, 8, 512], F32))

        ident = consts[:, 0:128]
        blk64 = consts[:, 128:256]
        ones = consts[:, 256:384]
        maskNM_bf = masks_bf[:, 0:128]
        maskL_bf = masks_bf[:, 128:192]
        maskI_bf = masks_bf[:, 192:256]

        S = Sched(nc)
        A = S.add
        bank_ctr = [0]

        def nb(n=1):
            b = bank_ctr[0]
            if n > 1:
                b = (b + n - 1) // n * n
            if b + n > 8:
                b = 0
            bank_ctr[0] = (b + n) % 8
            return b

        def nbp(par):
            b = bank_ctr[0]
            if b % 2 != par:
                b = (b + 1) % 8
            bank_ctr[0] = (b + 1) % 8
            return b

        par_ctr = [0]
        seq_ctr = [0, 0]

        par_nb = [6]

        def nb_par(n=1):
            m = par_nb[0]
            if n == 2:
                par_ctr[0] = (par_ctr[0] + 1) // 2 * 2
            b = par_ctr[0] % m
            par_ctr[0] = (par_ctr[0] + n) % m
            return b

        smp_ctr = [0]

        def nb_smp():
            smp_ctr[0] += 1
            return 4 + smp_ctr[0] % 2

        def nb_seq(par):
            return 6 + par

        def PB(b, n=1):
            return ["ps%d" % (b + i) for i in range(n)]

        def pf(name, j=0):
            c = PF[name] + j
            return pfm[:, c:c + 1]

        def dvc(name, j=0):
            c = DV[name] + j
            return dv[:, c:c + 1]

        wctr = [0]

        def load_w(src_ap, shape_view):
            i = wctr[0] % NWB
            wctr[0] += 1
            a, b = shape_view
            view = wbuf[i][:, :, :].rearrange("p a b -> p (a b)")[:, 0:a * b].rearrange("p (a b) -> p a b", a=a)
            rn = "wbuf%d" % i
            S.dma("pool", lambda e, view=view, src_ap=src_ap: e.dma_start(out=view, in_=src_ap), writes=[rn])
            return view, rn

        def dump(name, ap_sb, res):
            if dbg and name in dbg_out:
                S.dma("pool", lambda e: e.dma_start(out=dbg_out[name], in_=ap_sb), reads=res)

        S.dma("sp", lambda e: e.dma_start(out=consts[:], in_=consts_d[:, 0:384]), writes=["consts"])
        S.dma("pool", lambda e: e.dma_start(out=masks_bf[:], in_=consts_d[:, 384:640]))
        S.dma("sp", lambda e: e.dma_start(out=pfm[:], in_=pfm_d), writes=["pfm"])
        S.dma("pool", lambda e: e.dma_start(out=lw_w[:], in_=w2a2), writes=["lw_w"])
        S.dma("pool", lambda e: e.dma_start(out=g2_w[:], in_=g2), writes=["g2_w"])
        A("dve", lambda e: e.tensor_copy(out=ident_bf[:], in_=ident), ["consts"], ["ident_bf"])
        A("dve", lambda e: e.memset(carry[:], 0.0), [], ["carry"])
        A("dve", lambda e: e.memset(H0f[:], 0.0), [], ["H0f"])
        A("dve", lambda e: e.memset(H0bd[:], 0.0), [], ["H0b"])
        A("dve", lambda e: e.memset(S0f[:], 0.0), [], ["S0f"])
        A("dve", lambda e: e.memset(S0b[:], 0.0), [], ["S0b"])
        A("dve", lambda e: e.memset(sprevT[:], 0.0), [], ["sprevT"])
        A("dve", lambda e: e.tensor_scalar(out=dv[:, 0:14], in0=pfm[:, PF["mu"]:PF["mu"] + 14], scalar1=-1.0, scalar2=1.0,
                                            op0=ALU.mult, op1=ALU.add), ["pfm"], ["dv"])
        A("dve", lambda e: e.tensor_scalar(out=dv[:, 14:18], in0=pfm[:, PF["k_a"]:PF["k_a"] + 4], scalar1=-1.0, scalar2=1.0,
                                            op0=ALU.mult, op1=ALU.add), ["pfm"], ["dv"])
        A("dve", lambda e: e.tensor_tensor(out=dv[:, 18:22], in0=pfm[:, PF["lb0"]:PF["lb0"] + 4],
                                            in1=pfm[:, PF["lb1"]:PF["lb1"] + 4], op=ALU.subtract), ["pfm"], ["dv"])
        A("act", lambda e: e.activation(out=dv[:, 18:22], in_=dv[:, 18:22], func=AF.Sigmoid), ["dv"], ["dv"])
        A("dve", lambda e: e.tensor_scalar(out=dv[:, 22:26], in0=dv[:, 18:22], scalar1=-1.0, scalar2=1.0,
                                            op0=ALU.mult, op1=ALU.add), ["dv"], ["dv"])

        eps24, epsgn, epsrms = dv[:, 26:27], dv[:, 27:28], dv[:, 28:29]
        A("dve", lambda e: e.memset(dv[:, 26:27], 1e-24))
        A("dve", lambda e: e.memset(dv[:, 27:28], GN_EPS))
        A("dve", lambda e: e.memset(dv[:, 28:29], RMS_EPS))
        w_in_v = w_in.rearrange("(kc p) n -> p kc n", p=128)

        for pas in passes:
            import os as _os
            _skip = _os.environ.get("KSKIP", "")
            nsamp = NS if (pas == 0 and "nosamp" not in _skip) else 0
            W = HALF + nsamp
            blocks = [(0, 512), (512, 1024)] + ([(1024, 1040)] if nsamp else [])
            t0 = pas * HALF
            S.barrier()
            ar_ = Arena(arena_t, ARENA_WORDS)

            def norm_phase(tag, x_tok, xs_tok, g_dram, arena, out_cb, src_loaded, xsres):
                gB = arena.alloc(F32, [128, 1024])
                junk = arena.alloc(F32, [128, 1024])
                h_tok = arena.alloc(BF16, [128, 8, 1024])
                hs_tok = arena.alloc(BF16, [NS, 1024])
                R = tag
                S.dma("sp", lambda e: e.dma_start(out=gB, in_=g_dram.partition_broadcast(128)), writes=[R + "gB"])
                A("dve", lambda e: e.memset(stat[:, 0:32], 0.0), [], ["stat"])
                for tt in range(8):
                    A("act", lambda e, tt=tt: e.activation(out=junk, in_=x_tok[:, tt, :], func=AF.Square,
                                                           accum_out=stat[:, tt:tt + 1]),
                      [src_loaded(tt)], [R + "junk", "stat"])
                if nsamp:
                    A("act", lambda e: e.activation(out=junk[0:NS, :], in_=xs_tok, func=AF.Square,
                                                    accum_out=stat[0:NS, 8:9]), [xsres], [R + "junk", "stat"])
                A("dve", lambda e: e.tensor_scalar(out=stat[:, 16:25], in0=stat[:, 0:9], scalar1=1.0 / D, scalar2=RMS_EPS,
                                                    op0=ALU.mult, op1=ALU.add), ["stat"], ["stat"])
                A("act", lambda e: e.activation(out=stat[:, 16:25], in_=stat[:, 16:25], func=AF.Ln), ["stat"], ["stat"])
                A("act", lambda e: e.activation(out=stat[:, 16:25], in_=stat[:, 16:25], func=AF.Exp, scale=-0.5), ["stat"], ["stat"])
                out_cb(gB, h_tok, hs_tok, R)

            x_tok = ar_.alloc(F32, [128, 8, 1024])
            xs_tok = ar_.alloc(F32, [NS, 1024])
            for tt in range(8):
                S.dma("sp", lambda e, tt=tt: e.dma_start(out=x_tok[:, tt, :], in_=x_p[t0 + tt * 128:t0 + (tt + 1) * 128, :]),
                      writes=["P0x%d" % tt])
            if nsamp:
                S.dma("sp", lambda e: e.dma_start(out=xs_tok, in_=x_s), writes=["P0xs"])

            def to_hT(gB, h_tok, hs_tok, R, x_tok_, xs_tok_, xres, xsres):
                for tt in range(8):
                    A("dve", lambda e, tt=tt: e.scalar_tensor_tensor(out=h_tok[:, tt, :], in0=x_tok_[:, tt, :],
                                                                     scalar=stat[:, 16 + tt:17 + tt], in1=gB,
                                                                     op0=ALU.mult, op1=ALU.mult),
                      [xres(tt), "stat", R + "gB"], [R + "htok%d" % tt])
                    b = nb()
                    pT = ps[:, b, :].bitcast(BF16).rearrange("p (c t) -> p c t", c=8)
                    for dc in range(8):
                        A("pe", lambda e, tt=tt, dc=dc, pT=pT: e.transpose(out=pT[:, dc, :], in_=h_tok[:, tt, dc * 128:(dc + 1) * 128],
                                                                           identity=ident_bf[:]),
                          [R + "htok%d" % tt, "ident_bf"], PB(b))
                    eng = "act" if tt % 2 == 0 else "dve"
                    if eng == "act":
                        A("act", lambda e, tt=tt, pT=pT: e.copy(out=hT[:, :, tt * 128:(tt + 1) * 128], in_=pT), PB(b), ["hT"])
                    else:
                        A("dve", lambda e, tt=tt, pT=pT: e.tensor_copy(out=hT[:, :, tt * 128:(tt + 1) * 128], in_=pT), PB(b), ["hT"])
                if nsamp:
                    A("dve", lambda e: e.scalar_tensor_tensor(out=hs_tok, in0=xs_tok_, scalar=stat[0:NS, 24:25], in1=gB[0:NS, :],
                                                              op0=ALU.mult, op1=ALU.mult), [xsres, "stat", R + "gB"], [R + "hstok"])
                    b = nb()
                    pT = ps[:, b, :].bitcast(BF16).rearrange("p (c t) -> p c t", c=8)
                    for dc in range(8):
                        A("pe", lambda e, dc=dc, pT=pT: e.transpose(out=pT[:, dc, 0:NS], in_=hs_tok[:, dc * 128:(dc + 1) * 128],
                                                                    identity=ident_bf[0:NS, 0:NS]),
                          [R + "hstok", "ident_bf"], PB(b))
                    A("act", lambda e, pT=pT: e.copy(out=hT[:, :, HALF:HALF + NS], in_=pT[:, :, 0:NS]), PB(b), ["hT"])

            norm_phase("P0", x_tok, xs_tok, g_mix, ar_,
                       lambda gB, h_tok, hs_tok, R: to_hT(gB, h_tok, hs_tok, R, x_tok, xs_tok, lambda tt: "P0x%d" % tt, "P0xs"),
                       lambda tt: "P0x%d" % tt, "P0xs")
            if dbg and "hT" in dbg_out and pas == 0:
                dump("hT", hT[:], ["hT"])
            if stop_after == "P0":
                continue

            def proj_fm(wview, wres, ncol0, kcs, act_tile, act_res, evac):
                for (c0, c1) in blocks:
                    b = nb()
                    for kc in range(kcs):
                        A("pe", lambda e, kc=kc, b=b, c0=c0, c1=c1: e.matmul(ps[:, b, 0:c1 - c0], lhsT=wview[:, kc, ncol0:ncol0 + 128],
                                                                              rhs=act_tile[:, kc, c0:c1], start=(kc == 0),
                                                                              stop=(kc == kcs - 1)),
                          [wres, act_res], PB(b))
                    evac(b, c0, c1)

            S.tag = "p%d Rprep" % pas
            S.barrier()
            ar_ = Arena(arena_t, ARENA_WORDS)
            sig = ar_.alloc(F32, [128, 4, TW])
            g_bf = ar_.alloc(BF16, [128, 4, TW])
            ar_t = ar_.alloc(BF16, [128, 4, 16, 2, 64])
            bT = ar_.alloc(BF16, [128, 4, HALF])
            kT = ar_.alloc(BF16, [128, 4, HALF])
            vT = ar_.alloc(BF16, [128, 4, TW])
            bonus = ar_.alloc(BF16, [128, 4, TW])
            smp = ar_.alloc(F32, [128, 6, 4, NS])
            mark = ar_.top
            a_bf = ar_.alloc(BF16, [128, 4, TW])
            kpr = ar_.alloc(BF16, [128, 4, TW])
            praw = [ar_.alloc(F32, [128, 1048]) for _ in range(1)]
            NT = 7
            tmp = [ar_.alloc(F32, [128, TW]) for _ in range(NT)]
            cmask = ar_.alloc(BF16, [128, HALF])
            lwb = ar_.alloc(BF16, [128, TW])
            import os as _os
            _skip = _os.environ.get("KSKIP", "")
            if "cmask" not in _skip:
                S.dma("pool", lambda e: e.dma_start(out=cmask, in_=consts_d[:, 640:1664]), writes=["cmask"])

            if nsamp and "shiftT" not in _skip:
                sh_tok = tmp[0][0:NS, :]
                sh_tok2 = tmp[1][0:NS, :]
                S.dma("sp", lambda e: e.dma_start(out=sh_tok[:, 0:1024], in_=shift_s[:, 0:1024]), writes=["tmp0"])
                S.dma("sp", lambda e: e.dma_start(out=sh_tok2[:, 0:768], in_=shift_s[:, 1024:1792]), writes=["tmp1"])
                b = nb()
                for c in range(14):
                    src = sh_tok[:, c * 128:(c + 1) * 128] if c < 8 else sh_tok2[:, (c - 8) * 128:(c - 7) * 128]
                    A("pe", lambda e, c=c, src=src, b=b: e.transpose(out=ps[:, b, c * NS:(c + 1) * NS], in_=src, identity=ident[0:NS, 0:NS]),
                      ["tmp0", "tmp1", "consts"], PB(b))
                A("dve", lambda e, b=b: e.tensor_copy(out=sprevT[:, :, :], in_=ps[:, b, 0:14 * NS].rearrange("p (c n) -> p c n", c=14)),
                  PB(b), ["sprevT"])

            if stop_after == "shiftT":
                continue
            hv = [(0, 512), (512, W)]
            pctr = [0]

            def rwkv_chunk(c, wview, wres, ncol0, out_ap=None):
                pr = praw[0]
                t1 = tmp[6]
                A("dve", lambda e: e.tensor_copy(out=pr[:, 0:1], in_=carry[:, c:c + 1]))

                def evac(b, c0, c1):
                    A("act", lambda e: e.copy(out=pr[:, 1 + c0:1 + c1], in_=ps[:, b, 0:c1 - c0]))
                    A("act", lambda e: e.activation(out=t1[:, c0:c1], in_=ps[:, b, 0:c1 - c0], func=AF.Copy, scale=dvc("omu", c)))
                proj_fm(wview, wres, ncol0, 8, hT, "hT", evac)
                A("act", lambda e: e.copy(out=carry[:, c:c + 1], in_=pr[:, HALF:HALF + 1]))
                if nsamp:
                    A("act", lambda e: e.copy(out=shiftS[:, c, :], in_=pr[:, 1 + HALF:1 + HALF + NS]))
                out_ap_ = tmp[0] if out_ap is None else out_ap
                for (a_, b_) in hv:
                    b2 = min(b_, HALF)
                    A("dve", lambda e, a_=a_, b2=b2: e.scalar_tensor_tensor(out=out_ap_[:, a_:b2], in0=pr[:, a_:b2], scalar=pf("mu", c),
                                                                            in1=t1[:, a_:b2], op0=ALU.mult, op1=ALU.add))
                if nsamp:
                    A("dve", lambda e: e.scalar_tensor_tensor(out=out_ap_[:, HALF:HALF + NS], in0=sprevT[:, c, :], scalar=pf("mu", c),
                                                              in1=t1[:, HALF:HALF + NS], op0=ALU.mult, op1=ALU.add))
                return out_ap_

            wv, wr = load_w(w_in_v[:, :, 1536:1792], (8, 256))
            psm = rwkv_chunk(12, wv, wr, 0)
            for (a_, b_) in hv:
                A("act", lambda e, a_=a_, b_=b_: e.activation(out=lwb[0:64, a_:b_], in_=psm[0:64, a_:b_], func=AF.Tanh))
                A("act", lambda e, a_=a_, b_=b_: e.copy(out=lwb[64:128, a_:b_], in_=psm[64:128, a_:b_]))
            for j in range(4):
                for (c0, c1) in blocks:
                    b2_ = nb(2)
                    b = b2_
                    A("pe", lambda e, j=j, b=b, c0=c0, c1=c1: e.matmul(ps[:, b, 0:c1 - c0], lhsT=lw_w[0:64, j * 128:(j + 1) * 128],
                                                                        rhs=lwb[0:64, c0:c1], start=True, stop=True))
                    A("act", lambda e, j=j, b=b, c0=c0, c1=c1: e.activation(out=sig[:, j, c0:c1], in_=ps[:, b, 0:c1 - c0], func=AF.Sigmoid,
                                                                             bias=pf("w0", j)))
                    b = b2_ + 1
                    A("pe", lambda e, j=j, b=b, c0=c0, c1=c1: e.matmul(ps[:, b, 0:c1 - c0], lhsT=lw_w[64:128, j * 128:(j + 1) * 128],
                                                                        rhs=lwb[64:128, c0:c1], start=True, stop=True))
                    A("act", lambda e, j=j, b=b, c0=c0, c1=c1: e.activation(out=a_bf[:, j, c0:c1], in_=ps[:, b, 0:c1 - c0], func=AF.Sigmoid,
                                                                             bias=pf("a0", j)))
            psm = rwkv_chunk(13, wv, wr, 128)
            for (a_, b_) in hv:
                A("act", lambda e, a_=a_, b_=b_: e.activation(out=lwb[:, a_:b_], in_=psm[:, a_:b_], func=AF.Sigmoid))
            for j in range(4):
                for (c0, c1) in blocks:
                    b = nb()
                    A("pe", lambda e, j=j, b=b, c0=c0, c1=c1: e.matmul(ps[:, b, 0:c1 - c0], lhsT=g2_w[:, j * 128:(j + 1) * 128],
                                                                        rhs=lwb[:, c0:c1], start=True, stop=True))
                    A("dve", lambda e, j=j, b=b, c0=c0, c1=c1: e.tensor_copy(out=g_bf[:, j, c0:c1], in_=ps[:, b, 0:c1 - c0]))
            if dbg and pas == 0:
                dump("sig", sig, ["sig"])

            wv, wr = load_w(w_in_v[:, :, 1024:1536], (8, 512))
            for j in range(4):
                rwkv_chunk(8 + j, wv, wr, j * 128, out_ap=vT[:, j, :])
                if nsamp:
                    A("act", lambda e, j=j: e.copy(out=smp[:, 3, j, :], in_=vT[:, j, HALF:HALF + NS]))

            def fp32_blocksum(src_ap, src_res, mat, evac):
                for (c0, c1) in blocks:
                    b = nb()
                    A("pe", lambda e, b=b, c0=c0, c1=c1: e.matmul(ps[:, b, 0:c1 - c0], lhsT=mat, rhs=src_ap[:, c0:c1], start=True, stop=True))
                    evac(b, c0, c1)

            def cumsum_decay(j, cs_i):
                cs = tmp[cs_i]
                for (a_, b_) in hv:
                    b2 = min(b_, HALF)
                    A("dve", lambda e, a_=a_, b2=b2: e.tensor_tensor_scan(out=cs[:, a_:b2], data0=cmask[:, a_:b2], data1=sig[:, j, a_:b2], initial=0.0,
                                                                          op0=ALU.mult, op1=ALU.add))
                if nsamp:
                    A("dve", lambda e: e.tensor_copy(out=cs[:, HALF:HALF + NS], in_=sig[:, j, HALF:HALF + NS]))
                return cs

            def cview(ap2, a_, b2):
                return ap2[:, a_:b2].rearrange("p (c t) -> p c t", t=64)

            wv, wr = load_w(w_in_v[:, :, 512:1024], (8, 512))
            for j in range(4):
                k_ap = rwkv_chunk(4 + j, wv, wr, j * 128)
                kkr, rs, cs, en, de, ka = tmp[1], tmp[2], tmp[3], tmp[4], tmp[5], tmp[2]
                for (a_, b_) in hv:
                    A("act", lambda e, j=j, a_=a_, b_=b_: e.activation(out=kkr[:, a_:b_], in_=k_ap[:, a_:b_], func=AF.Copy, scale=pf("k_k", j)))
                    A("act", lambda e, a_=a_, b_=b_: e.activation(out=rs[:, a_:b_], in_=kkr[:, a_:b_], func=AF.Square))

                def ev_ss(b, c0, c1):
                    A("act", lambda e: e.activation(out=de[:, c0:c1], in_=ps[:, b, 0:c1 - c0], func=AF.Ln, bias=eps24[:, 0:1]))
                fp32_blocksum(rs, "tmp2", blk64, ev_ss)
                cumsum_decay(j, 3)
                for (a_, b_) in hv:
                    b2 = min(b_, HALF)
                    c_lo, c_hi = a_ // 64, b2 // 64
                    A("act", lambda e, a_=a_, b_=b_: e.activation(out=de[:, a_:b_], in_=de[:, a_:b_], func=AF.Exp, scale=-0.5))
                    A("dve", lambda e, a_=a_, b_=b_: e.tensor_tensor(out=kkr[:, a_:b_], in0=kkr[:, a_:b_], in1=de[:, a_:b_], op=ALU.mult))
                    A("act", lambda e, j=j, a_=a_, b2=b2, c_lo=c_lo, c_hi=c_hi: e.activation(out=WC[:, j, c_lo:c_hi], in_=cview(cs, a_, b2)[:, :, 63],
                                                                                             func=AF.Exp, scale=CDEC))
                    A("act", lambda e, a_=a_, b2=b2: e.activation(out=en[:, a_:b2], in_=cs[:, a_:b2], func=AF.Exp, scale=-CDEC))
                    if nsamp and b_ > HALF:
                        A("act", lambda e, j=j: e.activation(out=smp[:, 1, j, :], in_=cs[:, HALF:HALF + NS], func=AF.Exp, scale=CDEC))
                        A("act", lambda e, j=j: e.activation(out=smp[:, 4, j, :], in_=kkr[:, HALF:HALF + NS], func=AF.Copy, scale=-1.0))
                    A("dve", lambda e, j=j, a_=a_, b2=b2: e.tensor_tensor(out=de[:, a_:b2], in0=cs[:, a_:b2], in1=sig[:, j, a_:b2], op=ALU.subtract))
                    A("act", lambda e, a_=a_, b2=b2: e.activation(out=de[:, a_:b2], in_=de[:, a_:b2], func=AF.Exp, scale=CDEC))
                    A("dve", lambda e, j=j, a_=a_, b2=b2, c_lo=c_lo, c_hi=c_hi: e.scalar_tensor_tensor(
                        out=ar_t[:, j, c_lo:c_hi, 0, :], in0=cview(kkr, a_, b2), scalar=-1.0, in1=cview(de, a_, b2), op0=ALU.mult, op1=ALU.mult))
                    A("dve", lambda e, j=j, a_=a_, b_=b_: e.tensor_tensor(out=ka[:, a_:b_], in0=kkr[:, a_:b_], in1=a_bf[:, j, a_:b_], op=ALU.mult))
                    if nsamp and b_ > HALF:
                        A("act", lambda e, j=j: e.copy(out=smp[:, 5, j, :], in_=ka[:, HALF:HALF + NS]))
                    A("dve", lambda e, j=j, a_=a_, b2=b2: e.tensor_tensor(out=bT[:, j, a_:b2], in0=ka[:, a_:b2], in1=en[:, a_:b2], op=ALU.mult))
                    A("dve", lambda e, j=j, a_=a_, b_=b_: e.tensor_scalar(out=kkr[:, a_:b_], in0=a_bf[:, j, a_:b_], scalar1=pf("k_a", j),
                                                                          scalar2=dvc("omka", j), op0=ALU.mult, op1=ALU.add))
                    A("dve", lambda e, a_=a_, b_=b_: e.tensor_tensor(out=kkr[:, a_:b_], in0=kkr[:, a_:b_], in1=k_ap[:, a_:b_], op=ALU.mult))
                    A("dve", lambda e, j=j, a_=a_, b2=b2: e.tensor_tensor(out=kT[:, j, a_:b2], in0=kkr[:, a_:b2], in1=en[:, a_:b2], op=ALU.mult))
                    A("act", lambda e, j=j, a_=a_, b_=b_: e.activation(out=kpr[:, j, a_:b_], in_=kkr[:, a_:b_], func=AF.Copy, scale=pf("r_k", j)))
                    if nsamp and b_ > HALF:
                        A("act", lambda e, j=j: e.copy(out=smp[:, 2, j, :], in_=kkr[:, HALF:HALF + NS]))

            wv, wr = load_w(w_in_v[:, :, 0:512], (8, 512))
            for j in range(4):
                r_ap = rwkv_chunk(j, wv, wr, j * 128)
                cs = cumsum_decay(j, 3)
                rk = tmp[1]
                for (a_, b_) in hv:
                    b2 = min(b_, HALF)
                    c_lo, c_hi = a_ // 64, b2 // 64
                    A("act", lambda e, a_=a_, b2=b2: e.activation(out=cs[:, a_:b2], in_=cs[:, a_:b2], func=AF.Exp, scale=CDEC))
                    A("dve", lambda e, j=j, a_=a_, b2=b2, c_lo=c_lo, c_hi=c_hi: e.tensor_tensor(out=ar_t[:, j, c_lo:c_hi, 1, :], in0=cview(r_ap, a_, b2),
                                                                                                 in1=cview(cs, a_, b2), op=ALU.mult))
                    A("dve", lambda e, j=j, a_=a_, b_=b_: e.tensor_tensor(out=rk[:, a_:b_], in0=r_ap[:, a_:b_], in1=kpr[:, j, a_:b_], op=ALU.mult))
                if nsamp:
                    A("act", lambda e, j=j: e.copy(out=smp[:, 0, j, :], in_=r_ap[:, HALF:HALF + NS]))

                def ev_bon(b, c0, c1, j=j):
                    A("dve", lambda e: e.tensor_tensor(out=bonus[:, j, c0:c1], in0=ps[:, b, 0:c1 - c0], in1=vT[:, j, c0:c1], op=ALU.mult))
                fp32_blocksum(rk, "tmp1", blk64, ev_bon)
            if dbg and pas == 0:
                dump("ar", ar_t, ["ar"])
                dump("bT", bT, ["bT"])
                dump("kT", kT, ["kT"])
                dump("vT", vT, ["vT"])
                dump("bonus", bonus, ["bonus"])
            if stop_after == "Rprep":
                continue

            yT = sig
            if nsamp:
                ar_.top = mark
                L1v = ar_.alloc(F32, [128, 6, 64])
                saL = ar_.alloc(F32, [128, 64])
                yL1 = ar_.alloc(F32, [128, 64])
                rs_mark = ar_.top
                wf = [wbuf[i_][:, :, :].rearrange("p a b -> p (a b)").bitcast(F32) for i_ in range(4)]
                S_h = [wf[0].rearrange("p (v k) -> p v k", k=64), wf[1].rearrange("p (v k) -> p v k", k=64)]
                T_h = [wf[2].rearrange("p (v k) -> p v k", k=64), wf[3].rearrange("p (v k) -> p v k", k=64)]
                tok6h = [wf[2][0:NS, 0:1536].rearrange("p (v n) -> p v n", v=3), wf[3][0:NS, 0:1536].rearrange("p (v n) -> p v n", v=3)]
                ytok = wf[2][0:NS, 1536:2048]
                shtok = wf[3][0:NS, 0:2048]
                def emit_rsample():
                    for hf in range(2):
                        S.dma("sp", lambda e, hf=hf: e.dma_start(out=S_h[hf].rearrange("p v k -> p (v k)"), in_=wkv_s[:, hf * 2048:(hf + 1) * 2048]))
                    for vec in range(6):
                        b = nb_par()
                        for j in range(4):
                            A("pe", lambda e, vec=vec, j=j, b=b: e.transpose(out=ps[0:NS, b, j * 128:(j + 1) * 128], in_=smp[:, vec, j, :], identity=ident))
                        dst = tok6h[vec // 3][:, vec % 3, :]
                        if vec % 2 == 0:
                            A("act", lambda e, dst=dst, b=b: e.copy(out=dst, in_=ps[0:NS, b, :]))
                        else:
                            A("dve", lambda e, dst=dst, b=b: e.tensor_copy(out=dst, in_=ps[0:NS, b, :]))
                    for hf in range(2):
                        S.dma("sp", lambda e, hf=hf: e.dma_start(out=scr_a[:, hf * 3:(hf + 1) * 3, :], in_=tok6h[hf]))
                    for vec in range(6):
                        S.dma("sp", lambda e, vec=vec: e.dma_start(out=L1v[:, vec, :], in_=scr_a[:, vec, :].rearrange("b (h n) -> b h n", h=8)))

                    def bc_v(vec):
                        return L1v[:, vec, :].unsqueeze(1).broadcast_to([128, 32, 64])

                    def bc_k(ap2, hf):
                        return ap2[:, hf * 32:(hf + 1) * 32].unsqueeze(2).broadcast_to([128, 32, 64])
                    for hf in range(2):
                        Sx, Tx = S_h[hf], T_h[hf]
                        vs = slice(hf * 32, (hf + 1) * 32)
                        A("dve", lambda e, Sx=Sx, Tx=Tx: e.tensor_tensor(out=Tx, in0=Sx, in1=bc_v(4), op=ALU.mult))
                        A("dve", lambda e, Tx=Tx, vs=vs: e.tensor_reduce(out=saL[:, vs], in_=Tx, axis=AX.X, op=ALU.add))
                        A("dve", lambda e, Sx=Sx: e.tensor_tensor(out=Sx, in0=Sx, in1=bc_v(1), op=ALU.mult))
                        A("dve", lambda e, Tx=Tx, hf=hf: e.tensor_tensor(out=Tx, in0=bc_k(saL, hf), in1=bc_v(5), op=ALU.mult))
                        A("dve", lambda e, Sx=Sx, Tx=Tx: e.tensor_tensor(out=Sx, in0=Sx, in1=Tx, op=ALU.add))
                        A("dve", lambda e, Tx=Tx, hf=hf: e.tensor_tensor(out=Tx, in0=bc_k(L1v[:, 3, :], hf), in1=bc_v(2), op=ALU.mult))
                        A("dve", lambda e, Sx=Sx, Tx=Tx: e.tensor_tensor(out=Sx, in0=Sx, in1=Tx, op=ALU.add))
                        S.dma("sp", lambda e, Sx=Sx, hf=hf: e.dma_start(out=o_wkv_s[:, hf * 2048:(hf + 1) * 2048], in_=Sx.rearrange("p v k -> p (v k)")))
                        A("dve", lambda e, Sx=Sx, Tx=Tx: e.tensor_tensor(out=Tx, in0=Sx, in1=bc_v(0), op=ALU.mult))
                        A("dve", lambda e, Tx=Tx, vs=vs: e.tensor_reduce(out=yL1[:, vs], in_=Tx, axis=AX.X, op=ALU.add))
                    S.dma("sp", lambda e: e.dma_start(out=scr_y.rearrange("b (h n) -> (b h) n", h=8), in_=yL1))
                    S.dma("sp", lambda e: e.dma_start(out=ytok, in_=scr_y))
                    b = nb_par()
                    for j in range(4):
                        A("pe", lambda e, j=j, b=b: e.transpose(out=ps[:, b, j * NS:(j + 1) * NS], in_=ytok[:, j * 128:(j + 1) * 128],
                                                                identity=ident[0:NS, 0:NS]))
                    A("act", lambda e, b=b: e.copy(out=yT[:, :, HALF:HALF + NS], in_=ps[:, b, 0:4 * NS].rearrange("p (j n) -> p j n", j=4)))
                    for g_ in range(4):
                        cs_ = list(range(g_ * 4, min(14, g_ * 4 + 4)))
                        b = nb_par()
                        for ci, c in enumerate(cs_):
                            A("pe", lambda e, ci=ci, c=c, b=b: e.transpose(out=ps[0:NS, b, ci * 128:(ci + 1) * 128], in_=shiftS[:, c, :], identity=ident))
                        n_ = len(cs_) * 128
                        A("act", lambda e, g_=g_, b=b, n_=n_: e.copy(out=shtok[:, g_ * 512:g_ * 512 + n_], in_=ps[0:NS, b, 0:n_]))
                    S.dma("sp", lambda e: e.dma_start(out=o_shift_s, in_=shtok[:, 0:RPROJ]))

            if stop_after == "Rsample":
                continue
            ar_.top = rs_mark if nsamp else mark
            bk_tok = [ar_.alloc(BF16, [128, 2, 512]) for _ in range(2)]
            v_tok = [ar_.alloc(BF16, [128, 512]) for _ in range(2)]
            NM_sb = [ar_.alloc(BF16, [128, 8, 2, 128]) for _ in range(2)]
            P_sb2 = [[ar_.alloc(BF16, [128, 8, 64]) for _ in range(2)] for _ in range(2)]
            Tt_sb2 = [[ar_.alloc(BF16, [128, 8, 64]) for _ in range(1)] for _ in range(2)]
            QT_sb2 = [[ar_.alloc(BF16, [128, 8, 2, 64]) for _ in range(2)] for _ in range(2)]
            X_sb = [ar_.alloc(BF16, [128, 8, 64]) for _ in range(2)]
            U_sb = [ar_.alloc(BF16, [128, 8, 64]) for _ in range(2)]
            XV_sb = [ar_.alloc(F32, [128, 8, 64]) for _ in range(2)]
            Hs = ar_.alloc(F32, [128, 4, 64])
            gtmp = [ar_.alloc(F32, [128, TW]) for _ in range(3)]

            for i in range(8):
                q = i % 2
                P_sb, QT_sb, Tt_sb = P_sb2[q], QT_sb2[q], Tt_sb2[q]
                S.tag = "p%d Rpar%d" % (pas, i)
                b1 = nb_par()
                b2 = nb_par()
                pT1 = ps[:, b1, :].bitcast(BF16).rearrange("p (v n) -> p v n", v=2)
                pT2 = ps[:, b2, :].bitcast(BF16)
                for j in range(4):
                    A("pe", lambda e, i=i, j=j, pT1=pT1: e.transpose(out=pT1[:, 0, j * 128:(j + 1) * 128], in_=bT[:, j, i * 128:(i + 1) * 128],
                                                                     identity=ident_bf[:]), ["bT", "ident_bf"], PB(b1))
                    A("pe", lambda e, i=i, j=j, pT1=pT1: e.transpose(out=pT1[:, 1, j * 128:(j + 1) * 128], in_=kT[:, j, i * 128:(i + 1) * 128],
                                                                     identity=ident_bf[:]), ["kT", "ident_bf"], PB(b1))
                    A("pe", lambda e, i=i, j=j, pT2=pT2: e.transpose(out=pT2[:, j * 128:(j + 1) * 128], in_=vT[:, j, i * 128:(i + 1) * 128],
                                                                     identity=ident_bf[:]), ["vT", "ident_bf"], PB(b2))
                A("act", lambda e, q=q, pT1=pT1: e.copy(out=bk_tok[q][:], in_=pT1), PB(b1), ["bk_tok%d" % q])
                A("dve", lambda e, q=q, pT2=pT2: e.tensor_copy(out=v_tok[q][:], in_=pT2[:, 0:512]), PB(b2), ["v_tok%d" % q])
                if stop_after == "c1":
                    break
                for hg in range(2):
                    b = nb_par(2)
                    for hh4 in range(4):
                        h = hg * 4 + hh4
                        j, hh = h // 2, h % 2
                        for e_ in range(2):
                            c = 2 * i + e_
                            for x, src in ((0, bT), (1, kT)):
                                bb, oo = b + hh, ((hh4 // 2) * 2 + x) * 128
                                A("pe", lambda e, src=src, j=j, hh=hh, c=c, e_=e_, bb=bb, oo=oo: e.matmul(
                                    ps[e_ * 64:(e_ + 1) * 64, bb, oo:oo + 128], lhsT=src[hh * 64:(hh + 1) * 64, j, c * 64:(c + 1) * 64],
                                    rhs=ar_t[hh * 64:(hh + 1) * 64, j, c, :, :], start=True, stop=True),
                                  ["bT", "kT", "ar"], PB(b, 2))
                    for hh in range(2):
                        nmv = NM_sb[q][:, hg * 4 + hh:(hg + 1) * 4:2, :, :]
                        A("dve", lambda e, nmv=nmv, b=b, hh=hh: e.tensor_tensor(out=nmv, in0=ps[:, b + hh, :].rearrange("p (jj x n) -> p jj x n", jj=2, x=2),
                                                                                 in1=maskNM_bf.unsqueeze(1).unsqueeze(1).broadcast_to([128, 2, 2, 128]), op=ALU.mult))
                if stop_after == "c2":
                    break
                b = nb_par(2)
                for h in range(8):
                    j, hh = h // 2, h % 2
                    for e_ in range(2):
                        c = 2 * i + e_
                        A("pe", lambda e, j=j, hh=hh, c=c, e_=e_, h=h, b=b: e.matmul(
                            ps[e_ * 64:(e_ + 1) * 64, b + hh, j * 64:(j + 1) * 64], lhsT=ar_t[hh * 64:(hh + 1) * 64, j, c, 0, :],
                            rhs=bT[hh * 64:(hh + 1) * 64, j, c * 64:(c + 1) * 64], start=True, stop=True))
                for hh in range(2):
                    pv = P_sb[0][:, hh:8:2, :]
                    A("dve", lambda e, pv=pv, b=b, hh=hh: e.tensor_tensor(out=pv, in0=ps[:, b + hh, 0:256].rearrange("p (j s) -> p j s", j=4),
                                                                           in1=maskL_bf.unsqueeze(1).broadcast_to([128, 4, 64]), op=ALU.mult))
                if stop_after == "c3":
                    break
                Q0 = NM_sb[q][:, :, 0, 0:64]
                A("dve", lambda e, q=q: e.tensor_tensor(out=QT_sb[1][:, :, 1, :], in0=NM_sb[q][:, :, 0, 0:64],
                                                          in1=maskI_bf.unsqueeze(1).broadcast_to([128, 8, 64]), op=ALU.add))
                ev_ctr = [0]
                import os as _os3
                EVK = int(_os3.environ.get("EVK", "3"))

                def evac_half(bank, e_, dst_ap, shape_pat, **kw):
                    sl = slice(e_ * 64, (e_ + 1) * 64)
                    src = ps[sl, bank, :].rearrange(shape_pat, **kw)
                    ev_ctr[0] += 1
                    if ev_ctr[0] % EVK != 0:
                        A("act", lambda e: e.copy(out=dst_ap, in_=src))
                    else:
                        A("dve", lambda e: e.tensor_copy(out=dst_ap, in_=src))

                def evac2(bk, dst):
                    for e_ in range(2):
                        sl = slice(e_ * 64, (e_ + 1) * 64)
                        evac_half(bk + e_, e_, dst[sl, :, :], "p (h s) -> p h s", h=8)

                bA = nb_par(2)
                bB = nb_par(2)
                for h in range(8):
                    for e_ in range(2):
                        sl = slice(e_ * 64, (e_ + 1) * 64)
                        A("pe", lambda e, h=h, sl=sl, e_=e_, bA=bA: e.matmul(ps[sl, bA + e_, h * 64:(h + 1) * 64], lhsT=Q0[sl, h, :], rhs=P_sb[0][sl, h, :],
                                                                             start=True, stop=True))
                        A("pe", lambda e, h=h, sl=sl, e_=e_, bB=bB: e.matmul(ps[sl, bB + e_, h * 64:(h + 1) * 64], lhsT=P_sb[0][sl, h, :], rhs=Q0[sl, h, :],
                                                                             start=True, stop=True))
                evac2(bA, P_sb[1])
                for e_ in range(2):
                    sl = slice(e_ * 64, (e_ + 1) * 64)
                    evac_half(bB + e_, e_, QT_sb[1][sl, :, 0, :], "p (h s) -> p h s", h=8)
                Tc = None
                for lev in range(1, 6):
                    pi = lev % 2
                    Pc = P_sb[pi]
                    QTc = QT_sb[pi]
                    QTn = QT_sb[1 - pi]
                    last = (lev == 5)
                    if not last:
                        bA = nb_par(2)
                        for h in range(8):
                            for e_ in range(2):
                                sl = slice(e_ * 64, (e_ + 1) * 64)
                                A("pe", lambda e, h=h, sl=sl, e_=e_, bA=bA, QTc=QTc, Pc=Pc: e.matmul(ps[sl, bA + e_, h * 64:(h + 1) * 64], lhsT=QTc[sl, h, 0, :],
                                                                                                      rhs=Pc[sl, h, :], start=True, stop=True))
                    if not last:
                        for hg in range(2):
                            bB = nb_par(2)
                            for h4 in range(4):
                                h = hg * 4 + h4
                                for e_ in range(2):
                                    sl = slice(e_ * 64, (e_ + 1) * 64)
                                    A("pe", lambda e, h=h, h4=h4, sl=sl, e_=e_, bB=bB, QTc=QTc, Pc=Pc: e.matmul(
                                        ps[sl, bB + e_, h4 * 128:(h4 + 1) * 128], lhsT=Pc[sl, h, :], rhs=QTc[sl, h, :, :], start=True, stop=True))
                            for e_ in range(2):
                                sl = slice(e_ * 64, (e_ + 1) * 64)
                                src4 = ps[sl, bB + e_, :].rearrange("p (h x s) -> p h x s", h=4, x=2)
                                hsl = slice(hg * 4, (hg + 1) * 4)
                                A("act", lambda e, sl=sl, src4=src4, hsl=hsl, QTn=QTn: e.copy(out=QTn[sl, hsl, 0, :], in_=src4[:, :, 0, :]))
                                A("dve", lambda e, sl=sl, src4=src4, hsl=hsl, QTn=QTn, QTc=QTc: e.tensor_tensor(out=QTn[sl, hsl, 1, :], in0=src4[:, :, 1, :],
                                                                                                                 in1=QTc[sl, hsl, 1, :], op=ALU.add))
                        evac2(bA, P_sb[1 - pi])
                    else:
                        bB = nb_par(2)
                        for h in range(8):
                            for e_ in range(2):
                                sl = slice(e_ * 64, (e_ + 1) * 64)
                                A("pe", lambda e, h=h, sl=sl, e_=e_, bB=bB, QTc=QTc, Pc=Pc: e.matmul(
                                    ps[sl, bB + e_, h * 64:(h + 1) * 64], lhsT=Pc[sl, h, :], rhs=QTc[sl, h, 1, :], start=True, stop=True))
                        Tc = Tt_sb[0]
                        for e_ in range(2):
                            sl = slice(e_ * 64, (e_ + 1) * 64)
                            A("dve", lambda e, sl=sl, e_=e_, bB=bB, QTc=QTc, Tc=Tc: e.tensor_tensor(out=Tc[sl, :, :], in0=ps[sl, bB + e_, :].rearrange("p (h s) -> p h s", h=8),
                                                                                                     in1=QTc[sl, :, 1, :], op=ALU.add))
                bXV = nb_par(2)
                for h in range(8):
                    for e_ in range(2):
                        sl = slice(e_ * 64, (e_ + 1) * 64)
                        A("pe", lambda e, h=h, sl=sl, e_=e_, q=q, bXV=bXV: e.matmul(ps[sl, bXV + e_, h * 64:(h + 1) * 64], lhsT=NM_sb[q][sl, h, 1, 0:64],
                                                                                    rhs=v_tok[q][sl, h * 64:(h + 1) * 64], start=True, stop=True))
                evac2(bXV, XV_sb[q])
                if stop_after == "c4":
                    break
                S.tag = "p%d Rseq%d" % (pas, i)
                for e_ in range(2):
                    c = 2 * i + e_
                    sl = slice(e_ * 64, (e_ + 1) * 64)
                    xq = c % 2
                    bX = nb_seq(e_)
                    for j in range(4):
                        A("pe", lambda e, j=j, sl=sl, c=c, bX=bX: e.matmul(ps[sl, bX, j * 128:(j + 1) * 128], lhsT=ar_t[:, j, c, 0, :],
                                                                           rhs=H0bd[:, j, :], start=True, stop=True))
                    A("dve", lambda e, sl=sl, xq=xq, bX=bX, q=q: e.tensor_tensor(out=X_sb[xq][sl, :, :], in0=ps[sl, bX, :].rearrange("p (h s) -> p h s", h=8),
                                                                                  in1=XV_sb[q][sl, :, :], op=ALU.add))
                    bU = nb_seq(e_)
                    for h in range(8):
                        A("pe", lambda e, h=h, sl=sl, xq=xq, bU=bU, Tc=Tc: e.matmul(ps[sl, bU, h * 64:(h + 1) * 64], lhsT=Tc[sl, h, :],
                                                                                    rhs=X_sb[xq][sl, h, :], start=True, stop=True),
                          [], PB(bU))
                    A("dve", lambda e, sl=sl, xq=xq, bU=bU: e.tensor_copy(out=U_sb[xq][sl, :, :], in_=ps[sl, bU, :].rearrange("p (h s) -> p h s", h=8)),
                      PB(bU), ["U%d" % xq])
                    bY1 = nb_seq(1 - e_)
                    for j in range(4):
                        A("pe", lambda e, j=j, c=c, bY1=bY1: e.matmul(ps[:, bY1, j * 64:(j + 1) * 64], lhsT=H0bd[:, j, :],
                                                                      rhs=ar_t[:, j, c, 1, :], start=True, stop=True))
                    A("act", lambda e, c=c, bY1=bY1: e.copy(out=yT[:, :, c * 64:(c + 1) * 64], in_=ps[:, bY1, 0:256].rearrange("p (j t) -> p j t", j=4)))
                    bY = nb_seq(e_)
                    for h in range(8):
                        j, hh = h // 2, h % 2
                        hs = slice(hh * 64, (hh + 1) * 64)
                        A("pe", lambda e, h=h, j=j, hs=hs, sl=sl, xq=xq, q=q, bY=bY: e.matmul(ps[hs, bY, j * 64:(j + 1) * 64], lhsT=U_sb[xq][sl, h, :],
                                                                                              rhs=NM_sb[q][sl, h, 0, 64:128], start=True, stop=False))
                        A("pe", lambda e, h=h, j=j, hs=hs, sl=sl, q=q, bY=bY: e.matmul(ps[hs, bY, j * 64:(j + 1) * 64], lhsT=v_tok[q][sl, h * 64:(h + 1) * 64],
                                                                                       rhs=NM_sb[q][sl, h, 1, 64:128], start=False, stop=True))
                    A("dve", lambda e, c=c, bY=bY: e.tensor_tensor(out=yT[:, :, c * 64:(c + 1) * 64], in0=ps[:, bY, 0:256].rearrange("p (j t) -> p j t", j=4),
                                                                    in1=yT[:, :, c * 64:(c + 1) * 64], op=ALU.add))
                    bG = nb_seq(e_)
                    for h in range(8):
                        j, hh = h // 2, h % 2
                        hs = slice(hh * 64, (hh + 1) * 64)
                        A("pe", lambda e, h=h, j=j, hs=hs, sl=sl, xq=xq, q=q, bG=bG: e.matmul(ps[hs, bG, j * 64:(j + 1) * 64], lhsT=bk_tok[q][sl, 0, h * 64:(h + 1) * 64],
                                                                                              rhs=U_sb[xq][sl, h, :], start=True, stop=False),
                          ["bk_tok%d" % q, "U%d" % xq], PB(bG))
                        A("pe", lambda e, h=h, j=j, hs=hs, sl=sl, q=q, bG=bG: e.matmul(ps[hs, bG, j * 64:(j + 1) * 64], lhsT=bk_tok[q][sl, 1, h * 64:(h + 1) * 64],
                                                                                       rhs=v_tok[q][sl, h * 64:(h + 1) * 64], start=False, stop=True),
                          ["bk_tok%d" % q, "v_tok%d" % q], PB(bG))
                    A("dve", lambda e, bG=bG: e.tensor_tensor(out=Hs[:], in0=ps[:, bG, 0:256].rearrange("p (j v) -> p j v", j=4), in1=H0f[:], op=ALU.add),
                      PB(bG) + ["H0f"], ["Hs"])
                    A("dve", lambda e, c=c: e.tensor_tensor(out=H0f[:], in0=Hs[:], in1=WC[:, :, c:c + 1].broadcast_to([128, 4, 64]), op=ALU.mult),
                      ["Hs", "WC"], ["H0f"])
                    A("act", lambda e: e.copy(out=H0bd[0:64, :, 0:64], in_=H0f[0:64, :, :]), ["H0f"], ["H0b"])
                    A("act", lambda e: e.copy(out=H0bd[64:128, :, 64:128], in_=H0f[64:128, :, :]), ["H0f"], ["H0b"])
            if stop_after in ("c1", "c2", "c3", "c4"):
                continue
            if dbg and pas == 0:
                dump("yT", yT, ["yT"])
            if stop_after == "Rchunk":
                continue
            if nsamp:
                S.tag = "p%d Rsample" % pas
                emit_rsample()
            S.tag = "p%d Rpost" % pas
            for j in range(4):
                yj = yT[:, j, :]
                yc, sq_, rs_ = gtmp[0], gtmp[1], gtmp[2]

                def ev_mean(b, c0, c1, j=j):
                    A("dve", lambda e: e.scalar_tensor_tensor(out=yc[:, c0:c1], in0=ps[:, b, 0:c1 - c0], scalar=-1.0 / 64, in1=yT[:, j, c0:c1],
                                                              op0=ALU.mult, op1=ALU.add), PB(b) + ["yT"], ["gtmp0"])
                fp32_blocksum(yj, "yT", blk64, ev_mean)
                A("act", lambda e: e.activation(out=sq_[:, 0:W], in_=yc[:, 0:W], func=AF.Square), ["gtmp0"], ["gtmp1"])

                def ev_var(b, c0, c1):
                    A("act", lambda e: e.activation(out=rs_[:, c0:c1], in_=ps[:, b, 0:c1 - c0], func=AF.Ln, scale=1.0 / 64, bias=epsgn[:, 0:1]))
                fp32_blocksum(sq_, "gtmp1", blk64, ev_var)
                A("act", lambda e: e.activation(out=rs_[:, 0:W], in_=rs_[:, 0:W], func=AF.Exp, scale=-0.5), ["gtmp2"], ["gtmp2"])
                A("dve", lambda e: e.tensor_tensor(out=yc[:, 0:W], in0=yc[:, 0:W], in1=rs_[:, 0:W], op=ALU.mult), ["gtmp0", "gtmp2"], ["gtmp0"])
                A("dve", lambda e, j=j: e.tensor_scalar(out=yc[:, 0:W], in0=yc[:, 0:W], scalar1=pf("ln_g", j), scalar2=pf("ln_b", j),
                                                        op0=ALU.mult, op1=ALU.add), ["gtmp0", "pfm"], ["gtmp0"])
                A("dve", lambda e, j=j: e.tensor_tensor(out=yc[:, 0:W], in0=yc[:, 0:W], in1=bonus[:, j, 0:W], op=ALU.add), ["gtmp0", "bonus"], ["gtmp0"])
                A("dve", lambda e, j=j: e.tensor_tensor(out=oa[:, j, 0:W], in0=yc[:, 0:W], in1=g_bf[:, j, 0:W], op=ALU.mult), ["gtmp0", "g_bf"], ["oa"])
            if dbg and pas == 0:
                dump("oa", oa[:], ["oa"])
            if pas == passes[-1]:
                wst = gtmp[0][0:64, 0:512].rearrange("p (j n) -> p j n", j=4)
                b = nb()
                for j in range(4):
                    A("pe", lambda e, j=j, b=b: e.transpose(out=ps[0:64, b, j * 128:(j + 1) * 128], in_=H0f[:, j, :], identity=ident),
                      ["H0f", "consts"], PB(b))
                A("act", lambda e, b=b: e.copy(out=wst, in_=ps[0:64, b, :].rearrange("p (j n) -> p j n", j=4)), PB(b), ["gtmp0"])
                S.dma("sp", lambda e: e.dma_start(out=o_wkv_p.rearrange("(j hh) v k -> v j hh k", hh=2),
                                                   in_=wst.rearrange("p j (hh k) -> p j hh k", hh=2)), reads=["gtmp0"])
                b = nb()
                A("pe", lambda e, b=b: e.transpose(out=ps[0:14, b, 0:128], in_=carry[:, :], identity=ident), ["carry", "consts"], PB(b))
                A("act", lambda e, b=b: e.copy(out=gtmp[1][0:14, 0:128], in_=ps[0:14, b, 0:128]), PB(b), ["gtmp1"])
                S.dma("sp", lambda e: e.dma_start(out=o_shift_p.rearrange("(c p) -> c p", p=128), in_=gtmp[1][0:14, 0:128]), reads=["gtmp1"])
            if stop_after == "Rpost":
                continue

            S.tag = "p%d H" % pas
            S.barrier()
            ar_ = Arena(arena_t, ARENA_WORDS)
            Eb = ar_.alloc(F32, [128, 4, HALF])
            qT = ar_.alloc(BF16, [128, 4, HALF])
            hkT = ar_.alloc(BF16, [128, 4, HALF])
            hvT = ar_.alloc(BF16, [128, 4, TW])
            sgo = ar_.alloc(BF16, [128, 4, TW])
            oT = ar_.alloc(F32, [128, 4, TW])
            smpH = ar_.alloc(F32, [128, 4, 4, NS])
            hmark = ar_.top
            htmp = [ar_.alloc(F32, [128, TW]) for _ in range(8)]
            hset = [htmp[0:4], htmp[4:8]]
            cmaskH = ar_.alloc(BF16, [128, HALF])
            S.dma("pool", lambda e: e.dma_start(out=cmaskH, in_=consts_d[:, 640:1664]), writes=["cmaskH"])
            HB = RPROJ
            wv, wr = load_w(w_in_v[:, :, HB + 512:HB + 1024], (8, 512))
            hvh = [(0, 512), (512, W)]
            for h in range(4):
                T0, T1_, T2, T3 = hset[h % 2]

                def ev_f(b, c0, c1, T0=T0):
                    A("act", lambda e: e.activation(out=T0[:, c0:c1], in_=ps[:, b, 0:c1 - c0], func=AF.Sigmoid))
                proj_fm(wv, wr, h * 128, 8, hT, "hT", ev_f)
                for (a_, b_) in hvh:
                    b2 = min(b_, HALF)
                    A("dve", lambda e, h=h, a_=a_, b_=b_: e.tensor_scalar(out=T0[:, a_:b_], in0=T0[:, a_:b_], scalar1=dvc("omlb", h), scalar2=dvc("lb", h),
                                                                          op0=ALU.mult, op1=ALU.add))
                    A("dve", lambda e, a_=a_, b_=b_: e.tensor_scalar(out=T1_[:, a_:b_], in0=T0[:, a_:b_], scalar1=-1.0, scalar2=1.0, op0=ALU.mult, op1=ALU.add))
                    A("act", lambda e, a_=a_, b2=b2: e.activation(out=T2[:, a_:b2], in_=T0[:, a_:b2], func=AF.Ln))
                    A("dve", lambda e, a_=a_, b2=b2: e.tensor_tensor_scan(out=T3[:, a_:b2], data0=cmaskH[:, a_:b2], data1=T2[:, a_:b2], initial=0.0,
                                                                          op0=ALU.mult, op1=ALU.add))
                    A("act", lambda e, h=h, a_=a_, b2=b2: e.activation(out=Eb[:, h, a_:b2], in_=T3[:, a_:b2], func=AF.Exp))
                    A("act", lambda e, h=h, a_=a_, b2=b2: e.activation(out=DC[:, h, a_ // 64:b2 // 64], in_=T3[:, a_:b2].rearrange("p (c t) -> p c t", t=64)[:, :, 63],
                                                                       func=AF.Exp))
                    A("act", lambda e, a_=a_, b2=b2: e.activation(out=T2[:, a_:b2], in_=T3[:, a_:b2], func=AF.Exp, scale=-1.0))
                    A("dve", lambda e, h=h, a_=a_, b2=b2: e.tensor_tensor(out=hkT[:, h, a_:b2], in0=T1_[:, a_:b2], in1=T2[:, a_:b2], op=ALU.mult))
                if nsamp:
                    A("act", lambda e, h=h: e.copy(out=smpH[:, 1, h, :], in_=T0[:, HALF:HALF + NS]))
                    A("act", lambda e, h=h: e.copy(out=smpH[:, 2, h, :], in_=T1_[:, HALF:HALF + NS]))
            wv, wr = load_w(w_in_v[:, :, HB:HB + 512], (8, 512))
            for h in range(4):
                T0 = hset[h % 2][0]

                def ev_q(b, c0, c1, T0=T0, h=h):
                    A("act", lambda e: e.activation(out=T0[:, c0:c1], in_=ps[:, b, 0:c1 - c0], func=AF.Silu))
                    if c0 < HALF:
                        A("dve", lambda e: e.tensor_tensor(out=qT[:, h, c0:c1], in0=T0[:, c0:c1], in1=Eb[:, h, c0:c1], op=ALU.mult))
                proj_fm(wv, wr, h * 128, 8, hT, "hT", ev_q)
                if nsamp:
                    A("act", lambda e, h=h: e.copy(out=smpH[:, 0, h, :], in_=T0[:, HALF:HALF + NS]))
            wv, wr = load_w(w_in_v[:, :, HB + 1024:HB + 1536], (8, 512))
            for h in range(4):
                def ev_i(b, c0, c1, h=h):
                    A("act", lambda e: e.copy(out=hvT[:, h, c0:c1], in_=ps[:, b, 0:c1 - c0]), PB(b), ["hvT"])
                    if c0 >= HALF:
                        A("act", lambda e: e.copy(out=smpH[:, 3, h, :], in_=ps[:, b, 0:NS]), PB(b), ["smpH"])
                proj_fm(wv, wr, h * 128, 8, hT, "hT", ev_i)
            wv, wr = load_w(w_in_v[:, :, HB + 1536:HB + 2048], (8, 512))
            for h in range(4):
                def ev_og(b, c0, c1, h=h):
                    A("act", lambda e: e.activation(out=sgo[:, h, c0:c1], in_=ps[:, b, 0:c1 - c0], func=AF.Sigmoid), PB(b), ["sgo"])
                proj_fm(wv, wr, h * 128, 8, hT, "hT", ev_og)

            if nsamp:
                S.barrier()
                ar_.top = hmark
                S_s = ar_.alloc(F32, [128, NS, 4, 128])
                ktok_s = ar_.alloc(F32, [NS, 512])
                vtok_s = ar_.alloc(F32, [NS, 512])
                vm = [ar_.alloc(F32, [NS, 512]) for _ in range(2)]
                tS_l = [ar_.alloc(F32, [128, 4, 128]) for _ in range(3)]
                q_bf = ar_.alloc(BF16, [128, 4, NS])
                Sb_l = [ar_.alloc(BF16, [128, 4, 128]) for _ in range(3)]
                S.dma("sp", lambda e: e.dma_start(out=S_s, in_=hgrn_s.rearrange("b h k v -> k b h v")), writes=["S_s"])
                for vec, dst, dn in ((2, ktok_s, "ktok_s"), (3, vtok_s, "vtok_s")):
                    b = nb_smp()
                    for h in range(4):
                        A("pe", lambda e, vec=vec, h=h, b=b: e.transpose(out=ps[0:NS, b, h * 128:(h + 1) * 128], in_=smpH[:, vec, h, :], identity=ident),
                          ["smpH", "consts"], PB(b))
                    A("act", lambda e, dst=dst, b=b: e.copy(out=dst, in_=ps[0:NS, b, :]), PB(b), [dn])
                for bi in range(NS):
                    vq = bi % 2
                    tS = tS_l[bi % 3]
                    A("dve", lambda e, bi=bi, vq=vq: e.tensor_scalar(out=vm[vq], in0=vtok_s, scalar1=ident[0:NS, bi:bi + 1], scalar2=None, op0=ALU.mult),
                      ["vtok_s", "consts"], ["vm%d" % vq])
                    b = nb_smp()
                    for h in range(4):
                        A("pe", lambda e, h=h, vq=vq, b=b: e.matmul(ps[:, b, h * 128:(h + 1) * 128], lhsT=ktok_s[:, h * 128:(h + 1) * 128],
                                                                    rhs=vm[vq][:, h * 128:(h + 1) * 128], start=True, stop=True),
                          ["ktok_s", "vm%d" % vq], PB(b))
                    A("dve", lambda e, bi=bi: e.tensor_tensor(out=tS, in0=S_s[:, bi, :, :],
                                                               in1=smpH[:, 1, :, bi:bi + 1].broadcast_to([128, 4, 128]), op=ALU.mult),
                      ["S_s", "smpH"], ["tS"])
                    A("dve", lambda e, bi=bi, b=b: e.tensor_tensor(out=S_s[:, bi, :, :], in0=ps[:, b, :].rearrange("p (h v) -> p h v", h=4), in1=tS,
                                                                    op=ALU.add), PB(b) + ["tS"], ["S_s"])
                S.dma("sp", lambda e: e.dma_start(out=o_hgrn_s.rearrange("b h k v -> k b h v"), in_=S_s), reads=["S_s"])
                A("act", lambda e: e.copy(out=q_bf, in_=smpH[:, 0, :, :]))
                bO_ = nb_smp()
                for bi in range(NS):
                    Sb = Sb_l[bi % 3]
                    A("act", lambda e, bi=bi, Sb=Sb: e.copy(out=Sb, in_=S_s[:, bi, :, :]))
                    for h in range(4):
                        A("pe", lambda e, h=h, bi=bi, Sb=Sb, bO_=bO_: e.matmul(ps[:, bO_, h * NS + bi:h * NS + bi + 1], lhsT=Sb[:, h, :],
                                                                                rhs=q_bf[:, h, bi:bi + 1], start=True, stop=True))
                A("act", lambda e, bO_=bO_: e.copy(out=oT[:, :, HALF:HALF + NS], in_=ps[:, bO_, 0:4 * NS].rearrange("p (h n) -> p h n", h=4)))

            if not nsamp:
                ar_.top = hmark
            par_nb[0] = 4 if nsamp else 6
            par_ctr[0] = 0
            hk_tok = [ar_.alloc(BF16, [128, 512]) for _ in range(2)]
            hv_tok = [ar_.alloc(BF16, [128, 512]) for _ in range(2)]
            PT_sb = [ar_.alloc(BF16, [128, 4, 64]) for _ in range(2)]
            Ss = ar_.alloc(F32, [128, 4, 128])
            ar_.top = hmark
            htmp = [ar_.alloc(F32, [128, TW]) for _ in range(2)]
            for i in range(8):
                q = i % 2
                b1 = nb_par()
                pTk = ps[:, b1, :].bitcast(BF16).rearrange("p (v n) -> p v n", v=2)
                for h in range(4):
                    A("pe", lambda e, i=i, h=h, pTk=pTk: e.transpose(out=pTk[:, 0, h * 128:(h + 1) * 128], in_=hkT[:, h, i * 128:(i + 1) * 128],
                                                                     identity=ident_bf[:]), ["hkT", "ident_bf"], PB(b1))
                    A("pe", lambda e, i=i, h=h, pTk=pTk: e.transpose(out=pTk[:, 1, h * 128:(h + 1) * 128], in_=hvT[:, h, i * 128:(i + 1) * 128],
                                                                     identity=ident_bf[:]), ["hvT", "ident_bf"], PB(b1))
                A("act", lambda e, q=q, pTk=pTk: e.copy(out=hk_tok[q], in_=pTk[:, 0, :]), PB(b1), ["hk_tok%d" % q])
                A("dve", lambda e, q=q, pTk=pTk: e.tensor_copy(out=hv_tok[q], in_=pTk[:, 1, :]), PB(b1), ["hv_tok%d" % q])
                bS = nb_par()
                for h in range(4):
                    for e_ in range(2):
                        c = 2 * i + e_
                        A("pe", lambda e, h=h, e_=e_, c=c, bS=bS: e.matmul(ps[e_ * 64:(e_ + 1) * 64, bS, h * 64:(h + 1) * 64], lhsT=hkT[:, h, c * 64:(c + 1) * 64],
                                                                           rhs=qT[:, h, c * 64:(c + 1) * 64], start=True, stop=True), ["hkT", "qT"], PB(bS))
                A("dve", lambda e, q=q, bS=bS: e.tensor_tensor(out=PT_sb[q], in0=ps[:, bS, 0:256].rearrange("p (h t) -> p h t", h=4),
                                                                in1=masks_bf[:, 64:128].unsqueeze(1).broadcast_to([128, 4, 64]), op=ALU.mult),
                  PB(bS) + ["consts"], ["PT%d" % q])
                bO = nb_par()
                for e_ in range(2):
                    c = 2 * i + e_
                    sl = slice(e_ * 64, (e_ + 1) * 64)
                    for h in range(4):
                        oo = h * 128 + e_ * 64
                        A("pe", lambda e, h=h, c=c, oo=oo, bO=bO: e.matmul(ps[:, bO, oo:oo + 64], lhsT=S0b[:, h, :], rhs=qT[:, h, c * 64:(c + 1) * 64],
                                                                           start=True, stop=False), ["S0b", "qT"], PB(bO))
                        A("pe", lambda e, h=h, sl=sl, q=q, oo=oo, bO=bO: e.matmul(ps[:, bO, oo:oo + 64], lhsT=hv_tok[q][sl, h * 128:(h + 1) * 128],
                                                                                  rhs=PT_sb[q][sl, h, :], start=False, stop=True),
                          ["hv_tok%d" % q, "PT%d" % q], PB(bO))
                    bG = nb_seq(e_)
                    for h in range(4):
                        A("pe", lambda e, h=h, sl=sl, q=q, bG=bG: e.matmul(ps[:, bG, h * 128:(h + 1) * 128], lhsT=hk_tok[q][sl, h * 128:(h + 1) * 128],
                                                                           rhs=hv_tok[q][sl, h * 128:(h + 1) * 128], start=True, stop=True),
                          ["hk_tok%d" % q, "hv_tok%d" % q], PB(bG))
                    A("dve", lambda e, bG=bG: e.tensor_tensor(out=Ss, in0=ps[:, bG, :].rearrange("p (h v) -> p h v", h=4), in1=S0f[:], op=ALU.add),
                      PB(bG) + ["S0f"], ["Ss"])
                    A("dve", lambda e, c=c: e.tensor_tensor(out=S0f[:], in0=Ss, in1=DC[:, :, c:c + 1].broadcast_to([128, 4, 128]), op=ALU.mult),
                      ["Ss", "DC"], ["S0f"])
                    A("act", lambda e: e.copy(out=S0b[:], in_=S0f[:]), ["S0f"], ["S0b"])
                A("act", lambda e, i=i, bO=bO: e.copy(out=oT[:, :, i * 128:(i + 1) * 128], in_=ps[:, bO, :].rearrange("p (h t) -> p h t", h=4)),
                  PB(bO), ["oT"])
            par_nb[0] = 6
            if dbg and pas == 0:
                dump("oT", oT, ["oT"])
            for h in range(4):
                A("act", lambda e, h=h: e.activation(out=htmp[0][:, 0:W], in_=oT[:, h, 0:W], func=AF.Square), ["oT"], ["htmp0"])

                def ev_ms(b, c0, c1):
                    A("act", lambda e: e.activation(out=htmp[1][:, c0:c1], in_=ps[:, b, 0:c1 - c0], func=AF.Ln, scale=1.0 / 128, bias=epsrms[:, 0:1]))
                fp32_blocksum(htmp[0], "htmp0", ones, ev_ms)
                A("act", lambda e: e.activation(out=htmp[1][:, 0:W], in_=htmp[1][:, 0:W], func=AF.Exp, scale=-0.5), ["htmp1"], ["htmp1"])
                A("dve", lambda e, h=h: e.tensor_tensor(out=htmp[0][:, 0:W], in0=oT[:, h, 0:W], in1=htmp[1][:, 0:W], op=ALU.mult), ["oT", "htmp1"], ["htmp0"])
                A("dve", lambda e, h=h: e.scalar_tensor_tensor(out=ob[:, h, 0:W], in0=htmp[0][:, 0:W], scalar=pf("hng", h), in1=sgo[:, h, 0:W],
                                                               op0=ALU.mult, op1=ALU.mult), ["htmp0", "pfm", "sgo"], ["ob"])
            if dbg and pas == 0:
                dump("ob", ob[:], ["ob"])
            if pas == passes[-1]:
                S.dma("sp", lambda e: e.dma_start(out=o_hgrn_p.rearrange("h k v -> k h v"), in_=S0f[:]), reads=["S0f"])
            if stop_after == "H":
                continue

            S.barrier()
            ar_ = Arena(arena_t, ARENA_WORDS)
            x_tok = ar_.alloc(F32, [128, 8, 1024])
            xs_tok = ar_.alloc(F32, [NS, 1024])
            mergedT = ar_.alloc(BF16, [128, 8, TW])
            gm = [ar_.alloc(F32, [128, 512]) for _ in range(4)]
            GB = RPROJ + 2048
            w_upa_v = w_up_a.rearrange("(kc p) n -> p kc n", p=128)
            w_upb_v = w_up_b.rearrange("(kc p) n -> p kc n", p=128)
            for dcg in range(2):
                wga, wgar = load_w(w_in_v[:, :, GB + dcg * 512:GB + (dcg + 1) * 512], (8, 512))
                wgb, wgbr = load_w(w_in_v[:, :, GB + 1024 + dcg * 512:GB + 1024 + (dcg + 1) * 512], (8, 512))
                wu, wur = load_w(w_upa_v[:, :, dcg * 512:(dcg + 1) * 512], (4, 512))
                iu_ = (wctr[0] - 1) % NWB
                wub_view = wbuf[iu_][:, 4:8, :]
                S.dma("pool", lambda e, wub_view=wub_view, dcg=dcg: e.dma_start(out=wub_view, in_=w_upb_v[:, :, dcg * 512:(dcg + 1) * 512]), writes=[wur])
                for dc in range(4):
                    n0 = dc * 128
                    for (c0, c1) in blocks:
                        w_ = c1 - c0
                        b1, b2, b3, b4 = nb(), nb(), nb(), nb()
                        for kc in range(8):
                            A("pe", lambda e, kc=kc, b1=b1, c0=c0, c1=c1, n0=n0, wga=wga: e.matmul(ps[:, b1, 0:c1 - c0], lhsT=wga[:, kc, n0:n0 + 128],
                                                                                                   rhs=hT[:, kc, c0:c1], start=(kc == 0), stop=(kc == 7)),
                              [wgar, "hT"], PB(b1))
                        for kc in range(8):
                            A("pe", lambda e, kc=kc, b2=b2, c0=c0, c1=c1, n0=n0, wgb=wgb: e.matmul(ps[:, b2, 0:c1 - c0], lhsT=wgb[:, kc, n0:n0 + 128],
                                                                                                   rhs=hT[:, kc, c0:c1], start=(kc == 0), stop=(kc == 7)),
                              [wgbr, "hT"], PB(b2))
                        for kc in range(4):
                            A("pe", lambda e, kc=kc, b3=b3, c0=c0, c1=c1, n0=n0, wu=wu: e.matmul(ps[:, b3, 0:c1 - c0], lhsT=wu[:, kc, n0:n0 + 128],
                                                                                                 rhs=oa[:, kc, c0:c1], start=(kc == 0), stop=(kc == 3)),
                              [wur, "oa"], PB(b3))
                        for kc in range(4):
                            A("pe", lambda e, kc=kc, b4=b4, c0=c0, c1=c1, n0=n0, wub_view=wub_view: e.matmul(ps[:, b4, 0:c1 - c0], lhsT=wub_view[:, kc, n0:n0 + 128],
                                                                                                             rhs=ob[:, kc, c0:c1], start=(kc == 0), stop=(kc == 3)),
                              [wur, "ob"], PB(b4))
                        A("act", lambda e, b1=b1, w_=w_: e.activation(out=gm[0][:, 0:w_], in_=ps[:, b1, 0:w_], func=AF.Sigmoid), PB(b1), ["gm0"])
                        A("act", lambda e, b2=b2, w_=w_: e.activation(out=gm[1][:, 0:w_], in_=ps[:, b2, 0:w_], func=AF.Sigmoid), PB(b2), ["gm1"])
                        A("dve", lambda e, b3=b3, w_=w_: e.tensor_tensor(out=gm[2][:, 0:w_], in0=ps[:, b3, 0:w_], in1=gm[0][:, 0:w_], op=ALU.mult),
                          PB(b3) + ["gm0"], ["gm2"])
                        A("dve", lambda e, b4=b4, w_=w_: e.tensor_tensor(out=gm[3][:, 0:w_], in0=ps[:, b4, 0:w_], in1=gm[1][:, 0:w_], op=ALU.mult),
                          PB(b4) + ["gm1"], ["gm3"])
                        A("dve", lambda e, dcg=dcg, dc=dc, c0=c0, c1=c1, w_=w_: e.tensor_tensor(out=mergedT[:, dcg * 4 + dc, c0:c1], in0=gm[2][:, 0:w_],
                                                                                                 in1=gm[3][:, 0:w_], op=ALU.add),
                          ["gm2", "gm3"], ["mergedT"])
            if dbg and pas == 0:
                dump("mergedT", mergedT, ["mergedT"])
            if stop_after == "G":
                continue

            S.barrier()
            ar_.top = 0
            x_tok = ar_.alloc(F32, [128, 8, 1024])
            xs_tok = ar_.alloc(F32, [NS, 1024])
            mergedT = ar_.alloc(BF16, [128, 8, TW])
            for tt in range(8):
                S.dma("sp", lambda e, tt=tt: e.dma_start(out=x_tok[:, tt, :], in_=x_p[t0 + tt * 128:t0 + (tt + 1) * 128, :]),
                      writes=["x%d" % tt])
            if nsamp:
                S.dma("sp", lambda e: e.dma_start(out=xs_tok, in_=x_s), writes=["xs"])
            w_out_v = w_out.rearrange("(kc p) n -> p kc n", p=128)
            wos = [load_w(w_out_v[:, :, half * 512:(half + 1) * 512], (8, 512))[0] for half in range(2)]
            for tt in range(8):
                for half in range(2):
                    wo = wos[half]
                    b = nb()
                    for kc in range(8):
                        A("pe", lambda e, kc=kc, tt=tt, b=b, wo=wo: e.matmul(ps[:, b, :], lhsT=mergedT[:, kc, tt * 128:(tt + 1) * 128], rhs=wo[:, kc, :],
                                                                             start=(kc == 0), stop=(kc == 7)))
                    A("dve", lambda e, tt=tt, half=half, b=b: e.tensor_tensor(out=x_tok[:, tt, half * 512:(half + 1) * 512], in0=ps[:, b, :],
                                                                               in1=x_tok[:, tt, half * 512:(half + 1) * 512], op=ALU.add))
            if nsamp:
                for half in range(2):
                    wo = wos[half]
                    b = nb()
                    for kc in range(8):
                        A("pe", lambda e, kc=kc, b=b, wo=wo: e.matmul(ps[0:NS, b, :], lhsT=mergedT[:, kc, HALF:HALF + NS], rhs=wo[:, kc, :],
                                                                      start=(kc == 0), stop=(kc == 7)))
                    A("dve", lambda e, half=half, b=b: e.tensor_tensor(out=xs_tok[:, half * 512:(half + 1) * 512], in0=ps[0:NS, b, :],
                                                                        in1=xs_tok[:, half * 512:(half + 1) * 512], op=ALU.add))
            norm_phase("O", x_tok, xs_tok, g_ffn, ar_,
                       lambda gB, h_tok, hs_tok, R: to_hT(gB, h_tok, hs_tok, R, x_tok, xs_tok, lambda tt: "x%d" % tt, "xs"),
                       lambda tt: "x%d" % tt, "xs")
            if dbg and pas == 0:
                dump("hT2", hT[:], ["hT"])
            if stop_after == "O":
                continue

            S.barrier()
            ar_.top = 0
            x_tok = ar_.alloc(F32, [128, 8, 1024])
            xs_tok = ar_.alloc(F32, [NS, 1024])
            actT = ar_.alloc(BF16, [128, 22, TW])
            wdn = ar_.alloc(BF16, [128, 22, 1024])
            gBf = ftmp[:, :, :].rearrange("p a b -> p (a b)")
            junkD = ar_.alloc(BF16, [128, 1024])
            w_fd_v = w_fd.rearrange("(fc p) n -> p fc n", p=128)
            w_fg_v = w_fg.rearrange("(kc p) n -> p kc n", p=128)
            w_fu_v = w_fu.rearrange("(kc p) n -> p kc n", p=128)
            for fg in range(11):
                wg, wgr = load_w(w_fg_v[:, :, fg * 256:(fg + 1) * 256], (8, 256))
                wu, wur = load_w(w_fu_v[:, :, fg * 256:(fg + 1) * 256], (8, 256))
                if fg in (2, 4, 6, 8):
                    k_ = (fg - 2) // 2
                    lo, hi = (0, 6, 12, 18)[k_], (6, 12, 18, 22)[k_]
                    S.dma("pool", lambda e, lo=lo, hi=hi: e.dma_start(out=wdn[:, lo:hi, :], in_=w_fd_v[:, lo:hi, :]), writes=["wdn%d" % k_])
                for f2 in range(2):
                    fc = fg * 2 + f2
                    n0 = f2 * 128
                    for (c0, c1) in blocks:
                        w_ = c1 - c0
                        b1, b2 = nb(), nb()
                        for kc in range(8):
                            A("pe", lambda e, kc=kc, b1=b1, c0=c0, c1=c1, n0=n0, wg=wg: e.matmul(ps[:, b1, 0:c1 - c0], lhsT=wg[:, kc, n0:n0 + 128],
                                                                                                 rhs=hT[:, kc, c0:c1], start=(kc == 0), stop=(kc == 7)),
                              [wgr, "hT"], PB(b1))
                        for kc in range(8):
                            A("pe", lambda e, kc=kc, b2=b2, c0=c0, c1=c1, n0=n0, wu=wu: e.matmul(ps[:, b2, 0:c1 - c0], lhsT=wu[:, kc, n0:n0 + 128],
                                                                                                 rhs=hT[:, kc, c0:c1], start=(kc == 0), stop=(kc == 7)),
                              [wur, "hT"], PB(b2))
                        fq = (fc + (c0 // 512)) % 2
                        A("act", lambda e, b1=b1, w_=w_, fq=fq: e.activation(out=ftmp[:, fq, 0:w_], in_=ps[:, b1, 0:w_], func=AF.Silu), PB(b1), ["ftmp%d" % fq])
                        A("dve", lambda e, b2=b2, w_=w_, fq=fq, fc=fc, c0=c0, c1=c1: e.tensor_tensor(out=actT[:, fc, c0:c1], in0=ps[:, b2, 0:w_],
                                                                                                      in1=ftmp[:, fq, 0:w_], op=ALU.mult),
                          PB(b2) + ["ftmp%d" % fq], ["actT"])
            if stop_after == "F":
                continue

            S.barrier()
            S.dma("sp", lambda e: e.dma_start(out=gBf, in_=g_fin.partition_broadcast(128)), writes=["gBf"])
            A("dve", lambda e: e.memset(stat[:, 0:32], 0.0), [], ["stat"])
            wres_all = ["wdn0", "wdn1", "wdn2", "wdn3"]
            for tt in range(8):
                for half in range(2):
                    b = nb()
                    for fc in range(22):
                        A("pe", lambda e, fc=fc, tt=tt, half=half, b=b: e.matmul(ps[:, b, :], lhsT=actT[:, fc, tt * 128:(tt + 1) * 128],
                                                                                 rhs=wdn[:, fc, half * 512:(half + 1) * 512], start=(fc == 0), stop=(fc == 21)),
                          ["actT"] + wres_all, PB(b))
                    A("dve", lambda e, tt=tt, half=half, b=b: e.tensor_tensor(out=x_tok[:, tt, half * 512:(half + 1) * 512], in0=ps[:, b, :],
                                                                               in1=x_tok[:, tt, half * 512:(half + 1) * 512], op=ALU.add),
                      PB(b) + ["x%d" % tt], ["x%d" % tt])
                A("act", lambda e, tt=tt: e.activation(out=junkD, in_=x_tok[:, tt, :], func=AF.Square,
                                                       accum_out=stat[:, tt:tt + 1]), ["x%d" % tt, "stat"], ["ftmp0", "ftmp1", "stat%d" % tt])
                A("dve", lambda e, tt=tt: e.tensor_scalar(out=stat[:, 16 + tt:17 + tt], in0=stat[:, tt:tt + 1], scalar1=1.0 / D, scalar2=RMS_EPS,
                                                          op0=ALU.mult, op1=ALU.add), ["stat%d" % tt, "stat"], ["stat%d" % tt])
                A("act", lambda e, tt=tt: e.activation(out=stat[:, 16 + tt:17 + tt], in_=stat[:, 16 + tt:17 + tt], func=AF.Ln), ["stat%d" % tt], ["stat%d" % tt])
                A("act", lambda e, tt=tt: e.activation(out=stat[:, 16 + tt:17 + tt], in_=stat[:, 16 + tt:17 + tt], func=AF.Exp, scale=-0.5),
                  ["stat%d" % tt], ["stat%d" % tt])
                A("dve", lambda e, tt=tt: e.scalar_tensor_tensor(out=x_tok[:, tt, :], in0=x_tok[:, tt, :], scalar=stat[:, 16 + tt:17 + tt], in1=gBf,
                                                                 op0=ALU.mult, op1=ALU.mult), ["x%d" % tt, "stat%d" % tt, "gBf"], ["x%d" % tt])
                S.dma("sp", lambda e, tt=tt: e.dma_start(out=y_p[t0 + tt * 128:t0 + (tt + 1) * 128, :], in_=x_tok[:, tt, :]), reads=["x%d" % tt])
            if nsamp:
                for half in range(2):
                    b = nb()
                    for fc in range(22):
                        A("pe", lambda e, fc=fc, half=half, b=b: e.matmul(ps[0:NS, b, :], lhsT=actT[:, fc, HALF:HALF + NS],
                                                                          rhs=wdn[:, fc, half * 512:(half + 1) * 512], start=(fc == 0), stop=(fc == 21)),
                          ["actT"] + wres_all, PB(b))
                    A("dve", lambda e, half=half, b=b: e.tensor_tensor(out=xs_tok[:, half * 512:(half + 1) * 512], in0=ps[0:NS, b, :],
                                                                        in1=xs_tok[:, half * 512:(half + 1) * 512], op=ALU.add), PB(b) + ["xs"], ["xs"])
                A("act", lambda e: e.activation(out=junkD[0:NS, :], in_=xs_tok, func=AF.Square,
                                                accum_out=stat[0:NS, 8:9]), ["xs", "stat"], ["ftmp0", "ftmp1", "stat8"])
                A("dve", lambda e: e.tensor_scalar(out=stat[0:NS, 24:25], in0=stat[0:NS, 8:9], scalar1=1.0 / D, scalar2=RMS_EPS,
                                                    op0=ALU.mult, op1=ALU.add), ["stat8", "stat"], ["stat8"])
                A("act", lambda e: e.activation(out=stat[0:NS, 24:25], in_=stat[0:NS, 24:25], func=AF.Ln), ["stat8"], ["stat8"])
                A("act", lambda e: e.activation(out=stat[0:NS, 24:25], in_=stat[0:NS, 24:25], func=AF.Exp, scale=-0.5), ["stat8"], ["stat8"])
                A("dve", lambda e: e.scalar_tensor_tensor(out=xs_tok, in0=xs_tok, scalar=stat[0:NS, 24:25], in1=gBf[0:NS, :],
                                                          op0=ALU.mult, op1=ALU.mult), ["xs", "stat8", "gBf"], ["xs"])
                S.dma("sp", lambda e: e.dma_start(out=y_s, in_=xs_tok), reads=["xs"])
        S.finalize()
        S.emit(st)
    return nc


DBG_EXTRA = {"oa": [128, 4, 1040], "oT": [128, 4, 1040], "ob": [128, 4, 1040], "mergedT": [128, 8, 1040], "hT2": [128, 8, 1040]}


def _host_maps(inp, ncores=8):
    f = np.ascontiguousarray

    def a32(v):
        return np.asarray(v, dtype=np.float32)

    def fm(v, ncol):
        return f(a32(v).reshape(ncol, 128).T)

    pfm = np.concatenate([fm(inp['rwkv_mu'][0], 14), fm(inp['rwkv_w0'][0], 4), fm(inp['rwkv_a0'][0], 4), fm(inp['rwkv_k_k'][0], 4),
                          fm(inp['rwkv_k_a'][0], 4), fm(a32(inp['rwkv_r_k'][0]).reshape(-1), 4), fm(inp['rwkv_ln_g'][0], 4),
                          fm(inp['rwkv_ln_b'][0], 4), fm(inp['hgrn_lb'][0], 4), fm(inp['hgrn_lb'][1], 4), fm(inp['hgrn_norm_g'][0], 4)], axis=1)
    shared = dict(w_in=f(a32(inp['w_in'][0])), w2a2=f(np.concatenate([a32(inp['rwkv_w2'][0]), a32(inp['rwkv_a2'][0])], axis=0)),
                  g2=f(a32(inp['rwkv_g2'][0])), w_up_a=f(a32(inp['w_up_a'][0])), w_up_b=f(a32(inp['w_up_b'][0])), w_out=f(a32(inp['w_out'][0])),
                  w_fg=f(a32(inp['w_ffn_gate'][0])), w_fu=f(a32(inp['w_ffn_up'][0])), w_fd=f(a32(inp['w_ffn_down'][0])), pfm=f(pfm),
                  consts=make_consts(), g_mix=f(a32(inp['norm_mix_g'][0])), g_ffn=f(a32(inp['norm_ffn_g'][0])), g_fin=f(a32(inp['norm_final_g'])))
    maps = []
    for c in range(ncores):
        m = dict(shared)
        m['x_p'] = f(a32(inp['x_prompt'][c]))
        m['x_s'] = f(a32(inp['x_sample'][c * NS:(c + 1) * NS, 0]))
        m['wkv_s'] = f(a32(inp['state_rwkv_wkv'][0, c * NS:(c + 1) * NS]).reshape(NS * 8, 4096))
        m['shift_s'] = f(a32(inp['state_rwkv_shift'][0, c * NS:(c + 1) * NS]))
        m['hgrn_s'] = f(a32(inp['state_hgrn'][0, c * NS:(c + 1) * NS]))
        maps.append(m)
    return maps


_NC_CACHE = {}


def kernel(**inputs):
    ncores = 8
    if "nc" not in _NC_CACHE:
        _NC_CACHE["nc"] = build_program()
    nc = _NC_CACHE["nc"]
    maps = _host_maps(inputs, ncores)
    res = run_bass_kernel_spmd(nc, maps, core_ids=list(range(ncores)))
    r = res.results
    y_p = np.stack([r[c]["y_p"] for c in range(ncores)]).astype(np.float32)
    y_s = np.concatenate([r[c]["y_s"] for c in range(ncores)], axis=0).reshape(128, 1, D).astype(np.float32)
    wkv_p = np.stack([r[c]["o_wkv_p"] for c in range(ncores)])[None].astype(np.float32)
    shift_p = np.stack([r[c]["o_shift_p"] for c in range(ncores)])[None].astype(np.float32)
    hgrn_p = np.stack([r[c]["o_hgrn_p"] for c in range(ncores)])[None].astype(np.float32)
    wkv_s = np.concatenate([r[c]["o_wkv_s"].reshape(NS, 8, 64, 64) for c in range(ncores)], axis=0)[None].astype(np.float32)
    shift_s = np.concatenate([r[c]["o_shift_s"] for c in range(ncores)], axis=0)[None].astype(np.float32)
    hgrn_s = np.concatenate([r[c]["o_hgrn_s"] for c in range(ncores)], axis=0)[None].astype(np.float32)
    return (y_p, y_s, wkv_p, shift_p, hgrn_p, wkv_s, shift_s, hgrn_s)
```

```python
from contextlib import ExitStack
import numpy as np
import concourse.bass as bass
import concourse.mybir as mybir
from concourse.bass_utils import run_bass_kernel_spmd

F32 = mybir.dt.float32
BF16 = mybir.dt.bfloat16
AF = mybir.ActivationFunctionType
ALU = mybir.AluOpType
AX = mybir.AxisListType

COMPUTE = ("pe", "act", "dve", "pool")


class Op:
    __slots__ = ("idx", "eng", "fn", "call", "is_dma", "deps", "val", "marked", "slot", "acc", "dur", "xfer", "seq", "mode")

    def __init__(self, idx, eng, fn, call, is_dma):
        self.idx = idx
        self.eng = eng
        self.fn = fn
        self.call = call
        self.is_dma = is_dma
        self.deps = ()
        self.val = None
        self.marked = False
        self.slot = None
        self.acc = None
        self.dur = 100.0
        self.xfer = 0.0
        self.seq = 0
        self.mode = None


class _Rec:
    def __init__(self):
        self.call = None

    def __getattr__(self, name):
        def f(*a, **k):
            self.call = (name, a, k)
            return self
        return f


_WRITE_KW = ("out", "accum_out", "ap")
_ESZ = {}


def _esz(dt):
    if dt not in _ESZ:
        _ESZ[dt] = 4 if dt == F32 else 2
    return _ESZ[dt]


def _ap_intervals(ap):
    sp = str(ap.space)
    esz = _esz(ap.dtype)
    dims = list(ap.ap)
    if sp == "DRAM":
        base = ap.offset
        p0, p1 = 0, 1
        fd = dims
    else:
        pstride, pcnt = dims[0]
        p0 = ap.offset // pstride
        base = ap.offset % pstride
        p1 = p0 + pcnt
        fd = dims[1:]
    fd = sorted([(st_, c) for (st_, c) in fd if c > 1 and st_ != 0])
    run = 1
    k = 0
    while k < len(fd) and fd[k][0] == run:
        run *= fd[k][1]
        k += 1
    outer = fd[k:]
    n_outer = 1
    for st_, c in outer:
        n_outer *= c
    if n_outer <= 32:
        offs = [0]
        for st_, c in outer:
            offs = [o + i * st_ for o in offs for i in range(c)]
        iv = [((base + o) * esz, (base + o + run) * esz) for o in offs]
    else:
        ext = sum((c - 1) * st_ for st_, c in outer) + run
        iv = [(base * esz, (base + ext) * esz)]
    return sp, ap.name, p0, p1, iv


def _call_accesses(call):
    name, a, k = call
    items = []
    if name == "matmul":
        items.append((a[0] if a else k.get("out"), True))
        for kw in ("lhsT", "rhs"):
            items.append((k[kw], False))
    elif name == "memset":
        items.append((a[0] if a else k.get("ap"), True))
    else:
        for kw, v in k.items():
            if hasattr(v, "ap") and hasattr(v, "space"):
                items.append((v, kw in _WRITE_KW))
        for v in a:
            if hasattr(v, "ap") and hasattr(v, "space"):
                items.append((v, False))
    acc = []
    for ap, is_w in items:
        if ap is None:
            continue
        sp, nm, p0, p1, iv = _ap_intervals(ap)
        if sp == "PSUM":
            banks = set()
            for b0, b1 in iv:
                for bk in range(b0 // 2048, (b1 - 1) // 2048 + 1):
                    banks.add(bk)
            for bk in banks:
                acc.append((("ps", bk), 0, 128, 0, 1, is_w))
        else:
            for b0, b1 in iv:
                acc.append(((sp, nm), p0, p1, b0, b1, is_w))
    return acc


def _free_elems(ap):
    n = 1
    for s_ in ap.shape[1:]:
        n *= s_
    return n


_WI = {"fp32": 1.0, "dve": 1.0, "act": 1.0, "pe_small": 1.0, "pe_big": 1.0, "lat": 1.0, "pool": 1.0}


def _est(op):
    _est0(op)
    name = op.call[0]
    if op.is_dma:
        return
    if name == "matmul":
        k = op.call[2]
        if k["lhsT"].dtype == F32:
            op.dur *= _WI["fp32"]
        elif _free_elems(k["rhs"]) >= 512:
            op.dur *= _WI["pe_big"]
        else:
            op.dur *= _WI["pe_small"]
    elif op.eng in ("dve", "act", "pool"):
        op.dur *= _WI[op.eng]


def _est0(op):
    name, a, k = op.call
    if op.is_dma:
        ap = k.get("out")
        nbytes = 1
        for s_ in ap.shape:
            nbytes *= s_
        nbytes *= _esz(ap.dtype)
        op.dur = 1200.0 if op.eng == "pool" else 150.0
        op.xfer = 2000.0 + nbytes / 150.0
        return
    def _rnd(v):
        return 32 if v <= 32 else (64 if v <= 64 else 128)
    if name == "matmul":
        n = _free_elems(k["rhs"])
        f = 4.0 if k["lhsT"].dtype == F32 else 1.0
        op.dur = (16.0 + max(n, 48) * 0.42) * f
        op.mode = (_rnd(k["lhsT"].shape[0]), _rnd(_free_elems(k["lhsT"])), f)
    elif name == "transpose":
        op.dur = 70.0
        op.mode = ("T", _rnd(k["in_"].shape[0]), _rnd(_free_elems(k["in_"])))
    elif op.eng == "act":
        op.dur = 120.0 + _free_elems(k["out"]) * 0.85
        fn_ = k.get("func")
        if fn_ in (AF.Exp, AF.Ln):
            op.mode = "A"
        elif fn_ in (AF.Sigmoid, AF.Tanh):
            op.mode = "B"
        elif fn_ == AF.Silu:
            op.mode = "C"
    else:
        o = k.get("out") if "out" in k else (a[0] if a else None)
        n = _free_elems(o) if o is not None else 64
        if op.eng == "pool":
            op.dur = 100.0 + n * 2.2
        else:
            op.dur = 70.0 + n * 1.17


class Sched:
    def __init__(self, nc, dma_slots=None, reorder=True):
        self.nc = nc
        self.ops = []
        self.dma_slots = dma_slots or {"sp": 8, "act": 2, "pool": 6, "dve": 2, "pe": 2}
        self.reorder = reorder
        self.tag = ""
        self.tags = []

    def _mk(self, eng, fn, is_dma):
        rec = _Rec()
        fn(rec)
        name, a, k = rec.call
        op = Op(len(self.ops), eng, (lambda e: getattr(e, name)(*a, **k)), (name, a, k), is_dma)
        self.ops.append(op)
        self.tags.append(self.tag)
        return op

    def add(self, eng, fn, reads=(), writes=()):
        return self._mk(eng, fn, False)

    def dma(self, eng, fn, reads=(), writes=()):
        return self._mk(eng, fn, True)

    def barrier(self):
        pass

    def _build_dag(self):
        wlog = {}
        rlog = {}
        for op in self.ops:
            acc = _call_accesses(op.call)
            _est(op)
            deps = set()
            for key, p0, p1, b0, b1, is_w in acc:
                for (q0, q1, c0, c1, oi, oe) in wlog.get(key, ()):
                    if q0 < p1 and p0 < q1 and c0 < b1 and b0 < c1:
                        deps.add(oi)
                if is_w or key[0] == "ps":
                    for (q0, q1, c0, c1, oi, oe) in rlog.get(key, ()):
                        if q0 < p1 and p0 < q1 and c0 < b1 and b0 < c1:
                            if is_w or oe != op.eng:
                                deps.add(oi)
            for key, p0, p1, b0, b1, is_w in acc:
                if is_w:
                    for lg in (wlog, rlog):
                        l_ = lg.get(key)
                        if l_:
                            lg[key] = [e_ for e_ in l_ if not (p0 <= e_[0] and e_[1] <= p1 and b0 <= e_[2] and e_[3] <= b1)]
                    wlog.setdefault(key, []).append((p0, p1, b0, b1, op.idx, op.eng))
                else:
                    rlog.setdefault(key, []).append((p0, p1, b0, b1, op.idx, op.eng))
            deps.discard(op.idx)
            op.deps = tuple(sorted(deps))

    def _list_schedule(self):
        import heapq
        ops = self.ops
        n = len(ops)
        if not self.reorder:
            return list(range(n))
        npred = [len(o.deps) for o in ops]
        succ = [[] for _ in range(n)]
        for o in ops:
            for d in o.deps:
                succ[d].append(o.idx)
        ready_t = [0.0] * n
        fin = [0.0] * n
        import os as _os2
        PRI = _os2.environ.get("KPRI", "5")
        prio = list(range(n))
        if PRI != "idx":
            cp = [0.0] * n
            for i_ in range(n - 1, -1, -1):
                m = 0.0
                for s_ in succ[i_]:
                    if cp[s_] > m:
                        m = cp[s_]
                cp[i_] = m + ops[i_].dur + ops[i_].xfer + 200.0
            w_ = float(PRI)
            rank = sorted(range(n), key=lambda i_: (i_ - w_ * cp[i_] / 100.0))
            for r_, i_ in enumerate(rank):
                prio[i_] = r_
        engs = ("pe", "act", "dve", "pool", "sp")
        future = {e: [] for e in engs}
        avail = {e: [] for e in engs}
        free = {e: 0.0 for e in engs}
        for o in ops:
            if npred[o.idx] == 0:
                heapq.heappush(future[o.eng], (0.0, o.idx))
        order = []
        self.start_t = {}
        pe_mode = [None]
        act_set = [None]
        WINDOW = 3000
        done_upto = 0
        sched = [False] * n
        while len(order) < n:
            best = None
            for e in engs:
                fu, av = future[e], avail[e]
                while fu and fu[0][0] <= free[e]:
                    t_, i_ = heapq.heappop(fu)
                    heapq.heappush(av, (prio[i_], i_))
                if av:
                    pick = av[0]
                    if e == "pe" and len(av) > 1 and ops[pick[1]].mode != pe_mode[0]:
                        best_same = None
                        for i2 in av:
                            if ops[i2[1]].mode == pe_mode[0] and (best_same is None or i2 < best_same):
                                best_same = i2
                        if best_same is not None and best_same[0] - pick[0] < 400:
                            pick = best_same
                    if e == "act" and len(av) > 1 and ops[pick[1]].mode not in (None, act_set[0]):
                        best_same = None
                        for i2 in av:
                            if ops[i2[1]].mode in (None, act_set[0]) and (best_same is None or i2 < best_same):
                                best_same = i2
                        if best_same is not None and best_same[0] - pick[0] < 300:
                            pick = best_same
                    cand = (free[e], pick[0], e, True, pick)
                elif fu:
                    cand = (fu[0][0], prio[fu[0][1]], e, False, fu[0])
                else:
                    continue
                if best is None or cand[:2] < best[:2]:
                    best = cand
            start, _pr, e, from_av, item = best
            if from_av:
                i_ = item[1]
                if avail[e][0] == item:
                    heapq.heappop(avail[e])
                else:
                    avail[e].remove(item)
                    heapq.heapify(avail[e])
            else:
                i_ = item[1]
                heapq.heappop(future[e])
            o = ops[i_]
            if e == "pe":
                if o.mode != pe_mode[0]:
                    start += 150.0
                pe_mode[0] = o.mode
            elif e == "act" and o.mode is not None:
                if o.mode != act_set[0]:
                    start += 1300.0
                act_set[0] = o.mode
            free[e] = start + o.dur
            self.start_t[i_] = start
            fin[i_] = start + o.dur + o.xfer
            order.append(i_)
            for s_ in succ[i_]:
                lat = (40.0 if (ops[s_].eng == e and e == "pe" and not o.is_dma) else (150.0 if ops[s_].eng == e else 400.0)) * _WI["lat"]
                if fin[i_] + lat > ready_t[s_]:
                    ready_t[s_] = fin[i_] + lat
                npred[s_] -= 1
                if npred[s_] == 0:
                    heapq.heappush(future[ops[s_].eng], (ready_t[s_], s_))
        self.est_makespan = max(fin) if fin else 0.0
        return order

    def finalize(self):
        self._build_dag()
        order = self._list_schedule()
        self.order = order
        ops = self.ops
        dma_count = {}
        slot_last = {}
        extra = {}
        seqc = {}
        for i_ in order:
            op = ops[i_]
            seqc[op.eng] = seqc.get(op.eng, 0) + 1
            op.seq = seqc[op.eng]
            if op.is_dma:
                n_ = dma_count.get(op.eng, 0)
                dma_count[op.eng] = n_ + 1
                k = self.dma_slots[op.eng]
                op.slot = (op.eng, n_ % k)
                op.val = 16 * (n_ // k + 1)
                prev = slot_last.get(op.slot)
                if prev is not None:
                    extra[i_] = prev
                slot_last[op.slot] = i_
        known = {}
        vc = {}
        self.waits = {}
        for i_ in order:
            op = ops[i_]
            K = known.setdefault(op.eng, {})
            deps = list(op.deps)
            if i_ in extra:
                deps.append(extra[i_])
            cand = []
            for d in deps:
                p = ops[d]
                if (not p.is_dma) and p.eng == "pe" and op.eng == "pe" and not op.is_dma:
                    continue
                cand.append(d)
            cand.sort(key=lambda d: -ops[d].seq)
            w = []
            for d in cand:
                p = ops[d]
                key = ("dma", p.slot) if p.is_dma else ("eng", p.eng)
                val = p.val if p.is_dma else p.seq
                if K.get(key, -1) >= val:
                    continue
                w.append(d)
                p.marked = True
                for kk, vv in vc[d].items():
                    if K.get(kk, -1) < vv:
                        K[kk] = vv
            self.waits[i_] = w
            v = dict(K)
            if op.is_dma:
                v[("dma", op.slot)] = op.val
            else:
                if v.get(("eng", op.eng), -1) < op.seq and op.eng == "pe":
                    pass
                v[("eng", op.eng)] = max(v.get(("eng", op.eng), -1), op.seq)
            vc[i_] = v
        cnt = {}
        for i_ in order:
            op = ops[i_]
            if not op.is_dma and op.marked:
                cnt[op.eng] = cnt.get(op.eng, 0) + 1
                op.val = cnt[op.eng]
        self.counts = cnt

    def emit(self, stack):
        nc = self.nc
        sems = {}
        for e in COMPUTE:
            sems[("eng", e)] = stack.enter_context(nc.semaphore("s_" + e))
        for e, k in self.dma_slots.items():
            for i in range(k):
                sems[("dma", (e, i))] = stack.enter_context(nc.semaphore("d_%s%d" % (e, i)))
        block = stack.enter_context(nc.Block())
        ops = self.ops
        order = self.order
        waits = self.waits

        def run(engname, eng):
            for i_ in order:
                op = ops[i_]
                if op.eng != engname:
                    continue
                for d in waits[i_]:
                    p = ops[d]
                    s = sems[("dma", p.slot)] if p.is_dma else sems[("eng", p.eng)]
                    eng.wait_ge(s, p.val)
                ins = op.fn(eng)
                if op.is_dma:
                    ins.then_inc(sems[("dma", op.slot)], 16)
                elif op.marked:
                    ins.then_inc(sems[("eng", op.eng)], 1)
            last = {}
            for i_ in order:
                op = ops[i_]
                if op.is_dma and op.eng == engname:
                    last[op.slot] = op.val
            for slot, v in last.items():
                eng.wait_ge(sems[("dma", slot)], v)

        @block.sync
        def _(e):
            run("sp", e)

        @block.tensor
        def _(e):
            run("pe", e)

        @block.scalar
        def _(e):
            run("act", e)

        @block.vector
        def _(e):
            run("dve", e)

        @block.gpsimd
        def _(e):
            run("pool", e)


D = 1024
T = 2048
NS = 16
HALF = 1024
TW = HALF + NS
RW = 512
RPROJ = 1792
HW_ = 512
DFF = 2816
INP = 5888
CDEC = -0.6065306597126334
GN_EPS = 64e-5
RMS_EPS = 1e-6

PF = {}
_c = 0
for _n, _w in (("mu", 14), ("w0", 4), ("a0", 4), ("k_k", 4), ("k_a", 4), ("r_k", 4), ("ln_g", 4), ("ln_b", 4),
               ("lb0", 4), ("lb1", 4), ("hng", 4)):
    PF[_n] = _c
    _c += _w
PF_N = _c
DV = {"omu": 0, "omka": 14, "lb": 18, "omlb": 22}
DV_N = 29

CO = {"ident": 0, "blk64": 128, "ones": 256, "maskNM": 384, "maskL": 512, "maskI": 576, "cmask": 640}
CO_N = 640 + 1024 + 256


def make_consts():
    c = np.zeros((128, CO_N), np.float32)
    p = np.arange(128)
    c[:, 0:128] = np.eye(128, dtype=np.float32)
    c[:, 128:256] = (p[:, None] // 64 == p[None, :] // 64).astype(np.float32)
    c[:, 256:384] = 1.0
    s = (p % 64)[:, None]
    t = np.arange(64)[None, :]
    c[:, 384:448] = (s < t)
    c[:, 448:512] = (s <= t)
    c[:, 512:576] = (s > t)
    c[:, 576:640] = (s == t)
    tt = np.arange(1024)
    c[:, 640:1664] = (tt % 64 != 0).astype(np.float32)[None, :]
    c[:, 1664:1920] = np.eye(16, dtype=np.float32).reshape(-1)[None, :]
    return c


class Arena:
    def __init__(self, tile, nwords):
        self.t = tile
        self.n = nwords
        self.top = 0

    def alloc(self, dtype, shape):
        free = 1
        for s in shape[1:]:
            free *= s
        words = free if dtype == F32 else (free + 1) // 2
        words = (words + 7) // 8 * 8
        off = self.top
        self.top += words
        assert self.top <= self.n, ("arena overflow", self.top, self.n)
        v = self.t[0:shape[0], off:off + words]
        if dtype == BF16:
            v = v.bitcast(BF16)
        v = v[:, 0:free]
        if len(shape) > 2:
            names = ["d%d" % i for i in range(len(shape) - 1)]
            pat = "p (" + " ".join(names) + ") -> p " + " ".join(names)
            kw = {names[i]: shape[i + 1] for i in range(len(names))}
            v = v.rearrange(pat, **kw)
        return v


ARENA_WORDS = 32560


def build_program(dbg=None, passes=(0, 1), stop_after=None):
    nc = bass.Bass("TRN2", target_bir_lowering=False)

    def din(name, shape):
        return nc.dram_tensor(name, list(shape), F32, kind="ExternalInput").ap()

    def dout(name, shape):
        return nc.dram_tensor(name, list(shape), F32, kind="ExternalOutput").ap()

    x_p = din("x_p", [T, D])
    x_s = din("x_s", [NS, D])
    wkv_s = din("wkv_s", [NS * 8, 4096])
    shift_s = din("shift_s", [NS, RPROJ])
    hgrn_s = din("hgrn_s", [NS, 4, 128, 128])
    w_in = din("w_in", [D, INP])
    w2a2 = din("w2a2", [128, 512])
    g2 = din("g2", [128, 512])
    w_up_a = din("w_up_a", [RW, D])
    w_up_b = din("w_up_b", [HW_, D])
    w_out = din("w_out", [D, D])
    w_fg = din("w_fg", [D, DFF])
    w_fu = din("w_fu", [D, DFF])
    w_fd = din("w_fd", [DFF, D])
    pfm_d = din("pfm", [128, PF_N])
    consts_d = din("consts", [128, CO_N])
    g_mix = din("g_mix", [D])
    g_ffn = din("g_ffn", [D])
    g_fin = din("g_fin", [D])

    y_p = dout("y_p", [T, D])
    y_s = dout("y_s", [NS, D])
    o_wkv_p = dout("o_wkv_p", [8, 64, 64])
    o_shift_p = dout("o_shift_p", [RPROJ])
    o_hgrn_p = dout("o_hgrn_p", [4, 128, 128])
    o_wkv_s = dout("o_wkv_s", [NS * 8, 4096])
    o_shift_s = dout("o_shift_s", [NS, RPROJ])
    o_hgrn_s = dout("o_hgrn_s", [NS, 4, 128, 128])
    dbg_out = {}
    if dbg:
        for k, shp in dbg.items():
            dbg_out[k] = dout("dbg_" + k, shp)
    scr_a = nc.dram_tensor("scr_a", [NS, 6, 512], F32).ap()
    scr_y = nc.dram_tensor("scr_y", [NS, 512], F32).ap()

    st = ExitStack()
    with st:
        def sb(name, shape, dt):
            return st.enter_context(nc.sbuf_tensor(name, list(shape), dt))

        arena_t = sb("arena", [128, ARENA_WORDS], F32)
        hT = sb("hT", [128, 8, TW], BF16)
        oa = sb("oa", [128, 4, TW], BF16)
        ob = sb("ob", [128, 4, TW], BF16)
        NWB = 4
        wbuf = [sb("wbuf%d" % i, [128, 8, 512], BF16) for i in range(NWB)]
        consts = sb("consts_sb", [128, 384], F32)
        ident_bf = sb("ident_bf", [128, 128], BF16)
        masks_bf = sb("masks_bf", [128, 256], BF16)
        pfm = sb("pfm_sb", [128, PF_N], F32)
        dv = sb("dv", [128, DV_N], F32)
        lw_w = sb("lw_w", [128, 512], BF16)
        g2_w = sb("g2_w", [128, 512], BF16)
        carry = sb("carry", [128, 14], F32)
        shiftS = sb("shiftS", [128, 14, NS], F32)
        sprevT = sb("sprevT", [128, 14, NS], F32)
        H0f = sb("H0f", [128, 4, 64], F32)
        H0bd = sb("H0bd", [128, 4, 128], BF16)
        S0f = sb("S0f", [128, 4, 128], F32)
        S0b = sb("S0b", [128, 4, 128], BF16)
        WC = sb("WC", [128, 4, 16], F32)
        DC = sb("DC", [128, 4, 16], F32)
        stat = sb("stat", [128, 64], F32)
        ftmp = sb("ftmp", [128, 2, 512], F32)
        ps = st.enter_context(nc.psum_tensor("ps", [128, 8, 512], F32))

        ident = consts[:, 0:128]
        blk64 = consts[:, 128:256]
        ones = consts[:, 256:384]
        maskNM_bf = masks_bf[:, 0:128]
        maskL_bf = masks_bf[:, 128:192]
        maskI_bf = masks_bf[:, 192:256]

        S = Sched(nc)
        A = S.add
        bank_ctr = [0]

        def nb(n=1):
            b = bank_ctr[0]
            if n > 1:
                b = (b + n - 1) // n * n
            if b + n > 8:
                b = 0
            bank_ctr[0] = (b + n) % 8
            return b

        def nbp(par):
            b = bank_ctr[0]
            if b % 2 != par:
                b = (b + 1) % 8
            bank_ctr[0] = (b + 1) % 8
            return b

        par_ctr = [0]
        seq_ctr = [0, 0]

        par_nb = [6]

        def nb_par(n=1):
            m = par_nb[0]
            if n == 2:
                par_ctr[0] = (par_ctr[0] + 1) // 2 * 2
            b = par_ctr[0] % m
            par_ctr[0] = (par_ctr[0] + n) % m
            return b

        smp_ctr = [0]

        def nb_smp():
            smp_ctr[0] += 1
            return 4 + smp_ctr[0] % 2

        def nb_seq(par):
            return 6 + par

        def PB(b, n=1):
            return ["ps%d" % (b + i) for i in range(n)]

        def pf(name, j=0):
            c = PF[name] + j
            return pfm[:, c:c + 1]

        def dvc(name, j=0):
            c = DV[name] + j
            return dv[:, c:c + 1]

        wctr = [0]

        def load_w(src_ap, shape_view):
            i = wctr[0] % NWB
            wctr[0] += 1
            a, b = shape_view
            view = wbuf[i][:, :, :].rearrange("p a b -> p (a b)")[:, 0:a * b].rearrange("p (a b) -> p a b", a=a)
            rn = "wbuf%d" % i
            S.dma("pool", lambda e, view=view, src_ap=src_ap: e.dma_start(out=view, in_=src_ap), writes=[rn])
            return view, rn

        def dump(name, ap_sb, res):
            if dbg and name in dbg_out:
                S.dma("pool", lambda e: e.dma_start(out=dbg_out[name], in_=ap_sb), reads=res)

        S.dma("sp", lambda e: e.dma_start(out=consts[:], in_=consts_d[:, 0:384]), writes=["consts"])
        S.dma("pool", lambda e: e.dma_start(out=masks_bf[:], in_=consts_d[:, 384:640]))
        S.dma("sp", lambda e: e.dma_start(out=pfm[:], in_=pfm_d), writes=["pfm"])
        S.dma("pool", lambda e: e.dma_start(out=lw_w[:], in_=w2a2), writes=["lw_w"])
        S.dma("pool", lambda e: e.dma_start(out=g2_w[:], in_=g2), writes=["g2_w"])
        A("dve", lambda e: e.tensor_copy(out=ident_bf[:], in_=ident), ["consts"], ["ident_bf"])
        A("dve", lambda e: e.memset(carry[:], 0.0), [], ["carry"])
        A("dve", lambda e: e.memset(H0f[:], 0.0), [], ["H0f"])
        A("dve", lambda e: e.memset(H0bd[:], 0.0), [], ["H0b"])
        A("dve", lambda e: e.memset(S0f[:], 0.0), [], ["S0f"])
        A("dve", lambda e: e.memset(S0b[:], 0.0), [], ["S0b"])
        A("dve", lambda e: e.memset(sprevT[:], 0.0), [], ["sprevT"])
        A("dve", lambda e: e.tensor_scalar(out=dv[:, 0:14], in0=pfm[:, PF["mu"]:PF["mu"] + 14], scalar1=-1.0, scalar2=1.0,
                                            op0=ALU.mult, op1=ALU.add), ["pfm"], ["dv"])
        A("dve", lambda e: e.tensor_scalar(out=dv[:, 14:18], in0=pfm[:, PF["k_a"]:PF["k_a"] + 4], scalar1=-1.0, scalar2=1.0,
                                            op0=ALU.mult, op1=ALU.add), ["pfm"], ["dv"])
        A("dve", lambda e: e.tensor_tensor(out=dv[:, 18:22], in0=pfm[:, PF["lb0"]:PF["lb0"] + 4],
                                            in1=pfm[:, PF["lb1"]:PF["lb1"] + 4], op=ALU.subtract), ["pfm"], ["dv"])
        A("act", lambda e: e.activation(out=dv[:, 18:22], in_=dv[:, 18:22], func=AF.Sigmoid), ["dv"], ["dv"])
        A("dve", lambda e: e.tensor_scalar(out=dv[:, 22:26], in0=dv[:, 18:22], scalar1=-1.0, scalar2=1.0,
                                            op0=ALU.mult, op1=ALU.add), ["dv"], ["dv"])

        eps24, epsgn, epsrms = dv[:, 26:27], dv[:, 27:28], dv[:, 28:29]
        A("dve", lambda e: e.memset(dv[:, 26:27], 1e-24))
        A("dve", lambda e: e.memset(dv[:, 27:28], GN_EPS))
        A("dve", lambda e: e.memset(dv[:, 28:29], RMS_EPS))
        w_in_v = w_in.rearrange("(kc p) n -> p kc n", p=128)

        for pas in passes:
            import os as _os
            _skip = _os.environ.get("KSKIP", "")
            nsamp = NS if (pas == 0 and "nosamp" not in _skip) else 0
            W = HALF + nsamp
            blocks = [(0, 512), (512, 1024)] + ([(1024, 1040)] if nsamp else [])
            t0 = pas * HALF
            S.barrier()
            ar_ = Arena(arena_t, ARENA_WORDS)

            def norm_phase(tag, x_tok, xs_tok, g_dram, arena, out_cb, src_loaded, xsres):
                gB = arena.alloc(F32, [128, 1024])
                junk = arena.alloc(F32, [128, 1024])
                h_tok = arena.alloc(BF16, [128, 8, 1024])
                hs_tok = arena.alloc(BF16, [NS, 1024])
                R = tag
                S.dma("sp", lambda e: e.dma_start(out=gB, in_=g_dram.partition_broadcast(128)), writes=[R + "gB"])
                A("dve", lambda e: e.memset(stat[:, 0:32], 0.0), [], ["stat"])
                for tt in range(8):
                    A("act", lambda e, tt=tt: e.activation(out=junk, in_=x_tok[:, tt, :], func=AF.Square,
                                                           accum_out=stat[:, tt:tt + 1]),
                      [src_loaded(tt)], [R + "junk", "stat"])
                if nsamp:
                    A("act", lambda e: e.activation(out=junk[0:NS, :], in_=xs_tok, func=AF.Square,
                                                    accum_out=stat[0:NS, 8:9]), [xsres], [R + "junk", "stat"])
                A("dve", lambda e: e.tensor_scalar(out=stat[:, 16:25], in0=stat[:, 0:9], scalar1=1.0 / D, scalar2=RMS_EPS,
                                                    op0=ALU.mult, op1=ALU.add), ["stat"], ["stat"])
                A("act", lambda e: e.activation(out=stat[:, 16:25], in_=stat[:, 16:25], func=AF.Ln), ["stat"], ["stat"])
                A("act", lambda e: e.activation(out=stat[:, 16:25], in_=stat[:, 16:25], func=AF.Exp, scale=-0.5), ["stat"], ["stat"])
                out_cb(gB, h_tok, hs_tok, R)

            x_tok = ar_.alloc(F32, [128, 8, 1024])
            xs_tok = ar_.alloc(F32, [NS, 1024])
            for tt in range(8):
                S.dma("sp", lambda e, tt=tt: e.dma_start(out=x_tok[:, tt, :], in_=x_p[t0 + tt * 128:t0 + (tt + 1) * 128, :]),
                      writes=["P0x%d" % tt])
            if nsamp:
                S.dma("sp", lambda e: e.dma_start(out=xs_tok, in_=x_s), writes=["P0xs"])

            def to_hT(gB, h_tok, hs_tok, R, x_tok_, xs_tok_, xres, xsres):
                for tt in range(8):
                    A("dve", lambda e, tt=tt: e.scalar_tensor_tensor(out=h_tok[:, tt, :], in0=x_tok_[:, tt, :],
                                                                     scalar=stat[:, 16 + tt:17 + tt], in1=gB,
                                                                     op0=ALU.mult, op1=ALU.mult),
                      [xres(tt), "stat", R + "gB"], [R + "htok%d" % tt])
                    b = nb()
                    pT = ps[:, b, :].bitcast(BF16).rearrange("p (c t) -> p c t", c=8)
                    for dc in range(8):
                        A("pe", lambda e, tt=tt, dc=dc, pT=pT: e.transpose(out=pT[:, dc, :], in_=h_tok[:, tt, dc * 128:(dc + 1) * 128],
                                                                           identity=ident_bf[:]),
                          [R + "htok%d" % tt, "ident_bf"], PB(b))
                    eng = "act" if tt % 2 == 0 else "dve"
                    if eng == "act":
                        A("act", lambda e, tt=tt, pT=pT: e.copy(out=hT[:, :, tt * 128:(tt + 1) * 128], in_=pT), PB(b), ["hT"])
                    else:
                        A("dve", lambda e, tt=tt, pT=pT: e.tensor_copy(out=hT[:, :, tt * 128:(tt + 1) * 128], in_=pT), PB(b), ["hT"])
                if nsamp:
                    A("dve", lambda e: e.scalar_tensor_tensor(out=hs_tok, in0=xs_tok_, scalar=stat[0:NS, 24:25], in1=gB[0:NS, :],
                                                              op0=ALU.mult, op1=ALU.mult), [xsres, "stat", R + "gB"], [R + "hstok"])
                    b = nb()
                    pT = ps[:, b, :].bitcast(BF16).rearrange("p (c t) -> p c t", c=8)
                    for dc in range(8):
                        A("pe", lambda e, dc=dc, pT=pT: e.transpose(out=pT[:, dc, 0:NS], in_=hs_tok[:, dc * 128:(dc + 1) * 128],
                                                                    identity=ident_bf[0:NS, 0:NS]),
                          [R + "hstok", "ident_bf"], PB(b))
                    A("act", lambda e, pT=pT: e.copy(out=hT[:, :, HALF:HALF + NS], in_=pT[:, :, 0:NS]), PB(b), ["hT"])

            norm_phase("P0", x_tok, xs_tok, g_mix, ar_,
                       lambda gB, h_tok, hs_tok, R: to_hT(gB, h_tok, hs_tok, R, x_tok, xs_tok, lambda tt: "P0x%d" % tt, "P0xs"),
                       lambda tt: "P0x%d" % tt, "P0xs")
            if dbg and "hT" in dbg_out and pas == 0:
                dump("hT", hT[:], ["hT"])
            if stop_after == "P0":
                continue

            def proj_fm(wview, wres, ncol0, kcs, act_tile, act_res, evac):
                for (c0, c1) in blocks:
                    b = nb()
                    for kc in range(kcs):
                        A("pe", lambda e, kc=kc, b=b, c0=c0, c1=c1: e.matmul(ps[:, b, 0:c1 - c0], lhsT=wview[:, kc, ncol0:ncol0 + 128],
                                                                              rhs=act_tile[:, kc, c0:c1], start=(kc == 0),
                                                                              stop=(kc == kcs - 1)),
                          [wres, act_res], PB(b))
                    evac(b, c0, c1)

            S.tag = "p%d Rprep" % pas
            S.barrier()
            ar_ = Arena(arena_t, ARENA_WORDS)
            sig = ar_.alloc(F32, [128, 4, TW])
            g_bf = ar_.alloc(BF16, [128, 4, TW])
            ar_t = ar_.alloc(BF16, [128, 4, 16, 2, 64])
            bT = ar_.alloc(BF16, [128, 4, HALF])
            kT = ar_.alloc(BF16, [128, 4, HALF])
            vT = ar_.alloc(BF16, [128, 4, TW])
            bonus = ar_.alloc(BF16, [128, 4, TW])
            smp = ar_.alloc(F32, [128, 6, 4, NS])
            mark = ar_.top
            a_bf = ar_.alloc(BF16, [128, 4, TW])
            kpr = ar_.alloc(BF16, [128, 4, TW])
            praw = [ar_.alloc(F32, [128, 1048]) for _ in range(1)]
            NT = 7
            tmp = [ar_.alloc(F32, [128, TW]) for _ in range(NT)]
            cmask = ar_.alloc(BF16, [128, HALF])
            lwb = ar_.alloc(BF16, [128, TW])
            import os as _os
            _skip = _os.environ.get("KSKIP", "")
            if "cmask" not in _skip:
                S.dma("pool", lambda e: e.dma_start(out=cmask, in_=consts_d[:, 640:1664]), writes=["cmask"])

            if nsamp and "shiftT" not in _skip:
                sh_tok = tmp[0][0:NS, :]
                sh_tok2 = tmp[1][0:NS, :]
                S.dma("sp", lambda e: e.dma_start(out=sh_tok[:, 0:1024], in_=shift_s[:, 0:1024]), writes=["tmp0"])
                S.dma("sp", lambda e: e.dma_start(out=sh_tok2[:, 0:768], in_=shift_s[:, 1024:1792]), writes=["tmp1"])
                b = nb()
                for c in range(14):
                    src = sh_tok[:, c * 128:(c + 1) * 128] if c < 8 else sh_tok2[:, (c - 8) * 128:(c - 7) * 128]
                    A("pe", lambda e, c=c, src=src, b=b: e.transpose(out=ps[:, b, c * NS:(c + 1) * NS], in_=src, identity=ident[0:NS, 0:NS]),
                      ["tmp0", "tmp1", "consts"], PB(b))
                A("dve", lambda e, b=b: e.tensor_copy(out=sprevT[:, :, :], in_=ps[:, b, 0:14 * NS].rearrange("p (c n) -> p c n", c=14)),
                  PB(b), ["sprevT"])

            if stop_after == "shiftT":
                continue
            hv = [(0, 512), (512, W)]
            pctr = [0]

            def rwkv_chunk(c, wview, wres, ncol0, out_ap=None):
                pr = praw[0]
                t1 = tmp[6]
                A("dve", lambda e: e.tensor_copy(out=pr[:, 0:1], in_=carry[:, c:c + 1]))

                def evac(b, c0, c1):
                    A("act", lambda e: e.copy(out=pr[:, 1 + c0:1 + c1], in_=ps[:, b, 0:c1 - c0]))
                    A("act", lambda e: e.activation(out=t1[:, c0:c1], in_=ps[:, b, 0:c1 - c0], func=AF.Copy, scale=dvc("omu", c)))
                proj_fm(wview, wres, ncol0, 8, hT, "hT", evac)
                A("act", lambda e: e.copy(out=carry[:, c:c + 1], in_=pr[:, HALF:HALF + 1]))
                if nsamp:
                    A("act", lambda e: e.copy(out=shiftS[:, c, :], in_=pr[:, 1 + HALF:1 + HALF + NS]))
                out_ap_ = tmp[0] if out_ap is None else out_ap
                for (a_, b_) in hv:
                    b2 = min(b_, HALF)
                    A("dve", lambda e, a_=a_, b2=b2: e.scalar_tensor_tensor(out=out_ap_[:, a_:b2], in0=pr[:, a_:b2], scalar=pf("mu", c),
                                                                            in1=t1[:, a_:b2], op0=ALU.mult, op1=ALU.add))
                if nsamp:
                    A("dve", lambda e: e.scalar_tensor_tensor(out=out_ap_[:, HALF:HALF + NS], in0=sprevT[:, c, :], scalar=pf("mu", c),
                                                              in1=t1[:, HALF:HALF + NS], op0=ALU.mult, op1=ALU.add))
                return out_ap_

            wv, wr = load_w(w_in_v[:, :, 1536:1792], (8, 256))
            psm = rwkv_chunk(12, wv, wr, 0)
            for (a_, b_) in hv:
                A("act", lambda e, a_=a_, b_=b_: e.activation(out=lwb[0:64, a_:b_], in_=psm[0:64, a_:b_], func=AF.Tanh))
                A("act", lambda e, a_=a_, b_=b_: e.copy(out=lwb[64:128, a_:b_], in_=psm[64:128, a_:b_]))
            for j in range(4):
                for (c0, c1) in blocks:
                    b2_ = nb(2)
                    b = b2_
                    A("pe", lambda e, j=j, b=b, c0=c0, c1=c1: e.matmul(ps[:, b, 0:c1 - c0], lhsT=lw_w[0:64, j * 128:(j + 1) * 128],
                                                                        rhs=lwb[0:64, c0:c1], start=True, stop=True))
                    A("act", lambda e, j=j, b=b, c0=c0, c1=c1: e.activation(out=sig[:, j, c0:c1], in_=ps[:, b, 0:c1 - c0], func=AF.Sigmoid,
                                                                             bias=pf("w0", j)))
                    b = b2_ + 1
                    A("pe", lambda e, j=j, b=b, c0=c0, c1=c1: e.matmul(ps[:, b, 0:c1 - c0], lhsT=lw_w[64:128, j * 128:(j + 1) * 128],
                                                                        rhs=lwb[64:128, c0:c1], start=True, stop=True))
                    A("act", lambda e, j=j, b=b, c0=c0, c1=c1: e.activation(out=a_bf[:, j, c0:c1], in_=ps[:, b, 0:c1 - c0], func=AF.Sigmoid,
                                                                             bias=pf("a0", j)))
            psm = rwkv_chunk(13, wv, wr, 128)
            for (a_, b_) in hv:
                A("act", lambda e, a_=a_, b_=b_: e.activation(out=lwb[:, a_:b_], in_=psm[:, a_:b_], func=AF.Sigmoid))
            for j in range(4):
                for (c0, c1) in blocks:
                    b = nb()
                    A("pe", lambda e, j=j, b=b, c0=c0, c1=c1: e.matmul(ps[:, b, 0:c1 - c0], lhsT=g2_w[:, j * 128:(j + 1) * 128],
                                                                        rhs=lwb[:, c0:c1], start=True, stop=True))
                    A("dve", lambda e, j=j, b=b, c0=c0, c1=c1: e.tensor_copy(out=g_bf[:, j, c0:c1], in_=ps[:, b, 0:c1 - c0]))
            if dbg and pas == 0:
                dump("sig", sig, ["sig"])

            wv, wr = load_w(w_in_v[:, :, 1024:1536], (8, 512))
            for j in range(4):
                rwkv_chunk(8 + j, wv, wr, j * 128, out_ap=vT[:, j, :])
                if nsamp:
                    A("act", lambda e, j=j: e.copy(out=smp[:, 3, j, :], in_=vT[:, j, HALF:HALF + NS]))

            def fp32_blocksum(src_ap, src_res, mat, evac):
                for (c0, c1) in blocks:
                    b = nb()
                    A("pe", lambda e, b=b, c0=c0, c1=c1: e.matmul(ps[:, b, 0:c1 - c0], lhsT=mat, rhs=src_ap[:, c0:c1], start=True, stop=True))
                    evac(b, c0, c1)

            def cumsum_decay(j, cs_i):
                cs = tmp[cs_i]
                for (a_, b_) in hv:
                    b2 = min(b_, HALF)
                    A("dve", lambda e, a_=a_, b2=b2: e.tensor_tensor_scan(out=cs[:, a_:b2], data0=cmask[:, a_:b2], data1=sig[:, j, a_:b2], initial=0.0,
                                                                          op0=ALU.mult, op1=ALU.add))
                if nsamp:
                    A("dve", lambda e: e.tensor_copy(out=cs[:, HALF:HALF + NS], in_=sig[:, j, HALF:HALF + NS]))
                return cs

            def cview(ap2, a_, b2):
                return ap2[:, a_:b2].rearrange("p (c t) -> p c t", t=64)

            wv, wr = load_w(w_in_v[:, :, 512:1024], (8, 512))
            for j in range(4):
                k_ap = rwkv_chunk(4 + j, wv, wr, j * 128)
                kkr, rs, cs, en, de, ka = tmp[1], tmp[2], tmp[3], tmp[4], tmp[5], tmp[2]
                for (a_, b_) in hv:
                    A("act", lambda e, j=j, a_=a_, b_=b_: e.activation(out=kkr[:, a_:b_], in_=k_ap[:, a_:b_], func=AF.Copy, scale=pf("k_k", j)))
                    A("act", lambda e, a_=a_, b_=b_: e.activation(out=rs[:, a_:b_], in_=kkr[:, a_:b_], func=AF.Square))

                def ev_ss(b, c0, c1):
                    A("act", lambda e: e.activation(out=de[:, c0:c1], in_=ps[:, b, 0:c1 - c0], func=AF.Ln, bias=eps24[:, 0:1]))
                fp32_blocksum(rs, "tmp2", blk64, ev_ss)
                cumsum_decay(j, 3)
                for (a_, b_) in hv:
                    b2 = min(b_, HALF)
                    c_lo, c_hi = a_ // 64, b2 // 64
                    A("act", lambda e, a_=a_, b_=b_: e.activation(out=de[:, a_:b_], in_=de[:, a_:b_], func=AF.Exp, scale=-0.5))
                    A("dve", lambda e, a_=a_, b_=b_: e.tensor_tensor(out=kkr[:, a_:b_], in0=kkr[:, a_:b_], in1=de[:, a_:b_], op=ALU.mult))
                    A("act", lambda e, j=j, a_=a_, b2=b2, c_lo=c_lo, c_hi=c_hi: e.activation(out=WC[:, j, c_lo:c_hi], in_=cview(cs, a_, b2)[:, :, 63],
                                                                                             func=AF.Exp, scale=CDEC))
                    A("act", lambda e, a_=a_, b2=b2: e.activation(out=en[:, a_:b2], in_=cs[:, a_:b2], func=AF.Exp, scale=-CDEC))
                    if nsamp and b_ > HALF:
                        A("act", lambda e, j=j: e.activation(out=smp[:, 1, j, :], in_=cs[:, HALF:HALF + NS], func=AF.Exp, scale=CDEC))
                        A("act", lambda e, j=j: e.activation(out=smp[:, 4, j, :], in_=kkr[:, HALF:HALF + NS], func=AF.Copy, scale=-1.0))
                    A("dve", lambda e, j=j, a_=a_, b2=b2: e.tensor_tensor(out=de[:, a_:b2], in0=cs[:, a_:b2], in1=sig[:, j, a_:b2], op=ALU.subtract))
                    A("act", lambda e, a_=a_, b2=b2: e.activation(out=de[:, a_:b2], in_=de[:, a_:b2], func=AF.Exp, scale=CDEC))
                    A("dve", lambda e, j=j, a_=a_, b2=b2, c_lo=c_lo, c_hi=c_hi: e.scalar_tensor_tensor(
                        out=ar_t[:, j, c_lo:c_hi, 0, :], in0=cview(kkr, a_, b2), scalar=-1.0, in1=cview(de, a_, b2), op0=ALU.mult, op1=ALU.mult))
                    A("dve", lambda e, j=j, a_=a_, b_=b_: e.tensor_tensor(out=ka[:, a_:b_], in0=kkr[:, a_:b_], in1=a_bf[:, j, a_:b_], op=ALU.mult))
                    if nsamp and b_ > HALF:
                        A("act", lambda e, j=j: e.copy(out=smp[:, 5, j, :], in_=ka[:, HALF:HALF + NS]))
                    A("dve", lambda e, j=j, a_=a_, b2=b2: e.tensor_tensor(out=bT[:, j, a_:b2], in0=ka[:, a_:b2], in1=en[:, a_:b2], op=ALU.mult))
                    A("dve", lambda e, j=j, a_=a_, b_=b_: e.tensor_scalar(out=kkr[:, a_:b_], in0=a_bf[:, j, a_:b_], scalar1=pf("k_a", j),
                                                                          scalar2=dvc("omka", j), op0=ALU.mult, op1=ALU.add))
                    A("dve", lambda e, a_=a_, b_=b_: e.tensor_tensor(out=kkr[:, a_:b_], in0=kkr[:, a_:b_], in1=k_ap[:, a_:b_], op=ALU.mult))
                    A("dve", lambda e, j=j, a_=a_, b2=b2: e.tensor_tensor(out=kT[:, j, a_:b2], in0=kkr[:, a_:b2], in1=en[:, a_:b2], op=ALU.mult))
                    A("act", lambda e, j=j, a_=a_, b_=b_: e.activation(out=kpr[:, j, a_:b_], in_=kkr[:, a_:b_], func=AF.Copy, scale=pf("r_k", j)))
                    if nsamp and b_ > HALF:
                        A("act", lambda e, j=j: e.copy(out=smp[:, 2, j, :], in_=kkr[:, HALF:HALF + NS]))

            wv, wr = load_w(w_in_v[:, :, 0:512], (8, 512))
            for j in range(4):
                r_ap = rwkv_chunk(j, wv, wr, j * 128)
                cs = cumsum_decay(j, 3)
                rk = tmp[1]
                for (a_, b_) in hv:
                    b2 = min(b_, HALF)
                    c_lo, c_hi = a_ // 64, b2 // 64
                    A("act", lambda e, a_=a_, b2=b2: e.activation(out=cs[:, a_:b2], in_=cs[:, a_:b2], func=AF.Exp, scale=CDEC))
                    A("dve", lambda e, j=j, a_=a_, b2=b2, c_lo=c_lo, c_hi=c_hi: e.tensor_tensor(out=ar_t[:, j, c_lo:c_hi, 1, :], in0=cview(r_ap, a_, b2),
                                                                                                 in1=cview(cs, a_, b2), op=ALU.mult))
                    A("dve", lambda e, j=j, a_=a_, b_=b_: e.tensor_tensor(out=rk[:, a_:b_], in0=r_ap[:, a_:b_], in1=kpr[:, j, a_:b_], op=ALU.mult))
                if nsamp:
                    A("act", lambda e, j=j: e.copy(out=smp[:, 0, j, :], in_=r_ap[:, HALF:HALF + NS]))

                def ev_bon(b, c0, c1, j=j):
                    A("dve", lambda e: e.tensor_tensor(out=bonus[:, j, c0:c1], in0=ps[:, b, 0:c1 - c0], in1=vT[:, j, c0:c1], op=ALU.mult))
                fp32_blocksum(rk, "tmp1", blk64, ev_bon)
            if dbg and pas == 0:
                dump("ar", ar_t, ["ar"])
                dump("bT", bT, ["bT"])
                dump("kT", kT, ["kT"])
                dump("vT", vT, ["vT"])
                dump("bonus", bonus, ["bonus"])
            if stop_after == "Rprep":
                continue

            yT = sig
            if nsamp:
                ar_.top = mark
                L1v = ar_.alloc(F32, [128, 6, 64])
                saL = ar_.alloc(F32, [128, 64])
                yL1 = ar_.alloc(F32, [128, 64])
                rs_mark = ar_.top
                wf = [wbuf[i_][:, :, :].rearrange("p a b -> p (a b)").bitcast(F32) for i_ in range(4)]
                S_h = [wf[0].rearrange("p (v k) -> p v k", k=64), wf[1].rearrange("p (v k) -> p v k", k=64)]
                T_h = [wf[2].rearrange("p (v k) -> p v k", k=64), wf[3].rearrange("p (v k) -> p v k", k=64)]
                tok6h = [wf[2][0:NS, 0:1536].rearrange("p (v n) -> p v n", v=3), wf[3][0:NS, 0:1536].rearrange("p (v n) -> p v n", v=3)]
                ytok = wf[2][0:NS, 1536:2048]
                shtok = wf[3][0:NS, 0:2048]
                def emit_rsample():
                    for hf in range(2):
                        S.dma("sp", lambda e, hf=hf: e.dma_start(out=S_h[hf].rearrange("p v k -> p (v k)"), in_=wkv_s[:, hf * 2048:(hf + 1) * 2048]))
                    for vec in range(6):
                        b = nb_par()
                        for j in range(4):
                            A("pe", lambda e, vec=vec, j=j, b=b: e.transpose(out=ps[0:NS, b, j * 128:(j + 1) * 128], in_=smp[:, vec, j, :], identity=ident))
                        dst = tok6h[vec // 3][:, vec % 3, :]
                        if vec % 2 == 0:
                            A("act", lambda e, dst=dst, b=b: e.copy(out=dst, in_=ps[0:NS, b, :]))
                        else:
                            A("dve", lambda e, dst=dst, b=b: e.tensor_copy(out=dst, in_=ps[0:NS, b, :]))
                    for hf in range(2):
                        S.dma("sp", lambda e, hf=hf: e.dma_start(out=scr_a[:, hf * 3:(hf + 1) * 3, :], in_=tok6h[hf]))
                    for vec in range(6):
                        S.dma("sp", lambda e, vec=vec: e.dma_start(out=L1v[:, vec, :], in_=scr_a[:, vec, :].rearrange("b (h n) -> b h n", h=8)))

                    def bc_v(vec):
                        return L1v[:, vec, :].unsqueeze(1).broadcast_to([128, 32, 64])

                    def bc_k(ap2, hf):
                        return ap2[:, hf * 32:(hf + 1) * 32].unsqueeze(2).broadcast_to([128, 32, 64])
                    for hf in range(2):
                        Sx, Tx = S_h[hf], T_h[hf]
                        vs = slice(hf * 32, (hf + 1) * 32)
                        A("dve", lambda e, Sx=Sx, Tx=Tx: e.tensor_tensor(out=Tx, in0=Sx, in1=bc_v(4), op=ALU.mult))
                        A("dve", lambda e, Tx=Tx, vs=vs: e.tensor_reduce(out=saL[:, vs], in_=Tx, axis=AX.X, op=ALU.add))
                        A("dve", lambda e, Sx=Sx: e.tensor_tensor(out=Sx, in0=Sx, in1=bc_v(1), op=ALU.mult))
                        A("dve", lambda e, Tx=Tx, hf=hf: e.tensor_tensor(out=Tx, in0=bc_k(saL, hf), in1=bc_v(5), op=ALU.mult))
                        A("dve", lambda e, Sx=Sx, Tx=Tx: e.tensor_tensor(out=Sx, in0=Sx, in1=Tx, op=ALU.add))
                        A("dve", lambda e, Tx=Tx, hf=hf: e.tensor_tensor(out=Tx, in0=bc_k(L1v[:, 3, :], hf), in1=bc_v(2), op=ALU.mult))
                        A("dve", lambda e, Sx=Sx, Tx=Tx: e.tensor_tensor(out=Sx, in0=Sx, in1=Tx, op=ALU.add))
                        S.dma("sp", lambda e, Sx=Sx, hf=hf: e.dma_start(out=o_wkv_s[:, hf * 2048:(hf + 1) * 2048], in_=Sx.rearrange("p v k -> p (v k)")))
                        A("dve", lambda e, Sx=Sx, Tx=Tx: e.tensor_tensor(out=Tx, in0=Sx, in1=bc_v(0), op=ALU.mult))
                        A("dve", lambda e, Tx=Tx, vs=vs: e.tensor_reduce(out=yL1[:, vs], in_=Tx, axis=AX.X, op=ALU.add))
                    S.dma("sp", lambda e: e.dma_start(out=scr_y.rearrange("b (h n) -> (b h) n", h=8), in_=yL1))
                    S.dma("sp", lambda e: e.dma_start(out=ytok, in_=scr_y))
                    b = nb_par()
                    for j in range(4):
                        A("pe", lambda e, j=j, b=b: e.transpose(out=ps[:, b, j * NS:(j + 1) * NS], in_=ytok[:, j * 128:(j + 1) * 128],
                                                                identity=ident[0:NS, 0:NS]))
                    A("act", lambda e, b=b: e.copy(out=yT[:, :, HALF:HALF + NS], in_=ps[:, b, 0:4 * NS].rearrange("p (j n) -> p j n", j=4)))
                    for g_ in range(4):
                        cs_ = list(range(g_ * 4, min(14, g_ * 4 + 4)))
                        b = nb_par()
                        for ci, c in enumerate(cs_):
                            A("pe", lambda e, ci=ci, c=c, b=b: e.transpose(out=ps[0:NS, b, ci * 128:(ci + 1) * 128], in_=shiftS[:, c, :], identity=ident))
                        n_ = len(cs_) * 128
                        A("act", lambda e, g_=g_, b=b, n_=n_: e.copy(out=shtok[:, g_ * 512:g_ * 512 + n_], in_=ps[0:NS, b, 0:n_]))
                    S.dma("sp", lambda e: e.dma_start(out=o_shift_s, in_=shtok[:, 0:RPROJ]))

            if stop_after == "Rsample":
                continue
            ar_.top = rs_mark if nsamp else mark
            bk_tok = [ar_.alloc(BF16, [128, 2, 512]) for _ in range(2)]
            v_tok = [ar_.alloc(BF16, [128, 512]) for _ in range(2)]
            NM_sb = [ar_.alloc(BF16, [128, 8, 2, 128]) for _ in range(2)]
            P_sb2 = [[ar_.alloc(BF16, [128, 8, 64]) for _ in range(2)] for _ in range(2)]
            Tt_sb2 = [[ar_.alloc(BF16, [128, 8, 64]) for _ in range(1)] for _ in range(2)]
            QT_sb2 = [[ar_.alloc(BF16, [128, 8, 2, 64]) for _ in range(2)] for _ in range(2)]
            X_sb = [ar_.alloc(BF16, [128, 8, 64]) for _ in range(2)]
            U_sb = [ar_.alloc(BF16, [128, 8, 64]) for _ in range(2)]
            XV_sb = [ar_.alloc(F32, [128, 8, 64]) for _ in range(2)]
            Hs = ar_.alloc(F32, [128, 4, 64])
            gtmp = [ar_.alloc(F32, [128, TW]) for _ in range(3)]

            for i in range(8):
                q = i % 2
                P_sb, QT_sb, Tt_sb = P_sb2[q], QT_sb2[q], Tt_sb2[q]
                S.tag = "p%d Rpar%d" % (pas, i)
                b1 = nb_par()
                b2 = nb_par()
                pT1 = ps[:, b1, :].bitcast(BF16).rearrange("p (v n) -> p v n", v=2)
                pT2 = ps[:, b2, :].bitcast(BF16)
                for j in range(4):
                    A("pe", lambda e, i=i, j=j, pT1=pT1: e.transpose(out=pT1[:, 0, j * 128:(j + 1) * 128], in_=bT[:, j, i * 128:(i + 1) * 128],
                                                                     identity=ident_bf[:]), ["bT", "ident_bf"], PB(b1))
                    A("pe", lambda e, i=i, j=j, pT1=pT1: e.transpose(out=pT1[:, 1, j * 128:(j + 1) * 128], in_=kT[:, j, i * 128:(i + 1) * 128],
                                                                     identity=ident_bf[:]), ["kT", "ident_bf"], PB(b1))
                    A("pe", lambda e, i=i, j=j, pT2=pT2: e.transpose(out=pT2[:, j * 128:(j + 1) * 128], in_=vT[:, j, i * 128:(i + 1) * 128],
                                                                     identity=ident_bf[:]), ["vT", "ident_bf"], PB(b2))
                A("act", lambda e, q=q, pT1=pT1: e.copy(out=bk_tok[q][:], in_=pT1), PB(b1), ["bk_tok%d" % q])
                A("dve", lambda e, q=q, pT2=pT2: e.tensor_copy(out=v_tok[q][:], in_=pT2[:, 0:512]), PB(b2), ["v_tok%d" % q])
                if stop_after == "c1":
                    break
                for hg in range(2):
                    b = nb_par(2)
                    for hh4 in range(4):
                        h = hg * 4 + hh4
                        j, hh = h // 2, h % 2
                        for e_ in range(2):
                            c = 2 * i + e_
                            for x, src in ((0, bT), (1, kT)):
                                bb, oo = b + hh, ((hh4 // 2) * 2 + x) * 128
                                A("pe", lambda e, src=src, j=j, hh=hh, c=c, e_=e_, bb=bb, oo=oo: e.matmul(
                                    ps[e_ * 64:(e_ + 1) * 64, bb, oo:oo + 128], lhsT=src[hh * 64:(hh + 1) * 64, j, c * 64:(c + 1) * 64],
                                    rhs=ar_t[hh * 64:(hh + 1) * 64, j, c, :, :], start=True, stop=True),
                                  ["bT", "kT", "ar"], PB(b, 2))
                    for hh in range(2):
                        nmv = NM_sb[q][:, hg * 4 + hh:(hg + 1) * 4:2, :, :]
                        A("dve", lambda e, nmv=nmv, b=b, hh=hh: e.tensor_tensor(out=nmv, in0=ps[:, b + hh, :].rearrange("p (jj x n) -> p jj x n", jj=2, x=2),
                                                                                 in1=maskNM_bf.unsqueeze(1).unsqueeze(1).broadcast_to([128, 2, 2, 128]), op=ALU.mult))
                if stop_after == "c2":
                    break
                b = nb_par(2)
                for h in range(8):
                    j, hh = h // 2, h % 2
                    for e_ in range(2):
                        c = 2 * i + e_
                        A("pe", lambda e, j=j, hh=hh, c=c, e_=e_, h=h, b=b: e.matmul(
                            ps[e_ * 64:(e_ + 1) * 64, b + hh, j * 64:(j + 1) * 64], lhsT=ar_t[hh * 64:(hh + 1) * 64, j, c, 0, :],
                            rhs=bT[hh * 64:(hh + 1) * 64, j, c * 64:(c + 1) * 64], start=True, stop=True))
                for hh in range(2):
                    pv = P_sb[0][:, hh:8:2, :]
                    A("dve", lambda e, pv=pv, b=b, hh=hh: e.tensor_tensor(out=pv, in0=ps[:, b + hh, 0:256].rearrange("p (j s) -> p j s", j=4),
                                                                           in1=maskL_bf.unsqueeze(1).broadcast_to([128, 4, 64]), op=ALU.mult))
                if stop_after == "c3":
                    break
                Q0 = NM_sb[q][:, :, 0, 0:64]
                A("dve", lambda e, q=q: e.tensor_tensor(out=QT_sb[1][:, :, 1, :], in0=NM_sb[q][:, :, 0, 0:64],
                                                          in1=maskI_bf.unsqueeze(1).broadcast_to([128, 8, 64]), op=ALU.add))
                ev_ctr = [0]
                import os as _os3
                EVK = int(_os3.environ.get("EVK", "1000"))

                def evac_half(bank, e_, dst_ap, shape_pat, **kw):
                    sl = slice(e_ * 64, (e_ + 1) * 64)
                    src = ps[sl, bank, :].rearrange(shape_pat, **kw)
                    ev_ctr[0] += 1
                    if ev_ctr[0] % EVK != 0:
                        A("act", lambda e: e.copy(out=dst_ap, in_=src))
                    else:
                        A("dve", lambda e: e.tensor_copy(out=dst_ap, in_=src))

                def evac2(bk, dst):
                    for e_ in range(2):
                        sl = slice(e_ * 64, (e_ + 1) * 64)
                        evac_half(bk + e_, e_, dst[sl, :, :], "p (h s) -> p h s", h=8)

                bA = nb_par(2)
                bB = nb_par(2)
                for h in range(8):
                    for e_ in range(2):
                        sl = slice(e_ * 64, (e_ + 1) * 64)
                        A("pe", lambda e, h=h, sl=sl, e_=e_, bA=bA: e.matmul(ps[sl, bA + e_, h * 64:(h + 1) * 64], lhsT=Q0[sl, h, :], rhs=P_sb[0][sl, h, :],
                                                                             start=True, stop=True))
                        A("pe", lambda e, h=h, sl=sl, e_=e_, bB=bB: e.matmul(ps[sl, bB + e_, h * 64:(h + 1) * 64], lhsT=P_sb[0][sl, h, :], rhs=Q0[sl, h, :],
                                                                             start=True, stop=True))
                evac2(bA, P_sb[1])
                for e_ in range(2):
                    sl = slice(e_ * 64, (e_ + 1) * 64)
                    evac_half(bB + e_, e_, QT_sb[1][sl, :, 0, :], "p (h s) -> p h s", h=8)
                Tc = None
                for lev in range(1, 6):
                    pi = lev % 2
                    Pc = P_sb[pi]
                    QTc = QT_sb[pi]
                    QTn = QT_sb[1 - pi]
                    last = (lev == 5)
                    if not last:
                        bA = nb_par(2)
                        for h in range(8):
                            for e_ in range(2):
                                sl = slice(e_ * 64, (e_ + 1) * 64)
                                A("pe", lambda e, h=h, sl=sl, e_=e_, bA=bA, QTc=QTc, Pc=Pc: e.matmul(ps[sl, bA + e_, h * 64:(h + 1) * 64], lhsT=QTc[sl, h, 0, :],
                                                                                                      rhs=Pc[sl, h, :], start=True, stop=True))
                    if not last:
                        for hg in range(2):
                            bB = nb_par(2)
                            for h4 in range(4):
                                h = hg * 4 + h4
                                for e_ in range(2):
                                    sl = slice(e_ * 64, (e_ + 1) * 64)
                                    A("pe", lambda e, h=h, h4=h4, sl=sl, e_=e_, bB=bB, QTc=QTc, Pc=Pc: e.matmul(
                                        ps[sl, bB + e_, h4 * 128:(h4 + 1) * 128], lhsT=Pc[sl, h, :], rhs=QTc[sl, h, :, :], start=True, stop=True))
                            for e_ in range(2):
                                sl = slice(e_ * 64, (e_ + 1) * 64)
                                src4 = ps[sl, bB + e_, :].rearrange("p (h x s) -> p h x s", h=4, x=2)
                                hsl = slice(hg * 4, (hg + 1) * 4)
                                A("act", lambda e, sl=sl, src4=src4, hsl=hsl, QTn=QTn: e.copy(out=QTn[sl, hsl, 0, :], in_=src4[:, :, 0, :]))
                                A("dve", lambda e, sl=sl, src4=src4, hsl=hsl, QTn=QTn, QTc=QTc: e.tensor_tensor(out=QTn[sl, hsl, 1, :], in0=src4[:, :, 1, :],
                                                                                                                 in1=QTc[sl, hsl, 1, :], op=ALU.add))
                        evac2(bA, P_sb[1 - pi])
                    else:
                        bB = nb_par(2)
                        for h in range(8):
                            for e_ in range(2):
                                sl = slice(e_ * 64, (e_ + 1) * 64)
                                A("pe", lambda e, h=h, sl=sl, e_=e_, bB=bB, QTc=QTc, Pc=Pc: e.matmul(
                                    ps[sl, bB + e_, h * 64:(h + 1) * 64], lhsT=Pc[sl, h, :], rhs=QTc[sl, h, 1, :], start=True, stop=True))
                        Tc = Tt_sb[0]
                        for e_ in range(2):
                            sl = slice(e_ * 64, (e_ + 1) * 64)
                            A("dve", lambda e, sl=sl, e_=e_, bB=bB, QTc=QTc, Tc=Tc: e.tensor_tensor(out=Tc[sl, :, :], in0=ps[sl, bB + e_, :].rearrange("p (h s) -> p h s", h=8),
                                                                                                     in1=QTc[sl, :, 1, :], op=ALU.add))
                bXV = nb_par(2)
                for h in range(8):
                    for e_ in range(2):
                        sl = slice(e_ * 64, (e_ + 1) * 64)
                        A("pe", lambda e, h=h, sl=sl, e_=e_, q=q, bXV=bXV: e.matmul(ps[sl, bXV + e_, h * 64:(h + 1) * 64], lhsT=NM_sb[q][sl, h, 1, 0:64],
                                                                                    rhs=v_tok[q][sl, h * 64:(h + 1) * 64], start=True, stop=True))
                evac2(bXV, XV_sb[q])
                if stop_after == "c4":
                    break
                S.tag = "p%d Rseq%d" % (pas, i)
                for e_ in range(2):
                    c = 2 * i + e_
                    sl = slice(e_ * 64, (e_ + 1) * 64)
                    xq = c % 2
                    bX = nb_seq(e_)
                    for j in range(4):
                        A("pe", lambda e, j=j, sl=sl, c=c, bX=bX: e.matmul(ps[sl, bX, j * 128:(j + 1) * 128], lhsT=ar_t[:, j, c, 0, :],
                                                                           rhs=H0bd[:, j, :], start=True, stop=True))
                    A("dve", lambda e, sl=sl, xq=xq, bX=bX, q=q: e.tensor_tensor(out=X_sb[xq][sl, :, :], in0=ps[sl, bX, :].rearrange("p (h s) -> p h s", h=8),
                                                                                  in1=XV_sb[q][sl, :, :], op=ALU.add))
                    bU = nb_seq(e_)
                    for h in range(8):
                        A("pe", lambda e, h=h, sl=sl, xq=xq, bU=bU, Tc=Tc: e.matmul(ps[sl, bU, h * 64:(h + 1) * 64], lhsT=Tc[sl, h, :],
                                                                                    rhs=X_sb[xq][sl, h, :], start=True, stop=True),
                          [], PB(bU))
                    A("dve", lambda e, sl=sl, xq=xq, bU=bU: e.tensor_copy(out=U_sb[xq][sl, :, :], in_=ps[sl, bU, :].rearrange("p (h s) -> p h s", h=8)),
                      PB(bU), ["U%d" % xq])
                    bY1 = nb_seq(1 - e_)
                    for j in range(4):
                        A("pe", lambda e, j=j, c=c, bY1=bY1: e.matmul(ps[:, bY1, j * 64:(j + 1) * 64], lhsT=H0bd[:, j, :],
                                                                      rhs=ar_t[:, j, c, 1, :], start=True, stop=True))
                    A("act", lambda e, c=c, bY1=bY1: e.copy(out=yT[:, :, c * 64:(c + 1) * 64], in_=ps[:, bY1, 0:256].rearrange("p (j t) -> p j t", j=4)))
                    bY = nb_seq(e_)
                    for h in range(8):
                        j, hh = h // 2, h % 2
                        hs = slice(hh * 64, (hh + 1) * 64)
                        A("pe", lambda e, h=h, j=j, hs=hs, sl=sl, xq=xq, q=q, bY=bY: e.matmul(ps[hs, bY, j * 64:(j + 1) * 64], lhsT=U_sb[xq][sl, h, :],
                                                                                              rhs=NM_sb[q][sl, h, 0, 64:128], start=True, stop=False))
                        A("pe", lambda e, h=h, j=j, hs=hs, sl=sl, q=q, bY=bY: e.matmul(ps[hs, bY, j * 64:(j + 1) * 64], lhsT=v_tok[q][sl, h * 64:(h + 1) * 64],
                                                                                       rhs=NM_sb[q][sl, h, 1, 64:128], start=False, stop=True))
                    A("dve", lambda e, c=c, bY=bY: e.tensor_tensor(out=yT[:, :, c * 64:(c + 1) * 64], in0=ps[:, bY, 0:256].rearrange("p (j t) -> p j t", j=4),
                                                                    in1=yT[:, :, c * 64:(c + 1) * 64], op=ALU.add))
                    bG = nb_seq(e_)
                    for h in range(8):
                        j, hh = h // 2, h % 2
                        hs = slice(hh * 64, (hh + 1) * 64)
                        A("pe", lambda e, h=h, j=j, hs=hs, sl=sl, xq=xq, q=q, bG=bG: e.matmul(ps[hs, bG, j * 64:(j + 1) * 64], lhsT=bk_tok[q][sl, 0, h * 64:(h + 1) * 64],
                                                                                              rhs=U_sb[xq][sl, h, :], start=True, stop=False),
                          ["bk_tok%d" % q, "U%d" % xq], PB(bG))
                        A("pe", lambda e, h=h, j=j, hs=hs, sl=sl, q=q, bG=bG: e.matmul(ps[hs, bG, j * 64:(j + 1) * 64], lhsT=bk_tok[q][sl, 1, h * 64:(h + 1) * 64],
                                                                                       rhs=v_tok[q][sl, h * 64:(h + 1) * 64], start=False, stop=True),
                          ["bk_tok%d" % q, "v_tok%d" % q], PB(bG))
                    A("dve", lambda e, bG=bG: e.tensor_tensor(out=Hs[:], in0=ps[:, bG, 0:256].rearrange("p (j v) -> p j v", j=4), in1=H0f[:], op=ALU.add),
                      PB(bG) + ["H0f"], ["Hs"])
                    A("dve", lambda e, c=c: e.tensor_tensor(out=H0f[:], in0=Hs[:], in1=WC[:, :, c:c + 1].broadcast_to([128, 4, 64]), op=ALU.mult),
                      ["Hs", "WC"], ["H0f"])
                    A("act", lambda e: e.copy(out=H0bd[0:64, :, 0:64], in_=H0f[0:64, :, :]), ["H0f"], ["H0b"])
                    A("act", lambda e: e.copy(out=H0bd[64:128, :, 64:128], in_=H0f[64:128, :, :]), ["H0f"], ["H0b"])
            if stop_after in ("c1", "c2", "c3", "c4"):
                continue
            if dbg and pas == 0:
                dump("yT", yT, ["yT"])
            if stop_after == "Rchunk":
                continue
            if nsamp:
                S.tag = "p%d Rsample" % pas
                emit_rsample()
            S.tag = "p%d Rpost" % pas
            for j in range(4):
                yj = yT[:, j, :]
                yc, sq_, rs_ = gtmp[0], gtmp[1], gtmp[2]

                def ev_mean(b, c0, c1, j=j):
                    A("dve", lambda e: e.scalar_tensor_tensor(out=yc[:, c0:c1], in0=ps[:, b, 0:c1 - c0], scalar=-1.0 / 64, in1=yT[:, j, c0:c1],
                                                              op0=ALU.mult, op1=ALU.add), PB(b) + ["yT"], ["gtmp0"])
                fp32_blocksum(yj, "yT", blk64, ev_mean)
                A("act", lambda e: e.activation(out=sq_[:, 0:W], in_=yc[:, 0:W], func=AF.Square), ["gtmp0"], ["gtmp1"])

                def ev_var(b, c0, c1):
                    A("act", lambda e: e.activation(out=rs_[:, c0:c1], in_=ps[:, b, 0:c1 - c0], func=AF.Ln, scale=1.0 / 64, bias=epsgn[:, 0:1]))
                fp32_blocksum(sq_, "gtmp1", blk64, ev_var)
                A("act", lambda e: e.activation(out=rs_[:, 0:W], in_=rs_[:, 0:W], func=AF.Exp, scale=-0.5), ["gtmp2"], ["gtmp2"])
                A("dve", lambda e: e.tensor_tensor(out=yc[:, 0:W], in0=yc[:, 0:W], in1=rs_[:, 0:W], op=ALU.mult), ["gtmp0", "gtmp2"], ["gtmp0"])
                A("dve", lambda e, j=j: e.tensor_scalar(out=yc[:, 0:W], in0=yc[:, 0:W], scalar1=pf("ln_g", j), scalar2=pf("ln_b", j),
                                                        op0=ALU.mult, op1=ALU.add), ["gtmp0", "pfm"], ["gtmp0"])
                A("dve", lambda e, j=j: e.tensor_tensor(out=yc[:, 0:W], in0=yc[:, 0:W], in1=bonus[:, j, 0:W], op=ALU.add), ["gtmp0", "bonus"], ["gtmp0"])
                A("dve", lambda e, j=j: e.tensor_tensor(out=oa[:, j, 0:W], in0=yc[:, 0:W], in1=g_bf[:, j, 0:W], op=ALU.mult), ["gtmp0", "g_bf"], ["oa"])
            if dbg and pas == 0:
                dump("oa", oa[:], ["oa"])
            if pas == passes[-1]:
                wst = gtmp[0][0:64, 0:512].rearrange("p (j n) -> p j n", j=4)
                b = nb()
                for j in range(4):
                    A("pe", lambda e, j=j, b=b: e.transpose(out=ps[0:64, b, j * 128:(j + 1) * 128], in_=H0f[:, j, :], identity=ident),
                      ["H0f", "consts"], PB(b))
                A("act", lambda e, b=b: e.copy(out=wst, in_=ps[0:64, b, :].rearrange("p (j n) -> p j n", j=4)), PB(b), ["gtmp0"])
                S.dma("sp", lambda e: e.dma_start(out=o_wkv_p.rearrange("(j hh) v k -> v j hh k", hh=2),
                                                   in_=wst.rearrange("p j (hh k) -> p j hh k", hh=2)), reads=["gtmp0"])
                b = nb()
                A("pe", lambda e, b=b: e.transpose(out=ps[0:14, b, 0:128], in_=carry[:, :], identity=ident), ["carry", "consts"], PB(b))
                A("act", lambda e, b=b: e.copy(out=gtmp[1][0:14, 0:128], in_=ps[0:14, b, 0:128]), PB(b), ["gtmp1"])
                S.dma("sp", lambda e: e.dma_start(out=o_shift_p.rearrange("(c p) -> c p", p=128), in_=gtmp[1][0:14, 0:128]), reads=["gtmp1"])
            if stop_after == "Rpost":
                continue

            S.tag = "p%d H" % pas
            S.barrier()
            ar_ = Arena(arena_t, ARENA_WORDS)
            Eb = ar_.alloc(F32, [128, 4, HALF])
            qT = ar_.alloc(BF16, [128, 4, HALF])
            hkT = ar_.alloc(BF16, [128, 4, HALF])
            hvT = ar_.alloc(BF16, [128, 4, TW])
            sgo = ar_.alloc(BF16, [128, 4, TW])
            oT = ar_.alloc(F32, [128, 4, TW])
            smpH = ar_.alloc(F32, [128, 4, 4, NS])
            hmark = ar_.top
            htmp = [ar_.alloc(F32, [128, TW]) for _ in range(8)]
            hset = [htmp[0:4], htmp[4:8]]
            cmaskH = ar_.alloc(BF16, [128, HALF])
            S.dma("pool", lambda e: e.dma_start(out=cmaskH, in_=consts_d[:, 640:1664]), writes=["cmaskH"])
            HB = RPROJ
            wv, wr = load_w(w_in_v[:, :, HB + 512:HB + 1024], (8, 512))
            hvh = [(0, 512), (512, W)]
            for h in range(4):
                T0, T1_, T2, T3 = hset[h % 2]

                def ev_f(b, c0, c1, T0=T0):
                    A("act", lambda e: e.activation(out=T0[:, c0:c1], in_=ps[:, b, 0:c1 - c0], func=AF.Sigmoid))
                proj_fm(wv, wr, h * 128, 8, hT, "hT", ev_f)
                for (a_, b_) in hvh:
                    b2 = min(b_, HALF)
                    A("dve", lambda e, h=h, a_=a_, b_=b_: e.tensor_scalar(out=T0[:, a_:b_], in0=T0[:, a_:b_], scalar1=dvc("omlb", h), scalar2=dvc("lb", h),
                                                                          op0=ALU.mult, op1=ALU.add))
                    A("dve", lambda e, a_=a_, b_=b_: e.tensor_scalar(out=T1_[:, a_:b_], in0=T0[:, a_:b_], scalar1=-1.0, scalar2=1.0, op0=ALU.mult, op1=ALU.add))
                    A("act", lambda e, a_=a_, b2=b2: e.activation(out=T2[:, a_:b2], in_=T0[:, a_:b2], func=AF.Ln))
                    A("dve", lambda e, a_=a_, b2=b2: e.tensor_tensor_scan(out=T3[:, a_:b2], data0=cmaskH[:, a_:b2], data1=T2[:, a_:b2], initial=0.0,
                                                                          op0=ALU.mult, op1=ALU.add))
                    A("act", lambda e, h=h, a_=a_, b2=b2: e.activation(out=Eb[:, h, a_:b2], in_=T3[:, a_:b2], func=AF.Exp))
                    A("act", lambda e, h=h, a_=a_, b2=b2: e.activation(out=DC[:, h, a_ // 64:b2 // 64], in_=T3[:, a_:b2].rearrange("p (c t) -> p c t", t=64)[:, :, 63],
                                                                       func=AF.Exp))
                    A("act", lambda e, a_=a_, b2=b2: e.activation(out=T2[:, a_:b2], in_=T3[:, a_:b2], func=AF.Exp, scale=-1.0))
                    A("dve", lambda e, h=h, a_=a_, b2=b2: e.tensor_tensor(out=hkT[:, h, a_:b2], in0=T1_[:, a_:b2], in1=T2[:, a_:b2], op=ALU.mult))
                if nsamp:
                    A("act", lambda e, h=h: e.copy(out=smpH[:, 1, h, :], in_=T0[:, HALF:HALF + NS]))
                    A("act", lambda e, h=h: e.copy(out=smpH[:, 2, h, :], in_=T1_[:, HALF:HALF + NS]))
            wv, wr = load_w(w_in_v[:, :, HB:HB + 512], (8, 512))
            for h in range(4):
                T0 = hset[h % 2][0]

                def ev_q(b, c0, c1, T0=T0, h=h):
                    A("act", lambda e: e.activation(out=T0[:, c0:c1], in_=ps[:, b, 0:c1 - c0], func=AF.Silu))
                    if c0 < HALF:
                        A("dve", lambda e: e.tensor_tensor(out=qT[:, h, c0:c1], in0=T0[:, c0:c1], in1=Eb[:, h, c0:c1], op=ALU.mult))
                proj_fm(wv, wr, h * 128, 8, hT, "hT", ev_q)
                if nsamp:
                    A("act", lambda e, h=h: e.copy(out=smpH[:, 0, h, :], in_=T0[:, HALF:HALF + NS]))
            wv, wr = load_w(w_in_v[:, :, HB + 1024:HB + 1536], (8, 512))
            for h in range(4):
                def ev_i(b, c0, c1, h=h):
                    A("act", lambda e: e.copy(out=hvT[:, h, c0:c1], in_=ps[:, b, 0:c1 - c0]), PB(b), ["hvT"])
                    if c0 >= HALF:
                        A("act", lambda e: e.copy(out=smpH[:, 3, h, :], in_=ps[:, b, 0:NS]), PB(b), ["smpH"])
                proj_fm(wv, wr, h * 128, 8, hT, "hT", ev_i)
            wv, wr = load_w(w_in_v[:, :, HB + 1536:HB + 2048], (8, 512))
            for h in range(4):
                def ev_og(b, c0, c1, h=h):
                    A("act", lambda e: e.activation(out=sgo[:, h, c0:c1], in_=ps[:, b, 0:c1 - c0], func=AF.Sigmoid), PB(b), ["sgo"])
                proj_fm(wv, wr, h * 128, 8, hT, "hT", ev_og)

            if nsamp:
                S.barrier()
                ar_.top = hmark
                S_s = ar_.alloc(F32, [128, NS, 4, 128])
                ktok_s = ar_.alloc(F32, [NS, 512])
                vtok_s = ar_.alloc(F32, [NS, 512])
                vm = [ar_.alloc(F32, [NS, 512]) for _ in range(2)]
                tS_l = [ar_.alloc(F32, [128, 4, 128]) for _ in range(3)]
                q_bf = ar_.alloc(BF16, [128, 4, NS])
                Sb_l = [ar_.alloc(BF16, [128, 4, 128]) for _ in range(3)]
                S.dma("sp", lambda e: e.dma_start(out=S_s, in_=hgrn_s.rearrange("b h k v -> k b h v")), writes=["S_s"])
                for vec, dst, dn in ((2, ktok_s, "ktok_s"), (3, vtok_s, "vtok_s")):
                    b = nb_smp()
                    for h in range(4):
                        A("pe", lambda e, vec=vec, h=h, b=b: e.transpose(out=ps[0:NS, b, h * 128:(h + 1) * 128], in_=smpH[:, vec, h, :], identity=ident),
                          ["smpH", "consts"], PB(b))
                    A("act", lambda e, dst=dst, b=b: e.copy(out=dst, in_=ps[0:NS, b, :]), PB(b), [dn])
                for bi in range(NS):
                    vq = bi % 2
                    tS = tS_l[bi % 3]
                    A("dve", lambda e, bi=bi, vq=vq: e.tensor_scalar(out=vm[vq], in0=vtok_s, scalar1=ident[0:NS, bi:bi + 1], scalar2=None, op0=ALU.mult),
                      ["vtok_s", "consts"], ["vm%d" % vq])
                    b = nb_smp()
                    for h in range(4):
                        A("pe", lambda e, h=h, vq=vq, b=b: e.matmul(ps[:, b, h * 128:(h + 1) * 128], lhsT=ktok_s[:, h * 128:(h + 1) * 128],
                                                                    rhs=vm[vq][:, h * 128:(h + 1) * 128], start=True, stop=True),
                          ["ktok_s", "vm%d" % vq], PB(b))
                    A("dve", lambda e, bi=bi: e.tensor_tensor(out=tS, in0=S_s[:, bi, :, :],
                                                               in1=smpH[:, 1, :, bi:bi + 1].broadcast_to([128, 4, 128]), op=ALU.mult),
                      ["S_s", "smpH"], ["tS"])
                    A("dve", lambda e, bi=bi, b=b: e.tensor_tensor(out=S_s[:, bi, :, :], in0=ps[:, b, :].rearrange("p (h v) -> p h v", h=4), in1=tS,
                                                                    op=ALU.add), PB(b) + ["tS"], ["S_s"])
                S.dma("sp", lambda e: e.dma_start(out=o_hgrn_s.rearrange("b h k v -> k b h v"), in_=S_s), reads=["S_s"])
                A("act", lambda e: e.copy(out=q_bf, in_=smpH[:, 0, :, :]))
                bO_ = nb_smp()
                for bi in range(NS):
                    Sb = Sb_l[bi % 3]
                    A("act", lambda e, bi=bi, Sb=Sb: e.copy(out=Sb, in_=S_s[:, bi, :, :]))
                    for h in range(4):
                        A("pe", lambda e, h=h, bi=bi, Sb=Sb, bO_=bO_: e.matmul(ps[:, bO_, h * NS + bi:h * NS + bi + 1], lhsT=Sb[:, h, :],
                                                                                rhs=q_bf[:, h, bi:bi + 1], start=True, stop=True))
                A("act", lambda e, bO_=bO_: e.copy(out=oT[:, :, HALF:HALF + NS], in_=ps[:, bO_, 0:4 * NS].rearrange("p (h n) -> p h n", h=4)))

            if not nsamp:
                ar_.top = hmark
            par_nb[0] = 4 if nsamp else 6
            par_ctr[0] = 0
            hk_tok = [ar_.alloc(BF16, [128, 512]) for _ in range(2)]
            hv_tok = [ar_.alloc(BF16, [128, 512]) for _ in range(2)]
            PT_sb = [ar_.alloc(BF16, [128, 4, 64]) for _ in range(2)]
            Ss = ar_.alloc(F32, [128, 4, 128])
            ar_.top = hmark
            htmp = [ar_.alloc(F32, [128, TW]) for _ in range(2)]
            for i in range(8):
                q = i % 2
                b1 = nb_par()
                pTk = ps[:, b1, :].bitcast(BF16).rearrange("p (v n) -> p v n", v=2)
                for h in range(4):
                    A("pe", lambda e, i=i, h=h, pTk=pTk: e.transpose(out=pTk[:, 0, h * 128:(h + 1) * 128], in_=hkT[:, h, i * 128:(i + 1) * 128],
                                                                     identity=ident_bf[:]), ["hkT", "ident_bf"], PB(b1))
                    A("pe", lambda e, i=i, h=h, pTk=pTk: e.transpose(out=pTk[:, 1, h * 128:(h + 1) * 128], in_=hvT[:, h, i * 128:(i + 1) * 128],
                                                                     identity=ident_bf[:]), ["hvT", "ident_bf"], PB(b1))
                A("act", lambda e, q=q, pTk=pTk: e.copy(out=hk_tok[q], in_=pTk[:, 0, :]), PB(b1), ["hk_tok%d" % q])
                A("dve", lambda e, q=q, pTk=pTk: e.tensor_copy(out=hv_tok[q], in_=pTk[:, 1, :]), PB(b1), ["hv_tok%d" % q])
                bS = nb_par()
                for h in range(4):
                    for e_ in range(2):
                        c = 2 * i + e_
                        A("pe", lambda e, h=h, e_=e_, c=c, bS=bS: e.matmul(ps[e_ * 64:(e_ + 1) * 64, bS, h * 64:(h + 1) * 64], lhsT=hkT[:, h, c * 64:(c + 1) * 64],
                                                                           rhs=qT[:, h, c * 64:(c + 1) * 64], start=True, stop=True), ["hkT", "qT"], PB(bS))
                A("dve", lambda e, q=q, bS=bS: e.tensor_tensor(out=PT_sb[q], in0=ps[:, bS, 0:256].rearrange("p (h t) -> p h t", h=4),
                                                                in1=masks_bf[:, 64:128].unsqueeze(1).broadcast_to([128, 4, 64]), op=ALU.mult),
                  PB(bS) + ["consts"], ["PT%d" % q])
                bO = nb_par()
                for e_ in range(2):
                    c = 2 * i + e_
                    sl = slice(e_ * 64, (e_ + 1) * 64)
                    for h in range(4):
                        oo = h * 128 + e_ * 64
                        A("pe", lambda e, h=h, c=c, oo=oo, bO=bO: e.matmul(ps[:, bO, oo:oo + 64], lhsT=S0b[:, h, :], rhs=qT[:, h, c * 64:(c + 1) * 64],
                                                                           start=True, stop=False), ["S0b", "qT"], PB(bO))
                        A("pe", lambda e, h=h, sl=sl, q=q, oo=oo, bO=bO: e.matmul(ps[:, bO, oo:oo + 64], lhsT=hv_tok[q][sl, h * 128:(h + 1) * 128],
                                                                                  rhs=PT_sb[q][sl, h, :], start=False, stop=True),
                          ["hv_tok%d" % q, "PT%d" % q], PB(bO))
                    bG = nb_seq(e_)
                    for h in range(4):
                        A("pe", lambda e, h=h, sl=sl, q=q, bG=bG: e.matmul(ps[:, bG, h * 128:(h + 1) * 128], lhsT=hk_tok[q][sl, h * 128:(h + 1) * 128],
                                                                           rhs=hv_tok[q][sl, h * 128:(h + 1) * 128], start=True, stop=True),
                          ["hk_tok%d" % q, "hv_tok%d" % q], PB(bG))
                    A("dve", lambda e, bG=bG: e.tensor_tensor(out=Ss, in0=ps[:, bG, :].rearrange("p (h v) -> p h v", h=4), in1=S0f[:], op=ALU.add),
                      PB(bG) + ["S0f"], ["Ss"])
                    A("dve", lambda e, c=c: e.tensor_tensor(out=S0f[:], in0=Ss, in1=DC[:, :, c:c + 1].broadcast_to([128, 4, 128]), op=ALU.mult),
                      ["Ss", "DC"], ["S0f"])
                    A("act", lambda e: e.copy(out=S0b[:], in_=S0f[:]), ["S0f"], ["S0b"])
                A("act", lambda e, i=i, bO=bO: e.copy(out=oT[:, :, i * 128:(i + 1) * 128], in_=ps[:, bO, :].rearrange("p (h t) -> p h t", h=4)),
                  PB(bO), ["oT"])
            par_nb[0] = 6
            if dbg and pas == 0:
                dump("oT", oT, ["oT"])
            for h in range(4):
                A("act", lambda e, h=h: e.activation(out=htmp[0][:, 0:W], in_=oT[:, h, 0:W], func=AF.Square), ["oT"], ["htmp0"])

                def ev_ms(b, c0, c1):
                    A("act", lambda e: e.activation(out=htmp[1][:, c0:c1], in_=ps[:, b, 0:c1 - c0], func=AF.Ln, scale=1.0 / 128, bias=epsrms[:, 0:1]))
                fp32_blocksum(htmp[0], "htmp0", ones, ev_ms)
                A("act", lambda e: e.activation(out=htmp[1][:, 0:W], in_=htmp[1][:, 0:W], func=AF.Exp, scale=-0.5), ["htmp1"], ["htmp1"])
                A("dve", lambda e, h=h: e.tensor_tensor(out=htmp[0][:, 0:W], in0=oT[:, h, 0:W], in1=htmp[1][:, 0:W], op=ALU.mult), ["oT", "htmp1"], ["htmp0"])
                A("dve", lambda e, h=h: e.scalar_tensor_tensor(out=ob[:, h, 0:W], in0=htmp[0][:, 0:W], scalar=pf("hng", h), in1=sgo[:, h, 0:W],
                                                               op0=ALU.mult, op1=ALU.mult), ["htmp0", "pfm", "sgo"], ["ob"])
            if dbg and pas == 0:
                dump("ob", ob[:], ["ob"])
            if pas == passes[-1]:
                S.dma("sp", lambda e: e.dma_start(out=o_hgrn_p.rearrange("h k v -> k h v"), in_=S0f[:]), reads=["S0f"])
            if stop_after == "H":
                continue

            S.barrier()
            ar_ = Arena(arena_t, ARENA_WORDS)
            x_tok = ar_.alloc(F32, [128, 8, 1024])
            xs_tok = ar_.alloc(F32, [NS, 1024])
            mergedT = ar_.alloc(BF16, [128, 8, TW])
            gm = [ar_.alloc(F32, [128, 512]) for _ in range(4)]
            GB = RPROJ + 2048
            w_upa_v = w_up_a.rearrange("(kc p) n -> p kc n", p=128)
            w_upb_v = w_up_b.rearrange("(kc p) n -> p kc n", p=128)
            for dcg in range(2):
                wga, wgar = load_w(w_in_v[:, :, GB + dcg * 512:GB + (dcg + 1) * 512], (8, 512))
                wgb, wgbr = load_w(w_in_v[:, :, GB + 1024 + dcg * 512:GB + 1024 + (dcg + 1) * 512], (8, 512))
                wu, wur = load_w(w_upa_v[:, :, dcg * 512:(dcg + 1) * 512], (4, 512))
                iu_ = (wctr[0] - 1) % NWB
                wub_view = wbuf[iu_][:, 4:8, :]
                S.dma("pool", lambda e, wub_view=wub_view, dcg=dcg: e.dma_start(out=wub_view, in_=w_upb_v[:, :, dcg * 512:(dcg + 1) * 512]), writes=[wur])
                for dc in range(4):
                    n0 = dc * 128
                    for (c0, c1) in blocks:
                        w_ = c1 - c0
                        b1, b2, b3, b4 = nb(), nb(), nb(), nb()
                        for kc in range(8):
                            A("pe", lambda e, kc=kc, b1=b1, c0=c0, c1=c1, n0=n0, wga=wga: e.matmul(ps[:, b1, 0:c1 - c0], lhsT=wga[:, kc, n0:n0 + 128],
                                                                                                   rhs=hT[:, kc, c0:c1], start=(kc == 0), stop=(kc == 7)),
                              [wgar, "hT"], PB(b1))
                        for kc in range(8):
                            A("pe", lambda e, kc=kc, b2=b2, c0=c0, c1=c1, n0=n0, wgb=wgb: e.matmul(ps[:, b2, 0:c1 - c0], lhsT=wgb[:, kc, n0:n0 + 128],
                                                                                                   rhs=hT[:, kc, c0:c1], start=(kc == 0), stop=(kc == 7)),
                              [wgbr, "hT"], PB(b2))
                        for kc in range(4):
                            A("pe", lambda e, kc=kc, b3=b3, c0=c0, c1=c1, n0=n0, wu=wu: e.matmul(ps[:, b3, 0:c1 - c0], lhsT=wu[:, kc, n0:n0 + 128],
                                                                                                 rhs=oa[:, kc, c0:c1], start=(kc == 0), stop=(kc == 3)),
                              [wur, "oa"], PB(b3))
                        for kc in range(4):
                            A("pe", lambda e, kc=kc, b4=b4, c0=c0, c1=c1, n0=n0, wub_view=wub_view: e.matmul(ps[:, b4, 0:c1 - c0], lhsT=wub_view[:, kc, n0:n0 + 128],
                                                                                                             rhs=ob[:, kc, c0:c1], start=(kc == 0), stop=(kc == 3)),
                              [wur, "ob"], PB(b4))
                        A("act", lambda e, b1=b1, w_=w_: e.activation(out=gm[0][:, 0:w_], in_=ps[:, b1, 0:w_], func=AF.Sigmoid), PB(b1), ["gm0"])
                        A("act", lambda e, b2=b2, w_=w_: e.activation(out=gm[1][:, 0:w_], in_=ps[:, b2, 0:w_], func=AF.Sigmoid), PB(b2), ["gm1"])
                        A("dve", lambda e, b3=b3, w_=w_: e.tensor_tensor(out=gm[2][:, 0:w_], in0=ps[:, b3, 0:w_], in1=gm[0][:, 0:w_], op=ALU.mult),
                          PB(b3) + ["gm0"], ["gm2"])
                        A("dve", lambda e, b4=b4, w_=w_: e.tensor_tensor(out=gm[3][:, 0:w_], in0=ps[:, b4, 0:w_], in1=gm[1][:, 0:w_], op=ALU.mult),
                          PB(b4) + ["gm1"], ["gm3"])
                        A("dve", lambda e, dcg=dcg, dc=dc, c0=c0, c1=c1, w_=w_: e.tensor_tensor(out=mergedT[:, dcg * 4 + dc, c0:c1], in0=gm[2][:, 0:w_],
                                                                                                 in1=gm[3][:, 0:w_], op=ALU.add),
                          ["gm2", "gm3"], ["mergedT"])
            if dbg and pas == 0:
                dump("mergedT", mergedT, ["mergedT"])
            if stop_after == "G":
                continue

            S.barrier()
            ar_.top = 0
            x_tok = ar_.alloc(F32, [128, 8, 1024])
            xs_tok = ar_.alloc(F32, [NS, 1024])
            mergedT = ar_.alloc(BF16, [128, 8, TW])
            for tt in range(8):
                S.dma("sp", lambda e, tt=tt: e.dma_start(out=x_tok[:, tt, :], in_=x_p[t0 + tt * 128:t0 + (tt + 1) * 128, :]),
                      writes=["x%d" % tt])
            if nsamp:
                S.dma("sp", lambda e: e.dma_start(out=xs_tok, in_=x_s), writes=["xs"])
            w_out_v = w_out.rearrange("(kc p) n -> p kc n", p=128)
            wos = [load_w(w_out_v[:, :, half * 512:(half + 1) * 512], (8, 512))[0] for half in range(2)]
            for tt in range(8):
                for half in range(2):
                    wo = wos[half]
                    b = nb()
                    for kc in range(8):
                        A("pe", lambda e, kc=kc, tt=tt, b=b, wo=wo: e.matmul(ps[:, b, :], lhsT=mergedT[:, kc, tt * 128:(tt + 1) * 128], rhs=wo[:, kc, :],
                                                                             start=(kc == 0), stop=(kc == 7)))
                    A("dve", lambda e, tt=tt, half=half, b=b: e.tensor_tensor(out=x_tok[:, tt, half * 512:(half + 1) * 512], in0=ps[:, b, :],
                                                                               in1=x_tok[:, tt, half * 512:(half + 1) * 512], op=ALU.add))
            if nsamp:
                for half in range(2):
                    wo = wos[half]
                    b = nb()
                    for kc in range(8):
                        A("pe", lambda e, kc=kc, b=b, wo=wo: e.matmul(ps[0:NS, b, :], lhsT=mergedT[:, kc, HALF:HALF + NS], rhs=wo[:, kc, :],
                                                                      start=(kc == 0), stop=(kc == 7)))
                    A("dve", lambda e, half=half, b=b: e.tensor_tensor(out=xs_tok[:, half * 512:(half + 1) * 512], in0=ps[0:NS, b, :],
                                                                        in1=xs_tok[:, half * 512:(half + 1) * 512], op=ALU.add))
            norm_phase("O", x_tok, xs_tok, g_ffn, ar_,
                       lambda gB, h_tok, hs_tok, R: to_hT(gB, h_tok, hs_tok, R, x_tok, xs_tok, lambda tt: "x%d" % tt, "xs"),
                       lambda tt: "x%d" % tt, "xs")
            if dbg and pas == 0:
                dump("hT2", hT[:], ["hT"])
            if stop_after == "O":
                continue

            S.barrier()
            ar_.top = 0
            x_tok = ar_.alloc(F32, [128, 8, 1024])
            xs_tok = ar_.alloc(F32, [NS, 1024])
            actT = ar_.alloc(BF16, [128, 22, TW])
            wdn = ar_.alloc(BF16, [128, 22, 1024])
            gBf = ftmp[:, :, :].rearrange("p a b -> p (a b)")
            junkD = ar_.alloc(BF16, [128, 1024])
            w_fd_v = w_fd.rearrange("(fc p) n -> p fc n", p=128)
            w_fg_v = w_fg.rearrange("(kc p) n -> p kc n", p=128)
            w_fu_v = w_fu.rearrange("(kc p) n -> p kc n", p=128)
            for fg in range(11):
                wg, wgr = load_w(w_fg_v[:, :, fg * 256:(fg + 1) * 256], (8, 256))
                wu, wur = load_w(w_fu_v[:, :, fg * 256:(fg + 1) * 256], (8, 256))
                if fg in (2, 4, 6, 8):
                    k_ = (fg - 2) // 2
                    lo, hi = (0, 6, 12, 18)[k_], (6, 12, 18, 22)[k_]
                    S.dma("pool", lambda e, lo=lo, hi=hi: e.dma_start(out=wdn[:, lo:hi, :], in_=w_fd_v[:, lo:hi, :]), writes=["wdn%d" % k_])
                for f2 in range(2):
                    fc = fg * 2 + f2
                    n0 = f2 * 128
                    for (c0, c1) in blocks:
                        w_ = c1 - c0
                        b1, b2 = nb(), nb()
                        for kc in range(8):
                            A("pe", lambda e, kc=kc, b1=b1, c0=c0, c1=c1, n0=n0, wg=wg: e.matmul(ps[:, b1, 0:c1 - c0], lhsT=wg[:, kc, n0:n0 + 128],
                                                                                                 rhs=hT[:, kc, c0:c1], start=(kc == 0), stop=(kc == 7)),
                              [wgr, "hT"], PB(b1))
                        for kc in range(8):
                            A("pe", lambda e, kc=kc, b2=b2, c0=c0, c1=c1, n0=n0, wu=wu: e.matmul(ps[:, b2, 0:c1 - c0], lhsT=wu[:, kc, n0:n0 + 128],
                                                                                                 rhs=hT[:, kc, c0:c1], start=(kc == 0), stop=(kc == 7)),
                              [wur, "hT"], PB(b2))
                        fq = (fc + (c0 // 512)) % 2
                        A("act", lambda e, b1=b1, w_=w_, fq=fq: e.activation(out=ftmp[:, fq, 0:w_], in_=ps[:, b1, 0:w_], func=AF.Silu), PB(b1), ["ftmp%d" % fq])
                        A("dve", lambda e, b2=b2, w_=w_, fq=fq, fc=fc, c0=c0, c1=c1: e.tensor_tensor(out=actT[:, fc, c0:c1], in0=ps[:, b2, 0:w_],
                                                                                                      in1=ftmp[:, fq, 0:w_], op=ALU.mult),
                          PB(b2) + ["ftmp%d" % fq], ["actT"])
            if stop_after == "F":
                continue

            S.barrier()
            S.dma("sp", lambda e: e.dma_start(out=gBf, in_=g_fin.partition_broadcast(128)), writes=["gBf"])
            A("dve", lambda e: e.memset(stat[:, 0:32], 0.0), [], ["stat"])
            wres_all = ["wdn0", "wdn1", "wdn2", "wdn3"]
            for tt in range(8):
                for half in range(2):
                    b = nb()
                    for fc in range(22):
                        A("pe", lambda e, fc=fc, tt=tt, half=half, b=b: e.matmul(ps[:, b, :], lhsT=actT[:, fc, tt * 128:(tt + 1) * 128],
                                                                                 rhs=wdn[:, fc, half * 512:(half + 1) * 512], start=(fc == 0), stop=(fc == 21)),
                          ["actT"] + wres_all, PB(b))
                    A("dve", lambda e, tt=tt, half=half, b=b: e.tensor_tensor(out=x_tok[:, tt, half * 512:(half + 1) * 512], in0=ps[:, b, :],
                                                                               in1=x_tok[:, tt, half * 512:(half + 1) * 512], op=ALU.add),
                      PB(b) + ["x%d" % tt], ["x%d" % tt])
                A("act", lambda e, tt=tt: e.activation(out=junkD, in_=x_tok[:, tt, :], func=AF.Square,
                                                       accum_out=stat[:, tt:tt + 1]), ["x%d" % tt, "stat"], ["ftmp0", "ftmp1", "stat%d" % tt])
                A("dve", lambda e, tt=tt: e.tensor_scalar(out=stat[:, 16 + tt:17 + tt], in0=stat[:, tt:tt + 1], scalar1=1.0 / D, scalar2=RMS_EPS,
                                                          op0=ALU.mult, op1=ALU.add), ["stat%d" % tt, "stat"], ["stat%d" % tt])
                A("act", lambda e, tt=tt: e.activation(out=stat[:, 16 + tt:17 + tt], in_=stat[:, 16 + tt:17 + tt], func=AF.Ln), ["stat%d" % tt], ["stat%d" % tt])
                A("act", lambda e, tt=tt: e.activation(out=stat[:, 16 + tt:17 + tt], in_=stat[:, 16 + tt:17 + tt], func=AF.Exp, scale=-0.5),
                  ["stat%d" % tt], ["stat%d" % tt])
                A("dve", lambda e, tt=tt: e.scalar_tensor_tensor(out=x_tok[:, tt, :], in0=x_tok[:, tt, :], scalar=stat[:, 16 + tt:17 + tt], in1=gBf,
                                                                 op0=ALU.mult, op1=ALU.mult), ["x%d" % tt, "stat%d" % tt, "gBf"], ["x%d" % tt])
                S.dma("sp", lambda e, tt=tt: e.dma_start(out=y_p[t0 + tt * 128:t0 + (tt + 1) * 128, :], in_=x_tok[:, tt, :]), reads=["x%d" % tt])
            if nsamp:
                for half in range(2):
                    b = nb()
                    for fc in range(22):
                        A("pe", lambda e, fc=fc, half=half, b=b: e.matmul(ps[0:NS, b, :], lhsT=actT[:, fc, HALF:HALF + NS],
                                                                          rhs=wdn[:, fc, half * 512:(half + 1) * 512], start=(fc == 0), stop=(fc == 21)),
                          ["actT"] + wres_all, PB(b))
                    A("dve", lambda e, half=half, b=b: e.tensor_tensor(out=xs_tok[:, half * 512:(half + 1) * 512], in0=ps[0:NS, b, :],
                                                                        in1=xs_tok[:, half * 512:(half + 1) * 512], op=ALU.add), PB(b) + ["xs"], ["xs"])
                A("act", lambda e: e.activation(out=junkD[0:NS, :], in_=xs_tok, func=AF.Square,
                                                accum_out=stat[0:NS, 8:9]), ["xs", "stat"], ["ftmp0", "ftmp1", "stat8"])
                A("dve", lambda e: e.tensor_scalar(out=stat[0:NS, 24:25], in0=stat[0:NS, 8:9], scalar1=1.0 / D, scalar2=RMS_EPS,
                                                    op0=ALU.mult, op1=ALU.add), ["stat8", "stat"], ["stat8"])
                A("act", lambda e: e.activation(out=stat[0:NS, 24:25], in_=stat[0:NS, 24:25], func=AF.Ln), ["stat8"], ["stat8"])
                A("act", lambda e: e.activation(out=stat[0:NS, 24:25], in_=stat[0:NS, 24:25], func=AF.Exp, scale=-0.5), ["stat8"], ["stat8"])
                A("dve", lambda e: e.scalar_tensor_tensor(out=xs_tok, in0=xs_tok, scalar=stat[0:NS, 24:25], in1=gBf[0:NS, :],
                                                          op0=ALU.mult, op1=ALU.mult), ["xs", "stat8", "gBf"], ["xs"])
                S.dma("sp", lambda e: e.dma_start(out=y_s, in_=xs_tok), reads=["xs"])
        S.finalize()
        S.emit(st)
    return nc


DBG_EXTRA = {"oa": [128, 4, 1040], "oT": [128, 4, 1040], "ob": [128, 4, 1040], "mergedT": [128, 8, 1040], "hT2": [128, 8, 1040]}


def _host_maps(inp, ncores=8):
    f = np.ascontiguousarray

    def a32(v):
        return np.asarray(v, dtype=np.float32)

    def fm(v, ncol):
        return f(a32(v).reshape(ncol, 128).T)

    pfm = np.concatenate([fm(inp['rwkv_mu'][0], 14), fm(inp['rwkv_w0'][0], 4), fm(inp['rwkv_a0'][0], 4), fm(inp['rwkv_k_k'][0], 4),
                          fm(inp['rwkv_k_a'][0], 4), fm(a32(inp['rwkv_r_k'][0]).reshape(-1), 4), fm(inp['rwkv_ln_g'][0], 4),
                          fm(inp['rwkv_ln_b'][0], 4), fm(inp['hgrn_lb'][0], 4), fm(inp['hgrn_lb'][1], 4), fm(inp['hgrn_norm_g'][0], 4)], axis=1)
    shared = dict(w_in=f(a32(inp['w_in'][0])), w2a2=f(np.concatenate([a32(inp['rwkv_w2'][0]), a32(inp['rwkv_a2'][0])], axis=0)),
                  g2=f(a32(inp['rwkv_g2'][0])), w_up_a=f(a32(inp['w_up_a'][0])), w_up_b=f(a32(inp['w_up_b'][0])), w_out=f(a32(inp['w_out'][0])),
                  w_fg=f(a32(inp['w_ffn_gate'][0])), w_fu=f(a32(inp['w_ffn_up'][0])), w_fd=f(a32(inp['w_ffn_down'][0])), pfm=f(pfm),
                  consts=make_consts(), g_mix=f(a32(inp['norm_mix_g'][0])), g_ffn=f(a32(inp['norm_ffn_g'][0])), g_fin=f(a32(inp['norm_final_g'])))
    maps = []
    for c in range(ncores):
        m = dict(shared)
        m['x_p'] = f(a32(inp['x_prompt'][c]))
        m['x_s'] = f(a32(inp['x_sample'][c * NS:(c + 1) * NS, 0]))
        m['wkv_s'] = f(a32(inp['state_rwkv_wkv'][0, c * NS:(c + 1) * NS]).reshape(NS * 8, 4096))
        m['shift_s'] = f(a32(inp['state_rwkv_shift'][0, c * NS:(c + 1) * NS]))
        m['hgrn_s'] = f(a32(inp['state_hgrn'][0, c * NS:(c + 1) * NS]))
        maps.append(m)
    return maps


_NC_CACHE = {}


def kernel(**inputs):
    ncores = 8
    if "nc" not in _NC_CACHE:
        _NC_CACHE["nc"] = build_program()
    nc = _NC_CACHE["nc"]
    maps = _host_maps(inputs, ncores)
    res = run_bass_kernel_spmd(nc, maps, core_ids=list(range(ncores)))
    r = res.results
    y_p = np.stack([r[c]["y_p"] for c in range(ncores)]).astype(np.float32)
    y_s = np.concatenate([r[c]["y_s"] for c in range(ncores)], axis=0).reshape(128, 1, D).astype(np.float32)
    wkv_p = np.stack([r[c]["o_wkv_p"] for c in range(ncores)])[None].astype(np.float32)
    shift_p = np.stack([r[c]["o_shift_p"] for c in range(ncores)])[None].astype(np.float32)
    hgrn_p = np.stack([r[c]["o_hgrn_p"] for c in range(ncores)])[None].astype(np.float32)
    wkv_s = np.concatenate([r[c]["o_wkv_s"].reshape(NS, 8, 64, 64) for c in range(ncores)], axis=0)[None].astype(np.float32)
    shift_s = np.concatenate([r[c]["o_shift_s"] for c in range(ncores)], axis=0)[None].astype(np.float32)
    hgrn_s = np.concatenate([r[c]["o_hgrn_s"] for c in range(ncores)], axis=0)[None].astype(np.float32)
    return (y_p, y_s, wkv_p, shift_p, hgrn_p, wkv_s, shift_s, hgrn_s)
```

```python
from contextlib import ExitStack
import numpy as np
import concourse.bass as bass
import concourse.mybir as mybir
from concourse.bass_utils import run_bass_kernel_spmd

F32 = mybir.dt.float32
BF16 = mybir.dt.bfloat16
AF = mybir.ActivationFunctionType
ALU = mybir.AluOpType
AX = mybir.AxisListType

COMPUTE = ("pe", "act", "dve", "pool")


class Op:
    __slots__ = ("idx", "eng", "fn", "call", "is_dma", "deps", "val", "marked", "slot", "acc", "dur", "xfer", "seq", "mode")

    def __init__(self, idx, eng, fn, call, is_dma):
        self.idx = idx
        self.eng = eng
        self.fn = fn
        self.call = call
        self.is_dma = is_dma
        self.deps = ()
        self.val = None
        self.marked = False
        self.slot = None
        self.acc = None
        self.dur = 100.0
        self.xfer = 0.0
        self.seq = 0
        self.mode = None


class _Rec:
    def __init__(self):
        self.call = None

    def __getattr__(self, name):
        def f(*a, **k):
            self.call = (name, a, k)
            return self
        return f


_WRITE_KW = ("out", "accum_out", "ap")
_ESZ = {}


def _esz(dt):
    if dt not in _ESZ:
        _ESZ[dt] = 4 if dt == F32 else 2
    return _ESZ[dt]


def _ap_intervals(ap):
    sp = str(ap.space)
    esz = _esz(ap.dtype)
    dims = list(ap.ap)
    if sp == "DRAM":
        base = ap.offset
        p0, p1 = 0, 1
        fd = dims
    else:
        pstride, pcnt = dims[0]
        p0 = ap.offset // pstride
        base = ap.offset % pstride
        p1 = p0 + pcnt
        fd = dims[1:]
    fd = sorted([(st_, c) for (st_, c) in fd if c > 1 and st_ != 0])
    run = 1
    k = 0
    while k < len(fd) and fd[k][0] == run:
        run *= fd[k][1]
        k += 1
    outer = fd[k:]
    n_outer = 1
    for st_, c in outer:
        n_outer *= c
    if n_outer <= 32:
        offs = [0]
        for st_, c in outer:
            offs = [o + i * st_ for o in offs for i in range(c)]
        iv = [((base + o) * esz, (base + o + run) * esz) for o in offs]
    else:
        ext = sum((c - 1) * st_ for st_, c in outer) + run
        iv = [(base * esz, (base + ext) * esz)]
    return sp, ap.name, p0, p1, iv


def _call_accesses(call):
    name, a, k = call
    items = []
    if name == "matmul":
        items.append((a[0] if a else k.get("out"), True))
        for kw in ("lhsT", "rhs"):
            items.append((k[kw], False))
    elif name == "memset":
        items.append((a[0] if a else k.get("ap"), True))
    else:
        for kw, v in k.items():
            if hasattr(v, "ap") and hasattr(v, "space"):
                items.append((v, kw in _WRITE_KW))
        for v in a:
            if hasattr(v, "ap") and hasattr(v, "space"):
                items.append((v, False))
    acc = []
    for ap, is_w in items:
        if ap is None:
            continue
        sp, nm, p0, p1, iv = _ap_intervals(ap)
        if sp == "PSUM":
            banks = set()
            for b0, b1 in iv:
                for bk in range(b0 // 2048, (b1 - 1) // 2048 + 1):
                    banks.add(bk)
            for bk in banks:
                acc.append((("ps", bk), 0, 128, 0, 1, is_w))
        else:
            for b0, b1 in iv:
                acc.append(((sp, nm), p0, p1, b0, b1, is_w))
    return acc


def _free_elems(ap):
    n = 1
    for s_ in ap.shape[1:]:
        n *= s_
    return n


_WI = {"fp32": 1.0, "dve": 1.0, "act": 1.0, "pe_small": 1.0, "pe_big": 1.0, "lat": 1.0, "pool": 1.0}


def _est(op):
    _est0(op)
    name = op.call[0]
    if op.is_dma:
        return
    if name == "matmul":
        k = op.call[2]
        if k["lhsT"].dtype == F32:
            op.dur *= _WI["fp32"]
        elif _free_elems(k["rhs"]) >= 512:
            op.dur *= _WI["pe_big"]
        else:
            op.dur *= _WI["pe_small"]
    elif op.eng in ("dve", "act", "pool"):
        op.dur *= _WI[op.eng]


def _est0(op):
    name, a, k = op.call
    if op.is_dma:
        ap = k.get("out")
        nbytes = 1
        for s_ in ap.shape:
            nbytes *= s_
        nbytes *= _esz(ap.dtype)
        op.dur = 1200.0 if op.eng == "pool" else 150.0
        op.xfer = 2000.0 + nbytes / 150.0
        return
    def _rnd(v):
        return 32 if v <= 32 else (64 if v <= 64 else 128)
    if name == "matmul":
        n = _free_elems(k["rhs"])
        f = 4.0 if k["lhsT"].dtype == F32 else 1.0
        op.dur = (16.0 + max(n, 48) * 0.42) * f
        op.mode = (_rnd(k["lhsT"].shape[0]), _rnd(_free_elems(k["lhsT"])), f)
    elif name == "transpose":
        op.dur = 70.0
        op.mode = ("T", _rnd(k["in_"].shape[0]), _rnd(_free_elems(k["in_"])))
    elif op.eng == "act":
        op.dur = 120.0 + _free_elems(k["out"]) * 0.85
        fn_ = k.get("func")
        if fn_ in (AF.Exp, AF.Ln):
            op.mode = "A"
        elif fn_ in (AF.Sigmoid, AF.Tanh):
            op.mode = "B"
        elif fn_ == AF.Silu:
            op.mode = "C"
    else:
        o = k.get("out") if "out" in k else (a[0] if a else None)
        n = _free_elems(o) if o is not None else 64
        if op.eng == "pool":
            op.dur = 100.0 + n * 2.2
        else:
            op.dur = 70.0 + n * 1.17


class Sched:
    def __init__(self, nc, dma_slots=None, reorder=True):
        self.nc = nc
        self.ops = []
        self.dma_slots = dma_slots or {"sp": 8, "act": 2, "pool": 6, "dve": 2, "pe": 2}
        self.reorder = reorder
        self.tag = ""
        self.tags = []

    def _mk(self, eng, fn, is_dma):
        rec = _Rec()
        fn(rec)
        name, a, k = rec.call
        op = Op(len(self.ops), eng, (lambda e: getattr(e, name)(*a, **k)), (name, a, k), is_dma)
        self.ops.append(op)
        self.tags.append(self.tag)
        return op

    def add(self, eng, fn, reads=(), writes=()):
        return self._mk(eng, fn, False)

    def dma(self, eng, fn, reads=(), writes=()):
        return self._mk(eng, fn, True)

    def barrier(self):
        pass

    def _build_dag(self):
        wlog = {}
        rlog = {}
        for op in self.ops:
            acc = _call_accesses(op.call)
            _est(op)
            deps = set()
            for key, p0, p1, b0, b1, is_w in acc:
                for (q0, q1, c0, c1, oi, oe) in wlog.get(key, ()):
                    if q0 < p1 and p0 < q1 and c0 < b1 and b0 < c1:
                        deps.add(oi)
                if is_w or key[0] == "ps":
                    for (q0, q1, c0, c1, oi, oe) in rlog.get(key, ()):
                        if q0 < p1 and p0 < q1 and c0 < b1 and b0 < c1:
                            if is_w or oe != op.eng:
                                deps.add(oi)
            for key, p0, p1, b0, b1, is_w in acc:
                if is_w:
                    for lg in (wlog, rlog):
                        l_ = lg.get(key)
                        if l_:
                            lg[key] = [e_ for e_ in l_ if not (p0 <= e_[0] and e_[1] <= p1 and b0 <= e_[2] and e_[3] <= b1)]
                    wlog.setdefault(key, []).append((p0, p1, b0, b1, op.idx, op.eng))
                else:
                    rlog.setdefault(key, []).append((p0, p1, b0, b1, op.idx, op.eng))
            deps.discard(op.idx)
            op.deps = tuple(sorted(deps))

    def _list_schedule(self):
        import heapq
        ops = self.ops
        n = len(ops)
        if not self.reorder:
            return list(range(n))
        npred = [len(o.deps) for o in ops]
        succ = [[] for _ in range(n)]
        for o in ops:
            for d in o.deps:
                succ[d].append(o.idx)
        ready_t = [0.0] * n
        fin = [0.0] * n
        import os as _os2
        PRI = _os2.environ.get("KPRI", "5")
        prio = list(range(n))
        if PRI != "idx":
            cp = [0.0] * n
            for i_ in range(n - 1, -1, -1):
                m = 0.0
                for s_ in succ[i_]:
                    if cp[s_] > m:
                        m = cp[s_]
                cp[i_] = m + ops[i_].dur + ops[i_].xfer + 200.0
            w_ = float(PRI)
            rank = sorted(range(n), key=lambda i_: (i_ - w_ * cp[i_] / 100.0))
            for r_, i_ in enumerate(rank):
                prio[i_] = r_
        engs = ("pe", "act", "dve", "pool", "sp")
        future = {e: [] for e in engs}
        avail = {e: [] for e in engs}
        free = {e: 0.0 for e in engs}
        for o in ops:
            if npred[o.idx] == 0:
                heapq.heappush(future[o.eng], (0.0, o.idx))
        order = []
        self.start_t = {}
        pe_mode = [None]
        act_set = [None]
        WINDOW = 3000
        done_upto = 0
        sched = [False] * n
        while len(order) < n:
            best = None
            for e in engs:
                fu, av = future[e], avail[e]
                while fu and fu[0][0] <= free[e]:
                    t_, i_ = heapq.heappop(fu)
                    heapq.heappush(av, (prio[i_], i_))
                if av:
                    pick = av[0]
                    if e == "pe" and len(av) > 1 and ops[pick[1]].mode != pe_mode[0]:
                        best_same = None
                        for i2 in av:
                            if ops[i2[1]].mode == pe_mode[0] and (best_same is None or i2 < best_same):
                                best_same = i2
                        if best_same is not None and best_same[0] - pick[0] < 400:
                            pick = best_same
                    if e == "act" and len(av) > 1 and ops[pick[1]].mode not in (None, act_set[0]):
                        best_same = None
                        for i2 in av:
                            if ops[i2[1]].mode in (None, act_set[0]) and (best_same is None or i2 < best_same):
                                best_same = i2
                        if best_same is not None and best_same[0] - pick[0] < 300:
                            pick = best_same
                    cand = (free[e], pick[0], e, True, pick)
                elif fu:
                    cand = (fu[0][0], prio[fu[0][1]], e, False, fu[0])
                else:
                    continue
                if best is None or cand[:2] < best[:2]:
                    best = cand
            start, _pr, e, from_av, item = best
            if from_av:
                i_ = item[1]
                if avail[e][0] == item:
                    heapq.heappop(avail[e])
                else:
                    avail[e].remove(item)
                    heapq.heapify(avail[e])
            else:
                i_ = item[1]
                heapq.heappop(future[e])
            o = ops[i_]
            if e == "pe":
                if o.mode != pe_mode[0]:
                    start += 150.0
                pe_mode[0] = o.mode
            elif e == "act" and o.mode is not None:
                if o.mode != act_set[0]:
                    start += 1300.0
                act_set[0] = o.mode
            free[e] = start + o.dur
            self.start_t[i_] = start
            fin[i_] = start + o.dur + o.xfer
            order.append(i_)
            for s_ in succ[i_]:
                lat = (40.0 if (ops[s_].eng == e and e == "pe" and not o.is_dma) else (150.0 if ops[s_].eng == e else 400.0)) * _WI["lat"]
                if fin[i_] + lat > ready_t[s_]:
                    ready_t[s_] = fin[i_] + lat
                npred[s_] -= 1
                if npred[s_] == 0:
                    heapq.heappush(future[ops[s_].eng], (ready_t[s_], s_))
        self.est_makespan = max(fin) if fin else 0.0
        return order

    def finalize(self):
        self._build_dag()
        order = self._list_schedule()
        self.order = order
        ops = self.ops
        dma_count = {}
        slot_last = {}
        extra = {}
        seqc = {}
        for i_ in order:
            op = ops[i_]
            seqc[op.eng] = seqc.get(op.eng, 0) + 1
            op.seq = seqc[op.eng]
            if op.is_dma:
                n_ = dma_count.get(op.eng, 0)
                dma_count[op.eng] = n_ + 1
                k = self.dma_slots[op.eng]
                op.slot = (op.eng, n_ % k)
                op.val = 16 * (n_ // k + 1)
                prev = slot_last.get(op.slot)
                if prev is not None:
                    extra[i_] = prev
                slot_last[op.slot] = i_
        known = {}
        vc = {}
        self.waits = {}
        for i_ in order:
            op = ops[i_]
            K = known.setdefault(op.eng, {})
            deps = list(op.deps)
            if i_ in extra:
                deps.append(extra[i_])
            cand = []
            for d in deps:
                p = ops[d]
                if (not p.is_dma) and p.eng == "pe" and op.eng == "pe" and not op.is_dma:
                    continue
                cand.append(d)
            cand.sort(key=lambda d: -ops[d].seq)
            w = []
            for d in cand:
                p = ops[d]
                key = ("dma", p.slot) if p.is_dma else ("eng", p.eng)
                val = p.val if p.is_dma else p.seq
                if K.get(key, -1) >= val:
                    continue
                w.append(d)
                p.marked = True
                for kk, vv in vc[d].items():
                    if K.get(kk, -1) < vv:
                        K[kk] = vv
            self.waits[i_] = w
            v = dict(K)
            if op.is_dma:
                v[("dma", op.slot)] = op.val
            else:
                if v.get(("eng", op.eng), -1) < op.seq and op.eng == "pe":
                    pass
                v[("eng", op.eng)] = max(v.get(("eng", op.eng), -1), op.seq)
            vc[i_] = v
        cnt = {}
        for i_ in order:
            op = ops[i_]
            if not op.is_dma and op.marked:
                cnt[op.eng] = cnt.get(op.eng, 0) + 1
                op.val = cnt[op.eng]
        self.counts = cnt

    def emit(self, stack):
        nc = self.nc
        sems = {}
        for e in COMPUTE:
            sems[("eng", e)] = stack.enter_context(nc.semaphore("s_" + e))
        for e, k in self.dma_slots.items():
            for i in range(k):
                sems[("dma", (e, i))] = stack.enter_context(nc.semaphore("d_%s%d" % (e, i)))
        block = stack.enter_context(nc.Block())
        ops = self.ops
        order = self.order
        waits = self.waits

        def run(engname, eng):
            for i_ in order:
                op = ops[i_]
                if op.eng != engname:
                    continue
                for d in waits[i_]:
                    p = ops[d]
                    s = sems[("dma", p.slot)] if p.is_dma else sems[("eng", p.eng)]
                    eng.wait_ge(s, p.val)
                ins = op.fn(eng)
                if op.is_dma:
                    ins.then_inc(sems[("dma", op.slot)], 16)
                elif op.marked:
                    ins.then_inc(sems[("eng", op.eng)], 1)
            last = {}
            for i_ in order:
                op = ops[i_]
                if op.is_dma and op.eng == engname:
                    last[op.slot] = op.val
            for slot, v in last.items():
                eng.wait_ge(sems[("dma", slot)], v)

        @block.sync
        def _(e):
            run("sp", e)

        @block.tensor
        def _(e):
            run("pe", e)

        @block.scalar
        def _(e):
            run("act", e)

        @block.vector
        def _(e):
            run("dve", e)

        @block.gpsimd
        def _(e):
            run("pool", e)


D = 1024
T = 2048
NS = 16
HALF = 1024
TW = HALF + NS
RW = 512
RPROJ = 1792
HW_ = 512
DFF = 2816
INP = 5888
CDEC = -0.6065306597126334
GN_EPS = 64e-5
RMS_EPS = 1e-6

PF = {}
_c = 0
for _n, _w in (("mu", 14), ("w0", 4), ("a0", 4), ("k_k", 4), ("k_a", 4), ("r_k", 4), ("ln_g", 4), ("ln_b", 4),
               ("lb0", 4), ("lb1", 4), ("hng", 4)):
    PF[_n] = _c
    _c += _w
PF_N = _c
DV = {"omu": 0, "omka": 14, "lb": 18, "omlb": 22}
DV_N = 29

CO = {"ident": 0, "blk64": 128, "ones": 256, "maskNM": 384, "maskL": 512, "maskI": 576, "cmask": 640}
CO_N = 640 + 1024 + 256


def make_consts():
    c = np.zeros((128, CO_N), np.float32)
    p = np.arange(128)
    c[:, 0:128] = np.eye(128, dtype=np.float32)
    c[:, 128:256] = (p[:, None] // 64 == p[None, :] // 64).astype(np.float32)
    c[:, 256:384] = 1.0
    s = (p % 64)[:, None]
    t = np.arange(64)[None, :]
    c[:, 384:448] = (s < t)
    c[:, 448:512] = (s <= t)
    c[:, 512:576] = (s > t)
    c[:, 576:640] = (s == t)
    tt = np.arange(1024)
    c[:, 640:1664] = (tt % 64 != 0).astype(np.float32)[None, :]
    c[:, 1664:1920] = np.eye(16, dtype=np.float32).reshape(-1)[None, :]
    return c


class Arena:
    def __init__(self, tile, nwords):
        self.t = tile
        self.n = nwords
        self.top = 0

    def alloc(self, dtype, shape):
        free = 1
        for s in shape[1:]:
            free *= s
        words = free if dtype == F32 else (free + 1) // 2
        words = (words + 7) // 8 * 8
        off = self.top
        self.top += words
        assert self.top <= self.n, ("arena overflow", self.top, self.n)
        v = self.t[0:shape[0], off:off + words]
        if dtype == BF16:
            v = v.bitcast(BF16)
        v = v[:, 0:free]
        if len(shape) > 2:
            names = ["d%d" % i for i in range(len(shape) - 1)]
            pat = "p (" + " ".join(names) + ") -> p " + " ".join(names)
            kw = {names[i]: shape[i + 1] for i in range(len(names))}
            v = v.rearrange(pat, **kw)
        return v


ARENA_WORDS = 32560


def build_program(dbg=None, passes=(0, 1), stop_after=None):
    nc = bass.Bass("TRN2", target_bir_lowering=False)

    def din(name, shape):
        return nc.dram_tensor(name, list(shape), F32, kind="ExternalInput").ap()

    def dout(name, shape):
        return nc.dram_tensor(name, list(shape), F32, kind="ExternalOutput").ap()

    x_p = din("x_p", [T, D])
    x_s = din("x_s", [NS, D])
    wkv_s = din("wkv_s", [NS * 8, 4096])
    shift_s = din("shift_s", [NS, RPROJ])
    hgrn_s = din("hgrn_s", [NS, 4, 128, 128])
    w_in = din("w_in", [D, INP])
    w2a2 = din("w2a2", [128, 512])
    g2 = din("g2", [128, 512])
    w_up_a = din("w_up_a", [RW, D])
    w_up_b = din("w_up_b", [HW_, D])
    w_out = din("w_out", [D, D])
    w_fg = din("w_fg", [D, DFF])
    w_fu = din("w_fu", [D, DFF])
    w_fd = din("w_fd", [DFF, D])
    pfm_d = din("pfm", [128, PF_N])
    consts_d = din("consts", [128, CO_N])
    g_mix = din("g_mix", [D])
    g_ffn = din("g_ffn", [D])
    g_fin = din("g_fin", [D])

    y_p = dout("y_p", [T, D])
    y_s = dout("y_s", [NS, D])
    o_wkv_p = dout("o_wkv_p", [8, 64, 64])
    o_shift_p = dout("o_shift_p", [RPROJ])
    o_hgrn_p = dout("o_hgrn_p", [4, 128, 128])
    o_wkv_s = dout("o_wkv_s", [NS * 8, 4096])
    o_shift_s = dout("o_shift_s", [NS, RPROJ])
    o_hgrn_s = dout("o_hgrn_s", [NS, 4, 128, 128])
    dbg_out = {}
    if dbg:
        for k, shp in dbg.items():
            dbg_out[k] = dout("dbg_" + k, shp)
    scr_a = nc.dram_tensor("scr_a", [NS, 6, 512], F32).ap()
    scr_y = nc.dram_tensor("scr_y", [NS, 512], F32).ap()

    st = ExitStack()
    with st:
        def sb(name, shape, dt):
            return st.enter_context(nc.sbuf_tensor(name, list(shape), dt))

        arena_t = sb("arena", [128, ARENA_WORDS], F32)
        hT = sb("hT", [128, 8, TW], BF16)
        oa = sb("oa", [128, 4, TW], BF16)
        ob = sb("ob", [128, 4, TW], BF16)
        NWB = 4
        wbuf = [sb("wbuf%d" % i, [128, 8, 512], BF16) for i in range(NWB)]
        consts = sb("consts_sb", [128, 384], F32)
        ident_bf = sb("ident_bf", [128, 128], BF16)
        masks_bf = sb("masks_bf", [128, 256], BF16)
        pfm = sb("pfm_sb", [128, PF_N], F32)
        dv = sb("dv", [128, DV_N], F32)
        lw_w = sb("lw_w", [128, 512], BF16)
        g2_w = sb("g2_w", [128, 512], BF16)
        carry = sb("carry", [128, 14], F32)
        shiftS = sb("shiftS", [128, 14, NS], F32)
        sprevT = sb("sprevT", [128, 14, NS], F32)
        H0f = sb("H0f", [128, 4, 64], F32)
        H0bd = sb("H0bd", [128, 4, 128], BF16)
        S0f = sb("S0f", [128, 4, 128], F32)
        S0b = sb("S0b", [128, 4, 128], BF16)
        WC = sb("WC", [128, 4, 16], F32)
        DC = sb("DC", [128, 4, 16], F32)
        stat = sb("stat", [128, 64], F32)
        ftmp = sb("ftmp", [128, 2, 512], F32)
        ps = st.enter_context(nc.psum_tensor("ps", [128, 8, 512], F32))

        ident = consts[:, 0:128]
        blk64 = consts[:, 128:256]
        ones = consts[:, 256:384]
        maskNM_bf = masks_bf[:, 0:128]
        maskL_bf = masks_bf[:, 128:192]
        maskI_bf = masks_bf[:, 192:256]

        S = Sched(nc)
        A = S.add
        bank_ctr = [0]

        def nb(n=1):
            b = bank_ctr[0]
            if n > 1:
                b = (b + n - 1) // n * n
            if b + n > 8:
                b = 0
            bank_ctr[0] = (b + n) % 8
            return b

        def nbp(par):
            b = bank_ctr[0]
            if b % 2 != par:
                b = (b + 1) % 8
            bank_ctr[0] = (b + 1) % 8
            return b

        par_ctr = [0]
        seq_ctr = [0, 0]

        par_nb = [6]

        def nb_par(n=1):
            m = par_nb[0]
            if n == 2:
                par_ctr[0] = (par_ctr[0] + 1) // 2 * 2
            b = par_ctr[0] % m
            par_ctr[0] = (par_ctr[0] + n) % m
            return b

        smp_ctr = [0]

        def nb_smp():
            smp_ctr[0] += 1
            return 4 + smp_ctr[0] % 2

        def nb_seq(par):
            return 6 + par

        def PB(b, n=1):
            return ["ps%d" % (b + i) for i in range(n)]

        def pf(name, j=0):
            c = PF[name] + j
            return pfm[:, c:c + 1]

        def dvc(name, j=0):
            c = DV[name] + j
            return dv[:, c:c + 1]

        wctr = [0]

        def load_w(src_ap, shape_view):
            i = wctr[0] % NWB
            wctr[0] += 1
            a, b = shape_view
            view = wbuf[i][:, :, :].rearrange("p a b -> p (a b)")[:, 0:a * b].rearrange("p (a b) -> p a b", a=a)
            rn = "wbuf%d" % i
            S.dma("pool", lambda e, view=view, src_ap=src_ap: e.dma_start(out=view, in_=src_ap), writes=[rn])
            return view, rn

        def dump(name, ap_sb, res):
            if dbg and name in dbg_out:
                S.dma("pool", lambda e: e.dma_start(out=dbg_out[name], in_=ap_sb), reads=res)

        S.dma("sp", lambda e: e.dma_start(out=consts[:], in_=consts_d[:, 0:384]), writes=["consts"])
        S.dma("pool", lambda e: e.dma_start(out=masks_bf[:], in_=consts_d[:, 384:640]))
        S.dma("sp", lambda e: e.dma_start(out=pfm[:], in_=pfm_d), writes=["pfm"])
        S.dma("pool", lambda e: e.dma_start(out=lw_w[:], in_=w2a2), writes=["lw_w"])
        S.dma("pool", lambda e: e.dma_start(out=g2_w[:], in_=g2), writes=["g2_w"])
        A("dve", lambda e: e.tensor_copy(out=ident_bf[:], in_=ident), ["consts"], ["ident_bf"])
        A("dve", lambda e: e.memset(carry[:], 0.0), [], ["carry"])
        A("dve", lambda e: e.memset(H0f[:], 0.0), [], ["H0f"])
        A("dve", lambda e: e.memset(H0bd[:], 0.0), [], ["H0b"])
        A("dve", lambda e: e.memset(S0f[:], 0.0), [], ["S0f"])
        A("dve", lambda e: e.memset(S0b[:], 0.0), [], ["S0b"])
        A("dve", lambda e: e.memset(sprevT[:], 0.0), [], ["sprevT"])
        A("dve", lambda e: e.tensor_scalar(out=dv[:, 0:14], in0=pfm[:, PF["mu"]:PF["mu"] + 14], scalar1=-1.0, scalar2=1.0,
                                            op0=ALU.mult, op1=ALU.add), ["pfm"], ["dv"])
        A("dve", lambda e: e.tensor_scalar(out=dv[:, 14:18], in0=pfm[:, PF["k_a"]:PF["k_a"] + 4], scalar1=-1.0, scalar2=1.0,
                                            op0=ALU.mult, op1=ALU.add), ["pfm"], ["dv"])
        A("dve", lambda e: e.tensor_tensor(out=dv[:, 18:22], in0=pfm[:, PF["lb0"]:PF["lb0"] + 4],
                                            in1=pfm[:, PF["lb1"]:PF["lb1"] + 4], op=ALU.subtract), ["pfm"], ["dv"])
        A("act", lambda e: e.activation(out=dv[:, 18:22], in_=dv[:, 18:22], func=AF.Sigmoid), ["dv"], ["dv"])
        A("dve", lambda e: e.tensor_scalar(out=dv[:, 22:26], in0=dv[:, 18:22], scalar1=-1.0, scalar2=1.0,
                                            op0=ALU.mult, op1=ALU.add), ["dv"], ["dv"])

        eps24, epsgn, epsrms = dv[:, 26:27], dv[:, 27:28], dv[:, 28:29]
        A("dve", lambda e: e.memset(dv[:, 26:27], 1e-24))
        A("dve", lambda e: e.memset(dv[:, 27:28], GN_EPS))
        A("dve", lambda e: e.memset(dv[:, 28:29], RMS_EPS))
        w_in_v = w_in.rearrange("(kc p) n -> p kc n", p=128)

        for pas in passes:
            import os as _os
            _skip = _os.environ.get("KSKIP", "")
            nsamp = NS if (pas == 0 and "nosamp" not in _skip) else 0
            W = HALF + nsamp
            blocks = [(0, 512), (512, 1024)] + ([(1024, 1040)] if nsamp else [])
            t0 = pas * HALF
            S.barrier()
            ar_ = Arena(arena_t, ARENA_WORDS)

            def norm_phase(tag, x_tok, xs_tok, g_dram, arena, out_cb, src_loaded, xsres):
                gB = arena.alloc(F32, [128, 1024])
                junk = arena.alloc(F32, [128, 1024])
                h_tok = arena.alloc(BF16, [128, 8, 1024])
                hs_tok = arena.alloc(BF16, [NS, 1024])
                R = tag
                S.dma("sp", lambda e: e.dma_start(out=gB, in_=g_dram.partition_broadcast(128)), writes=[R + "gB"])
                A("dve", lambda e: e.memset(stat[:, 0:32], 0.0), [], ["stat"])
                for tt in range(8):
                    A("act", lambda e, tt=tt: e.activation(out=junk, in_=x_tok[:, tt, :], func=AF.Square,
                                                           accum_out=stat[:, tt:tt + 1]),
                      [src_loaded(tt)], [R + "junk", "stat"])
                if nsamp:
                    A("act", lambda e: e.activation(out=junk[0:NS, :], in_=xs_tok, func=AF.Square,
                                                    accum_out=stat[0:NS, 8:9]), [xsres], [R + "junk", "stat"])
                A("dve", lambda e: e.tensor_scalar(out=stat[:, 16:25], in0=stat[:, 0:9], scalar1=1.0 / D, scalar2=RMS_EPS,
                                                    op0=ALU.mult, op1=ALU.add), ["stat"], ["stat"])
                A("act", lambda e: e.activation(out=stat[:, 16:25], in_=stat[:, 16:25], func=AF.Ln), ["stat"], ["stat"])
                A("act", lambda e: e.activation(out=stat[:, 16:25], in_=stat[:, 16:25], func=AF.Exp, scale=-0.5), ["stat"], ["stat"])
                out_cb(gB, h_tok, hs_tok, R)

            x_tok = ar_.alloc(F32, [128, 8, 1024])
            xs_tok = ar_.alloc(F32, [NS, 1024])
            for tt in range(8):
                S.dma("sp", lambda e, tt=tt: e.dma_start(out=x_tok[:, tt, :], in_=x_p[t0 + tt * 128:t0 + (tt + 1) * 128, :]),
                      writes=["P0x%d" % tt])
            if nsamp:
                S.dma("sp", lambda e: e.dma_start(out=xs_tok, in_=x_s), writes=["P0xs"])

            def to_hT(gB, h_tok, hs_tok, R, x_tok_, xs_tok_, xres, xsres):
                for tt in range(8):
                    A("dve", lambda e, tt=tt: e.scalar_tensor_tensor(out=h_tok[:, tt, :], in0=x_tok_[:, tt, :],
                                                                     scalar=stat[:, 16 + tt:17 + tt], in1=gB,
                                                                     op0=ALU.mult, op1=ALU.mult),
                      [xres(tt), "stat", R + "gB"], [R + "htok%d" % tt])
                    b = nb()
                    pT = ps[:, b, :].bitcast(BF16).rearrange("p (c t) -> p c t", c=8)
                    for dc in range(8):
                        A("pe", lambda e, tt=tt, dc=dc, pT=pT: e.transpose(out=pT[:, dc, :], in_=h_tok[:, tt, dc * 128:(dc + 1) * 128],
                                                                           identity=ident_bf[:]),
                          [R + "htok%d" % tt, "ident_bf"], PB(b))
                    eng = "act" if tt % 2 == 0 else "dve"
                    if eng == "act":
                        A("act", lambda e, tt=tt, pT=pT: e.copy(out=hT[:, :, tt * 128:(tt + 1) * 128], in_=pT), PB(b), ["hT"])
                    else:
                        A("dve", lambda e, tt=tt, pT=pT: e.tensor_copy(out=hT[:, :, tt * 128:(tt + 1) * 128], in_=pT), PB(b), ["hT"])
                if nsamp:
                    A("dve", lambda e: e.scalar_tensor_tensor(out=hs_tok, in0=xs_tok_, scalar=stat[0:NS, 24:25], in1=gB[0:NS, :],
                                                              op0=ALU.mult, op1=ALU.mult), [xsres, "stat", R + "gB"], [R + "hstok"])
                    b = nb()
                    pT = ps[:, b, :].bitcast(BF16).rearrange("p (c t) -> p c t", c=8)
                    for dc in range(8):
                        A("pe", lambda e, dc=dc, pT=pT: e.transpose(out=pT[:, dc, 0:NS], in_=hs_tok[:, dc * 128:(dc + 1) * 128],
                                                                    identity=ident_bf[0:NS, 0:NS]),
                          [R + "hstok", "ident_bf"], PB(b))
                    A("act", lambda e, pT=pT: e.copy(out=hT[:, :, HALF:HALF + NS], in_=pT[:, :, 0:NS]), PB(b), ["hT"])

            norm_phase("P0", x_tok, xs_tok, g_mix, ar_,
                       lambda gB, h_tok, hs_tok, R: to_hT(gB, h_tok, hs_tok, R, x_tok, xs_tok, lambda tt: "P0x%d" % tt, "P0xs"),
                       lambda tt: "P0x%d" % tt, "P0xs")
            if dbg and "hT" in dbg_out and pas == 0:
                dump("hT", hT[:], ["hT"])
            if stop_after == "P0":
                continue

            def proj_fm(wview, wres, ncol0, kcs, act_tile, act_res, evac):
                for (c0, c1) in blocks:
                    b = nb()
                    for kc in range(kcs):
                        A("pe", lambda e, kc=kc, b=b, c0=c0, c1=c1: e.matmul(ps[:, b, 0:c1 - c0], lhsT=wview[:, kc, ncol0:ncol0 + 128],
                                                                              rhs=act_tile[:, kc, c0:c1], start=(kc == 0),
                                                                              stop=(kc == kcs - 1)),
                          [wres, act_res], PB(b))
                    evac(b, c0, c1)

            S.tag = "p%d Rprep" % pas
            S.barrier()
            ar_ = Arena(arena_t, ARENA_WORDS)
            sig = ar_.alloc(F32, [128, 4, TW])
            g_bf = ar_.alloc(BF16, [128, 4, TW])
            ar_t = ar_.alloc(BF16, [128, 4, 16, 2, 64])
            bT = ar_.alloc(BF16, [128, 4, HALF])
            kT = ar_.alloc(BF16, [128, 4, HALF])
            vT = ar_.alloc(BF16, [128, 4, TW])
            bonus = ar_.alloc(BF16, [128, 4, TW])
            smp = ar_.alloc(F32, [128, 6, 4, NS])
            mark = ar_.top
            a_bf = ar_.alloc(BF16, [128, 4, TW])
            kpr = ar_.alloc(BF16, [128, 4, TW])
            praw = [ar_.alloc(F32, [128, 1048]) for _ in range(1)]
            NT = 7
            tmp = [ar_.alloc(F32, [128, TW]) for _ in range(NT)]
            cmask = ar_.alloc(BF16, [128, HALF])
            lwb = ar_.alloc(BF16, [128, TW])
            import os as _os
            _skip = _os.environ.get("KSKIP", "")
            if "cmask" not in _skip:
                S.dma("pool", lambda e: e.dma_start(out=cmask, in_=consts_d[:, 640:1664]), writes=["cmask"])

            if nsamp and "shiftT" not in _skip:
                sh_tok = tmp[0][0:NS, :]
                sh_tok2 = tmp[1][0:NS, :]
                S.dma("sp", lambda e: e.dma_start(out=sh_tok[:, 0:1024], in_=shift_s[:, 0:1024]), writes=["tmp0"])
                S.dma("sp", lambda e: e.dma_start(out=sh_tok2[:, 0:768], in_=shift_s[:, 1024:1792]), writes=["tmp1"])
                b = nb()
                for c in range(14):
                    src = sh_tok[:, c * 128:(c + 1) * 128] if c < 8 else sh_tok2[:, (c - 8) * 128:(c - 7) * 128]
                    A("pe", lambda e, c=c, src=src, b=b: e.transpose(out=ps[:, b, c * NS:(c + 1) * NS], in_=src, identity=ident[0:NS, 0:NS]),
                      ["tmp0", "tmp1", "consts"], PB(b))
                A("dve", lambda e, b=b: e.tensor_copy(out=sprevT[:, :, :], in_=ps[:, b, 0:14 * NS].rearrange("p (c n) -> p c n", c=14)),
                  PB(b), ["sprevT"])

            if stop_after == "shiftT":
                continue
            hv = [(0, 512), (512, W)]
            pctr = [0]

            def rwkv_chunk(c, wview, wres, ncol0, out_ap=None):
                pr = praw[0]
                t1 = tmp[6]
                A("dve", lambda e: e.tensor_copy(out=pr[:, 0:1], in_=carry[:, c:c + 1]))

                def evac(b, c0, c1):
                    A("act", lambda e: e.copy(out=pr[:, 1 + c0:1 + c1], in_=ps[:, b, 0:c1 - c0]))
                    A("act", lambda e: e.activation(out=t1[:, c0:c1], in_=ps[:, b, 0:c1 - c0], func=AF.Copy, scale=dvc("omu", c)))
                proj_fm(wview, wres, ncol0, 8, hT, "hT", evac)
                A("act", lambda e: e.copy(out=carry[:, c:c + 1], in_=pr[:, HALF:HALF + 1]))
                if nsamp:
                    A("act", lambda e: e.copy(out=shiftS[:, c, :], in_=pr[:, 1 + HALF:1 + HALF + NS]))
                out_ap_ = tmp[0] if out_ap is None else out_ap
                for (a_, b_) in hv:
                    b2 = min(b_, HALF)
                    A("dve", lambda e, a_=a_, b2=b2: e.scalar_tensor_tensor(out=out_ap_[:, a_:b2], in0=pr[:, a_:b2], scalar=pf("mu", c),
                                                                            in1=t1[:, a_:b2], op0=ALU.mult, op1=ALU.add))
                if nsamp:
                    A("dve", lambda e: e.scalar_tensor_tensor(out=out_ap_[:, HALF:HALF + NS], in0=sprevT[:, c, :], scalar=pf("mu", c),
                                                              in1=t1[:, HALF:HALF + NS], op0=ALU.mult, op1=ALU.add))
                return out_ap_

            wv, wr = load_w(w_in_v[:, :, 1536:1792], (8, 256))
            psm = rwkv_chunk(12, wv, wr, 0)
            for (a_, b_) in hv:
                A("act", lambda e, a_=a_, b_=b_: e.activation(out=lwb[0:64, a_:b_], in_=psm[0:64, a_:b_], func=AF.Tanh))
                A("act", lambda e, a_=a_, b_=b_: e.copy(out=lwb[64:128, a_:b_], in_=psm[64:128, a_:b_]))
            for j in range(4):
                for (c0, c1) in blocks:
                    b2_ = nb(2)
                    b = b2_
                    A("pe", lambda e, j=j, b=b, c0=c0, c1=c1: e.matmul(ps[:, b, 0:c1 - c0], lhsT=lw_w[0:64, j * 128:(j + 1) * 128],
                                                                        rhs=lwb[0:64, c0:c1], start=True, stop=True))
                    A("act", lambda e, j=j, b=b, c0=c0, c1=c1: e.activation(out=sig[:, j, c0:c1], in_=ps[:, b, 0:c1 - c0], func=AF.Sigmoid,
                                                                             bias=pf("w0", j)))
                    b = b2_ + 1
                    A("pe", lambda e, j=j, b=b, c0=c0, c1=c1: e.matmul(ps[:, b, 0:c1 - c0], lhsT=lw_w[64:128, j * 128:(j + 1) * 128],
                                                                        rhs=lwb[64:128, c0:c1], start=True, stop=True))
                    A("act", lambda e, j=j, b=b, c0=c0, c1=c1: e.activation(out=a_bf[:, j, c0:c1], in_=ps[:, b, 0:c1 - c0], func=AF.Sigmoid,
                                                                             bias=pf("a0", j)))
            psm = rwkv_chunk(13, wv, wr, 128)
            for (a_, b_) in hv:
                A("act", lambda e, a_=a_, b_=b_: e.activation(out=lwb[:, a_:b_], in_=psm[:, a_:b_], func=AF.Sigmoid))
            for j in range(4):
                for (c0, c1) in blocks:
                    b = nb()
                    A("pe", lambda e, j=j, b=b, c0=c0, c1=c1: e.matmul(ps[:, b, 0:c1 - c0], lhsT=g2_w[:, j * 128:(j + 1) * 128],
                                                                        rhs=lwb[:, c0:c1], start=True, stop=True))
                    A("dve", lambda e, j=j, b=b, c0=c0, c1=c1: e.tensor_copy(out=g_bf[:, j, c0:c1], in_=ps[:, b, 0:c1 - c0]))
            if dbg and pas == 0:
                dump("sig", sig, ["sig"])

            wv, wr = load_w(w_in_v[:, :, 1024:1536], (8, 512))
            for j in range(4):
                rwkv_chunk(8 + j, wv, wr, j * 128, out_ap=vT[:, j, :])
                if nsamp:
                    A("act", lambda e, j=j: e.copy(out=smp[:, 3, j, :], in_=vT[:, j, HALF:HALF + NS]))

            def fp32_blocksum(src_ap, src_res, mat, evac):
                for (c0, c1) in blocks:
                    b = nb()
                    A("pe", lambda e, b=b, c0=c0, c1=c1: e.matmul(ps[:, b, 0:c1 - c0], lhsT=mat, rhs=src_ap[:, c0:c1], start=True, stop=True))
                    evac(b, c0, c1)

            def cumsum_decay(j, cs_i):
                cs = tmp[cs_i]
                for (a_, b_) in hv:
                    b2 = min(b_, HALF)
                    A("dve", lambda e, a_=a_, b2=b2: e.tensor_tensor_scan(out=cs[:, a_:b2], data0=cmask[:, a_:b2], data1=sig[:, j, a_:b2], initial=0.0,
                                                                          op0=ALU.mult, op1=ALU.add))
                if nsamp:
                    A("dve", lambda e: e.tensor_copy(out=cs[:, HALF:HALF + NS], in_=sig[:, j, HALF:HALF + NS]))
                return cs

            def cview(ap2, a_, b2):
                return ap2[:, a_:b2].rearrange("p (c t) -> p c t", t=64)

            wv, wr = load_w(w_in_v[:, :, 512:1024], (8, 512))
            for j in range(4):
                k_ap = rwkv_chunk(4 + j, wv, wr, j * 128)
                kkr, rs, cs, en, de, ka = tmp[1], tmp[2], tmp[3], tmp[4], tmp[5], tmp[2]
                for (a_, b_) in hv:
                    A("act", lambda e, j=j, a_=a_, b_=b_: e.activation(out=kkr[:, a_:b_], in_=k_ap[:, a_:b_], func=AF.Copy, scale=pf("k_k", j)))
                    A("act", lambda e, a_=a_, b_=b_: e.activation(out=rs[:, a_:b_], in_=kkr[:, a_:b_], func=AF.Square))

                def ev_ss(b, c0, c1):
                    A("act", lambda e: e.activation(out=de[:, c0:c1], in_=ps[:, b, 0:c1 - c0], func=AF.Ln, bias=eps24[:, 0:1]))
                fp32_blocksum(rs, "tmp2", blk64, ev_ss)
                cumsum_decay(j, 3)
                for (a_, b_) in hv:
                    b2 = min(b_, HALF)
                    c_lo, c_hi = a_ // 64, b2 // 64
                    A("act", lambda e, a_=a_, b_=b_: e.activation(out=de[:, a_:b_], in_=de[:, a_:b_], func=AF.Exp, scale=-0.5))
                    A("dve", lambda e, a_=a_, b_=b_: e.tensor_tensor(out=kkr[:, a_:b_], in0=kkr[:, a_:b_], in1=de[:, a_:b_], op=ALU.mult))
                    A("act", lambda e, j=j, a_=a_, b2=b2, c_lo=c_lo, c_hi=c_hi: e.activation(out=WC[:, j, c_lo:c_hi], in_=cview(cs, a_, b2)[:, :, 63],
                                                                                             func=AF.Exp, scale=CDEC))
                    A("act", lambda e, a_=a_, b2=b2: e.activation(out=en[:, a_:b2], in_=cs[:, a_:b2], func=AF.Exp, scale=-CDEC))
                    if nsamp and b_ > HALF:
                        A("act", lambda e, j=j: e.activation(out=smp[:, 1, j, :], in_=cs[:, HALF:HALF + NS], func=AF.Exp, scale=CDEC))
                        A("act", lambda e, j=j: e.activation(out=smp[:, 4, j, :], in_=kkr[:, HALF:HALF + NS], func=AF.Copy, scale=-1.0))
                    A("dve", lambda e, j=j, a_=a_, b2=b2: e.tensor_tensor(out=de[:, a_:b2], in0=cs[:, a_:b2], in1=sig[:, j, a_:b2], op=ALU.subtract))
                    A("act", lambda e, a_=a_, b2=b2: e.activation(out=de[:, a_:b2], in_=de[:, a_:b2], func=AF.Exp, scale=CDEC))
                    A("dve", lambda e, j=j, a_=a_, b2=b2, c_lo=c_lo, c_hi=c_hi: e.scalar_tensor_tensor(
                        out=ar_t[:, j, c_lo:c_hi, 0, :], in0=cview(kkr, a_, b2), scalar=-1.0, in1=cview(de, a_, b2), op0=ALU.mult, op1=ALU.mult))
                    A("dve", lambda e, j=j, a_=a_, b_=b_: e.tensor_tensor(out=ka[:, a_:b_], in0=kkr[:, a_:b_], in1=a_bf[:, j, a_:b_], op=ALU.mult))
                    if nsamp and b_ > HALF:
                        A("act", lambda e, j=j: e.copy(out=smp[:, 5, j, :], in_=ka[:, HALF:HALF + NS]))
                    A("dve", lambda e, j=j, a_=a_, b2=b2: e.tensor_tensor(out=bT[:, j, a_:b2], in0=ka[:, a_:b2], in1=en[:, a_:b2], op=ALU.mult))
                    A("act", lambda e, j=j, a_=a_, b_=b_: e.activation(out=kkr[:, a_:b_], in_=a_bf[:, j, a_:b_], func=AF.Identity,
                                                                       scale=pf("k_a", j), bias=dvc("omka", j)))
                    A("dve", lambda e, a_=a_, b_=b_: e.tensor_tensor(out=kkr[:, a_:b_], in0=kkr[:, a_:b_], in1=k_ap[:, a_:b_], op=ALU.mult))
                    A("dve", lambda e, j=j, a_=a_, b2=b2: e.tensor_tensor(out=kT[:, j, a_:b2], in0=kkr[:, a_:b2], in1=en[:, a_:b2], op=ALU.mult))
                    A("act", lambda e, j=j, a_=a_, b_=b_: e.activation(out=kpr[:, j, a_:b_], in_=kkr[:, a_:b_], func=AF.Copy, scale=pf("r_k", j)))
                    if nsamp and b_ > HALF:
                        A("act", lambda e, j=j: e.copy(out=smp[:, 2, j, :], in_=kkr[:, HALF:HALF + NS]))

            wv, wr = load_w(w_in_v[:, :, 0:512], (8, 512))
            for j in range(4):
                r_ap = rwkv_chunk(j, wv, wr, j * 128)
                cs = cumsum_decay(j, 3)
                rk = tmp[1]
                for (a_, b_) in hv:
                    b2 = min(b_, HALF)
                    c_lo, c_hi = a_ // 64, b2 // 64
                    A("act", lambda e, a_=a_, b2=b2: e.activation(out=cs[:, a_:b2], in_=cs[:, a_:b2], func=AF.Exp, scale=CDEC))
                    A("dve", lambda e, j=j, a_=a_, b2=b2, c_lo=c_lo, c_hi=c_hi: e.tensor_tensor(out=ar_t[:, j, c_lo:c_hi, 1, :], in0=cview(r_ap, a_, b2),
                                                                                                 in1=cview(cs, a_, b2), op=ALU.mult))
                    A("dve", lambda e, j=j, a_=a_, b_=b_: e.tensor_tensor(out=rk[:, a_:b_], in0=r_ap[:, a_:b_], in1=kpr[:, j, a_:b_], op=ALU.mult))
                if nsamp:
                    A("act", lambda e, j=j: e.copy(out=smp[:, 0, j, :], in_=r_ap[:, HALF:HALF + NS]))

                def ev_bon(b, c0, c1, j=j):
                    A("dve", lambda e: e.tensor_tensor(out=bonus[:, j, c0:c1], in0=ps[:, b, 0:c1 - c0], in1=vT[:, j, c0:c1], op=ALU.mult))
                fp32_blocksum(rk, "tmp1", blk64, ev_bon)
            if dbg and pas == 0:
                dump("ar", ar_t, ["ar"])
                dump("bT", bT, ["bT"])
                dump("kT", kT, ["kT"])
                dump("vT", vT, ["vT"])
                dump("bonus", bonus, ["bonus"])
            if stop_after == "Rprep":
                continue

            yT = sig
            if nsamp:
                ar_.top = mark
                L1v = ar_.alloc(F32, [128, 6, 64])
                saL = ar_.alloc(F32, [128, 64])
                yL1 = ar_.alloc(F32, [128, 64])
                rs_mark = ar_.top
                wf = [wbuf[i_][:, :, :].rearrange("p a b -> p (a b)").bitcast(F32) for i_ in range(4)]
                S_h = [wf[0].rearrange("p (v k) -> p v k", k=64), wf[1].rearrange("p (v k) -> p v k", k=64)]
                T_h = [wf[2].rearrange("p (v k) -> p v k", k=64), wf[3].rearrange("p (v k) -> p v k", k=64)]
                tok6h = [wf[2][0:NS, 0:1536].rearrange("p (v n) -> p v n", v=3), wf[3][0:NS, 0:1536].rearrange("p (v n) -> p v n", v=3)]
                ytok = wf[2][0:NS, 1536:2048]
                shtok = wf[3][0:NS, 0:2048]
                def emit_rsample():
                    for hf in range(2):
                        S.dma("sp", lambda e, hf=hf: e.dma_start(out=S_h[hf].rearrange("p v k -> p (v k)"), in_=wkv_s[:, hf * 2048:(hf + 1) * 2048]))
                    for vec in range(6):
                        b = nb_par()
                        for j in range(4):
                            A("pe", lambda e, vec=vec, j=j, b=b: e.transpose(out=ps[0:NS, b, j * 128:(j + 1) * 128], in_=smp[:, vec, j, :], identity=ident))
                        dst = tok6h[vec // 3][:, vec % 3, :]
                        if vec % 2 == 0:
                            A("act", lambda e, dst=dst, b=b: e.copy(out=dst, in_=ps[0:NS, b, :]))
                        else:
                            A("dve", lambda e, dst=dst, b=b: e.tensor_copy(out=dst, in_=ps[0:NS, b, :]))
                    for hf in range(2):
                        S.dma("sp", lambda e, hf=hf: e.dma_start(out=scr_a[:, hf * 3:(hf + 1) * 3, :], in_=tok6h[hf]))
                    for vec in range(6):
                        S.dma("sp", lambda e, vec=vec: e.dma_start(out=L1v[:, vec, :], in_=scr_a[:, vec, :].rearrange("b (h n) -> b h n", h=8)))

                    def bc_v(vec):
                        return L1v[:, vec, :].unsqueeze(1).broadcast_to([128, 32, 64])

                    def bc_k(ap2, hf):
                        return ap2[:, hf * 32:(hf + 1) * 32].unsqueeze(2).broadcast_to([128, 32, 64])
                    for hf in range(2):
                        Sx, Tx = S_h[hf], T_h[hf]
                        vs = slice(hf * 32, (hf + 1) * 32)
                        A("dve", lambda e, Sx=Sx, Tx=Tx: e.tensor_tensor(out=Tx, in0=Sx, in1=bc_v(4), op=ALU.mult))
                        A("dve", lambda e, Tx=Tx, vs=vs: e.tensor_reduce(out=saL[:, vs], in_=Tx, axis=AX.X, op=ALU.add))
                        A("dve", lambda e, Sx=Sx: e.tensor_tensor(out=Sx, in0=Sx, in1=bc_v(1), op=ALU.mult))
                        A("dve", lambda e, Tx=Tx, hf=hf: e.tensor_tensor(out=Tx, in0=bc_k(saL, hf), in1=bc_v(5), op=ALU.mult))
                        A("dve", lambda e, Sx=Sx, Tx=Tx: e.tensor_tensor(out=Sx, in0=Sx, in1=Tx, op=ALU.add))
                        A("dve", lambda e, Tx=Tx, hf=hf: e.tensor_tensor(out=Tx, in0=bc_k(L1v[:, 3, :], hf), in1=bc_v(2), op=ALU.mult))
                        A("dve", lambda e, Sx=Sx, Tx=Tx: e.tensor_tensor(out=Sx, in0=Sx, in1=Tx, op=ALU.add))
                        S.dma("sp", lambda e, Sx=Sx, hf=hf: e.dma_start(out=o_wkv_s[:, hf * 2048:(hf + 1) * 2048], in_=Sx.rearrange("p v k -> p (v k)")))
                        A("dve", lambda e, Sx=Sx, Tx=Tx: e.tensor_tensor(out=Tx, in0=Sx, in1=bc_v(0), op=ALU.mult))
                        A("dve", lambda e, Tx=Tx, vs=vs: e.tensor_reduce(out=yL1[:, vs], in_=Tx, axis=AX.X, op=ALU.add))
                    S.dma("sp", lambda e: e.dma_start(out=scr_y.rearrange("b (h n) -> (b h) n", h=8), in_=yL1))
                    S.dma("sp", lambda e: e.dma_start(out=ytok, in_=scr_y))
                    b = nb_par()
                    for j in range(4):
                        A("pe", lambda e, j=j, b=b: e.transpose(out=ps[:, b, j * NS:(j + 1) * NS], in_=ytok[:, j * 128:(j + 1) * 128],
                                                                identity=ident[0:NS, 0:NS]))
                    A("act", lambda e, b=b: e.copy(out=yT[:, :, HALF:HALF + NS], in_=ps[:, b, 0:4 * NS].rearrange("p (j n) -> p j n", j=4)))
                    for g_ in range(4):
                        cs_ = list(range(g_ * 4, min(14, g_ * 4 + 4)))
                        b = nb_par()
                        for ci, c in enumerate(cs_):
                            A("pe", lambda e, ci=ci, c=c, b=b: e.transpose(out=ps[0:NS, b, ci * 128:(ci + 1) * 128], in_=shiftS[:, c, :], identity=ident))
                        n_ = len(cs_) * 128
                        A("act", lambda e, g_=g_, b=b, n_=n_: e.copy(out=shtok[:, g_ * 512:g_ * 512 + n_], in_=ps[0:NS, b, 0:n_]))
                    S.dma("sp", lambda e: e.dma_start(out=o_shift_s, in_=shtok[:, 0:RPROJ]))

            if stop_after == "Rsample":
                continue
            ar_.top = rs_mark if nsamp else mark
            bk_tok = [ar_.alloc(BF16, [128, 2, 512]) for _ in range(2)]
            v_tok = [ar_.alloc(BF16, [128, 512]) for _ in range(2)]
            NM_sb = [ar_.alloc(BF16, [128, 8, 2, 128]) for _ in range(2)]
            P_sb2 = [[ar_.alloc(BF16, [128, 8, 64]) for _ in range(2)] for _ in range(2)]
            Tt_sb2 = [[ar_.alloc(BF16, [128, 8, 64]) for _ in range(1)] for _ in range(2)]
            QT_sb2 = [[ar_.alloc(BF16, [128, 8, 2, 64]) for _ in range(2)] for _ in range(2)]
            X_sb = [ar_.alloc(BF16, [128, 8, 64]) for _ in range(2)]
            U_sb = [ar_.alloc(BF16, [128, 8, 64]) for _ in range(2)]
            XV_sb = [ar_.alloc(F32, [128, 8, 64]) for _ in range(2)]
            Hs = ar_.alloc(F32, [128, 4, 64])
            gtmp = [ar_.alloc(F32, [128, TW]) for _ in range(3)]

            for i in range(8):
                q = i % 2
                P_sb, QT_sb, Tt_sb = P_sb2[q], QT_sb2[q], Tt_sb2[q]
                S.tag = "p%d Rpar%d" % (pas, i)
                b1 = nb_par()
                b2 = nb_par()
                pT1 = ps[:, b1, :].bitcast(BF16).rearrange("p (v n) -> p v n", v=2)
                pT2 = ps[:, b2, :].bitcast(BF16)
                for j in range(4):
                    A("pe", lambda e, i=i, j=j, pT1=pT1: e.transpose(out=pT1[:, 0, j * 128:(j + 1) * 128], in_=bT[:, j, i * 128:(i + 1) * 128],
                                                                     identity=ident_bf[:]), ["bT", "ident_bf"], PB(b1))
                    A("pe", lambda e, i=i, j=j, pT1=pT1: e.transpose(out=pT1[:, 1, j * 128:(j + 1) * 128], in_=kT[:, j, i * 128:(i + 1) * 128],
                                                                     identity=ident_bf[:]), ["kT", "ident_bf"], PB(b1))
                    A("pe", lambda e, i=i, j=j, pT2=pT2: e.transpose(out=pT2[:, j * 128:(j + 1) * 128], in_=vT[:, j, i * 128:(i + 1) * 128],
                                                                     identity=ident_bf[:]), ["vT", "ident_bf"], PB(b2))
                A("act", lambda e, q=q, pT1=pT1: e.copy(out=bk_tok[q][:], in_=pT1), PB(b1), ["bk_tok%d" % q])
                A("dve", lambda e, q=q, pT2=pT2: e.tensor_copy(out=v_tok[q][:], in_=pT2[:, 0:512]), PB(b2), ["v_tok%d" % q])
                if stop_after == "c1":
                    break
                for hg in range(2):
                    b = nb_par(2)
                    for hh4 in range(4):
                        h = hg * 4 + hh4
                        j, hh = h // 2, h % 2
                        for e_ in range(2):
                            c = 2 * i + e_
                            for x, src in ((0, bT), (1, kT)):
                                bb, oo = b + hh, ((hh4 // 2) * 2 + x) * 128
                                A("pe", lambda e, src=src, j=j, hh=hh, c=c, e_=e_, bb=bb, oo=oo: e.matmul(
                                    ps[e_ * 64:(e_ + 1) * 64, bb, oo:oo + 128], lhsT=src[hh * 64:(hh + 1) * 64, j, c * 64:(c + 1) * 64],
                                    rhs=ar_t[hh * 64:(hh + 1) * 64, j, c, :, :], start=True, stop=True),
                                  ["bT", "kT", "ar"], PB(b, 2))
                    for hh in range(2):
                        nmv = NM_sb[q][:, hg * 4 + hh:(hg + 1) * 4:2, :, :]
                        A("dve", lambda e, nmv=nmv, b=b, hh=hh: e.tensor_tensor(out=nmv, in0=ps[:, b + hh, :].rearrange("p (jj x n) -> p jj x n", jj=2, x=2),
                                                                                 in1=maskNM_bf.unsqueeze(1).unsqueeze(1).broadcast_to([128, 2, 2, 128]), op=ALU.mult))
                if stop_after == "c2":
                    break
                b = nb_par(2)
                for h in range(8):
                    j, hh = h // 2, h % 2
                    for e_ in range(2):
                        c = 2 * i + e_
                        A("pe", lambda e, j=j, hh=hh, c=c, e_=e_, h=h, b=b: e.matmul(
                            ps[e_ * 64:(e_ + 1) * 64, b + hh, j * 64:(j + 1) * 64], lhsT=ar_t[hh * 64:(hh + 1) * 64, j, c, 0, :],
                            rhs=bT[hh * 64:(hh + 1) * 64, j, c * 64:(c + 1) * 64], start=True, stop=True))
                for hh in range(2):
                    pv = P_sb[0][:, hh:8:2, :]
                    A("dve", lambda e, pv=pv, b=b, hh=hh: e.tensor_tensor(out=pv, in0=ps[:, b + hh, 0:256].rearrange("p (j s) -> p j s", j=4),
                                                                           in1=maskL_bf.unsqueeze(1).broadcast_to([128, 4, 64]), op=ALU.mult))
                if stop_after == "c3":
                    break
                Q0 = NM_sb[q][:, :, 0, 0:64]
                A("dve", lambda e, q=q: e.tensor_tensor(out=QT_sb[1][:, :, 1, :], in0=NM_sb[q][:, :, 0, 0:64],
                                                          in1=maskI_bf.unsqueeze(1).broadcast_to([128, 8, 64]), op=ALU.add))
                ev_ctr = [0]
                import os as _os3
                EVK = int(_os3.environ.get("EVK", "3"))

                def evac_half(bank, e_, dst_ap, shape_pat, **kw):
                    sl = slice(e_ * 64, (e_ + 1) * 64)
                    src = ps[sl, bank, :].rearrange(shape_pat, **kw)
                    ev_ctr[0] += 1
                    if ev_ctr[0] % EVK != 0:
                        A("act", lambda e: e.copy(out=dst_ap, in_=src))
                    else:
                        A("dve", lambda e: e.tensor_copy(out=dst_ap, in_=src))

                def evac2(bk, dst):
                    for e_ in range(2):
                        sl = slice(e_ * 64, (e_ + 1) * 64)
                        evac_half(bk + e_, e_, dst[sl, :, :], "p (h s) -> p h s", h=8)

                bA = nb_par(2)
                bB = nb_par(2)
                for h in range(8):
                    for e_ in range(2):
                        sl = slice(e_ * 64, (e_ + 1) * 64)
                        A("pe", lambda e, h=h, sl=sl, e_=e_, bA=bA: e.matmul(ps[sl, bA + e_, h * 64:(h + 1) * 64], lhsT=Q0[sl, h, :], rhs=P_sb[0][sl, h, :],
                                                                             start=True, stop=True))
                        A("pe", lambda e, h=h, sl=sl, e_=e_, bB=bB: e.matmul(ps[sl, bB + e_, h * 64:(h + 1) * 64], lhsT=P_sb[0][sl, h, :], rhs=Q0[sl, h, :],
                                                                             start=True, stop=True))
                evac2(bA, P_sb[1])
                for e_ in range(2):
                    sl = slice(e_ * 64, (e_ + 1) * 64)
                    evac_half(bB + e_, e_, QT_sb[1][sl, :, 0, :], "p (h s) -> p h s", h=8)
                Tc = None
                for lev in range(1, 6):
                    pi = lev % 2
                    Pc = P_sb[pi]
                    QTc = QT_sb[pi]
                    QTn = QT_sb[1 - pi]
                    last = (lev == 5)
                    if not last:
                        bA = nb_par(2)
                        for h in range(8):
                            for e_ in range(2):
                                sl = slice(e_ * 64, (e_ + 1) * 64)
                                A("pe", lambda e, h=h, sl=sl, e_=e_, bA=bA, QTc=QTc, Pc=Pc: e.matmul(ps[sl, bA + e_, h * 64:(h + 1) * 64], lhsT=QTc[sl, h, 0, :],
                                                                                                      rhs=Pc[sl, h, :], start=True, stop=True))
                    if not last:
                        for hg in range(2):
                            bB = nb_par(2)
                            for h4 in range(4):
                                h = hg * 4 + h4
                                for e_ in range(2):
                                    sl = slice(e_ * 64, (e_ + 1) * 64)
                                    A("pe", lambda e, h=h, h4=h4, sl=sl, e_=e_, bB=bB, QTc=QTc, Pc=Pc: e.matmul(
                                        ps[sl, bB + e_, h4 * 128:(h4 + 1) * 128], lhsT=Pc[sl, h, :], rhs=QTc[sl, h, :, :], start=True, stop=True))
                            for e_ in range(2):
                                sl = slice(e_ * 64, (e_ + 1) * 64)
                                src4 = ps[sl, bB + e_, :].rearrange("p (h x s) -> p h x s", h=4, x=2)
                                hsl = slice(hg * 4, (hg + 1) * 4)
                                A("act", lambda e, sl=sl, src4=src4, hsl=hsl, QTn=QTn: e.copy(out=QTn[sl, hsl, 0, :], in_=src4[:, :, 0, :]))
                                A("dve", lambda e, sl=sl, src4=src4, hsl=hsl, QTn=QTn, QTc=QTc: e.tensor_tensor(out=QTn[sl, hsl, 1, :], in0=src4[:, :, 1, :],
                                                                                                                 in1=QTc[sl, hsl, 1, :], op=ALU.add))
                        evac2(bA, P_sb[1 - pi])
                    else:
                        bB = nb_par(2)
                        for h in range(8):
                            for e_ in range(2):
                                sl = slice(e_ * 64, (e_ + 1) * 64)
                                A("pe", lambda e, h=h, sl=sl, e_=e_, bB=bB, QTc=QTc, Pc=Pc: e.matmul(
                                    ps[sl, bB + e_, h * 64:(h + 1) * 64], lhsT=Pc[sl, h, :], rhs=QTc[sl, h, 1, :], start=True, stop=True))
                        Tc = Tt_sb[0]
                        for e_ in range(2):
                            sl = slice(e_ * 64, (e_ + 1) * 64)
                            A("dve", lambda e, sl=sl, e_=e_, bB=bB, QTc=QTc, Tc=Tc: e.tensor_tensor(out=Tc[sl, :, :], in0=ps[sl, bB + e_, :].rearrange("p (h s) -> p h s", h=8),
                                                                                                     in1=QTc[sl, :, 1, :], op=ALU.add))
                bXV = nb_par(2)
                for h in range(8):
                    for e_ in range(2):
                        sl = slice(e_ * 64, (e_ + 1) * 64)
                        A("pe", lambda e, h=h, sl=sl, e_=e_, q=q, bXV=bXV: e.matmul(ps[sl, bXV + e_, h * 64:(h + 1) * 64], lhsT=NM_sb[q][sl, h, 1, 0:64],
                                                                                    rhs=v_tok[q][sl, h * 64:(h + 1) * 64], start=True, stop=True))
                evac2(bXV, XV_sb[q])
                if stop_after == "c4":
                    break
                S.tag = "p%d Rseq%d" % (pas, i)
                for e_ in range(2):
                    c = 2 * i + e_
                    sl = slice(e_ * 64, (e_ + 1) * 64)
                    xq = c % 2
                    bX = nb_seq(e_)
                    for j in range(4):
                        A("pe", lambda e, j=j, sl=sl, c=c, bX=bX: e.matmul(ps[sl, bX, j * 128:(j + 1) * 128], lhsT=ar_t[:, j, c, 0, :],
                                                                           rhs=H0bd[:, j, :], start=True, stop=True))
                    A("dve", lambda e, sl=sl, xq=xq, bX=bX, q=q: e.tensor_tensor(out=X_sb[xq][sl, :, :], in0=ps[sl, bX, :].rearrange("p (h s) -> p h s", h=8),
                                                                                  in1=XV_sb[q][sl, :, :], op=ALU.add))
                    bU = nb_seq(e_)
                    for h in range(8):
                        A("pe", lambda e, h=h, sl=sl, xq=xq, bU=bU, Tc=Tc: e.matmul(ps[sl, bU, h * 64:(h + 1) * 64], lhsT=Tc[sl, h, :],
                                                                                    rhs=X_sb[xq][sl, h, :], start=True, stop=True),
                          [], PB(bU))
                    A("dve", lambda e, sl=sl, xq=xq, bU=bU: e.tensor_copy(out=U_sb[xq][sl, :, :], in_=ps[sl, bU, :].rearrange("p (h s) -> p h s", h=8)),
                      PB(bU), ["U%d" % xq])
                    bY1 = nb_seq(1 - e_)
                    for j in range(4):
                        A("pe", lambda e, j=j, c=c, bY1=bY1: e.matmul(ps[:, bY1, j * 64:(j + 1) * 64], lhsT=H0bd[:, j, :],
                                                                      rhs=ar_t[:, j, c, 1, :], start=True, stop=True))
                    A("act", lambda e, c=c, bY1=bY1: e.copy(out=yT[:, :, c * 64:(c + 1) * 64], in_=ps[:, bY1, 0:256].rearrange("p (j t) -> p j t", j=4)))
                    bY = nb_seq(e_)
                    for h in range(8):
                        j, hh = h // 2, h % 2
                        hs = slice(hh * 64, (hh + 1) * 64)
                        A("pe", lambda e, h=h, j=j, hs=hs, sl=sl, xq=xq, q=q, bY=bY: e.matmul(ps[hs, bY, j * 64:(j + 1) * 64], lhsT=U_sb[xq][sl, h, :],
                                                                                              rhs=NM_sb[q][sl, h, 0, 64:128], start=True, stop=False))
                        A("pe", lambda e, h=h, j=j, hs=hs, sl=sl, q=q, bY=bY: e.matmul(ps[hs, bY, j * 64:(j + 1) * 64], lhsT=v_tok[q][sl, h * 64:(h + 1) * 64],
                                                                                       rhs=NM_sb[q][sl, h, 1, 64:128], start=False, stop=True))
                    A("dve", lambda e, c=c, bY=bY: e.tensor_tensor(out=yT[:, :, c * 64:(c + 1) * 64], in0=ps[:, bY, 0:256].rearrange("p (j t) -> p j t", j=4),
                                                                    in1=yT[:, :, c * 64:(c + 1) * 64], op=ALU.add))
                    bG = nb_seq(e_)
                    for h in range(8):
                        j, hh = h // 2, h % 2
                        hs = slice(hh * 64, (hh + 1) * 64)
                        A("pe", lambda e, h=h, j=j, hs=hs, sl=sl, xq=xq, q=q, bG=bG: e.matmul(ps[hs, bG, j * 64:(j + 1) * 64], lhsT=bk_tok[q][sl, 0, h * 64:(h + 1) * 64],
                                                                                              rhs=U_sb[xq][sl, h, :], start=True, stop=False),
                          ["bk_tok%d" % q, "U%d" % xq], PB(bG))
                        A("pe", lambda e, h=h, j=j, hs=hs, sl=sl, q=q, bG=bG: e.matmul(ps[hs, bG, j * 64:(j + 1) * 64], lhsT=bk_tok[q][sl, 1, h * 64:(h + 1) * 64],
                                                                                       rhs=v_tok[q][sl, h * 64:(h + 1) * 64], start=False, stop=True),
                          ["bk_tok%d" % q, "v_tok%d" % q], PB(bG))
                    A("dve", lambda e, bG=bG: e.tensor_tensor(out=Hs[:], in0=ps[:, bG, 0:256].rearrange("p (j v) -> p j v", j=4), in1=H0f[:], op=ALU.add),
                      PB(bG) + ["H0f"], ["Hs"])
                    A("dve", lambda e, c=c: e.tensor_tensor(out=H0f[:], in0=Hs[:], in1=WC[:, :, c:c + 1].broadcast_to([128, 4, 64]), op=ALU.mult),
                      ["Hs", "WC"], ["H0f"])
                    A("act", lambda e: e.copy(out=H0bd[0:64, :, 0:64], in_=H0f[0:64, :, :]), ["H0f"], ["H0b"])
                    A("act", lambda e: e.copy(out=H0bd[64:128, :, 64:128], in_=H0f[64:128, :, :]), ["H0f"], ["H0b"])
            if stop_after in ("c1", "c2", "c3", "c4"):
                continue
            if dbg and pas == 0:
                dump("yT", yT, ["yT"])
            if stop_after == "Rchunk":
                continue
            if nsamp:
                S.tag = "p%d Rsample" % pas
                emit_rsample()
            S.tag = "p%d Rpost" % pas
            for j in range(4):
                yj = yT[:, j, :]
                yc, sq_, rs_ = gtmp[0], gtmp[1], gtmp[2]

                def ev_mean(b, c0, c1, j=j):
                    A("dve", lambda e: e.scalar_tensor_tensor(out=yc[:, c0:c1], in0=ps[:, b, 0:c1 - c0], scalar=-1.0 / 64, in1=yT[:, j, c0:c1],
                                                              op0=ALU.mult, op1=ALU.add), PB(b) + ["yT"], ["gtmp0"])
                fp32_blocksum(yj, "yT", blk64, ev_mean)
                A("act", lambda e: e.activation(out=sq_[:, 0:W], in_=yc[:, 0:W], func=AF.Square), ["gtmp0"], ["gtmp1"])

                def ev_var(b, c0, c1):
                    A("act", lambda e: e.activation(out=rs_[:, c0:c1], in_=ps[:, b, 0:c1 - c0], func=AF.Ln, scale=1.0 / 64, bias=epsgn[:, 0:1]))
                fp32_blocksum(sq_, "gtmp1", blk64, ev_var)
                A("act", lambda e: e.activation(out=rs_[:, 0:W], in_=rs_[:, 0:W], func=AF.Exp, scale=-0.5), ["gtmp2"], ["gtmp2"])
                A("dve", lambda e: e.tensor_tensor(out=yc[:, 0:W], in0=yc[:, 0:W], in1=rs_[:, 0:W], op=ALU.mult), ["gtmp0", "gtmp2"], ["gtmp0"])
                A("act", lambda e, j=j: e.activation(out=yc[:, 0:W], in_=yc[:, 0:W], func=AF.Identity, scale=pf("ln_g", j), bias=pf("ln_b", j)))
                A("dve", lambda e, j=j: e.tensor_tensor(out=yc[:, 0:W], in0=yc[:, 0:W], in1=bonus[:, j, 0:W], op=ALU.add), ["gtmp0", "bonus"], ["gtmp0"])
                A("dve", lambda e, j=j: e.tensor_tensor(out=oa[:, j, 0:W], in0=yc[:, 0:W], in1=g_bf[:, j, 0:W], op=ALU.mult), ["gtmp0", "g_bf"], ["oa"])
            if dbg and pas == 0:
                dump("oa", oa[:], ["oa"])
            if pas == passes[-1]:
                wst = gtmp[0][0:64, 0:512].rearrange("p (j n) -> p j n", j=4)
                b = nb()
                for j in range(4):
                    A("pe", lambda e, j=j, b=b: e.transpose(out=ps[0:64, b, j * 128:(j + 1) * 128], in_=H0f[:, j, :], identity=ident),
                      ["H0f", "consts"], PB(b))
                A("act", lambda e, b=b: e.copy(out=wst, in_=ps[0:64, b, :].rearrange("p (j n) -> p j n", j=4)), PB(b), ["gtmp0"])
                S.dma("sp", lambda e: e.dma_start(out=o_wkv_p.rearrange("(j hh) v k -> v j hh k", hh=2),
                                                   in_=wst.rearrange("p j (hh k) -> p j hh k", hh=2)), reads=["gtmp0"])
                b = nb()
                A("pe", lambda e, b=b: e.transpose(out=ps[0:14, b, 0:128], in_=carry[:, :], identity=ident), ["carry", "consts"], PB(b))
                A("act", lambda e, b=b: e.copy(out=gtmp[1][0:14, 0:128], in_=ps[0:14, b, 0:128]), PB(b), ["gtmp1"])
                S.dma("sp", lambda e: e.dma_start(out=o_shift_p.rearrange("(c p) -> c p", p=128), in_=gtmp[1][0:14, 0:128]), reads=["gtmp1"])
            if stop_after == "Rpost":
                continue

            S.tag = "p%d H" % pas
            S.barrier()
            ar_ = Arena(arena_t, ARENA_WORDS)
            Eb = ar_.alloc(F32, [128, 4, HALF])
            qT = ar_.alloc(BF16, [128, 4, HALF])
            hkT = ar_.alloc(BF16, [128, 4, HALF])
            hvT = ar_.alloc(BF16, [128, 4, TW])
            sgo = ar_.alloc(BF16, [128, 4, TW])
            oT = ar_.alloc(F32, [128, 4, TW])
            smpH = ar_.alloc(F32, [128, 4, 4, NS])
            hmark = ar_.top
            htmp = [ar_.alloc(F32, [128, TW]) for _ in range(8)]
            hset = [htmp[0:4], htmp[4:8]]
            cmaskH = ar_.alloc(BF16, [128, HALF])
            S.dma("pool", lambda e: e.dma_start(out=cmaskH, in_=consts_d[:, 640:1664]), writes=["cmaskH"])
            HB = RPROJ
            wv, wr = load_w(w_in_v[:, :, HB + 512:HB + 1024], (8, 512))
            hvh = [(0, 512), (512, W)]
            for h in range(4):
                T0, T1_, T2, T3 = hset[h % 2]

                def ev_f(b, c0, c1, T0=T0):
                    A("act", lambda e: e.activation(out=T0[:, c0:c1], in_=ps[:, b, 0:c1 - c0], func=AF.Sigmoid))
                proj_fm(wv, wr, h * 128, 8, hT, "hT", ev_f)
                for (a_, b_) in hvh:
                    b2 = min(b_, HALF)
                    A("dve", lambda e, h=h, a_=a_, b_=b_: e.tensor_scalar(out=T0[:, a_:b_], in0=T0[:, a_:b_], scalar1=dvc("omlb", h), scalar2=dvc("lb", h),
                                                                          op0=ALU.mult, op1=ALU.add))
                    A("dve", lambda e, a_=a_, b_=b_: e.tensor_scalar(out=T1_[:, a_:b_], in0=T0[:, a_:b_], scalar1=-1.0, scalar2=1.0, op0=ALU.mult, op1=ALU.add))
                    A("act", lambda e, a_=a_, b2=b2: e.activation(out=T2[:, a_:b2], in_=T0[:, a_:b2], func=AF.Ln))
                    A("dve", lambda e, a_=a_, b2=b2: e.tensor_tensor_scan(out=T3[:, a_:b2], data0=cmaskH[:, a_:b2], data1=T2[:, a_:b2], initial=0.0,
                                                                          op0=ALU.mult, op1=ALU.add))
                    A("act", lambda e, h=h, a_=a_, b2=b2: e.activation(out=Eb[:, h, a_:b2], in_=T3[:, a_:b2], func=AF.Exp))
                    A("act", lambda e, h=h, a_=a_, b2=b2: e.activation(out=DC[:, h, a_ // 64:b2 // 64], in_=T3[:, a_:b2].rearrange("p (c t) -> p c t", t=64)[:, :, 63],
                                                                       func=AF.Exp))
                    A("act", lambda e, a_=a_, b2=b2: e.activation(out=T2[:, a_:b2], in_=T3[:, a_:b2], func=AF.Exp, scale=-1.0))
                    A("dve", lambda e, h=h, a_=a_, b2=b2: e.tensor_tensor(out=hkT[:, h, a_:b2], in0=T1_[:, a_:b2], in1=T2[:, a_:b2], op=ALU.mult))
                if nsamp:
                    A("act", lambda e, h=h: e.copy(out=smpH[:, 1, h, :], in_=T0[:, HALF:HALF + NS]))
                    A("act", lambda e, h=h: e.copy(out=smpH[:, 2, h, :], in_=T1_[:, HALF:HALF + NS]))
            wv, wr = load_w(w_in_v[:, :, HB:HB + 512], (8, 512))
            for h in range(4):
                T0 = hset[h % 2][0]

                def ev_q(b, c0, c1, T0=T0, h=h):
                    A("act", lambda e: e.activation(out=T0[:, c0:c1], in_=ps[:, b, 0:c1 - c0], func=AF.Silu))
                    if c0 < HALF:
                        A("dve", lambda e: e.tensor_tensor(out=qT[:, h, c0:c1], in0=T0[:, c0:c1], in1=Eb[:, h, c0:c1], op=ALU.mult))
                proj_fm(wv, wr, h * 128, 8, hT, "hT", ev_q)
                if nsamp:
                    A("act", lambda e, h=h: e.copy(out=smpH[:, 0, h, :], in_=T0[:, HALF:HALF + NS]))
            wv, wr = load_w(w_in_v[:, :, HB + 1024:HB + 1536], (8, 512))
            for h in range(4):
                def ev_i(b, c0, c1, h=h):
                    A("dve", lambda e: e.tensor_copy(out=hvT[:, h, c0:c1], in_=ps[:, b, 0:c1 - c0]), PB(b), ["hvT"])
                    if c0 >= HALF:
                        A("dve", lambda e: e.tensor_copy(out=smpH[:, 3, h, :], in_=ps[:, b, 0:NS]), PB(b), ["smpH"])
                proj_fm(wv, wr, h * 128, 8, hT, "hT", ev_i)
            wv, wr = load_w(w_in_v[:, :, HB + 1536:HB + 2048], (8, 512))
            for h in range(4):
                def ev_og(b, c0, c1, h=h):
                    A("act", lambda e: e.activation(out=sgo[:, h, c0:c1], in_=ps[:, b, 0:c1 - c0], func=AF.Sigmoid), PB(b), ["sgo"])
                proj_fm(wv, wr, h * 128, 8, hT, "hT", ev_og)

            if nsamp:
                S.barrier()
                ar_.top = hmark
                S_s = ar_.alloc(F32, [128, NS, 4, 128])
                ktok_s = ar_.alloc(F32, [NS, 512])
                vtok_s = ar_.alloc(F32, [NS, 512])
                vm = [ar_.alloc(F32, [NS, 512]) for _ in range(2)]
                tS_l = [ar_.alloc(F32, [128, 4, 128]) for _ in range(3)]
                q_bf = ar_.alloc(BF16, [128, 4, NS])
                Sb_l = [ar_.alloc(BF16, [128, 4, 128]) for _ in range(3)]
                S.dma("sp", lambda e: e.dma_start(out=S_s, in_=hgrn_s.rearrange("b h k v -> k b h v")), writes=["S_s"])
                for vec, dst, dn in ((2, ktok_s, "ktok_s"), (3, vtok_s, "vtok_s")):
                    b = nb_smp()
                    for h in range(4):
                        A("pe", lambda e, vec=vec, h=h, b=b: e.transpose(out=ps[0:NS, b, h * 128:(h + 1) * 128], in_=smpH[:, vec, h, :], identity=ident),
                          ["smpH", "consts"], PB(b))
                    A("act", lambda e, dst=dst, b=b: e.copy(out=dst, in_=ps[0:NS, b, :]), PB(b), [dn])
                for bi in range(NS):
                    vq = bi % 2
                    tS = tS_l[bi % 3]
                    A("dve", lambda e, bi=bi, vq=vq: e.tensor_scalar(out=vm[vq], in0=vtok_s, scalar1=ident[0:NS, bi:bi + 1], scalar2=None, op0=ALU.mult),
                      ["vtok_s", "consts"], ["vm%d" % vq])
                    b = nb_smp()
                    for h in range(4):
                        A("pe", lambda e, h=h, vq=vq, b=b: e.matmul(ps[:, b, h * 128:(h + 1) * 128], lhsT=ktok_s[:, h * 128:(h + 1) * 128],
                                                                    rhs=vm[vq][:, h * 128:(h + 1) * 128], start=True, stop=True),
                          ["ktok_s", "vm%d" % vq], PB(b))
                    A("dve", lambda e, bi=bi: e.tensor_tensor(out=tS, in0=S_s[:, bi, :, :],
                                                               in1=smpH[:, 1, :, bi:bi + 1].broadcast_to([128, 4, 128]), op=ALU.mult),
                      ["S_s", "smpH"], ["tS"])
                    A("dve", lambda e, bi=bi, b=b: e.tensor_tensor(out=S_s[:, bi, :, :], in0=ps[:, b, :].rearrange("p (h v) -> p h v", h=4), in1=tS,
                                                                    op=ALU.add), PB(b) + ["tS"], ["S_s"])
                S.dma("sp", lambda e: e.dma_start(out=o_hgrn_s.rearrange("b h k v -> k b h v"), in_=S_s), reads=["S_s"])
                A("act", lambda e: e.copy(out=q_bf, in_=smpH[:, 0, :, :]))
                bO_ = nb_smp()
                for bi in range(NS):
                    Sb = Sb_l[bi % 3]
                    A("act", lambda e, bi=bi, Sb=Sb: e.copy(out=Sb, in_=S_s[:, bi, :, :]))
                    for h in range(4):
                        A("pe", lambda e, h=h, bi=bi, Sb=Sb, bO_=bO_: e.matmul(ps[:, bO_, h * NS + bi:h * NS + bi + 1], lhsT=Sb[:, h, :],
                                                                                rhs=q_bf[:, h, bi:bi + 1], start=True, stop=True))
                A("act", lambda e, bO_=bO_: e.copy(out=oT[:, :, HALF:HALF + NS], in_=ps[:, bO_, 0:4 * NS].rearrange("p (h n) -> p h n", h=4)))

            if not nsamp:
                ar_.top = hmark
            par_nb[0] = 4 if nsamp else 6
            par_ctr[0] = 0
            hk_tok = [ar_.alloc(BF16, [128, 512]) for _ in range(2)]
            hv_tok = [ar_.alloc(BF16, [128, 512]) for _ in range(2)]
            PT_sb = [ar_.alloc(BF16, [128, 4, 64]) for _ in range(2)]
            Ss = ar_.alloc(F32, [128, 4, 128])
            ar_.top = hmark
            htmp = [ar_.alloc(F32, [128, TW]) for _ in range(2)]
            for i in range(8):
                q = i % 2
                b1 = nb_par()
                pTk = ps[:, b1, :].bitcast(BF16).rearrange("p (v n) -> p v n", v=2)
                for h in range(4):
                    A("pe", lambda e, i=i, h=h, pTk=pTk: e.transpose(out=pTk[:, 0, h * 128:(h + 1) * 128], in_=hkT[:, h, i * 128:(i + 1) * 128],
                                                                     identity=ident_bf[:]), ["hkT", "ident_bf"], PB(b1))
                    A("pe", lambda e, i=i, h=h, pTk=pTk: e.transpose(out=pTk[:, 1, h * 128:(h + 1) * 128], in_=hvT[:, h, i * 128:(i + 1) * 128],
                                                                     identity=ident_bf[:]), ["hvT", "ident_bf"], PB(b1))
                A("act", lambda e, q=q, pTk=pTk: e.copy(out=hk_tok[q], in_=pTk[:, 0, :]), PB(b1), ["hk_tok%d" % q])
                A("dve", lambda e, q=q, pTk=pTk: e.tensor_copy(out=hv_tok[q], in_=pTk[:, 1, :]), PB(b1), ["hv_tok%d" % q])
                bS = nb_par()
                for h in range(4):
                    for e_ in range(2):
                        c = 2 * i + e_
                        A("pe", lambda e, h=h, e_=e_, c=c, bS=bS: e.matmul(ps[e_ * 64:(e_ + 1) * 64, bS, h * 64:(h + 1) * 64], lhsT=hkT[:, h, c * 64:(c + 1) * 64],
                                                                           rhs=qT[:, h, c * 64:(c + 1) * 64], start=True, stop=True), ["hkT", "qT"], PB(bS))
                A("dve", lambda e, q=q, bS=bS: e.tensor_tensor(out=PT_sb[q], in0=ps[:, bS, 0:256].rearrange("p (h t) -> p h t", h=4),
                                                                in1=masks_bf[:, 64:128].unsqueeze(1).broadcast_to([128, 4, 64]), op=ALU.mult),
                  PB(bS) + ["consts"], ["PT%d" % q])
                bO = nb_par()
                for e_ in range(2):
                    c = 2 * i + e_
                    sl = slice(e_ * 64, (e_ + 1) * 64)
                    for h in range(4):
                        oo = h * 128 + e_ * 64
                        A("pe", lambda e, h=h, c=c, oo=oo, bO=bO: e.matmul(ps[:, bO, oo:oo + 64], lhsT=S0b[:, h, :], rhs=qT[:, h, c * 64:(c + 1) * 64],
                                                                           start=True, stop=False), ["S0b", "qT"], PB(bO))
                        A("pe", lambda e, h=h, sl=sl, q=q, oo=oo, bO=bO: e.matmul(ps[:, bO, oo:oo + 64], lhsT=hv_tok[q][sl, h * 128:(h + 1) * 128],
                                                                                  rhs=PT_sb[q][sl, h, :], start=False, stop=True),
                          ["hv_tok%d" % q, "PT%d" % q], PB(bO))
                    bG = nb_seq(e_)
                    for h in range(4):
                        A("pe", lambda e, h=h, sl=sl, q=q, bG=bG: e.matmul(ps[:, bG, h * 128:(h + 1) * 128], lhsT=hk_tok[q][sl, h * 128:(h + 1) * 128],
                                                                           rhs=hv_tok[q][sl, h * 128:(h + 1) * 128], start=True, stop=True),
                          ["hk_tok%d" % q, "hv_tok%d" % q], PB(bG))
                    A("dve", lambda e, bG=bG: e.tensor_tensor(out=Ss, in0=ps[:, bG, :].rearrange("p (h v) -> p h v", h=4), in1=S0f[:], op=ALU.add),
                      PB(bG) + ["S0f"], ["Ss"])
                    A("dve", lambda e, c=c: e.tensor_tensor(out=S0f[:], in0=Ss, in1=DC[:, :, c:c + 1].broadcast_to([128, 4, 128]), op=ALU.mult),
                      ["Ss", "DC"], ["S0f"])
                    A("act", lambda e: e.copy(out=S0b[:], in_=S0f[:]), ["S0f"], ["S0b"])
                A("act", lambda e, i=i, bO=bO: e.copy(out=oT[:, :, i * 128:(i + 1) * 128], in_=ps[:, bO, :].rearrange("p (h t) -> p h t", h=4)),
                  PB(bO), ["oT"])
            par_nb[0] = 6
            if dbg and pas == 0:
                dump("oT", oT, ["oT"])
            for h in range(4):
                A("act", lambda e, h=h: e.activation(out=htmp[0][:, 0:W], in_=oT[:, h, 0:W], func=AF.Square), ["oT"], ["htmp0"])

                def ev_ms(b, c0, c1):
                    A("act", lambda e: e.activation(out=htmp[1][:, c0:c1], in_=ps[:, b, 0:c1 - c0], func=AF.Ln, scale=1.0 / 128, bias=epsrms[:, 0:1]))
                fp32_blocksum(htmp[0], "htmp0", ones, ev_ms)
                A("act", lambda e: e.activation(out=htmp[1][:, 0:W], in_=htmp[1][:, 0:W], func=AF.Exp, scale=-0.5), ["htmp1"], ["htmp1"])
                A("dve", lambda e, h=h: e.tensor_tensor(out=htmp[0][:, 0:W], in0=oT[:, h, 0:W], in1=htmp[1][:, 0:W], op=ALU.mult), ["oT", "htmp1"], ["htmp0"])
                A("dve", lambda e, h=h: e.scalar_tensor_tensor(out=ob[:, h, 0:W], in0=htmp[0][:, 0:W], scalar=pf("hng", h), in1=sgo[:, h, 0:W],
                                                               op0=ALU.mult, op1=ALU.mult), ["htmp0", "pfm", "sgo"], ["ob"])
            if dbg and pas == 0:
                dump("ob", ob[:], ["ob"])
            if pas == passes[-1]:
                S.dma("sp", lambda e: e.dma_start(out=o_hgrn_p.rearrange("h k v -> k h v"), in_=S0f[:]), reads=["S0f"])
            if stop_after == "H":
                continue

            S.barrier()
            ar_ = Arena(arena_t, ARENA_WORDS)
            x_tok = ar_.alloc(F32, [128, 8, 1024])
            xs_tok = ar_.alloc(F32, [NS, 1024])
            mergedT = ar_.alloc(BF16, [128, 8, TW])
            gm = [ar_.alloc(F32, [128, 512]) for _ in range(4)]
            GB = RPROJ + 2048
            w_upa_v = w_up_a.rearrange("(kc p) n -> p kc n", p=128)
            w_upb_v = w_up_b.rearrange("(kc p) n -> p kc n", p=128)
            for dcg in range(2):
                wga, wgar = load_w(w_in_v[:, :, GB + dcg * 512:GB + (dcg + 1) * 512], (8, 512))
                wgb, wgbr = load_w(w_in_v[:, :, GB + 1024 + dcg * 512:GB + 1024 + (dcg + 1) * 512], (8, 512))
                wu, wur = load_w(w_upa_v[:, :, dcg * 512:(dcg + 1) * 512], (4, 512))
                iu_ = (wctr[0] - 1) % NWB
                wub_view = wbuf[iu_][:, 4:8, :]
                S.dma("pool", lambda e, wub_view=wub_view, dcg=dcg: e.dma_start(out=wub_view, in_=w_upb_v[:, :, dcg * 512:(dcg + 1) * 512]), writes=[wur])
                for dc in range(4):
                    n0 = dc * 128
                    for (c0, c1) in blocks:
                        w_ = c1 - c0
                        b1, b2, b3, b4 = nb(), nb(), nb(), nb()
                        for kc in range(8):
                            A("pe", lambda e, kc=kc, b1=b1, c0=c0, c1=c1, n0=n0, wga=wga: e.matmul(ps[:, b1, 0:c1 - c0], lhsT=wga[:, kc, n0:n0 + 128],
                                                                                                   rhs=hT[:, kc, c0:c1], start=(kc == 0), stop=(kc == 7)),
                              [wgar, "hT"], PB(b1))
                        for kc in range(8):
                            A("pe", lambda e, kc=kc, b2=b2, c0=c0, c1=c1, n0=n0, wgb=wgb: e.matmul(ps[:, b2, 0:c1 - c0], lhsT=wgb[:, kc, n0:n0 + 128],
                                                                                                   rhs=hT[:, kc, c0:c1], start=(kc == 0), stop=(kc == 7)),
                              [wgbr, "hT"], PB(b2))
                        for kc in range(4):
                            A("pe", lambda e, kc=kc, b3=b3, c0=c0, c1=c1, n0=n0, wu=wu: e.matmul(ps[:, b3, 0:c1 - c0], lhsT=wu[:, kc, n0:n0 + 128],
                                                                                                 rhs=oa[:, kc, c0:c1], start=(kc == 0), stop=(kc == 3)),
                              [wur, "oa"], PB(b3))
                        for kc in range(4):
                            A("pe", lambda e, kc=kc, b4=b4, c0=c0, c1=c1, n0=n0, wub_view=wub_view: e.matmul(ps[:, b4, 0:c1 - c0], lhsT=wub_view[:, kc, n0:n0 + 128],
                                                                                                             rhs=ob[:, kc, c0:c1], start=(kc == 0), stop=(kc == 3)),
                              [wur, "ob"], PB(b4))
                        A("act", lambda e, b1=b1, w_=w_: e.activation(out=gm[0][:, 0:w_], in_=ps[:, b1, 0:w_], func=AF.Sigmoid), PB(b1), ["gm0"])
                        A("act", lambda e, b2=b2, w_=w_: e.activation(out=gm[1][:, 0:w_], in_=ps[:, b2, 0:w_], func=AF.Sigmoid), PB(b2), ["gm1"])
                        A("dve", lambda e, b3=b3, w_=w_: e.tensor_tensor(out=gm[2][:, 0:w_], in0=ps[:, b3, 0:w_], in1=gm[0][:, 0:w_], op=ALU.mult),
                          PB(b3) + ["gm0"], ["gm2"])
                        A("dve", lambda e, b4=b4, w_=w_: e.tensor_tensor(out=gm[3][:, 0:w_], in0=ps[:, b4, 0:w_], in1=gm[1][:, 0:w_], op=ALU.mult),
                          PB(b4) + ["gm1"], ["gm3"])
                        A("dve", lambda e, dcg=dcg, dc=dc, c0=c0, c1=c1, w_=w_: e.tensor_tensor(out=mergedT[:, dcg * 4 + dc, c0:c1], in0=gm[2][:, 0:w_],
                                                                                                 in1=gm[3][:, 0:w_], op=ALU.add),
                          ["gm2", "gm3"], ["mergedT"])
            if dbg and pas == 0:
                dump("mergedT", mergedT, ["mergedT"])
            if stop_after == "G":
                continue

            S.barrier()
            ar_.top = 0
            x_tok = ar_.alloc(F32, [128, 8, 1024])
            xs_tok = ar_.alloc(F32, [NS, 1024])
            mergedT = ar_.alloc(BF16, [128, 8, TW])
            for tt in range(8):
                S.dma("sp", lambda e, tt=tt: e.dma_start(out=x_tok[:, tt, :], in_=x_p[t0 + tt * 128:t0 + (tt + 1) * 128, :]),
                      writes=["x%d" % tt])
            if nsamp:
                S.dma("sp", lambda e: e.dma_start(out=xs_tok, in_=x_s), writes=["xs"])
            w_out_v = w_out.rearrange("(kc p) n -> p kc n", p=128)
            wos = [load_w(w_out_v[:, :, half * 512:(half + 1) * 512], (8, 512))[0] for half in range(2)]
            for tt in range(8):
                for half in range(2):
                    wo = wos[half]
                    b = nb()
                    for kc in range(8):
                        A("pe", lambda e, kc=kc, tt=tt, b=b, wo=wo: e.matmul(ps[:, b, :], lhsT=mergedT[:, kc, tt * 128:(tt + 1) * 128], rhs=wo[:, kc, :],
                                                                             start=(kc == 0), stop=(kc == 7)))
                    A("dve", lambda e, tt=tt, half=half, b=b: e.tensor_tensor(out=x_tok[:, tt, half * 512:(half + 1) * 512], in0=ps[:, b, :],
                                                                               in1=x_tok[:, tt, half * 512:(half + 1) * 512], op=ALU.add))
            if nsamp:
                for half in range(2):
                    wo = wos[half]
                    b = nb()
                    for kc in range(8):
                        A("pe", lambda e, kc=kc, b=b, wo=wo: e.matmul(ps[0:NS, b, :], lhsT=mergedT[:, kc, HALF:HALF + NS], rhs=wo[:, kc, :],
                                                                      start=(kc == 0), stop=(kc == 7)))
                    A("dve", lambda e, half=half, b=b: e.tensor_tensor(out=xs_tok[:, half * 512:(half + 1) * 512], in0=ps[0:NS, b, :],
                                                                        in1=xs_tok[:, half * 512:(half + 1) * 512], op=ALU.add))
            norm_phase("O", x_tok, xs_tok, g_ffn, ar_,
                       lambda gB, h_tok, hs_tok, R: to_hT(gB, h_tok, hs_tok, R, x_tok, xs_tok, lambda tt: "x%d" % tt, "xs"),
                       lambda tt: "x%d" % tt, "xs")
            if dbg and pas == 0:
                dump("hT2", hT[:], ["hT"])
            if stop_after == "O":
                continue

            S.barrier()
            ar_.top = 0
            x_tok = ar_.alloc(F32, [128, 8, 1024])
            xs_tok = ar_.alloc(F32, [NS, 1024])
            actT = ar_.alloc(BF16, [128, 22, TW])
            wdn = ar_.alloc(BF16, [128, 22, 1024])
            gBf = ftmp[:, :, :].rearrange("p a b -> p (a b)")
            junkD = ar_.alloc(BF16, [128, 1024])
            w_fd_v = w_fd.rearrange("(fc p) n -> p fc n", p=128)
            w_fg_v = w_fg.rearrange("(kc p) n -> p kc n", p=128)
            w_fu_v = w_fu.rearrange("(kc p) n -> p kc n", p=128)
            for fg in range(11):
                wg, wgr = load_w(w_fg_v[:, :, fg * 256:(fg + 1) * 256], (8, 256))
                wu, wur = load_w(w_fu_v[:, :, fg * 256:(fg + 1) * 256], (8, 256))
                if fg in (2, 4, 6, 8):
                    k_ = (fg - 2) // 2
                    lo, hi = (0, 6, 12, 18)[k_], (6, 12, 18, 22)[k_]
                    S.dma("pool", lambda e, lo=lo, hi=hi: e.dma_start(out=wdn[:, lo:hi, :], in_=w_fd_v[:, lo:hi, :]), writes=["wdn%d" % k_])
                for f2 in range(2):
                    fc = fg * 2 + f2
                    n0 = f2 * 128
                    for (c0, c1) in blocks:
                        w_ = c1 - c0
                        b1, b2 = nb(), nb()
                        for kc in range(8):
                            A("pe", lambda e, kc=kc, b1=b1, c0=c0, c1=c1, n0=n0, wg=wg: e.matmul(ps[:, b1, 0:c1 - c0], lhsT=wg[:, kc, n0:n0 + 128],
                                                                                                 rhs=hT[:, kc, c0:c1], start=(kc == 0), stop=(kc == 7)),
                              [wgr, "hT"], PB(b1))
                        for kc in range(8):
                            A("pe", lambda e, kc=kc, b2=b2, c0=c0, c1=c1, n0=n0, wu=wu: e.matmul(ps[:, b2, 0:c1 - c0], lhsT=wu[:, kc, n0:n0 + 128],
                                                                                                 rhs=hT[:, kc, c0:c1], start=(kc == 0), stop=(kc == 7)),
                              [wur, "hT"], PB(b2))
                        fq = (fc + (c0 // 512)) % 2
                        A("act", lambda e, b1=b1, w_=w_, fq=fq: e.activation(out=ftmp[:, fq, 0:w_], in_=ps[:, b1, 0:w_], func=AF.Silu), PB(b1), ["ftmp%d" % fq])
                        A("dve", lambda e, b2=b2, w_=w_, fq=fq, fc=fc, c0=c0, c1=c1: e.tensor_tensor(out=actT[:, fc, c0:c1], in0=ps[:, b2, 0:w_],
                                                                                                      in1=ftmp[:, fq, 0:w_], op=ALU.mult),
                          PB(b2) + ["ftmp%d" % fq], ["actT"])
            if stop_after == "F":
                continue

            S.barrier()
            S.dma("sp", lambda e: e.dma_start(out=gBf, in_=g_fin.partition_broadcast(128)), writes=["gBf"])
            A("dve", lambda e: e.memset(stat[:, 0:32], 0.0), [], ["stat"])
            wres_all = ["wdn0", "wdn1", "wdn2", "wdn3"]
            for tt in range(8):
                for half in range(2):
                    b = nb()
                    for fc in range(22):
                        A("pe", lambda e, fc=fc, tt=tt, half=half, b=b: e.matmul(ps[:, b, :], lhsT=actT[:, fc, tt * 128:(tt + 1) * 128],
                                                                                 rhs=wdn[:, fc, half * 512:(half + 1) * 512], start=(fc == 0), stop=(fc == 21)),
                          ["actT"] + wres_all, PB(b))
                    A("dve", lambda e, tt=tt, half=half, b=b: e.tensor_tensor(out=x_tok[:, tt, half * 512:(half + 1) * 512], in0=ps[:, b, :],
                                                                               in1=x_tok[:, tt, half * 512:(half + 1) * 512], op=ALU.add),
                      PB(b) + ["x%d" % tt], ["x%d" % tt])
                A("act", lambda e, tt=tt: e.activation(out=junkD, in_=x_tok[:, tt, :], func=AF.Square,
                                                       accum_out=stat[:, tt:tt + 1]), ["x%d" % tt, "stat"], ["ftmp0", "ftmp1", "stat%d" % tt])
                A("dve", lambda e, tt=tt: e.tensor_scalar(out=stat[:, 16 + tt:17 + tt], in0=stat[:, tt:tt + 1], scalar1=1.0 / D, scalar2=RMS_EPS,
                                                          op0=ALU.mult, op1=ALU.add), ["stat%d" % tt, "stat"], ["stat%d" % tt])
                A("act", lambda e, tt=tt: e.activation(out=stat[:, 16 + tt:17 + tt], in_=stat[:, 16 + tt:17 + tt], func=AF.Ln), ["stat%d" % tt], ["stat%d" % tt])
                A("act", lambda e, tt=tt: e.activation(out=stat[:, 16 + tt:17 + tt], in_=stat[:, 16 + tt:17 + tt], func=AF.Exp, scale=-0.5),
                  ["stat%d" % tt], ["stat%d" % tt])
                A("dve", lambda e, tt=tt: e.scalar_tensor_tensor(out=x_tok[:, tt, :], in0=x_tok[:, tt, :], scalar=stat[:, 16 + tt:17 + tt], in1=gBf,
                                                                 op0=ALU.mult, op1=ALU.mult), ["x%d" % tt, "stat%d" % tt, "gBf"], ["x%d" % tt])
                S.dma("sp", lambda e, tt=tt: e.dma_start(out=y_p[t0 + tt * 128:t0 + (tt + 1) * 128, :], in_=x_tok[:, tt, :]), reads=["x%d" % tt])
            if nsamp:
                for half in range(2):
                    b = nb()
                    for fc in range(22):
                        A("pe", lambda e, fc=fc, half=half, b=b: e.matmul(ps[0:NS, b, :], lhsT=actT[:, fc, HALF:HALF + NS],
                                                                          rhs=wdn[:, fc, half * 512:(half + 1) * 512], start=(fc == 0), stop=(fc == 21)),
                          ["actT"] + wres_all, PB(b))
                    A("dve", lambda e, half=half, b=b: e.tensor_tensor(out=xs_tok[:, half * 512:(half + 1) * 512], in0=ps[0:NS, b, :],
                                                                        in1=xs_tok[:, half * 512:(half + 1) * 512], op=ALU.add), PB(b) + ["xs"], ["xs"])
                A("act", lambda e: e.activation(out=junkD[0:NS, :], in_=xs_tok, func=AF.Square,
                                                accum_out=stat[0:NS, 8:9]), ["xs", "stat"], ["ftmp0", "ftmp1", "stat8"])
                A("dve", lambda e: e.tensor_scalar(out=stat[0:NS, 24:25], in0=stat[0:NS, 8:9], scalar1=1.0 / D, scalar2=RMS_EPS,
                                                    op0=ALU.mult, op1=ALU.add), ["stat8", "stat"], ["stat8"])
                A("act", lambda e: e.activation(out=stat[0:NS, 24:25], in_=stat[0:NS, 24:25], func=AF.Ln), ["stat8"], ["stat8"])
                A("act", lambda e: e.activation(out=stat[0:NS, 24:25], in_=stat[0:NS, 24:25], func=AF.Exp, scale=-0.5), ["stat8"], ["stat8"])
                A("dve", lambda e: e.scalar_tensor_tensor(out=xs_tok, in0=xs_tok, scalar=stat[0:NS, 24:25], in1=gBf[0:NS, :],
                                                          op0=ALU.mult, op1=ALU.mult), ["xs", "stat8", "gBf"], ["xs"])
                S.dma("sp", lambda e: e.dma_start(out=y_s, in_=xs_tok), reads=["xs"])
        S.finalize()
        S.emit(st)
    return nc


DBG_EXTRA = {"oa": [128, 4, 1040], "oT": [128, 4, 1040], "ob": [128, 4, 1040], "mergedT": [128, 8, 1040], "hT2": [128, 8, 1040]}


def _host_maps(inp, ncores=8):
    f = np.ascontiguousarray

    def a32(v):
        return np.asarray(v, dtype=np.float32)

    def fm(v, ncol):
        return f(a32(v).reshape(ncol, 128).T)

    pfm = np.concatenate([fm(inp['rwkv_mu'][0], 14), fm(inp['rwkv_w0'][0], 4), fm(inp['rwkv_a0'][0], 4), fm(inp['rwkv_k_k'][0], 4),
                          fm(inp['rwkv_k_a'][0], 4), fm(a32(inp['rwkv_r_k'][0]).reshape(-1), 4), fm(inp['rwkv_ln_g'][0], 4),
                          fm(inp['rwkv_ln_b'][0], 4), fm(inp['hgrn_lb'][0], 4), fm(inp['hgrn_lb'][1], 4), fm(inp['hgrn_norm_g'][0], 4)], axis=1)
    shared = dict(w_in=f(a32(inp['w_in'][0])), w2a2=f(np.concatenate([a32(inp['rwkv_w2'][0]), a32(inp['rwkv_a2'][0])], axis=0)),
                  g2=f(a32(inp['rwkv_g2'][0])), w_up_a=f(a32(inp['w_up_a'][0])), w_up_b=f(a32(inp['w_up_b'][0])), w_out=f(a32(inp['w_out'][0])),
                  w_fg=f(a32(inp['w_ffn_gate'][0])), w_fu=f(a32(inp['w_ffn_up'][0])), w_fd=f(a32(inp['w_ffn_down'][0])), pfm=f(pfm),
                  consts=make_consts(), g_mix=f(a32(inp['norm_mix_g'][0])), g_ffn=f(a32(inp['norm_ffn_g'][0])), g_fin=f(a32(inp['norm_final_g'])))
    maps = []
    for c in range(ncores):
        m = dict(shared)
        m['x_p'] = f(a32(inp['x_prompt'][c]))
        m['x_s'] = f(a32(inp['x_sample'][c * NS:(c + 1) * NS, 0]))
        m['wkv_s'] = f(a32(inp['state_rwkv_wkv'][0, c * NS:(c + 1) * NS]).reshape(NS * 8, 4096))
        m['shift_s'] = f(a32(inp['state_rwkv_shift'][0, c * NS:(c + 1) * NS]))
        m['hgrn_s'] = f(a32(inp['state_hgrn'][0, c * NS:(c + 1) * NS]))
        maps.append(m)
    return maps


_NC_CACHE = {}


def kernel(**inputs):
    ncores = 8
    if "nc" not in _NC_CACHE:
        _NC_CACHE["nc"] = build_program()
    nc = _NC_CACHE["nc"]
    maps = _host_maps(inputs, ncores)
    res = run_bass_kernel_spmd(nc, maps, core_ids=list(range(ncores)))
    r = res.results
    y_p = np.stack([r[c]["y_p"] for c in range(ncores)]).astype(np.float32)
    y_s = np.concatenate([r[c]["y_s"] for c in range(ncores)], axis=0).reshape(128, 1, D).astype(np.float32)
    wkv_p = np.stack([r[c]["o_wkv_p"] for c in range(ncores)])[None].astype(np.float32)
    shift_p = np.stack([r[c]["o_shift_p"] for c in range(ncores)])[None].astype(np.float32)
    hgrn_p = np.stack([r[c]["o_hgrn_p"] for c in range(ncores)])[None].astype(np.float32)
    wkv_s = np.concatenate([r[c]["o_wkv_s"].reshape(NS, 8, 64, 64) for c in range(ncores)], axis=0)[None].astype(np.float32)
    shift_s = np.concatenate([r[c]["o_shift_s"] for c in range(ncores)], axis=0)[None].astype(np.float32)
    hgrn_s = np.concatenate([r[c]["o_hgrn_s"] for c in range(ncores)], axis=0)[None].astype(np.float32)
    return (y_p, y_s, wkv_p, shift_p, hgrn_p, wkv_s, shift_s, hgrn_s)
```

```python
from contextlib import ExitStack
import numpy as np
import concourse.bass as bass
import concourse.mybir as mybir
from concourse.bass_utils import run_bass_kernel_spmd

F32 = mybir.dt.float32
BF16 = mybir.dt.bfloat16
AF = mybir.ActivationFunctionType
ALU = mybir.AluOpType
AX = mybir.AxisListType

COMPUTE = ("pe", "act", "dve", "pool")


class Op:
    __slots__ = ("idx", "eng", "fn", "call", "is_dma", "deps", "val", "marked", "slot", "acc", "dur", "xfer", "seq", "mode")

    def __init__(self, idx, eng, fn, call, is_dma):
        self.idx = idx
        self.eng = eng
        self.fn = fn
        self.call = call
        self.is_dma = is_dma
        self.deps = ()
        self.val = None
        self.marked = False
        self.slot = None
        self.acc = None
        self.dur = 100.0
        self.xfer = 0.0
        self.seq = 0
        self.mode = None


class _Rec:
    def __init__(self):
        self.call = None

    def __getattr__(self, name):
        def f(*a, **k):
            self.call = (name, a, k)
            return self
        return f


_WRITE_KW = ("out", "accum_out", "ap")
_ESZ = {}


def _esz(dt):
    if dt not in _ESZ:
        _ESZ[dt] = 4 if dt == F32 else 2
    return _ESZ[dt]


def _ap_intervals(ap):
    sp = str(ap.space)
    esz = _esz(ap.dtype)
    dims = list(ap.ap)
    if sp == "DRAM":
        base = ap.offset
        p0, p1 = 0, 1
        fd = dims
    else:
        pstride, pcnt = dims[0]
        p0 = ap.offset // pstride
        base = ap.offset % pstride
        p1 = p0 + pcnt
        fd = dims[1:]
    fd = sorted([(st_, c) for (st_, c) in fd if c > 1 and st_ != 0])
    run = 1
    k = 0
    while k < len(fd) and fd[k][0] == run:
        run *= fd[k][1]
        k += 1
    outer = fd[k:]
    n_outer = 1
    for st_, c in outer:
        n_outer *= c
    if n_outer <= 32:
        offs = [0]
        for st_, c in outer:
            offs = [o + i * st_ for o in offs for i in range(c)]
        iv = [((base + o) * esz, (base + o + run) * esz) for o in offs]
    else:
        ext = sum((c - 1) * st_ for st_, c in outer) + run
        iv = [(base * esz, (base + ext) * esz)]
    return sp, ap.name, p0, p1, iv


def _call_accesses(call):
    name, a, k = call
    items = []
    if name == "matmul":
        items.append((a[0] if a else k.get("out"), True))
        for kw in ("lhsT", "rhs"):
            items.append((k[kw], False))
    elif name == "memset":
        items.append((a[0] if a else k.get("ap"), True))
    else:
        for kw, v in k.items():
            if hasattr(v, "ap") and hasattr(v, "space"):
                items.append((v, kw in _WRITE_KW))
        for v in a:
            if hasattr(v, "ap") and hasattr(v, "space"):
                items.append((v, False))
    acc = []
    for ap, is_w in items:
        if ap is None:
            continue
        sp, nm, p0, p1, iv = _ap_intervals(ap)
        if sp == "PSUM":
            banks = set()
            for b0, b1 in iv:
                for bk in range(b0 // 2048, (b1 - 1) // 2048 + 1):
                    banks.add(bk)
            for bk in banks:
                acc.append((("ps", bk), 0, 128, 0, 1, is_w))
        else:
            for b0, b1 in iv:
                acc.append(((sp, nm), p0, p1, b0, b1, is_w))
    return acc


def _free_elems(ap):
    n = 1
    for s_ in ap.shape[1:]:
        n *= s_
    return n


_WI = {"fp32": 1.0, "dve": 1.0, "act": 1.0, "pe_small": 1.0, "pe_big": 1.0, "lat": 1.0, "pool": 1.0}


def _est(op):
    _est0(op)
    name = op.call[0]
    if op.is_dma:
        return
    if name == "matmul":
        k = op.call[2]
        if k["lhsT"].dtype == F32:
            op.dur *= _WI["fp32"]
        elif _free_elems(k["rhs"]) >= 512:
            op.dur *= _WI["pe_big"]
        else:
            op.dur *= _WI["pe_small"]
    elif op.eng in ("dve", "act", "pool"):
        op.dur *= _WI[op.eng]


def _est0(op):
    name, a, k = op.call
    if op.is_dma:
        ap = k.get("out")
        nbytes = 1
        for s_ in ap.shape:
            nbytes *= s_
        nbytes *= _esz(ap.dtype)
        op.dur = 1200.0 if op.eng == "pool" else 150.0
        op.xfer = 2000.0 + nbytes / 150.0
        return
    def _rnd(v):
        return 32 if v <= 32 else (64 if v <= 64 else 128)
    if name == "matmul":
        n = _free_elems(k["rhs"])
        f = 4.0 if k["lhsT"].dtype == F32 else 1.0
        op.dur = (16.0 + max(n, 48) * 0.42) * f
        op.mode = (_rnd(k["lhsT"].shape[0]), _rnd(_free_elems(k["lhsT"])), f)
    elif name == "transpose":
        op.dur = 70.0
        op.mode = ("T", _rnd(k["in_"].shape[0]), _rnd(_free_elems(k["in_"])))
    elif op.eng == "act":
        op.dur = 120.0 + _free_elems(k["out"]) * 0.85
        fn_ = k.get("func")
        if fn_ in (AF.Exp, AF.Ln):
            op.mode = "A"
        elif fn_ in (AF.Sigmoid, AF.Tanh):
            op.mode = "B"
        elif fn_ == AF.Silu:
            op.mode = "C"
    else:
        o = k.get("out") if "out" in k else (a[0] if a else None)
        n = _free_elems(o) if o is not None else 64
        if op.eng == "pool":
            op.dur = 100.0 + n * 2.2
        else:
            op.dur = 70.0 + n * 1.17


class Sched:
    def __init__(self, nc, dma_slots=None, reorder=True):
        self.nc = nc
        self.ops = []
        self.dma_slots = dma_slots or {"sp": 8, "act": 2, "pool": 6, "dve": 2, "pe": 2}
        self.reorder = reorder
        self.tag = ""
        self.tags = []

    def _mk(self, eng, fn, is_dma):
        rec = _Rec()
        fn(rec)
        name, a, k = rec.call
        op = Op(len(self.ops), eng, (lambda e: getattr(e, name)(*a, **k)), (name, a, k), is_dma)
        self.ops.append(op)
        self.tags.append(self.tag)
        return op

    def add(self, eng, fn, reads=(), writes=()):
        return self._mk(eng, fn, False)

    def dma(self, eng, fn, reads=(), writes=()):
        return self._mk(eng, fn, True)

    def barrier(self):
        pass

    def _build_dag(self):
        wlog = {}
        rlog = {}
        for op in self.ops:
            acc = _call_accesses(op.call)
            _est(op)
            deps = set()
            for key, p0, p1, b0, b1, is_w in acc:
                for (q0, q1, c0, c1, oi, oe) in wlog.get(key, ()):
                    if q0 < p1 and p0 < q1 and c0 < b1 and b0 < c1:
                        deps.add(oi)
                if is_w or key[0] == "ps":
                    for (q0, q1, c0, c1, oi, oe) in rlog.get(key, ()):
                        if q0 < p1 and p0 < q1 and c0 < b1 and b0 < c1:
                            if is_w or oe != op.eng:
                                deps.add(oi)
            for key, p0, p1, b0, b1, is_w in acc:
                if is_w:
                    for lg in (wlog, rlog):
                        l_ = lg.get(key)
                        if l_:
                            lg[key] = [e_ for e_ in l_ if not (p0 <= e_[0] and e_[1] <= p1 and b0 <= e_[2] and e_[3] <= b1)]
                    wlog.setdefault(key, []).append((p0, p1, b0, b1, op.idx, op.eng))
                else:
                    rlog.setdefault(key, []).append((p0, p1, b0, b1, op.idx, op.eng))
            deps.discard(op.idx)
            op.deps = tuple(sorted(deps))

    def _list_schedule(self):
        import heapq
        ops = self.ops
        n = len(ops)
        if not self.reorder:
            return list(range(n))
        npred = [len(o.deps) for o in ops]
        succ = [[] for _ in range(n)]
        for o in ops:
            for d in o.deps:
                succ[d].append(o.idx)
        ready_t = [0.0] * n
        fin = [0.0] * n
        import os as _os2
        PRI = _os2.environ.get("KPRI", "5")
        prio = list(range(n))
        if PRI != "idx":
            cp = [0.0] * n
            for i_ in range(n - 1, -1, -1):
                m = 0.0
                for s_ in succ[i_]:
                    if cp[s_] > m:
                        m = cp[s_]
                cp[i_] = m + ops[i_].dur + ops[i_].xfer + 200.0
            w_ = float(PRI)
            rank = sorted(range(n), key=lambda i_: (i_ - w_ * cp[i_] / 100.0))
            for r_, i_ in enumerate(rank):
                prio[i_] = r_
        engs = ("pe", "act", "dve", "pool", "sp")
        future = {e: [] for e in engs}
        avail = {e: [] for e in engs}
        free = {e: 0.0 for e in engs}
        for o in ops:
            if npred[o.idx] == 0:
                heapq.heappush(future[o.eng], (0.0, o.idx))
        order = []
        self.start_t = {}
        pe_mode = [None]
        act_set = [None]
        WINDOW = 3000
        done_upto = 0
        sched = [False] * n
        while len(order) < n:
            best = None
            for e in engs:
                fu, av = future[e], avail[e]
                while fu and fu[0][0] <= free[e]:
                    t_, i_ = heapq.heappop(fu)
                    heapq.heappush(av, (prio[i_], i_))
                if av:
                    pick = av[0]
                    if e == "pe" and len(av) > 1 and ops[pick[1]].mode != pe_mode[0]:
                        best_same = None
                        for i2 in av:
                            if ops[i2[1]].mode == pe_mode[0] and (best_same is None or i2 < best_same):
                                best_same = i2
                        if best_same is not None and best_same[0] - pick[0] < 400:
                            pick = best_same
                    if e == "act" and len(av) > 1 and ops[pick[1]].mode not in (None, act_set[0]):
                        best_same = None
                        for i2 in av:
                            if ops[i2[1]].mode in (None, act_set[0]) and (best_same is None or i2 < best_same):
                                best_same = i2
                        if best_same is not None and best_same[0] - pick[0] < 300:
                            pick = best_same
                    cand = (free[e], pick[0], e, True, pick)
                elif fu:
                    cand = (fu[0][0], prio[fu[0][1]], e, False, fu[0])
                else:
                    continue
                if best is None or cand[:2] < best[:2]:
                    best = cand
            start, _pr, e, from_av, item = best
            if from_av:
                i_ = item[1]
                if avail[e][0] == item:
                    heapq.heappop(avail[e])
                else:
                    avail[e].remove(item)
                    heapq.heapify(avail[e])
            else:
                i_ = item[1]
                heapq.heappop(future[e])
            o = ops[i_]
            if e == "pe":
                if o.mode != pe_mode[0]:
                    start += 150.0
                pe_mode[0] = o.mode
            elif e == "act" and o.mode is not None:
                if o.mode != act_set[0]:
                    start += 1300.0
                act_set[0] = o.mode
            free[e] = start + o.dur
            self.start_t[i_] = start
            fin[i_] = start + o.dur + o.xfer
            order.append(i_)
            for s_ in succ[i_]:
                lat = (40.0 if (ops[s_].eng == e and e == "pe" and not o.is_dma) else (150.0 if ops[s_].eng == e else 400.0)) * _WI["lat"]
                if fin[i_] + lat > ready_t[s_]:
                    ready_t[s_] = fin[i_] + lat
                npred[s_] -= 1
                if npred[s_] == 0:
                    heapq.heappush(future[ops[s_].eng], (ready_t[s_], s_))
        self.est_makespan = max(fin) if fin else 0.0
        return order

    def finalize(self):
        self._build_dag()
        order = self._list_schedule()
        self.order = order
        ops = self.ops
        dma_count = {}
        slot_last = {}
        extra = {}
        seqc = {}
        for i_ in order:
            op = ops[i_]
            seqc[op.eng] = seqc.get(op.eng, 0) + 1
            op.seq = seqc[op.eng]
            if op.is_dma:
                n_ = dma_count.get(op.eng, 0)
                dma_count[op.eng] = n_ + 1
                k = self.dma_slots[op.eng]
                op.slot = (op.eng, n_ % k)
                op.val = 16 * (n_ // k + 1)
                prev = slot_last.get(op.slot)
                if prev is not None:
                    extra[i_] = prev
                slot_last[op.slot] = i_
        known = {}
        vc = {}
        self.waits = {}
        for i_ in order:
            op = ops[i_]
            K = known.setdefault(op.eng, {})
            deps = list(op.deps)
            if i_ in extra:
                deps.append(extra[i_])
            cand = []
            for d in deps:
                p = ops[d]
                if (not p.is_dma) and p.eng == "pe" and op.eng == "pe" and not op.is_dma:
                    continue
                cand.append(d)
            cand.sort(key=lambda d: -ops[d].seq)
            w = []
            for d in cand:
                p = ops[d]
                key = ("dma", p.slot) if p.is_dma else ("eng", p.eng)
                val = p.val if p.is_dma else p.seq
                if K.get(key, -1) >= val:
                    continue
                w.append(d)
                p.marked = True
                for kk, vv in vc[d].items():
                    if K.get(kk, -1) < vv:
                        K[kk] = vv
            self.waits[i_] = w
            v = dict(K)
            if op.is_dma:
                v[("dma", op.slot)] = op.val
            else:
                if v.get(("eng", op.eng), -1) < op.seq and op.eng == "pe":
                    pass
                v[("eng", op.eng)] = max(v.get(("eng", op.eng), -1), op.seq)
            vc[i_] = v
        cnt = {}
        for i_ in order:
            op = ops[i_]
            if not op.is_dma and op.marked:
                cnt[op.eng] = cnt.get(op.eng, 0) + 1
                op.val = cnt[op.eng]
        self.counts = cnt

    def emit(self, stack):
        nc = self.nc
        sems = {}
        for e in COMPUTE:
            sems[("eng", e)] = stack.enter_context(nc.semaphore("s_" + e))
        for e, k in self.dma_slots.items():
            for i in range(k):
                sems[("dma", (e, i))] = stack.enter_context(nc.semaphore("d_%s%d" % (e, i)))
        block = stack.enter_context(nc.Block())
        ops = self.ops
        order = self.order
        waits = self.waits

        def run(engname, eng):
            for i_ in order:
                op = ops[i_]
                if op.eng != engname:
                    continue
                for d in waits[i_]:
                    p = ops[d]
                    s = sems[("dma", p.slot)] if p.is_dma else sems[("eng", p.eng)]
                    eng.wait_ge(s, p.val)
                ins = op.fn(eng)
                if op.is_dma:
                    ins.then_inc(sems[("dma", op.slot)], 16)
                elif op.marked:
                    ins.then_inc(sems[("eng", op.eng)], 1)
            last = {}
            for i_ in order:
                op = ops[i_]
                if op.is_dma and op.eng == engname:
                    last[op.slot] = op.val
            for slot, v in last.items():
                eng.wait_ge(sems[("dma", slot)], v)

        @block.sync
        def _(e):
            run("sp", e)

        @block.tensor
        def _(e):
            run("pe", e)

        @block.scalar
        def _(e):
            run("act", e)

        @block.vector
        def _(e):
            run("dve", e)

        @block.gpsimd
        def _(e):
            run("pool", e)


D = 1024
T = 2048
NS = 16
HALF = 1024
TW = HALF + NS
RW = 512
RPROJ = 1792
HW_ = 512
DFF = 2816
INP = 5888
CDEC = -0.6065306597126334
GN_EPS = 64e-5
RMS_EPS = 1e-6

PF = {}
_c = 0
for _n, _w in (("mu", 14), ("w0", 4), ("a0", 4), ("k_k", 4), ("k_a", 4), ("r_k", 4), ("ln_g", 4), ("ln_b", 4),
               ("lb0", 4), ("lb1", 4), ("hng", 4)):
    PF[_n] = _c
    _c += _w
PF_N = _c
DV = {"omu": 0, "omka": 14, "lb": 18, "omlb": 22}
DV_N = 29

CO = {"ident": 0, "blk64": 128, "ones": 256, "maskNM": 384, "maskL": 512, "maskI": 576, "cmask": 640}
CO_N = 640 + 1024 + 256


def make_consts():
    c = np.zeros((128, CO_N), np.float32)
    p = np.arange(128)
    c[:, 0:128] = np.eye(128, dtype=np.float32)
    c[:, 128:256] = (p[:, None] // 64 == p[None, :] // 64).astype(np.float32)
    c[:, 256:384] = 1.0
    s = (p % 64)[:, None]
    t = np.arange(64)[None, :]
    c[:, 384:448] = (s < t)
    c[:, 448:512] = (s <= t)
    c[:, 512:576] = (s > t)
    c[:, 576:640] = (s == t)
    tt = np.arange(1024)
    c[:, 640:1664] = (tt % 64 != 0).astype(np.float32)[None, :]
    c[:, 1664:1920] = np.eye(16, dtype=np.float32).reshape(-1)[None, :]
    return c


class Arena:
    def __init__(self, tile, nwords):
        self.t = tile
        self.n = nwords
        self.top = 0

    def alloc(self, dtype, shape):
        free = 1
        for s in shape[1:]:
            free *= s
        words = free if dtype == F32 else (free + 1) // 2
        words = (words + 7) // 8 * 8
        off = self.top
        self.top += words
        assert self.top <= self.n, ("arena overflow", self.top, self.n)
        v = self.t[0:shape[0], off:off + words]
        if dtype == BF16:
            v = v.bitcast(BF16)
        v = v[:, 0:free]
        if len(shape) > 2:
            names = ["d%d" % i for i in range(len(shape) - 1)]
            pat = "p (" + " ".join(names) + ") -> p " + " ".join(names)
            kw = {names[i]: shape[i + 1] for i in range(len(names))}
            v = v.rearrange(pat, **kw)
        return v


ARENA_WORDS = 32560


def build_program(dbg=None, passes=(0, 1), stop_after=None):
    nc = bass.Bass("TRN2", target_bir_lowering=False)

    def din(name, shape):
        return nc.dram_tensor(name, list(shape), F32, kind="ExternalInput").ap()

    def dout(name, shape):
        return nc.dram_tensor(name, list(shape), F32, kind="ExternalOutput").ap()

    x_p = din("x_p", [T, D])
    x_s = din("x_s", [NS, D])
    wkv_s = din("wkv_s", [NS * 8, 4096])
    shift_s = din("shift_s", [NS, RPROJ])
    hgrn_s = din("hgrn_s", [NS, 4, 128, 128])
    w_in = din("w_in", [D, INP])
    w2a2 = din("w2a2", [128, 512])
    g2 = din("g2", [128, 512])
    w_up_a = din("w_up_a", [RW, D])
    w_up_b = din("w_up_b", [HW_, D])
    w_out = din("w_out", [D, D])
    w_fg = din("w_fg", [D, DFF])
    w_fu = din("w_fu", [D, DFF])
    w_fd = din("w_fd", [DFF, D])
    pfm_d = din("pfm", [128, PF_N])
    consts_d = din("consts", [128, CO_N])
    g_mix = din("g_mix", [D])
    g_ffn = din("g_ffn", [D])
    g_fin = din("g_fin", [D])

    y_p = dout("y_p", [T, D])
    y_s = dout("y_s", [NS, D])
    o_wkv_p = dout("o_wkv_p", [8, 64, 64])
    o_shift_p = dout("o_shift_p", [RPROJ])
    o_hgrn_p = dout("o_hgrn_p", [4, 128, 128])
    o_wkv_s = dout("o_wkv_s", [NS * 8, 4096])
    o_shift_s = dout("o_shift_s", [NS, RPROJ])
    o_hgrn_s = dout("o_hgrn_s", [NS, 4, 128, 128])
    dbg_out = {}
    if dbg:
        for k, shp in dbg.items():
            dbg_out[k] = dout("dbg_" + k, shp)
    scr_a = nc.dram_tensor("scr_a", [NS, 6, 512], F32).ap()
    scr_y = nc.dram_tensor("scr_y", [NS, 512], F32).ap()

    st = ExitStack()
    with st:
        def sb(name, shape, dt):
            return st.enter_context(nc.sbuf_tensor(name, list(shape), dt))

        arena_t = sb("arena", [128, ARENA_WORDS], F32)
        hT = sb("hT", [128, 8, TW], BF16)
        oa = sb("oa", [128, 4, TW], BF16)
        ob = sb("ob", [128, 4, TW], BF16)
        NWB = 4
        wbuf = [sb("wbuf%d" % i, [128, 8, 512], BF16) for i in range(NWB)]
        consts = sb("consts_sb", [128, 384], F32)
        ident_bf = sb("ident_bf", [128, 128], BF16)
        masks_bf = sb("masks_bf", [128, 256], BF16)
        pfm = sb("pfm_sb", [128, PF_N], F32)
        dv = sb("dv", [128, DV_N], F32)
        lw_w = sb("lw_w", [128, 512], BF16)
        g2_w = sb("g2_w", [128, 512], BF16)
        carry = sb("carry", [128, 14], F32)
        shiftS = sb("shiftS", [128, 14, NS], F32)
        sprevT = sb("sprevT", [128, 14, NS], F32)
        H0f = sb("H0f", [128, 4, 64], F32)
        H0bd = sb("H0bd", [128, 4, 128], BF16)
        S0f = sb("S0f", [128, 4, 128], F32)
        S0b = sb("S0b", [128, 4, 128], BF16)
        WC = sb("WC", [128, 4, 16], F32)
        DC = sb("DC", [128, 4, 16], F32)
        stat = sb("stat", [128, 64], F32)
        ftmp = sb("ftmp", [128, 2, 512], F32)
        ps = st.enter_context(nc.psum_tensor("ps", [128, 8, 512], F32))

        ident = consts[:, 0:128]
        blk64 = consts[:, 128:256]
        ones = consts[:, 256:384]
        maskNM_bf = masks_bf[:, 0:128]
        maskL_bf = masks_bf[:, 128:192]
        maskI_bf = masks_bf[:, 192:256]

        S = Sched(nc)
        A = S.add
        bank_ctr = [0]

        def nb(n=1):
            b = bank_ctr[0]
            if n > 1:
                b = (b + n - 1) // n * n
            if b + n > 8:
                b = 0
            bank_ctr[0] = (b + n) % 8
            return b

        def nbp(par):
            b = bank_ctr[0]
            if b % 2 != par:
                b = (b + 1) % 8
            bank_ctr[0] = (b + 1) % 8
            return b

        par_ctr = [0]
        seq_ctr = [0, 0]

        par_nb = [6]

        def nb_par(n=1):
            m = par_nb[0]
            if n == 2:
                par_ctr[0] = (par_ctr[0] + 1) // 2 * 2
            b = par_ctr[0] % m
            par_ctr[0] = (par_ctr[0] + n) % m
            return b

        smp_ctr = [0]

        def nb_smp():
            smp_ctr[0] += 1
            return 4 + smp_ctr[0] % 2

        def nb_seq(par):
            return 6 + par

        def PB(b, n=1):
            return ["ps%d" % (b + i) for i in range(n)]

        def pf(name, j=0):
            c = PF[name] + j
            return pfm[:, c:c + 1]

        def dvc(name, j=0):
            c = DV[name] + j
            return dv[:, c:c + 1]

        wctr = [0]

        def load_w(src_ap, shape_view):
            i = wctr[0] % NWB
            wctr[0] += 1
            a, b = shape_view
            view = wbuf[i][:, :, :].rearrange("p a b -> p (a b)")[:, 0:a * b].rearrange("p (a b) -> p a b", a=a)
            rn = "wbuf%d" % i
            S.dma("pool", lambda e, view=view, src_ap=src_ap: e.dma_start(out=view, in_=src_ap), writes=[rn])
            return view, rn

        def dump(name, ap_sb, res):
            if dbg and name in dbg_out:
                S.dma("pool", lambda e: e.dma_start(out=dbg_out[name], in_=ap_sb), reads=res)

        S.dma("sp", lambda e: e.dma_start(out=consts[:], in_=consts_d[:, 0:384]), writes=["consts"])
        S.dma("pool", lambda e: e.dma_start(out=masks_bf[:], in_=consts_d[:, 384:640]))
        S.dma("sp", lambda e: e.dma_start(out=pfm[:], in_=pfm_d), writes=["pfm"])
        S.dma("pool", lambda e: e.dma_start(out=lw_w[:], in_=w2a2), writes=["lw_w"])
        S.dma("pool", lambda e: e.dma_start(out=g2_w[:], in_=g2), writes=["g2_w"])
        A("dve", lambda e: e.tensor_copy(out=ident_bf[:], in_=ident), ["consts"], ["ident_bf"])
        A("dve", lambda e: e.memset(carry[:], 0.0), [], ["carry"])
        A("dve", lambda e: e.memset(H0f[:], 0.0), [], ["H0f"])
        A("dve", lambda e: e.memset(H0bd[:], 0.0), [], ["H0b"])
        A("dve", lambda e: e.memset(S0f[:], 0.0), [], ["S0f"])
        A("dve", lambda e: e.memset(S0b[:], 0.0), [], ["S0b"])
        A("dve", lambda e: e.memset(sprevT[:], 0.0), [], ["sprevT"])
        A("dve", lambda e: e.tensor_scalar(out=dv[:, 0:14], in0=pfm[:, PF["mu"]:PF["mu"] + 14], scalar1=-1.0, scalar2=1.0,
                                            op0=ALU.mult, op1=ALU.add), ["pfm"], ["dv"])
        A("dve", lambda e: e.tensor_scalar(out=dv[:, 14:18], in0=pfm[:, PF["k_a"]:PF["k_a"] + 4], scalar1=-1.0, scalar2=1.0,
                                            op0=ALU.mult, op1=ALU.add), ["pfm"], ["dv"])
        A("dve", lambda e: e.tensor_tensor(out=dv[:, 18:22], in0=pfm[:, PF["lb0"]:PF["lb0"] + 4],
                                            in1=pfm[:, PF["lb1"]:PF["lb1"] + 4], op=ALU.subtract), ["pfm"], ["dv"])
        A("act", lambda e: e.activation(out=dv[:, 18:22], in_=dv[:, 18:22], func=AF.Sigmoid), ["dv"], ["dv"])
        A("dve", lambda e: e.tensor_scalar(out=dv[:, 22:26], in0=dv[:, 18:22], scalar1=-1.0, scalar2=1.0,
                                            op0=ALU.mult, op1=ALU.add), ["dv"], ["dv"])

        eps24, epsgn, epsrms = dv[:, 26:27], dv[:, 27:28], dv[:, 28:29]
        A("dve", lambda e: e.memset(dv[:, 26:27], 1e-24))
        A("dve", lambda e: e.memset(dv[:, 27:28], GN_EPS))
        A("dve", lambda e: e.memset(dv[:, 28:29], RMS_EPS))
        w_in_v = w_in.rearrange("(kc p) n -> p kc n", p=128)

        for pas in passes:
            import os as _os
            _skip = _os.environ.get("KSKIP", "")
            nsamp = NS if (pas == 0 and "nosamp" not in _skip) else 0
            W = HALF + nsamp
            blocks = [(0, 512), (512, 1024)] + ([(1024, 1040)] if nsamp else [])
            t0 = pas * HALF
            S.barrier()
            ar_ = Arena(arena_t, ARENA_WORDS)

            def norm_phase(tag, x_tok, xs_tok, g_dram, arena, out_cb, src_loaded, xsres):
                gB = arena.alloc(F32, [128, 1024])
                junk = arena.alloc(F32, [128, 1024])
                h_tok = arena.alloc(BF16, [128, 8, 1024])
                hs_tok = arena.alloc(BF16, [NS, 1024])
                R = tag
                S.dma("sp", lambda e: e.dma_start(out=gB, in_=g_dram.partition_broadcast(128)), writes=[R + "gB"])
                A("dve", lambda e: e.memset(stat[:, 0:32], 0.0), [], ["stat"])
                for tt in range(8):
                    A("act", lambda e, tt=tt: e.activation(out=junk, in_=x_tok[:, tt, :], func=AF.Square,
                                                           accum_out=stat[:, tt:tt + 1]),
                      [src_loaded(tt)], [R + "junk", "stat"])
                if nsamp:
                    A("act", lambda e: e.activation(out=junk[0:NS, :], in_=xs_tok, func=AF.Square,
                                                    accum_out=stat[0:NS, 8:9]), [xsres], [R + "junk", "stat"])
                A("dve", lambda e: e.tensor_scalar(out=stat[:, 16:25], in0=stat[:, 0:9], scalar1=1.0 / D, scalar2=RMS_EPS,
                                                    op0=ALU.mult, op1=ALU.add), ["stat"], ["stat"])
                A("act", lambda e: e.activation(out=stat[:, 16:25], in_=stat[:, 16:25], func=AF.Ln), ["stat"], ["stat"])
                A("act", lambda e: e.activation(out=stat[:, 16:25], in_=stat[:, 16:25], func=AF.Exp, scale=-0.5), ["stat"], ["stat"])
                out_cb(gB, h_tok, hs_tok, R)

            x_tok = ar_.alloc(F32, [128, 8, 1024])
            xs_tok = ar_.alloc(F32, [NS, 1024])
            for tt in range(8):
                S.dma("sp", lambda e, tt=tt: e.dma_start(out=x_tok[:, tt, :], in_=x_p[t0 + tt * 128:t0 + (tt + 1) * 128, :]),
                      writes=["P0x%d" % tt])
            if nsamp:
                S.dma("sp", lambda e: e.dma_start(out=xs_tok, in_=x_s), writes=["P0xs"])

            def to_hT(gB, h_tok, hs_tok, R, x_tok_, xs_tok_, xres, xsres):
                for tt in range(8):
                    A("dve", lambda e, tt=tt: e.scalar_tensor_tensor(out=h_tok[:, tt, :], in0=x_tok_[:, tt, :],
                                                                     scalar=stat[:, 16 + tt:17 + tt], in1=gB,
                                                                     op0=ALU.mult, op1=ALU.mult),
                      [xres(tt), "stat", R + "gB"], [R + "htok%d" % tt])
                    b = nb()
                    pT = ps[:, b, :].bitcast(BF16).rearrange("p (c t) -> p c t", c=8)
                    for dc in range(8):
                        A("pe", lambda e, tt=tt, dc=dc, pT=pT: e.transpose(out=pT[:, dc, :], in_=h_tok[:, tt, dc * 128:(dc + 1) * 128],
                                                                           identity=ident_bf[:]),
                          [R + "htok%d" % tt, "ident_bf"], PB(b))
                    eng = "act" if tt % 2 == 0 else "dve"
                    if eng == "act":
                        A("act", lambda e, tt=tt, pT=pT: e.copy(out=hT[:, :, tt * 128:(tt + 1) * 128], in_=pT), PB(b), ["hT"])
                    else:
                        A("dve", lambda e, tt=tt, pT=pT: e.tensor_copy(out=hT[:, :, tt * 128:(tt + 1) * 128], in_=pT), PB(b), ["hT"])
                if nsamp:
                    A("dve", lambda e: e.scalar_tensor_tensor(out=hs_tok, in0=xs_tok_, scalar=stat[0:NS, 24:25], in1=gB[0:NS, :],
                                                              op0=ALU.mult, op1=ALU.mult), [xsres, "stat", R + "gB"], [R + "hstok"])
                    b = nb()
                    pT = ps[:, b, :].bitcast(BF16).rearrange("p (c t) -> p c t", c=8)
                    for dc in range(8):
                        A("pe", lambda e, dc=dc, pT=pT: e.transpose(out=pT[:, dc, 0:NS], in_=hs_tok[:, dc * 128:(dc + 1) * 128],
                                                                    identity=ident_bf[0:NS, 0:NS]),
                          [R + "hstok", "ident_bf"], PB(b))
                    A("act", lambda e, pT=pT: e.copy(out=hT[:, :, HALF:HALF + NS], in_=pT[:, :, 0:NS]), PB(b), ["hT"])

            norm_phase("P0", x_tok, xs_tok, g_mix, ar_,
                       lambda gB, h_tok, hs_tok, R: to_hT(gB, h_tok, hs_tok, R, x_tok, xs_tok, lambda tt: "P0x%d" % tt, "P0xs"),
                       lambda tt: "P0x%d" % tt, "P0xs")
            if dbg and "hT" in dbg_out and pas == 0:
                dump("hT", hT[:], ["hT"])
            if stop_after == "P0":
                continue

            def proj_fm(wview, wres, ncol0, kcs, act_tile, act_res, evac):
                for (c0, c1) in blocks:
                    b = nb()
                    for kc in range(kcs):
                        A("pe", lambda e, kc=kc, b=b, c0=c0, c1=c1: e.matmul(ps[:, b, 0:c1 - c0], lhsT=wview[:, kc, ncol0:ncol0 + 128],
                                                                              rhs=act_tile[:, kc, c0:c1], start=(kc == 0),
                                                                              stop=(kc == kcs - 1)),
                          [wres, act_res], PB(b))
                    evac(b, c0, c1)

            S.tag = "p%d Rprep" % pas
            S.barrier()
            ar_ = Arena(arena_t, ARENA_WORDS)
            sig = ar_.alloc(F32, [128, 4, TW])
            g_bf = ar_.alloc(BF16, [128, 4, TW])
            ar_t = ar_.alloc(BF16, [128, 4, 16, 2, 64])
            bT = ar_.alloc(BF16, [128, 4, HALF])
            kT = ar_.alloc(BF16, [128, 4, HALF])
            vT = ar_.alloc(BF16, [128, 4, TW])
            bonus = ar_.alloc(BF16, [128, 4, TW])
            smp = ar_.alloc(F32, [128, 6, 4, NS])
            mark = ar_.top
            a_bf = ar_.alloc(BF16, [128, 4, TW])
            kpr = ar_.alloc(BF16, [128, 4, TW])
            praw = [ar_.alloc(F32, [128, 1048]) for _ in range(1)]
            NT = 7
            tmp = [ar_.alloc(F32, [128, TW]) for _ in range(NT)]
            cmask = ar_.alloc(BF16, [128, HALF])
            lwb = ar_.alloc(BF16, [128, TW])
            import os as _os
            _skip = _os.environ.get("KSKIP", "")
            if "cmask" not in _skip:
                S.dma("pool", lambda e: e.dma_start(out=cmask, in_=consts_d[:, 640:1664]), writes=["cmask"])

            if nsamp and "shiftT" not in _skip:
                sh_tok = tmp[0][0:NS, :]
                sh_tok2 = tmp[1][0:NS, :]
                S.dma("sp", lambda e: e.dma_start(out=sh_tok[:, 0:1024], in_=shift_s[:, 0:1024]), writes=["tmp0"])
                S.dma("sp", lambda e: e.dma_start(out=sh_tok2[:, 0:768], in_=shift_s[:, 1024:1792]), writes=["tmp1"])
                b = nb()
                for c in range(14):
                    src = sh_tok[:, c * 128:(c + 1) * 128] if c < 8 else sh_tok2[:, (c - 8) * 128:(c - 7) * 128]
                    A("pe", lambda e, c=c, src=src, b=b: e.transpose(out=ps[:, b, c * NS:(c + 1) * NS], in_=src, identity=ident[0:NS, 0:NS]),
                      ["tmp0", "tmp1", "consts"], PB(b))
                A("dve", lambda e, b=b: e.tensor_copy(out=sprevT[:, :, :], in_=ps[:, b, 0:14 * NS].rearrange("p (c n) -> p c n", c=14)),
                  PB(b), ["sprevT"])

            if stop_after == "shiftT":
                continue
            hv = [(0, 512), (512, W)]
            pctr = [0]

            def rwkv_chunk(c, wview, wres, ncol0, out_ap=None):
                pr = praw[0]
                t1 = tmp[6]
                A("dve", lambda e: e.tensor_copy(out=pr[:, 0:1], in_=carry[:, c:c + 1]))

                def evac(b, c0, c1):
                    A("act", lambda e: e.copy(out=pr[:, 1 + c0:1 + c1], in_=ps[:, b, 0:c1 - c0]))
                    A("act", lambda e: e.activation(out=t1[:, c0:c1], in_=ps[:, b, 0:c1 - c0], func=AF.Copy, scale=dvc("omu", c)))
                proj_fm(wview, wres, ncol0, 8, hT, "hT", evac)
                A("act", lambda e: e.copy(out=carry[:, c:c + 1], in_=pr[:, HALF:HALF + 1]))
                if nsamp:
                    A("act", lambda e: e.copy(out=shiftS[:, c, :], in_=pr[:, 1 + HALF:1 + HALF + NS]))
                out_ap_ = tmp[0] if out_ap is None else out_ap
                for (a_, b_) in hv:
                    b2 = min(b_, HALF)
                    A("dve", lambda e, a_=a_, b2=b2: e.scalar_tensor_tensor(out=out_ap_[:, a_:b2], in0=pr[:, a_:b2], scalar=pf("mu", c),
                                                                            in1=t1[:, a_:b2], op0=ALU.mult, op1=ALU.add))
                if nsamp:
                    A("dve", lambda e: e.scalar_tensor_tensor(out=out_ap_[:, HALF:HALF + NS], in0=sprevT[:, c, :], scalar=pf("mu", c),
                                                              in1=t1[:, HALF:HALF + NS], op0=ALU.mult, op1=ALU.add))
                return out_ap_

            wv, wr = load_w(w_in_v[:, :, 1536:1792], (8, 256))
            psm = rwkv_chunk(12, wv, wr, 0)
            for (a_, b_) in hv:
                A("act", lambda e, a_=a_, b_=b_: e.activation(out=lwb[0:64, a_:b_], in_=psm[0:64, a_:b_], func=AF.Tanh))
                A("act", lambda e, a_=a_, b_=b_: e.copy(out=lwb[64:128, a_:b_], in_=psm[64:128, a_:b_]))
            for j in range(4):
                for (c0, c1) in blocks:
                    b2_ = nb(2)
                    b = b2_
                    A("pe", lambda e, j=j, b=b, c0=c0, c1=c1: e.matmul(ps[:, b, 0:c1 - c0], lhsT=lw_w[0:64, j * 128:(j + 1) * 128],
                                                                        rhs=lwb[0:64, c0:c1], start=True, stop=True))
                    A("act", lambda e, j=j, b=b, c0=c0, c1=c1: e.activation(out=sig[:, j, c0:c1], in_=ps[:, b, 0:c1 - c0], func=AF.Sigmoid,
                                                                             bias=pf("w0", j)))
                    b = b2_ + 1
                    A("pe", lambda e, j=j, b=b, c0=c0, c1=c1: e.matmul(ps[:, b, 0:c1 - c0], lhsT=lw_w[64:128, j * 128:(j + 1) * 128],
                                                                        rhs=lwb[64:128, c0:c1], start=True, stop=True))
                    A("act", lambda e, j=j, b=b, c0=c0, c1=c1: e.activation(out=a_bf[:, j, c0:c1], in_=ps[:, b, 0:c1 - c0], func=AF.Sigmoid,
                                                                             bias=pf("a0", j)))
            psm = rwkv_chunk(13, wv, wr, 128)
            for (a_, b_) in hv:
                A("act", lambda e, a_=a_, b_=b_: e.activation(out=lwb[:, a_:b_], in_=psm[:, a_:b_], func=AF.Sigmoid))
            for j in range(4):
                for (c0, c1) in blocks:
                    b = nb()
                    A("pe", lambda e, j=j, b=b, c0=c0, c1=c1: e.matmul(ps[:, b, 0:c1 - c0], lhsT=g2_w[:, j * 128:(j + 1) * 128],
                                                                        rhs=lwb[:, c0:c1], start=True, stop=True))
                    A("dve", lambda e, j=j, b=b, c0=c0, c1=c1: e.tensor_copy(out=g_bf[:, j, c0:c1], in_=ps[:, b, 0:c1 - c0]))
            if dbg and pas == 0:
                dump("sig", sig, ["sig"])

            wv, wr = load_w(w_in_v[:, :, 1024:1536], (8, 512))
            for j in range(4):
                rwkv_chunk(8 + j, wv, wr, j * 128, out_ap=vT[:, j, :])
                if nsamp:
                    A("act", lambda e, j=j: e.copy(out=smp[:, 3, j, :], in_=vT[:, j, HALF:HALF + NS]))

            def fp32_blocksum(src_ap, src_res, mat, evac):
                for (c0, c1) in blocks:
                    b = nb()
                    A("pe", lambda e, b=b, c0=c0, c1=c1: e.matmul(ps[:, b, 0:c1 - c0], lhsT=mat, rhs=src_ap[:, c0:c1], start=True, stop=True))
                    evac(b, c0, c1)

            def cumsum_decay(j, cs_i):
                cs = tmp[cs_i]
                for (a_, b_) in hv:
                    b2 = min(b_, HALF)
                    A("dve", lambda e, a_=a_, b2=b2: e.tensor_tensor_scan(out=cs[:, a_:b2], data0=cmask[:, a_:b2], data1=sig[:, j, a_:b2], initial=0.0,
                                                                          op0=ALU.mult, op1=ALU.add))
                if nsamp:
                    A("dve", lambda e: e.tensor_copy(out=cs[:, HALF:HALF + NS], in_=sig[:, j, HALF:HALF + NS]))
                return cs

            def cview(ap2, a_, b2):
                return ap2[:, a_:b2].rearrange("p (c t) -> p c t", t=64)

            wv, wr = load_w(w_in_v[:, :, 512:1024], (8, 512))
            for j in range(4):
                k_ap = rwkv_chunk(4 + j, wv, wr, j * 128)
                kkr, rs, cs, en, de, ka = tmp[1], tmp[2], tmp[3], tmp[4], tmp[5], tmp[2]
                for (a_, b_) in hv:
                    A("act", lambda e, j=j, a_=a_, b_=b_: e.activation(out=kkr[:, a_:b_], in_=k_ap[:, a_:b_], func=AF.Copy, scale=pf("k_k", j)))
                    A("act", lambda e, a_=a_, b_=b_: e.activation(out=rs[:, a_:b_], in_=kkr[:, a_:b_], func=AF.Square))

                def ev_ss(b, c0, c1):
                    A("act", lambda e: e.activation(out=de[:, c0:c1], in_=ps[:, b, 0:c1 - c0], func=AF.Ln, bias=eps24[:, 0:1]))
                fp32_blocksum(rs, "tmp2", blk64, ev_ss)
                cumsum_decay(j, 3)
                for (a_, b_) in hv:
                    b2 = min(b_, HALF)
                    c_lo, c_hi = a_ // 64, b2 // 64
                    A("act", lambda e, a_=a_, b_=b_: e.activation(out=de[:, a_:b_], in_=de[:, a_:b_], func=AF.Exp, scale=-0.5))
                    A("dve", lambda e, a_=a_, b_=b_: e.tensor_tensor(out=kkr[:, a_:b_], in0=kkr[:, a_:b_], in1=de[:, a_:b_], op=ALU.mult))
                    A("act", lambda e, j=j, a_=a_, b2=b2, c_lo=c_lo, c_hi=c_hi: e.activation(out=WC[:, j, c_lo:c_hi], in_=cview(cs, a_, b2)[:, :, 63],
                                                                                             func=AF.Exp, scale=CDEC))
                    A("act", lambda e, a_=a_, b2=b2: e.activation(out=en[:, a_:b2], in_=cs[:, a_:b2], func=AF.Exp, scale=-CDEC))
                    if nsamp and b_ > HALF:
                        A("act", lambda e, j=j: e.activation(out=smp[:, 1, j, :], in_=cs[:, HALF:HALF + NS], func=AF.Exp, scale=CDEC))
                        A("act", lambda e, j=j: e.activation(out=smp[:, 4, j, :], in_=kkr[:, HALF:HALF + NS], func=AF.Copy, scale=-1.0))
                    A("dve", lambda e, j=j, a_=a_, b2=b2: e.tensor_tensor(out=de[:, a_:b2], in0=cs[:, a_:b2], in1=sig[:, j, a_:b2], op=ALU.subtract))
                    A("act", lambda e, a_=a_, b2=b2: e.activation(out=de[:, a_:b2], in_=de[:, a_:b2], func=AF.Exp, scale=CDEC))
                    A("dve", lambda e, j=j, a_=a_, b2=b2, c_lo=c_lo, c_hi=c_hi: e.scalar_tensor_tensor(
                        out=ar_t[:, j, c_lo:c_hi, 0, :], in0=cview(kkr, a_, b2), scalar=-1.0, in1=cview(de, a_, b2), op0=ALU.mult, op1=ALU.mult))
                    A("dve", lambda e, j=j, a_=a_, b_=b_: e.tensor_tensor(out=ka[:, a_:b_], in0=kkr[:, a_:b_], in1=a_bf[:, j, a_:b_], op=ALU.mult))
                    if nsamp and b_ > HALF:
                        A("act", lambda e, j=j: e.copy(out=smp[:, 5, j, :], in_=ka[:, HALF:HALF + NS]))
                    A("dve", lambda e, j=j, a_=a_, b2=b2: e.tensor_tensor(out=bT[:, j, a_:b2], in0=ka[:, a_:b2], in1=en[:, a_:b2], op=ALU.mult))
                    A("dve", lambda e, j=j, a_=a_, b_=b_: e.tensor_scalar(out=kkr[:, a_:b_], in0=a_bf[:, j, a_:b_], scalar1=pf("k_a", j),
                                                                          scalar2=dvc("omka", j), op0=ALU.mult, op1=ALU.add))
                    A("dve", lambda e, a_=a_, b_=b_: e.tensor_tensor(out=kkr[:, a_:b_], in0=kkr[:, a_:b_], in1=k_ap[:, a_:b_], op=ALU.mult))
                    A("dve", lambda e, j=j, a_=a_, b2=b2: e.tensor_tensor(out=kT[:, j, a_:b2], in0=kkr[:, a_:b2], in1=en[:, a_:b2], op=ALU.mult))
                    A("act", lambda e, j=j, a_=a_, b_=b_: e.activation(out=kpr[:, j, a_:b_], in_=kkr[:, a_:b_], func=AF.Copy, scale=pf("r_k", j)))
                    if nsamp and b_ > HALF:
                        A("act", lambda e, j=j: e.copy(out=smp[:, 2, j, :], in_=kkr[:, HALF:HALF + NS]))

            wv, wr = load_w(w_in_v[:, :, 0:512], (8, 512))
            for j in range(4):
                r_ap = rwkv_chunk(j, wv, wr, j * 128)
                cs = cumsum_decay(j, 3)
                rk = tmp[1]
                for (a_, b_) in hv:
                    b2 = min(b_, HALF)
                    c_lo, c_hi = a_ // 64, b2 // 64
                    A("act", lambda e, a_=a_, b2=b2: e.activation(out=cs[:, a_:b2], in_=cs[:, a_:b2], func=AF.Exp, scale=CDEC))
                    A("dve", lambda e, j=j, a_=a_, b2=b2, c_lo=c_lo, c_hi=c_hi: e.tensor_tensor(out=ar_t[:, j, c_lo:c_hi, 1, :], in0=cview(r_ap, a_, b2),
                                                                                                 in1=cview(cs, a_, b2), op=ALU.mult))
                    A("dve", lambda e, j=j, a_=a_, b_=b_: e.tensor_tensor(out=rk[:, a_:b_], in0=r_ap[:, a_:b_], in1=kpr[:, j, a_:b_], op=ALU.mult))
                if nsamp:
                    A("act", lambda e, j=j: e.copy(out=smp[:, 0, j, :], in_=r_ap[:, HALF:HALF + NS]))

                def ev_bon(b, c0, c1, j=j):
                    A("dve", lambda e: e.tensor_tensor(out=bonus[:, j, c0:c1], in0=ps[:, b, 0:c1 - c0], in1=vT[:, j, c0:c1], op=ALU.mult))
                fp32_blocksum(rk, "tmp1", blk64, ev_bon)
            if dbg and pas == 0:
                dump("ar", ar_t, ["ar"])
                dump("bT", bT, ["bT"])
                dump("kT", kT, ["kT"])
                dump("vT", vT, ["vT"])
                dump("bonus", bonus, ["bonus"])
            if stop_after == "Rprep":
                continue

            yT = sig
            if nsamp:
                ar_.top = mark
                L1v = ar_.alloc(F32, [128, 6, 64])
                saL = ar_.alloc(F32, [128, 64])
                yL1 = ar_.alloc(F32, [128, 64])
                rs_mark = ar_.top
                wf = [wbuf[i_][:, :, :].rearrange("p a b -> p (a b)").bitcast(F32) for i_ in range(4)]
                S_h = [wf[0].rearrange("p (v k) -> p v k", k=64), wf[1].rearrange("p (v k) -> p v k", k=64)]
                T_h = [wf[2].rearrange("p (v k) -> p v k", k=64), wf[3].rearrange("p (v k) -> p v k", k=64)]
                tok6h = [wf[2][0:NS, 0:1536].rearrange("p (v n) -> p v n", v=3), wf[3][0:NS, 0:1536].rearrange("p (v n) -> p v n", v=3)]
                ytok = wf[2][0:NS, 1536:2048]
                shtok = wf[3][0:NS, 0:2048]
                def emit_rsample():
                    for hf in range(2):
                        S.dma("sp", lambda e, hf=hf: e.dma_start(out=S_h[hf].rearrange("p v k -> p (v k)"), in_=wkv_s[:, hf * 2048:(hf + 1) * 2048]))
                    for vec in range(6):
                        b = nb_par()
                        for j in range(4):
                            A("pe", lambda e, vec=vec, j=j, b=b: e.transpose(out=ps[0:NS, b, j * 128:(j + 1) * 128], in_=smp[:, vec, j, :], identity=ident))
                        dst = tok6h[vec // 3][:, vec % 3, :]
                        if vec % 2 == 0:
                            A("act", lambda e, dst=dst, b=b: e.copy(out=dst, in_=ps[0:NS, b, :]))
                        else:
                            A("dve", lambda e, dst=dst, b=b: e.tensor_copy(out=dst, in_=ps[0:NS, b, :]))
                    for hf in range(2):
                        S.dma("sp", lambda e, hf=hf: e.dma_start(out=scr_a[:, hf * 3:(hf + 1) * 3, :], in_=tok6h[hf]))
                    for vec in range(6):
                        S.dma("sp", lambda e, vec=vec: e.dma_start(out=L1v[:, vec, :], in_=scr_a[:, vec, :].rearrange("b (h n) -> b h n", h=8)))

                    def bc_v(vec):
                        return L1v[:, vec, :].unsqueeze(1).broadcast_to([128, 32, 64])

                    def bc_k(ap2, hf):
                        return ap2[:, hf * 32:(hf + 1) * 32].unsqueeze(2).broadcast_to([128, 32, 64])
                    for hf in range(2):
                        Sx, Tx = S_h[hf], T_h[hf]
                        vs = slice(hf * 32, (hf + 1) * 32)
                        A("dve", lambda e, Sx=Sx, Tx=Tx: e.tensor_tensor(out=Tx, in0=Sx, in1=bc_v(4), op=ALU.mult))
                        A("dve", lambda e, Tx=Tx, vs=vs: e.tensor_reduce(out=saL[:, vs], in_=Tx, axis=AX.X, op=ALU.add))
                        A("dve", lambda e, Sx=Sx: e.tensor_tensor(out=Sx, in0=Sx, in1=bc_v(1), op=ALU.mult))
                        A("dve", lambda e, Tx=Tx, hf=hf: e.tensor_tensor(out=Tx, in0=bc_k(saL, hf), in1=bc_v(5), op=ALU.mult))
                        A("dve", lambda e, Sx=Sx, Tx=Tx: e.tensor_tensor(out=Sx, in0=Sx, in1=Tx, op=ALU.add))
                        A("dve", lambda e, Tx=Tx, hf=hf: e.tensor_tensor(out=Tx, in0=bc_k(L1v[:, 3, :], hf), in1=bc_v(2), op=ALU.mult))
                        A("dve", lambda e, Sx=Sx, Tx=Tx: e.tensor_tensor(out=Sx, in0=Sx, in1=Tx, op=ALU.add))
                        S.dma("sp", lambda e, Sx=Sx, hf=hf: e.dma_start(out=o_wkv_s[:, hf * 2048:(hf + 1) * 2048], in_=Sx.rearrange("p v k -> p (v k)")))
                        A("dve", lambda e, Sx=Sx, Tx=Tx: e.tensor_tensor(out=Tx, in0=Sx, in1=bc_v(0), op=ALU.mult))
                        A("dve", lambda e, Tx=Tx, vs=vs: e.tensor_reduce(out=yL1[:, vs], in_=Tx, axis=AX.X, op=ALU.add))
                    S.dma("sp", lambda e: e.dma_start(out=scr_y.rearrange("b (h n) -> (b h) n", h=8), in_=yL1))
                    S.dma("sp", lambda e: e.dma_start(out=ytok, in_=scr_y))
                    b = nb_par()
                    for j in range(4):
                        A("pe", lambda e, j=j, b=b: e.transpose(out=ps[:, b, j * NS:(j + 1) * NS], in_=ytok[:, j * 128:(j + 1) * 128],
                                                                identity=ident[0:NS, 0:NS]))
                    A("act", lambda e, b=b: e.copy(out=yT[:, :, HALF:HALF + NS], in_=ps[:, b, 0:4 * NS].rearrange("p (j n) -> p j n", j=4)))
                    for g_ in range(4):
                        cs_ = list(range(g_ * 4, min(14, g_ * 4 + 4)))
                        b = nb_par()
                        for ci, c in enumerate(cs_):
                            A("pe", lambda e, ci=ci, c=c, b=b: e.transpose(out=ps[0:NS, b, ci * 128:(ci + 1) * 128], in_=shiftS[:, c, :], identity=ident))
                        n_ = len(cs_) * 128
                        A("act", lambda e, g_=g_, b=b, n_=n_: e.copy(out=shtok[:, g_ * 512:g_ * 512 + n_], in_=ps[0:NS, b, 0:n_]))
                    S.dma("sp", lambda e: e.dma_start(out=o_shift_s, in_=shtok[:, 0:RPROJ]))

            if stop_after == "Rsample":
                continue
            ar_.top = rs_mark if nsamp else mark
            bk_tok = [ar_.alloc(BF16, [128, 2, 512]) for _ in range(2)]
            v_tok = [ar_.alloc(BF16, [128, 512]) for _ in range(2)]
            NM_sb = [ar_.alloc(BF16, [128, 8, 2, 128]) for _ in range(2)]
            P_sb2 = [[ar_.alloc(BF16, [128, 8, 64]) for _ in range(2)] for _ in range(2)]
            Tt_sb2 = [[ar_.alloc(BF16, [128, 8, 64]) for _ in range(1)] for _ in range(2)]
            QT_sb2 = [[ar_.alloc(BF16, [128, 8, 2, 64]) for _ in range(2)] for _ in range(2)]
            X_sb = [ar_.alloc(BF16, [128, 8, 64]) for _ in range(2)]
            U_sb = [ar_.alloc(BF16, [128, 8, 64]) for _ in range(2)]
            XV_sb = [ar_.alloc(F32, [128, 8, 64]) for _ in range(2)]
            Hs = ar_.alloc(F32, [128, 4, 64])
            gtmp = [ar_.alloc(F32, [128, TW]) for _ in range(3)]

            for i in range(8):
                q = i % 2
                P_sb, QT_sb, Tt_sb = P_sb2[q], QT_sb2[q], Tt_sb2[q]
                S.tag = "p%d Rpar%d" % (pas, i)
                b1 = nb_par()
                b2 = nb_par()
                pT1 = ps[:, b1, :].bitcast(BF16).rearrange("p (v n) -> p v n", v=2)
                pT2 = ps[:, b2, :].bitcast(BF16)
                for j in range(4):
                    A("pe", lambda e, i=i, j=j, pT1=pT1: e.transpose(out=pT1[:, 0, j * 128:(j + 1) * 128], in_=bT[:, j, i * 128:(i + 1) * 128],
                                                                     identity=ident_bf[:]), ["bT", "ident_bf"], PB(b1))
                    A("pe", lambda e, i=i, j=j, pT1=pT1: e.transpose(out=pT1[:, 1, j * 128:(j + 1) * 128], in_=kT[:, j, i * 128:(i + 1) * 128],
                                                                     identity=ident_bf[:]), ["kT", "ident_bf"], PB(b1))
                    A("pe", lambda e, i=i, j=j, pT2=pT2: e.transpose(out=pT2[:, j * 128:(j + 1) * 128], in_=vT[:, j, i * 128:(i + 1) * 128],
                                                                     identity=ident_bf[:]), ["vT", "ident_bf"], PB(b2))
                A("act", lambda e, q=q, pT1=pT1: e.copy(out=bk_tok[q][:], in_=pT1), PB(b1), ["bk_tok%d" % q])
                A("dve", lambda e, q=q, pT2=pT2: e.tensor_copy(out=v_tok[q][:], in_=pT2[:, 0:512]), PB(b2), ["v_tok%d" % q])
                if stop_after == "c1":
                    break
                for hg in range(2):
                    b = nb_par(2)
                    for hh4 in range(4):
                        h = hg * 4 + hh4
                        j, hh = h // 2, h % 2
                        for e_ in range(2):
                            c = 2 * i + e_
                            for x, src in ((0, bT), (1, kT)):
                                bb, oo = b + hh, ((hh4 // 2) * 2 + x) * 128
                                A("pe", lambda e, src=src, j=j, hh=hh, c=c, e_=e_, bb=bb, oo=oo: e.matmul(
                                    ps[e_ * 64:(e_ + 1) * 64, bb, oo:oo + 128], lhsT=src[hh * 64:(hh + 1) * 64, j, c * 64:(c + 1) * 64],
                                    rhs=ar_t[hh * 64:(hh + 1) * 64, j, c, :, :], start=True, stop=True),
                                  ["bT", "kT", "ar"], PB(b, 2))
                    for hh in range(2):
                        nmv = NM_sb[q][:, hg * 4 + hh:(hg + 1) * 4:2, :, :]
                        A("dve", lambda e, nmv=nmv, b=b, hh=hh: e.tensor_tensor(out=nmv, in0=ps[:, b + hh, :].rearrange("p (jj x n) -> p jj x n", jj=2, x=2),
                                                                                 in1=maskNM_bf.unsqueeze(1).unsqueeze(1).broadcast_to([128, 2, 2, 128]), op=ALU.mult))
                if stop_after == "c2":
                    break
                b = nb_par(2)
                for h in range(8):
                    j, hh = h // 2, h % 2
                    for e_ in range(2):
                        c = 2 * i + e_
                        A("pe", lambda e, j=j, hh=hh, c=c, e_=e_, h=h, b=b: e.matmul(
                            ps[e_ * 64:(e_ + 1) * 64, b + hh, j * 64:(j + 1) * 64], lhsT=ar_t[hh * 64:(hh + 1) * 64, j, c, 0, :],
                            rhs=bT[hh * 64:(hh + 1) * 64, j, c * 64:(c + 1) * 64], start=True, stop=True))
                for hh in range(2):
                    pv = P_sb[0][:, hh:8:2, :]
                    A("dve", lambda e, pv=pv, b=b, hh=hh: e.tensor_tensor(out=pv, in0=ps[:, b + hh, 0:256].rearrange("p (j s) -> p j s", j=4),
                                                                           in1=maskL_bf.unsqueeze(1).broadcast_to([128, 4, 64]), op=ALU.mult))
                if stop_after == "c3":
                    break
                Q0 = NM_sb[q][:, :, 0, 0:64]
                A("dve", lambda e, q=q: e.tensor_tensor(out=QT_sb[1][:, :, 1, :], in0=NM_sb[q][:, :, 0, 0:64],
                                                          in1=maskI_bf.unsqueeze(1).broadcast_to([128, 8, 64]), op=ALU.add))
                ev_ctr = [0]
                import os as _os3
                EVK = int(_os3.environ.get("EVK", "3"))

                def evac_half(bank, e_, dst_ap, shape_pat, **kw):
                    sl = slice(e_ * 64, (e_ + 1) * 64)
                    src = ps[sl, bank, :].rearrange(shape_pat, **kw)
                    ev_ctr[0] += 1
                    if ev_ctr[0] % EVK != 0:
                        A("act", lambda e: e.copy(out=dst_ap, in_=src))
                    else:
                        A("dve", lambda e: e.tensor_copy(out=dst_ap, in_=src))

                def evac2(bk, dst):
                    for e_ in range(2):
                        sl = slice(e_ * 64, (e_ + 1) * 64)
                        evac_half(bk + e_, e_, dst[sl, :, :], "p (h s) -> p h s", h=8)

                bA = nb_par(2)
                bB = nb_par(2)
                for h in range(8):
                    for e_ in range(2):
                        sl = slice(e_ * 64, (e_ + 1) * 64)
                        A("pe", lambda e, h=h, sl=sl, e_=e_, bA=bA: e.matmul(ps[sl, bA + e_, h * 64:(h + 1) * 64], lhsT=Q0[sl, h, :], rhs=P_sb[0][sl, h, :],
                                                                             start=True, stop=True))
                        A("pe", lambda e, h=h, sl=sl, e_=e_, bB=bB: e.matmul(ps[sl, bB + e_, h * 64:(h + 1) * 64], lhsT=P_sb[0][sl, h, :], rhs=Q0[sl, h, :],
                                                                             start=True, stop=True))
                evac2(bA, P_sb[1])
                for e_ in range(2):
                    sl = slice(e_ * 64, (e_ + 1) * 64)
                    evac_half(bB + e_, e_, QT_sb[1][sl, :, 0, :], "p (h s) -> p h s", h=8)
                Tc = None
                for lev in range(1, 6):
                    pi = lev % 2
                    Pc = P_sb[pi]
                    QTc = QT_sb[pi]
                    QTn = QT_sb[1 - pi]
                    last = (lev == 5)
                    if not last:
                        bA = nb_par(2)
                        for h in range(8):
                            for e_ in range(2):
                                sl = slice(e_ * 64, (e_ + 1) * 64)
                                A("pe", lambda e, h=h, sl=sl, e_=e_, bA=bA, QTc=QTc, Pc=Pc: e.matmul(ps[sl, bA + e_, h * 64:(h + 1) * 64], lhsT=QTc[sl, h, 0, :],
                                                                                                      rhs=Pc[sl, h, :], start=True, stop=True))
                    if not last:
                        for hg in range(2):
                            bB = nb_par(2)
                            for h4 in range(4):
                                h = hg * 4 + h4
                                for e_ in range(2):
                                    sl = slice(e_ * 64, (e_ + 1) * 64)
                                    A("pe", lambda e, h=h, h4=h4, sl=sl, e_=e_, bB=bB, QTc=QTc, Pc=Pc: e.matmul(
                                        ps[sl, bB + e_, h4 * 128:(h4 + 1) * 128], lhsT=Pc[sl, h, :], rhs=QTc[sl, h, :, :], start=True, stop=True))
                            for e_ in range(2):
                                sl = slice(e_ * 64, (e_ + 1) * 64)
                                src4 = ps[sl, bB + e_, :].rearrange("p (h x s) -> p h x s", h=4, x=2)
                                hsl = slice(hg * 4, (hg + 1) * 4)
                                A("act", lambda e, sl=sl, src4=src4, hsl=hsl, QTn=QTn: e.copy(out=QTn[sl, hsl, 0, :], in_=src4[:, :, 0, :]))
                                A("dve", lambda e, sl=sl, src4=src4, hsl=hsl, QTn=QTn, QTc=QTc: e.tensor_tensor(out=QTn[sl, hsl, 1, :], in0=src4[:, :, 1, :],
                                                                                                                 in1=QTc[sl, hsl, 1, :], op=ALU.add))
                        evac2(bA, P_sb[1 - pi])
                    else:
                        bB = nb_par(2)
                        for h in range(8):
                            for e_ in range(2):
                                sl = slice(e_ * 64, (e_ + 1) * 64)
                                A("pe", lambda e, h=h, sl=sl, e_=e_, bB=bB, QTc=QTc, Pc=Pc: e.matmul(
                                    ps[sl, bB + e_, h * 64:(h + 1) * 64], lhsT=Pc[sl, h, :], rhs=QTc[sl, h, 1, :], start=True, stop=True))
                        Tc = Tt_sb[0]
                        for e_ in range(2):
                            sl = slice(e_ * 64, (e_ + 1) * 64)
                            A("dve", lambda e, sl=sl, e_=e_, bB=bB, QTc=QTc, Tc=Tc: e.tensor_tensor(out=Tc[sl, :, :], in0=ps[sl, bB + e_, :].rearrange("p (h s) -> p h s", h=8),
                                                                                                     in1=QTc[sl, :, 1, :], op=ALU.add))
                bXV = nb_par(2)
                for h in range(8):
                    for e_ in range(2):
                        sl = slice(e_ * 64, (e_ + 1) * 64)
                        A("pe", lambda e, h=h, sl=sl, e_=e_, q=q, bXV=bXV: e.matmul(ps[sl, bXV + e_, h * 64:(h + 1) * 64], lhsT=NM_sb[q][sl, h, 1, 0:64],
                                                                                    rhs=v_tok[q][sl, h * 64:(h + 1) * 64], start=True, stop=True))
                evac2(bXV, XV_sb[q])
                if stop_after == "c4":
                    break
                S.tag = "p%d Rseq%d" % (pas, i)
                for e_ in range(2):
                    c = 2 * i + e_
                    sl = slice(e_ * 64, (e_ + 1) * 64)
                    xq = c % 2
                    bX = nb_seq(e_)
                    for j in range(4):
                        A("pe", lambda e, j=j, sl=sl, c=c, bX=bX: e.matmul(ps[sl, bX, j * 128:(j + 1) * 128], lhsT=ar_t[:, j, c, 0, :],
                                                                           rhs=H0bd[:, j, :], start=True, stop=True))
                    A("dve", lambda e, sl=sl, xq=xq, bX=bX, q=q: e.tensor_tensor(out=X_sb[xq][sl, :, :], in0=ps[sl, bX, :].rearrange("p (h s) -> p h s", h=8),
                                                                                  in1=XV_sb[q][sl, :, :], op=ALU.add))
                    bU = nb_seq(e_)
                    for h in range(8):
                        A("pe", lambda e, h=h, sl=sl, xq=xq, bU=bU, Tc=Tc: e.matmul(ps[sl, bU, h * 64:(h + 1) * 64], lhsT=Tc[sl, h, :],
                                                                                    rhs=X_sb[xq][sl, h, :], start=True, stop=True),
                          [], PB(bU))
                    A("dve", lambda e, sl=sl, xq=xq, bU=bU: e.tensor_copy(out=U_sb[xq][sl, :, :], in_=ps[sl, bU, :].rearrange("p (h s) -> p h s", h=8)),
                      PB(bU), ["U%d" % xq])
                    bY1 = nb_seq(1 - e_)
                    for j in range(4):
                        A("pe", lambda e, j=j, c=c, bY1=bY1: e.matmul(ps[:, bY1, j * 64:(j + 1) * 64], lhsT=H0bd[:, j, :],
                                                                      rhs=ar_t[:, j, c, 1, :], start=True, stop=True))
                    A("act", lambda e, c=c, bY1=bY1: e.copy(out=yT[:, :, c * 64:(c + 1) * 64], in_=ps[:, bY1, 0:256].rearrange("p (j t) -> p j t", j=4)))
                    bY = nb_seq(e_)
                    for h in range(8):
                        j, hh = h // 2, h % 2
                        hs = slice(hh * 64, (hh + 1) * 64)
                        A("pe", lambda e, h=h, j=j, hs=hs, sl=sl, xq=xq, q=q, bY=bY: e.matmul(ps[hs, bY, j * 64:(j + 1) * 64], lhsT=U_sb[xq][sl, h, :],
                                                                                              rhs=NM_sb[q][sl, h, 0, 64:128], start=True, stop=False))
                        A("pe", lambda e, h=h, j=j, hs=hs, sl=sl, q=q, bY=bY: e.matmul(ps[hs, bY, j * 64:(j + 1) * 64], lhsT=v_tok[q][sl, h * 64:(h + 1) * 64],
                                                                                       rhs=NM_sb[q][sl, h, 1, 64:128], start=False, stop=True))
                    A("dve", lambda e, c=c, bY=bY: e.tensor_tensor(out=yT[:, :, c * 64:(c + 1) * 64], in0=ps[:, bY, 0:256].rearrange("p (j t) -> p j t", j=4),
                                                                    in1=yT[:, :, c * 64:(c + 1) * 64], op=ALU.add))
                    bG = nb_seq(e_)
                    for h in range(8):
                        j, hh = h // 2, h % 2
                        hs = slice(hh * 64, (hh + 1) * 64)
                        A("pe", lambda e, h=h, j=j, hs=hs, sl=sl, xq=xq, q=q, bG=bG: e.matmul(ps[hs, bG, j * 64:(j + 1) * 64], lhsT=bk_tok[q][sl, 0, h * 64:(h + 1) * 64],
                                                                                              rhs=U_sb[xq][sl, h, :], start=True, stop=False),
                          ["bk_tok%d" % q, "U%d" % xq], PB(bG))
                        A("pe", lambda e, h=h, j=j, hs=hs, sl=sl, q=q, bG=bG: e.matmul(ps[hs, bG, j * 64:(j + 1) * 64], lhsT=bk_tok[q][sl, 1, h * 64:(h + 1) * 64],
                                                                                       rhs=v_tok[q][sl, h * 64:(h + 1) * 64], start=False, stop=True),
                          ["bk_tok%d" % q, "v_tok%d" % q], PB(bG))
                    A("dve", lambda e, bG=bG: e.tensor_tensor(out=Hs[:], in0=ps[:, bG, 0:256].rearrange("p (j v) -> p j v", j=4), in1=H0f[:], op=ALU.add),
                      PB(bG) + ["H0f"], ["Hs"])
                    A("dve", lambda e, c=c: e.tensor_tensor(out=H0f[:], in0=Hs[:], in1=WC[:, :, c:c + 1].broadcast_to([128, 4, 64]), op=ALU.mult),
                      ["Hs", "WC"], ["H0f"])
                    A("act", lambda e: e.copy(out=H0bd[0:64, :, 0:64], in_=H0f[0:64, :, :]), ["H0f"], ["H0b"])
                    A("act", lambda e: e.copy(out=H0bd[64:128, :, 64:128], in_=H0f[64:128, :, :]), ["H0f"], ["H0b"])
            if stop_after in ("c1", "c2", "c3", "c4"):
                continue
            if dbg and pas == 0:
                dump("yT", yT, ["yT"])
            if stop_after == "Rchunk":
                continue
            if nsamp:
                S.tag = "p%d Rsample" % pas
                emit_rsample()
            S.tag = "p%d Rpost" % pas
            for j in range(4):
                yj = yT[:, j, :]
                yc, sq_, rs_ = gtmp[0], gtmp[1], gtmp[2]

                def ev_mean(b, c0, c1, j=j):
                    A("dve", lambda e: e.scalar_tensor_tensor(out=yc[:, c0:c1], in0=ps[:, b, 0:c1 - c0], scalar=-1.0 / 64, in1=yT[:, j, c0:c1],
                                                              op0=ALU.mult, op1=ALU.add), PB(b) + ["yT"], ["gtmp0"])
                fp32_blocksum(yj, "yT", blk64, ev_mean)
                A("act", lambda e: e.activation(out=sq_[:, 0:W], in_=yc[:, 0:W], func=AF.Square), ["gtmp0"], ["gtmp1"])

                def ev_var(b, c0, c1):
                    A("act", lambda e: e.activation(out=rs_[:, c0:c1], in_=ps[:, b, 0:c1 - c0], func=AF.Ln, scale=1.0 / 64, bias=epsgn[:, 0:1]))
                fp32_blocksum(sq_, "gtmp1", blk64, ev_var)
                A("act", lambda e: e.activation(out=rs_[:, 0:W], in_=rs_[:, 0:W], func=AF.Exp, scale=-0.5), ["gtmp2"], ["gtmp2"])
                A("dve", lambda e: e.tensor_tensor(out=yc[:, 0:W], in0=yc[:, 0:W], in1=rs_[:, 0:W], op=ALU.mult), ["gtmp0", "gtmp2"], ["gtmp0"])
                A("dve", lambda e, j=j: e.tensor_scalar(out=yc[:, 0:W], in0=yc[:, 0:W], scalar1=pf("ln_g", j), scalar2=pf("ln_b", j),
                                                        op0=ALU.mult, op1=ALU.add), ["gtmp0", "pfm"], ["gtmp0"])
                A("dve", lambda e, j=j: e.tensor_tensor(out=yc[:, 0:W], in0=yc[:, 0:W], in1=bonus[:, j, 0:W], op=ALU.add), ["gtmp0", "bonus"], ["gtmp0"])
                A("dve", lambda e, j=j: e.tensor_tensor(out=oa[:, j, 0:W], in0=yc[:, 0:W], in1=g_bf[:, j, 0:W], op=ALU.mult), ["gtmp0", "g_bf"], ["oa"])
            if dbg and pas == 0:
                dump("oa", oa[:], ["oa"])
            if pas == passes[-1]:
                wst = gtmp[0][0:64, 0:512].rearrange("p (j n) -> p j n", j=4)
                b = nb()
                for j in range(4):
                    A("pe", lambda e, j=j, b=b: e.transpose(out=ps[0:64, b, j * 128:(j + 1) * 128], in_=H0f[:, j, :], identity=ident),
                      ["H0f", "consts"], PB(b))
                A("act", lambda e, b=b: e.copy(out=wst, in_=ps[0:64, b, :].rearrange("p (j n) -> p j n", j=4)), PB(b), ["gtmp0"])
                S.dma("sp", lambda e: e.dma_start(out=o_wkv_p.rearrange("(j hh) v k -> v j hh k", hh=2),
                                                   in_=wst.rearrange("p j (hh k) -> p j hh k", hh=2)), reads=["gtmp0"])
                b = nb()
                A("pe", lambda e, b=b: e.transpose(out=ps[0:14, b, 0:128], in_=carry[:, :], identity=ident), ["carry", "consts"], PB(b))
                A("act", lambda e, b=b: e.copy(out=gtmp[1][0:14, 0:128], in_=ps[0:14, b, 0:128]), PB(b), ["gtmp1"])
                S.dma("sp", lambda e: e.dma_start(out=o_shift_p.rearrange("(c p) -> c p", p=128), in_=gtmp[1][0:14, 0:128]), reads=["gtmp1"])
            if stop_after == "Rpost":
                continue

            S.tag = "p%d H" % pas
            S.barrier()
            ar_ = Arena(arena_t, ARENA_WORDS)
            Eb = ar_.alloc(F32, [128, 4, HALF])
            qT = ar_.alloc(BF16, [128, 4, HALF])
            hkT = ar_.alloc(BF16, [128, 4, HALF])
            hvT = ar_.alloc(BF16, [128, 4, TW])
            sgo = ar_.alloc(BF16, [128, 4, TW])
            oT = ar_.alloc(F32, [128, 4, TW])
            smpH = ar_.alloc(F32, [128, 4, 4, NS])
            hmark = ar_.top
            htmp = [ar_.alloc(F32, [128, TW]) for _ in range(8)]
            hset = [htmp[0:4], htmp[4:8]]
            cmaskH = ar_.alloc(BF16, [128, HALF])
            S.dma("pool", lambda e: e.dma_start(out=cmaskH, in_=consts_d[:, 640:1664]), writes=["cmaskH"])
            HB = RPROJ
            wv, wr = load_w(w_in_v[:, :, HB + 512:HB + 1024], (8, 512))
            hvh = [(0, 512), (512, W)]
            for h in range(4):
                T0, T1_, T2, T3 = hset[h % 2]

                def ev_f(b, c0, c1, T0=T0):
                    A("act", lambda e: e.activation(out=T0[:, c0:c1], in_=ps[:, b, 0:c1 - c0], func=AF.Sigmoid))
                proj_fm(wv, wr, h * 128, 8, hT, "hT", ev_f)
                for (a_, b_) in hvh:
                    b2 = min(b_, HALF)
                    A("dve", lambda e, h=h, a_=a_, b_=b_: e.tensor_scalar(out=T0[:, a_:b_], in0=T0[:, a_:b_], scalar1=dvc("omlb", h), scalar2=dvc("lb", h),
                                                                          op0=ALU.mult, op1=ALU.add))
                    A("dve", lambda e, a_=a_, b_=b_: e.tensor_scalar(out=T1_[:, a_:b_], in0=T0[:, a_:b_], scalar1=-1.0, scalar2=1.0, op0=ALU.mult, op1=ALU.add))
                    A("act", lambda e, a_=a_, b2=b2: e.activation(out=T2[:, a_:b2], in_=T0[:, a_:b2], func=AF.Ln))
                    A("dve", lambda e, a_=a_, b2=b2: e.tensor_tensor_scan(out=T3[:, a_:b2], data0=cmaskH[:, a_:b2], data1=T2[:, a_:b2], initial=0.0,
                                                                          op0=ALU.mult, op1=ALU.add))
                    A("act", lambda e, h=h, a_=a_, b2=b2: e.activation(out=Eb[:, h, a_:b2], in_=T3[:, a_:b2], func=AF.Exp))
                    A("act", lambda e, h=h, a_=a_, b2=b2: e.activation(out=DC[:, h, a_ // 64:b2 // 64], in_=T3[:, a_:b2].rearrange("p (c t) -> p c t", t=64)[:, :, 63],
                                                                       func=AF.Exp))
                    A("act", lambda e, a_=a_, b2=b2: e.activation(out=T2[:, a_:b2], in_=T3[:, a_:b2], func=AF.Exp, scale=-1.0))
                    A("dve", lambda e, h=h, a_=a_, b2=b2: e.tensor_tensor(out=hkT[:, h, a_:b2], in0=T1_[:, a_:b2], in1=T2[:, a_:b2], op=ALU.mult))
                if nsamp:
                    A("act", lambda e, h=h: e.copy(out=smpH[:, 1, h, :], in_=T0[:, HALF:HALF + NS]))
                    A("act", lambda e, h=h: e.copy(out=smpH[:, 2, h, :], in_=T1_[:, HALF:HALF + NS]))
            wv, wr = load_w(w_in_v[:, :, HB:HB + 512], (8, 512))
            for h in range(4):
                T0 = hset[h % 2][0]

                def ev_q(b, c0, c1, T0=T0, h=h):
                    A("act", lambda e: e.activation(out=T0[:, c0:c1], in_=ps[:, b, 0:c1 - c0], func=AF.Silu))
                    if c0 < HALF:
                        A("dve", lambda e: e.tensor_tensor(out=qT[:, h, c0:c1], in0=T0[:, c0:c1], in1=Eb[:, h, c0:c1], op=ALU.mult))
                proj_fm(wv, wr, h * 128, 8, hT, "hT", ev_q)
                if nsamp:
                    A("act", lambda e, h=h: e.copy(out=smpH[:, 0, h, :], in_=T0[:, HALF:HALF + NS]))
            wv, wr = load_w(w_in_v[:, :, HB + 1024:HB + 1536], (8, 512))
            for h in range(4):
                def ev_i(b, c0, c1, h=h):
                    A("dve", lambda e: e.tensor_copy(out=hvT[:, h, c0:c1], in_=ps[:, b, 0:c1 - c0]), PB(b), ["hvT"])
                    if c0 >= HALF:
                        A("dve", lambda e: e.tensor_copy(out=smpH[:, 3, h, :], in_=ps[:, b, 0:NS]), PB(b), ["smpH"])
                proj_fm(wv, wr, h * 128, 8, hT, "hT", ev_i)
            wv, wr = load_w(w_in_v[:, :, HB + 1536:HB + 2048], (8, 512))
            for h in range(4):
                def ev_og(b, c0, c1, h=h):
                    A("act", lambda e: e.activation(out=sgo[:, h, c0:c1], in_=ps[:, b, 0:c1 - c0], func=AF.Sigmoid), PB(b), ["sgo"])
                proj_fm(wv, wr, h * 128, 8, hT, "hT", ev_og)

            if nsamp:
                S.barrier()
                ar_.top = hmark
                S_s = ar_.alloc(F32, [128, NS, 4, 128])
                ktok_s = ar_.alloc(F32, [NS, 512])
                vtok_s = ar_.alloc(F32, [NS, 512])
                vm = [ar_.alloc(F32, [NS, 512]) for _ in range(2)]
                tS_l = [ar_.alloc(F32, [128, 4, 128]) for _ in range(3)]
                q_bf = ar_.alloc(BF16, [128, 4, NS])
                Sb_l = [ar_.alloc(BF16, [128, 4, 128]) for _ in range(3)]
                S.dma("sp", lambda e: e.dma_start(out=S_s, in_=hgrn_s.rearrange("b h k v -> k b h v")), writes=["S_s"])
                for vec, dst, dn in ((2, ktok_s, "ktok_s"), (3, vtok_s, "vtok_s")):
                    b = nb_smp()
                    for h in range(4):
                        A("pe", lambda e, vec=vec, h=h, b=b: e.transpose(out=ps[0:NS, b, h * 128:(h + 1) * 128], in_=smpH[:, vec, h, :], identity=ident),
                          ["smpH", "consts"], PB(b))
                    A("act", lambda e, dst=dst, b=b: e.copy(out=dst, in_=ps[0:NS, b, :]), PB(b), [dn])
                for bi in range(NS):
                    vq = bi % 2
                    tS = tS_l[bi % 3]
                    A("dve", lambda e, bi=bi, vq=vq: e.tensor_scalar(out=vm[vq], in0=vtok_s, scalar1=ident[0:NS, bi:bi + 1], scalar2=None, op0=ALU.mult),
                      ["vtok_s", "consts"], ["vm%d" % vq])
                    b = nb_smp()
                    for h in range(4):
                        A("pe", lambda e, h=h, vq=vq, b=b: e.matmul(ps[:, b, h * 128:(h + 1) * 128], lhsT=ktok_s[:, h * 128:(h + 1) * 128],
                                                                    rhs=vm[vq][:, h * 128:(h + 1) * 128], start=True, stop=True),
                          ["ktok_s", "vm%d" % vq], PB(b))
                    A("dve", lambda e, bi=bi: e.tensor_tensor(out=tS, in0=S_s[:, bi, :, :],
                                                               in1=smpH[:, 1, :, bi:bi + 1].broadcast_to([128, 4, 128]), op=ALU.mult),
                      ["S_s", "smpH"], ["tS"])
                    A("dve", lambda e, bi=bi, b=b: e.tensor_tensor(out=S_s[:, bi, :, :], in0=ps[:, b, :].rearrange("p (h v) -> p h v", h=4), in1=tS,
                                                                    op=ALU.add), PB(b) + ["tS"], ["S_s"])
                S.dma("sp", lambda e: e.dma_start(out=o_hgrn_s.rearrange("b h k v -> k b h v"), in_=S_s), reads=["S_s"])
                A("act", lambda e: e.copy(out=q_bf, in_=smpH[:, 0, :, :]))
                bO_ = nb_smp()
                for bi in range(NS):
                    Sb = Sb_l[bi % 3]
                    A("act", lambda e, bi=bi, Sb=Sb: e.copy(out=Sb, in_=S_s[:, bi, :, :]))
                    for h in range(4):
                        A("pe", lambda e, h=h, bi=bi, Sb=Sb, bO_=bO_: e.matmul(ps[:, bO_, h * NS + bi:h * NS + bi + 1], lhsT=Sb[:, h, :],
                                                                                rhs=q_bf[:, h, bi:bi + 1], start=True, stop=True))
                A("act", lambda e, bO_=bO_: e.copy(out=oT[:, :, HALF:HALF + NS], in_=ps[:, bO_, 0:4 * NS].rearrange("p (h n) -> p h n", h=4)))

            if not nsamp:
                ar_.top = hmark
            par_nb[0] = 4 if nsamp else 6
            par_ctr[0] = 0
            hk_tok = [ar_.alloc(BF16, [128, 512]) for _ in range(2)]
            hv_tok = [ar_.alloc(BF16, [128, 512]) for _ in range(2)]
            PT_sb = [ar_.alloc(BF16, [128, 4, 64]) for _ in range(2)]
            Ss = ar_.alloc(F32, [128, 4, 128])
            ar_.top = hmark
            htmp = [ar_.alloc(F32, [128, TW]) for _ in range(2)]
            for i in range(8):
                q = i % 2
                b1 = nb_par()
                pTk = ps[:, b1, :].bitcast(BF16).rearrange("p (v n) -> p v n", v=2)
                for h in range(4):
                    A("pe", lambda e, i=i, h=h, pTk=pTk: e.transpose(out=pTk[:, 0, h * 128:(h + 1) * 128], in_=hkT[:, h, i * 128:(i + 1) * 128],
                                                                     identity=ident_bf[:]), ["hkT", "ident_bf"], PB(b1))
                    A("pe", lambda e, i=i, h=h, pTk=pTk: e.transpose(out=pTk[:, 1, h * 128:(h + 1) * 128], in_=hvT[:, h, i * 128:(i + 1) * 128],
                                                                     identity=ident_bf[:]), ["hvT", "ident_bf"], PB(b1))
                A("act", lambda e, q=q, pTk=pTk: e.copy(out=hk_tok[q], in_=pTk[:, 0, :]), PB(b1), ["hk_tok%d" % q])
                A("dve", lambda e, q=q, pTk=pTk: e.tensor_copy(out=hv_tok[q], in_=pTk[:, 1, :]), PB(b1), ["hv_tok%d" % q])
                bS = nb_par()
                for h in range(4):
                    for e_ in range(2):
                        c = 2 * i + e_
                        A("pe", lambda e, h=h, e_=e_, c=c, bS=bS: e.matmul(ps[e_ * 64:(e_ + 1) * 64, bS, h * 64:(h + 1) * 64], lhsT=hkT[:, h, c * 64:(c + 1) * 64],
                                                                           rhs=qT[:, h, c * 64:(c + 1) * 64], start=True, stop=True), ["hkT", "qT"], PB(bS))
                A("dve", lambda e, q=q, bS=bS: e.tensor_tensor(out=PT_sb[q], in0=ps[:, bS, 0:256].rearrange("p (h t) -> p h t", h=4),
                                                                in1=masks_bf[:, 64:128].unsqueeze(1).broadcast_to([128, 4, 64]), op=ALU.mult),
                  PB(bS) + ["consts"], ["PT%d" % q])
                bO = nb_par()
                for e_ in range(2):
                    c = 2 * i + e_
                    sl = slice(e_ * 64, (e_ + 1) * 64)
                    for h in range(4):
                        oo = h * 128 + e_ * 64
                        A("pe", lambda e, h=h, c=c, oo=oo, bO=bO: e.matmul(ps[:, bO, oo:oo + 64], lhsT=S0b[:, h, :], rhs=qT[:, h, c * 64:(c + 1) * 64],
                                                                           start=True, stop=False), ["S0b", "qT"], PB(bO))
                        A("pe", lambda e, h=h, sl=sl, q=q, oo=oo, bO=bO: e.matmul(ps[:, bO, oo:oo + 64], lhsT=hv_tok[q][sl, h * 128:(h + 1) * 128],
                                                                                  rhs=PT_sb[q][sl, h, :], start=False, stop=True),
                          ["hv_tok%d" % q, "PT%d" % q], PB(bO))
                    bG = nb_seq(e_)
                    for h in range(4):
                        A("pe", lambda e, h=h, sl=sl, q=q, bG=bG: e.matmul(ps[:, bG, h * 128:(h + 1) * 128], lhsT=hk_tok[q][sl, h * 128:(h + 1) * 128],
                                                                           rhs=hv_tok[q][sl, h * 128:(h + 1) * 128], start=True, stop=True),
                          ["hk_tok%d" % q, "hv_tok%d" % q], PB(bG))
                    A("dve", lambda e, bG=bG: e.tensor_tensor(out=Ss, in0=ps[:, bG, :].rearrange("p (h v) -> p h v", h=4), in1=S0f[:], op=ALU.add),
                      PB(bG) + ["S0f"], ["Ss"])
                    A("dve", lambda e, c=c: e.tensor_tensor(out=S0f[:], in0=Ss, in1=DC[:, :, c:c + 1].broadcast_to([128, 4, 128]), op=ALU.mult),
                      ["Ss", "DC"], ["S0f"])
                    A("act", lambda e: e.copy(out=S0b[:], in_=S0f[:]), ["S0f"], ["S0b"])
                A("act", lambda e, i=i, bO=bO: e.copy(out=oT[:, :, i * 128:(i + 1) * 128], in_=ps[:, bO, :].rearrange("p (h t) -> p h t", h=4)),
                  PB(bO), ["oT"])
            par_nb[0] = 6
            if dbg and pas == 0:
                dump("oT", oT, ["oT"])
            for h in range(4):
                A("act", lambda e, h=h: e.activation(out=htmp[0][:, 0:W], in_=oT[:, h, 0:W], func=AF.Square), ["oT"], ["htmp0"])

                def ev_ms(b, c0, c1):
                    A("act", lambda e: e.activation(out=htmp[1][:, c0:c1], in_=ps[:, b, 0:c1 - c0], func=AF.Ln, scale=1.0 / 128, bias=epsrms[:, 0:1]))
                fp32_blocksum(htmp[0], "htmp0", ones, ev_ms)
                A("act", lambda e: e.activation(out=htmp[1][:, 0:W], in_=htmp[1][:, 0:W], func=AF.Exp, scale=-0.5), ["htmp1"], ["htmp1"])
                A("dve", lambda e, h=h: e.tensor_tensor(out=htmp[0][:, 0:W], in0=oT[:, h, 0:W], in1=htmp[1][:, 0:W], op=ALU.mult), ["oT", "htmp1"], ["htmp0"])
                A("dve", lambda e, h=h: e.scalar_tensor_tensor(out=ob[:, h, 0:W], in0=htmp[0][:, 0:W], scalar=pf("hng", h), in1=sgo[:, h, 0:W],
                                                               op0=ALU.mult, op1=ALU.mult), ["htmp0", "pfm", "sgo"], ["ob"])
            if dbg and pas == 0:
                dump("ob", ob[:], ["ob"])
            if pas == passes[-1]:
                S.dma("sp", lambda e: e.dma_start(out=o_hgrn_p.rearrange("h k v -> k h v"), in_=S0f[:]), reads=["S0f"])
            if stop_after == "H":
                continue

            S.barrier()
            ar_ = Arena(arena_t, ARENA_WORDS)
            x_tok = ar_.alloc(F32, [128, 8, 1024])
            xs_tok = ar_.alloc(F32, [NS, 1024])
            mergedT = ar_.alloc(BF16, [128, 8, TW])
            gm_all = [[ar_.alloc(F32, [128, 512]) for _ in range(4)] for _ in range(2)]
            gm_ctr = [0]
            GB = RPROJ + 2048
            w_upa_v = w_up_a.rearrange("(kc p) n -> p kc n", p=128)
            w_upb_v = w_up_b.rearrange("(kc p) n -> p kc n", p=128)
            for dcg in range(2):
                wga, wgar = load_w(w_in_v[:, :, GB + dcg * 512:GB + (dcg + 1) * 512], (8, 512))
                wgb, wgbr = load_w(w_in_v[:, :, GB + 1024 + dcg * 512:GB + 1024 + (dcg + 1) * 512], (8, 512))
                wu, wur = load_w(w_upa_v[:, :, dcg * 512:(dcg + 1) * 512], (4, 512))
                iu_ = (wctr[0] - 1) % NWB
                wub_view = wbuf[iu_][:, 4:8, :]
                S.dma("pool", lambda e, wub_view=wub_view, dcg=dcg: e.dma_start(out=wub_view, in_=w_upb_v[:, :, dcg * 512:(dcg + 1) * 512]), writes=[wur])
                for dc in range(4):
                    n0 = dc * 128
                    for (c0, c1) in blocks:
                        w_ = c1 - c0
                        gm = gm_all[gm_ctr[0] % 2]
                        gm_ctr[0] += 1
                        b1, b2, b3, b4 = nb(), nb(), nb(), nb()
                        for kc in range(8):
                            A("pe", lambda e, kc=kc, b1=b1, c0=c0, c1=c1, n0=n0, wga=wga: e.matmul(ps[:, b1, 0:c1 - c0], lhsT=wga[:, kc, n0:n0 + 128],
                                                                                                   rhs=hT[:, kc, c0:c1], start=(kc == 0), stop=(kc == 7)),
                              [wgar, "hT"], PB(b1))
                        for kc in range(8):
                            A("pe", lambda e, kc=kc, b2=b2, c0=c0, c1=c1, n0=n0, wgb=wgb: e.matmul(ps[:, b2, 0:c1 - c0], lhsT=wgb[:, kc, n0:n0 + 128],
                                                                                                   rhs=hT[:, kc, c0:c1], start=(kc == 0), stop=(kc == 7)),
                              [wgbr, "hT"], PB(b2))
                        for kc in range(4):
                            A("pe", lambda e, kc=kc, b3=b3, c0=c0, c1=c1, n0=n0, wu=wu: e.matmul(ps[:, b3, 0:c1 - c0], lhsT=wu[:, kc, n0:n0 + 128],
                                                                                                 rhs=oa[:, kc, c0:c1], start=(kc == 0), stop=(kc == 3)),
                              [wur, "oa"], PB(b3))
                        for kc in range(4):
                            A("pe", lambda e, kc=kc, b4=b4, c0=c0, c1=c1, n0=n0, wub_view=wub_view: e.matmul(ps[:, b4, 0:c1 - c0], lhsT=wub_view[:, kc, n0:n0 + 128],
                                                                                                             rhs=ob[:, kc, c0:c1], start=(kc == 0), stop=(kc == 3)),
                              [wur, "ob"], PB(b4))
                        A("act", lambda e, b1=b1, w_=w_: e.activation(out=gm[0][:, 0:w_], in_=ps[:, b1, 0:w_], func=AF.Sigmoid), PB(b1), ["gm0"])
                        A("act", lambda e, b2=b2, w_=w_: e.activation(out=gm[1][:, 0:w_], in_=ps[:, b2, 0:w_], func=AF.Sigmoid), PB(b2), ["gm1"])
                        A("dve", lambda e, b3=b3, w_=w_: e.tensor_tensor(out=gm[2][:, 0:w_], in0=ps[:, b3, 0:w_], in1=gm[0][:, 0:w_], op=ALU.mult),
                          PB(b3) + ["gm0"], ["gm2"])
                        A("dve", lambda e, b4=b4, w_=w_: e.tensor_tensor(out=gm[3][:, 0:w_], in0=ps[:, b4, 0:w_], in1=gm[1][:, 0:w_], op=ALU.mult),
                          PB(b4) + ["gm1"], ["gm3"])
                        A("dve", lambda e, dcg=dcg, dc=dc, c0=c0, c1=c1, w_=w_: e.tensor_tensor(out=mergedT[:, dcg * 4 + dc, c0:c1], in0=gm[2][:, 0:w_],
                                                                                                 in1=gm[3][:, 0:w_], op=ALU.add),
                          ["gm2", "gm3"], ["mergedT"])
            if dbg and pas == 0:
                dump("mergedT", mergedT, ["mergedT"])
            if stop_after == "G":
                continue

            S.barrier()
            ar_.top = 0
            x_tok = ar_.alloc(F32, [128, 8, 1024])
            xs_tok = ar_.alloc(F32, [NS, 1024])
            mergedT = ar_.alloc(BF16, [128, 8, TW])
            for tt in range(8):
                S.dma("sp", lambda e, tt=tt: e.dma_start(out=x_tok[:, tt, :], in_=x_p[t0 + tt * 128:t0 + (tt + 1) * 128, :]),
                      writes=["x%d" % tt])
            if nsamp:
                S.dma("sp", lambda e: e.dma_start(out=xs_tok, in_=x_s), writes=["xs"])
            w_out_v = w_out.rearrange("(kc p) n -> p kc n", p=128)
            wos = [load_w(w_out_v[:, :, half * 512:(half + 1) * 512], (8, 512))[0] for half in range(2)]
            for tt in range(8):
                for half in range(2):
                    wo = wos[half]
                    b = nb()
                    for kc in range(8):
                        A("pe", lambda e, kc=kc, tt=tt, b=b, wo=wo: e.matmul(ps[:, b, :], lhsT=mergedT[:, kc, tt * 128:(tt + 1) * 128], rhs=wo[:, kc, :],
                                                                             start=(kc == 0), stop=(kc == 7)))
                    A("dve", lambda e, tt=tt, half=half, b=b: e.tensor_tensor(out=x_tok[:, tt, half * 512:(half + 1) * 512], in0=ps[:, b, :],
                                                                               in1=x_tok[:, tt, half * 512:(half + 1) * 512], op=ALU.add))
            if nsamp:
                for half in range(2):
                    wo = wos[half]
                    b = nb()
                    for kc in range(8):
                        A("pe", lambda e, kc=kc, b=b, wo=wo: e.matmul(ps[0:NS, b, :], lhsT=mergedT[:, kc, HALF:HALF + NS], rhs=wo[:, kc, :],
                                                                      start=(kc == 0), stop=(kc == 7)))
                    A("dve", lambda e, half=half, b=b: e.tensor_tensor(out=xs_tok[:, half * 512:(half + 1) * 512], in0=ps[0:NS, b, :],
                                                                        in1=xs_tok[:, half * 512:(half + 1) * 512], op=ALU.add))
            norm_phase("O", x_tok, xs_tok, g_ffn, ar_,
                       lambda gB, h_tok, hs_tok, R: to_hT(gB, h_tok, hs_tok, R, x_tok, xs_tok, lambda tt: "x%d" % tt, "xs"),
                       lambda tt: "x%d" % tt, "xs")
            if dbg and pas == 0:
                dump("hT2", hT[:], ["hT"])
            if stop_after == "O":
                continue

            S.barrier()
            ar_.top = 0
            x_tok = ar_.alloc(F32, [128, 8, 1024])
            xs_tok = ar_.alloc(F32, [NS, 1024])
            actT = ar_.alloc(BF16, [128, 22, TW])
            wdn = ar_.alloc(BF16, [128, 22, 1024])
            gBf = ftmp[:, :, :].rearrange("p a b -> p (a b)")
            junkD = ar_.alloc(BF16, [128, 1024])
            w_fd_v = w_fd.rearrange("(fc p) n -> p fc n", p=128)
            w_fg_v = w_fg.rearrange("(kc p) n -> p kc n", p=128)
            w_fu_v = w_fu.rearrange("(kc p) n -> p kc n", p=128)
            for fg in range(11):
                wg, wgr = load_w(w_fg_v[:, :, fg * 256:(fg + 1) * 256], (8, 256))
                wu, wur = load_w(w_fu_v[:, :, fg * 256:(fg + 1) * 256], (8, 256))
                if fg in (2, 4, 6, 8):
                    k_ = (fg - 2) // 2
                    lo, hi = (0, 6, 12, 18)[k_], (6, 12, 18, 22)[k_]
                    S.dma("pool", lambda e, lo=lo, hi=hi: e.dma_start(out=wdn[:, lo:hi, :], in_=w_fd_v[:, lo:hi, :]), writes=["wdn%d" % k_])
                for f2 in range(2):
                    fc = fg * 2 + f2
                    n0 = f2 * 128
                    for (c0, c1) in blocks:
                        w_ = c1 - c0
                        b1, b2 = nb(), nb()
                        for kc in range(8):
                            A("pe", lambda e, kc=kc, b1=b1, c0=c0, c1=c1, n0=n0, wg=wg: e.matmul(ps[:, b1, 0:c1 - c0], lhsT=wg[:, kc, n0:n0 + 128],
                                                                                                 rhs=hT[:, kc, c0:c1], start=(kc == 0), stop=(kc == 7)),
                              [wgr, "hT"], PB(b1))
                        for kc in range(8):
                            A("pe", lambda e, kc=kc, b2=b2, c0=c0, c1=c1, n0=n0, wu=wu: e.matmul(ps[:, b2, 0:c1 - c0], lhsT=wu[:, kc, n0:n0 + 128],
                                                                                                 rhs=hT[:, kc, c0:c1], start=(kc == 0), stop=(kc == 7)),
                              [wur, "hT"], PB(b2))
                        fq = (fc + (c0 // 512)) % 2
                        A("act", lambda e, b1=b1, w_=w_, fq=fq: e.activation(out=ftmp[:, fq, 0:w_], in_=ps[:, b1, 0:w_], func=AF.Silu), PB(b1), ["ftmp%d" % fq])
                        A("dve", lambda e, b2=b2, w_=w_, fq=fq, fc=fc, c0=c0, c1=c1: e.tensor_tensor(out=actT[:, fc, c0:c1], in0=ps[:, b2, 0:w_],
                                                                                                      in1=ftmp[:, fq, 0:w_], op=ALU.mult),
                          PB(b2) + ["ftmp%d" % fq], ["actT"])
            if stop_after == "F":
                continue

            S.barrier()
            S.dma("sp", lambda e: e.dma_start(out=gBf, in_=g_fin.partition_broadcast(128)), writes=["gBf"])
            A("dve", lambda e: e.memset(stat[:, 0:32], 0.0), [], ["stat"])
            wres_all = ["wdn0", "wdn1", "wdn2", "wdn3"]
            for tt in range(8):
                for half in range(2):
                    b = nb()
                    for fc in range(22):
                        A("pe", lambda e, fc=fc, tt=tt, half=half, b=b: e.matmul(ps[:, b, :], lhsT=actT[:, fc, tt * 128:(tt + 1) * 128],
                                                                                 rhs=wdn[:, fc, half * 512:(half + 1) * 512], start=(fc == 0), stop=(fc == 21)),
                          ["actT"] + wres_all, PB(b))
                    A("dve", lambda e, tt=tt, half=half, b=b: e.tensor_tensor(out=x_tok[:, tt, half * 512:(half + 1) * 512], in0=ps[:, b, :],
                                                                               in1=x_tok[:, tt, half * 512:(half + 1) * 512], op=ALU.add),
                      PB(b) + ["x%d" % tt], ["x%d" % tt])
                A("act", lambda e, tt=tt: e.activation(out=junkD, in_=x_tok[:, tt, :], func=AF.Square,
                                                       accum_out=stat[:, tt:tt + 1]), ["x%d" % tt, "stat"], ["ftmp0", "ftmp1", "stat%d" % tt])
                A("dve", lambda e, tt=tt: e.tensor_scalar(out=stat[:, 16 + tt:17 + tt], in0=stat[:, tt:tt + 1], scalar1=1.0 / D, scalar2=RMS_EPS,
                                                          op0=ALU.mult, op1=ALU.add), ["stat%d" % tt, "stat"], ["stat%d" % tt])
                A("act", lambda e, tt=tt: e.activation(out=stat[:, 16 + tt:17 + tt], in_=stat[:, 16 + tt:17 + tt], func=AF.Ln), ["stat%d" % tt], ["stat%d" % tt])
                A("act", lambda e, tt=tt: e.activation(out=stat[:, 16 + tt:17 + tt], in_=stat[:, 16 + tt:17 + tt], func=AF.Exp, scale=-0.5),
                  ["stat%d" % tt], ["stat%d" % tt])
                A("dve", lambda e, tt=tt: e.scalar_tensor_tensor(out=x_tok[:, tt, :], in0=x_tok[:, tt, :], scalar=stat[:, 16 + tt:17 + tt], in1=gBf,
                                                                 op0=ALU.mult, op1=ALU.mult), ["x%d" % tt, "stat%d" % tt, "gBf"], ["x%d" % tt])
                S.dma("sp", lambda e, tt=tt: e.dma_start(out=y_p[t0 + tt * 128:t0 + (tt + 1) * 128, :], in_=x_tok[:, tt, :]), reads=["x%d" % tt])
            if nsamp:
                for half in range(2):
                    b = nb()
                    for fc in range(22):
                        A("pe", lambda e, fc=fc, half=half, b=b: e.matmul(ps[0:NS, b, :], lhsT=actT[:, fc, HALF:HALF + NS],
                                                                          rhs=wdn[:, fc, half * 512:(half + 1) * 512], start=(fc == 0), stop=(fc == 21)),
                          ["actT"] + wres_all, PB(b))
                    A("dve", lambda e, half=half, b=b: e.tensor_tensor(out=xs_tok[:, half * 512:(half + 1) * 512], in0=ps[0:NS, b, :],
                                                                        in1=xs_tok[:, half * 512:(half + 1) * 512], op=ALU.add), PB(b) + ["xs"], ["xs"])
                A("act", lambda e: e.activation(out=junkD[0:NS, :], in_=xs_tok, func=AF.Square,
                                                accum_out=stat[0:NS, 8:9]), ["xs", "stat"], ["ftmp0", "ftmp1", "stat8"])
                A("dve", lambda e: e.tensor_scalar(out=stat[0:NS, 24:25], in0=stat[0:NS, 8:9], scalar1=1.0 / D, scalar2=RMS_EPS,
                                                    op0=ALU.mult, op1=ALU.add), ["stat8", "stat"], ["stat8"])
                A("act", lambda e: e.activation(out=stat[0:NS, 24:25], in_=stat[0:NS, 24:25], func=AF.Ln), ["stat8"], ["stat8"])
                A("act", lambda e: e.activation(out=stat[0:NS, 24:25], in_=stat[0:NS, 24:25], func=AF.Exp, scale=-0.5), ["stat8"], ["stat8"])
                A("dve", lambda e: e.scalar_tensor_tensor(out=xs_tok, in0=xs_tok, scalar=stat[0:NS, 24:25], in1=gBf[0:NS, :],
                                                          op0=ALU.mult, op1=ALU.mult), ["xs", "stat8", "gBf"], ["xs"])
                S.dma("sp", lambda e: e.dma_start(out=y_s, in_=xs_tok), reads=["xs"])
        S.finalize()
        S.emit(st)
    return nc


DBG_EXTRA = {"oa": [128, 4, 1040], "oT": [128, 4, 1040], "ob": [128, 4, 1040], "mergedT": [128, 8, 1040], "hT2": [128, 8, 1040]}


def _host_maps(inp, ncores=8):
    f = np.ascontiguousarray

    def a32(v):
        return np.asarray(v, dtype=np.float32)

    def fm(v, ncol):
        return f(a32(v).reshape(ncol, 128).T)

    pfm = np.concatenate([fm(inp['rwkv_mu'][0], 14), fm(inp['rwkv_w0'][0], 4), fm(inp['rwkv_a0'][0], 4), fm(inp['rwkv_k_k'][0], 4),
                          fm(inp['rwkv_k_a'][0], 4), fm(a32(inp['rwkv_r_k'][0]).reshape(-1), 4), fm(inp['rwkv_ln_g'][0], 4),
                          fm(inp['rwkv_ln_b'][0], 4), fm(inp['hgrn_lb'][0], 4), fm(inp['hgrn_lb'][1], 4), fm(inp['hgrn_norm_g'][0], 4)], axis=1)
    shared = dict(w_in=f(a32(inp['w_in'][0])), w2a2=f(np.concatenate([a32(inp['rwkv_w2'][0]), a32(inp['rwkv_a2'][0])], axis=0)),
                  g2=f(a32(inp['rwkv_g2'][0])), w_up_a=f(a32(inp['w_up_a'][0])), w_up_b=f(a32(inp['w_up_b'][0])), w_out=f(a32(inp['w_out'][0])),
                  w_fg=f(a32(inp['w_ffn_gate'][0])), w_fu=f(a32(inp['w_ffn_up'][0])), w_fd=f(a32(inp['w_ffn_down'][0])), pfm=f(pfm),
                  consts=make_consts(), g_mix=f(a32(inp['norm_mix_g'][0])), g_ffn=f(a32(inp['norm_ffn_g'][0])), g_fin=f(a32(inp['norm_final_g'])))
    maps = []
    for c in range(ncores):
        m = dict(shared)
        m['x_p'] = f(a32(inp['x_prompt'][c]))
        m['x_s'] = f(a32(inp['x_sample'][c * NS:(c + 1) * NS, 0]))
        m['wkv_s'] = f(a32(inp['state_rwkv_wkv'][0, c * NS:(c + 1) * NS]).reshape(NS * 8, 4096))
        m['shift_s'] = f(a32(inp['state_rwkv_shift'][0, c * NS:(c + 1) * NS]))
        m['hgrn_s'] = f(a32(inp['state_hgrn'][0, c * NS:(c + 1) * NS]))
        maps.append(m)
    return maps


_NC_CACHE = {}


def kernel(**inputs):
    ncores = 8
    if "nc" not in _NC_CACHE:
        _NC_CACHE["nc"] = build_program()
    nc = _NC_CACHE["nc"]
    maps = _host_maps(inputs, ncores)
    res = run_bass_kernel_spmd(nc, maps, core_ids=list(range(ncores)))
    r = res.results
    y_p = np.stack([r[c]["y_p"] for c in range(ncores)]).astype(np.float32)
    y_s = np.concatenate([r[c]["y_s"] for c in range(ncores)], axis=0).reshape(128, 1, D).astype(np.float32)
    wkv_p = np.stack([r[c]["o_wkv_p"] for c in range(ncores)])[None].astype(np.float32)
    shift_p = np.stack([r[c]["o_shift_p"] for c in range(ncores)])[None].astype(np.float32)
    hgrn_p = np.stack([r[c]["o_hgrn_p"] for c in range(ncores)])[None].astype(np.float32)
    wkv_s = np.concatenate([r[c]["o_wkv_s"].reshape(NS, 8, 64, 64) for c in range(ncores)], axis=0)[None].astype(np.float32)
    shift_s = np.concatenate([r[c]["o_shift_s"] for c in range(ncores)], axis=0)[None].astype(np.float32)
    hgrn_s = np.concatenate([r[c]["o_hgrn_s"] for c in range(ncores)], axis=0)[None].astype(np.float32)
    return (y_p, y_s, wkv_p, shift_p, hgrn_p, wkv_s, shift_s, hgrn_s)
```

```python
from contextlib import ExitStack
import numpy as np
import concourse.bass as bass
import concourse.mybir as mybir
from concourse.bass_utils import run_bass_kernel_spmd

F32 = mybir.dt.float32
BF16 = mybir.dt.bfloat16
AF = mybir.ActivationFunctionType
ALU = mybir.AluOpType
AX = mybir.AxisListType

COMPUTE = ("pe", "act", "dve", "pool")


class Op:
    __slots__ = ("idx", "eng", "fn", "call", "is_dma", "deps", "val", "marked", "slot", "acc", "dur", "xfer", "seq", "mode")

    def __init__(self, idx, eng, fn, call, is_dma):
        self.idx = idx
        self.eng = eng
        self.fn = fn
        self.call = call
        self.is_dma = is_dma
        self.deps = ()
        self.val = None
        self.marked = False
        self.slot = None
        self.acc = None
        self.dur = 100.0
        self.xfer = 0.0
        self.seq = 0
        self.mode = None


class _Rec:
    def __init__(self):
        self.call = None

    def __getattr__(self, name):
        def f(*a, **k):
            self.call = (name, a, k)
            return self
        return f


_WRITE_KW = ("out", "accum_out", "ap")
_ESZ = {}


def _esz(dt):
    if dt not in _ESZ:
        _ESZ[dt] = 4 if dt == F32 else 2
    return _ESZ[dt]


def _ap_intervals(ap):
    sp = str(ap.space)
    esz = _esz(ap.dtype)
    dims = list(ap.ap)
    if sp == "DRAM":
        base = ap.offset
        p0, p1 = 0, 1
        fd = dims
    else:
        pstride, pcnt = dims[0]
        p0 = ap.offset // pstride
        base = ap.offset % pstride
        p1 = p0 + pcnt
        fd = dims[1:]
    fd = sorted([(st_, c) for (st_, c) in fd if c > 1 and st_ != 0])
    run = 1
    k = 0
    while k < len(fd) and fd[k][0] == run:
        run *= fd[k][1]
        k += 1
    outer = fd[k:]
    n_outer = 1
    for st_, c in outer:
        n_outer *= c
    if n_outer <= 32:
        offs = [0]
        for st_, c in outer:
            offs = [o + i * st_ for o in offs for i in range(c)]
        iv = [((base + o) * esz, (base + o + run) * esz) for o in offs]
    else:
        ext = sum((c - 1) * st_ for st_, c in outer) + run
        iv = [(base * esz, (base + ext) * esz)]
    return sp, ap.name, p0, p1, iv


def _call_accesses(call):
    name, a, k = call
    items = []
    if name == "matmul":
        items.append((a[0] if a else k.get("out"), True))
        for kw in ("lhsT", "rhs"):
            items.append((k[kw], False))
    elif name == "memset":
        items.append((a[0] if a else k.get("ap"), True))
    else:
        for kw, v in k.items():
            if hasattr(v, "ap") and hasattr(v, "space"):
                items.append((v, kw in _WRITE_KW))
        for v in a:
            if hasattr(v, "ap") and hasattr(v, "space"):
                items.append((v, False))
    acc = []
    for ap, is_w in items:
        if ap is None:
            continue
        sp, nm, p0, p1, iv = _ap_intervals(ap)
        if sp == "PSUM":
            banks = set()
            for b0, b1 in iv:
                for bk in range(b0 // 2048, (b1 - 1) // 2048 + 1):
                    banks.add(bk)
            for bk in banks:
                acc.append((("ps", bk), 0, 128, 0, 1, is_w))
        else:
            for b0, b1 in iv:
                acc.append(((sp, nm), p0, p1, b0, b1, is_w))
    return acc


def _free_elems(ap):
    n = 1
    for s_ in ap.shape[1:]:
        n *= s_
    return n


_WI = {"fp32": 1.0, "dve": 1.0, "act": 1.0, "pe_small": 1.0, "pe_big": 1.0, "lat": 1.0, "pool": 1.0}


def _est(op):
    _est0(op)
    name = op.call[0]
    if op.is_dma:
        return
    if name == "matmul":
        k = op.call[2]
        if k["lhsT"].dtype == F32:
            op.dur *= _WI["fp32"]
        elif _free_elems(k["rhs"]) >= 512:
            op.dur *= _WI["pe_big"]
        else:
            op.dur *= _WI["pe_small"]
    elif op.eng in ("dve", "act", "pool"):
        op.dur *= _WI[op.eng]


def _est0(op):
    name, a, k = op.call
    if op.is_dma:
        ap = k.get("out")
        nbytes = 1
        for s_ in ap.shape:
            nbytes *= s_
        nbytes *= _esz(ap.dtype)
        op.dur = 1200.0 if op.eng == "pool" else 150.0
        op.xfer = 2000.0 + nbytes / 150.0
        return
    def _rnd(v):
        return 32 if v <= 32 else (64 if v <= 64 else 128)
    if name == "matmul":
        n = _free_elems(k["rhs"])
        f = 4.0 if k["lhsT"].dtype == F32 else 1.0
        op.dur = (16.0 + max(n, 48) * 0.42) * f
        op.mode = (_rnd(k["lhsT"].shape[0]), _rnd(_free_elems(k["lhsT"])), f)
    elif name == "transpose":
        op.dur = 70.0
        op.mode = ("T", _rnd(k["in_"].shape[0]), _rnd(_free_elems(k["in_"])))
    elif op.eng == "act":
        op.dur = 120.0 + _free_elems(k["out"]) * 0.85
        fn_ = k.get("func")
        if fn_ in (AF.Exp, AF.Ln):
            op.mode = "A"
        elif fn_ in (AF.Sigmoid, AF.Tanh):
            op.mode = "B"
        elif fn_ == AF.Silu:
            op.mode = "C"
    else:
        o = k.get("out") if "out" in k else (a[0] if a else None)
        n = _free_elems(o) if o is not None else 64
        if op.eng == "pool":
            op.dur = 100.0 + n * 2.2
        else:
            op.dur = 70.0 + n * 1.17


class Sched:
    def __init__(self, nc, dma_slots=None, reorder=True):
        self.nc = nc
        self.ops = []
        self.dma_slots = dma_slots or {"sp": 8, "act": 2, "pool": 6, "dve": 2, "pe": 2}
        self.reorder = reorder
        self.tag = ""
        self.tags = []

    def _mk(self, eng, fn, is_dma):
        rec = _Rec()
        fn(rec)
        name, a, k = rec.call
        op = Op(len(self.ops), eng, (lambda e: getattr(e, name)(*a, **k)), (name, a, k), is_dma)
        self.ops.append(op)
        self.tags.append(self.tag)
        return op

    def add(self, eng, fn, reads=(), writes=()):
        return self._mk(eng, fn, False)

    def dma(self, eng, fn, reads=(), writes=()):
        return self._mk(eng, fn, True)

    def barrier(self):
        pass

    def _build_dag(self):
        wlog = {}
        rlog = {}
        for op in self.ops:
            acc = _call_accesses(op.call)
            _est(op)
            deps = set()
            for key, p0, p1, b0, b1, is_w in acc:
                for (q0, q1, c0, c1, oi, oe) in wlog.get(key, ()):
                    if q0 < p1 and p0 < q1 and c0 < b1 and b0 < c1:
                        deps.add(oi)
                if is_w or key[0] == "ps":
                    for (q0, q1, c0, c1, oi, oe) in rlog.get(key, ()):
                        if q0 < p1 and p0 < q1 and c0 < b1 and b0 < c1:
                            if is_w or oe != op.eng:
                                deps.add(oi)
            for key, p0, p1, b0, b1, is_w in acc:
                if is_w:
                    for lg in (wlog, rlog):
                        l_ = lg.get(key)
                        if l_:
                            lg[key] = [e_ for e_ in l_ if not (p0 <= e_[0] and e_[1] <= p1 and b0 <= e_[2] and e_[3] <= b1)]
                    wlog.setdefault(key, []).append((p0, p1, b0, b1, op.idx, op.eng))
                else:
                    rlog.setdefault(key, []).append((p0, p1, b0, b1, op.idx, op.eng))
            deps.discard(op.idx)
            op.deps = tuple(sorted(deps))

    def _list_schedule(self):
        import heapq
        ops = self.ops
        n = len(ops)
        if not self.reorder:
            return list(range(n))
        npred = [len(o.deps) for o in ops]
        succ = [[] for _ in range(n)]
        for o in ops:
            for d in o.deps:
                succ[d].append(o.idx)
        ready_t = [0.0] * n
        fin = [0.0] * n
        import os as _os2
        PRI = _os2.environ.get("KPRI", "5")
        prio = list(range(n))
        if PRI != "idx":
            cp = [0.0] * n
            for i_ in range(n - 1, -1, -1):
                m = 0.0
                for s_ in succ[i_]:
                    if cp[s_] > m:
                        m = cp[s_]
                cp[i_] = m + ops[i_].dur + ops[i_].xfer + 200.0
            w_ = float(PRI)
            rank = sorted(range(n), key=lambda i_: (i_ - w_ * cp[i_] / 100.0))
            for r_, i_ in enumerate(rank):
                prio[i_] = r_
            for i_ in range(min(getattr(self, "front", 0), n)):
                prio[i_] = i_ - n
        engs = ("pe", "act", "dve", "pool", "sp")
        future = {e: [] for e in engs}
        avail = {e: [] for e in engs}
        free = {e: 0.0 for e in engs}
        for o in ops:
            if npred[o.idx] == 0:
                heapq.heappush(future[o.eng], (0.0, o.idx))
        order = []
        self.start_t = {}
        pe_mode = [None]
        act_set = [None]
        WINDOW = 3000
        done_upto = 0
        sched = [False] * n
        while len(order) < n:
            best = None
            for e in engs:
                fu, av = future[e], avail[e]
                while fu and fu[0][0] <= free[e]:
                    t_, i_ = heapq.heappop(fu)
                    heapq.heappush(av, (prio[i_], i_))
                if av:
                    pick = av[0]
                    if e == "pe" and len(av) > 1 and ops[pick[1]].mode != pe_mode[0]:
                        best_same = None
                        for i2 in av:
                            if ops[i2[1]].mode == pe_mode[0] and (best_same is None or i2 < best_same):
                                best_same = i2
                        if best_same is not None and best_same[0] - pick[0] < 400:
                            pick = best_same
                    if e == "act" and len(av) > 1 and ops[pick[1]].mode not in (None, act_set[0]):
                        best_same = None
                        for i2 in av:
                            if ops[i2[1]].mode in (None, act_set[0]) and (best_same is None or i2 < best_same):
                                best_same = i2
                        if best_same is not None and best_same[0] - pick[0] < 300:
                            pick = best_same
                    cand = (free[e], pick[0], e, True, pick)
                elif fu:
                    cand = (fu[0][0], prio[fu[0][1]], e, False, fu[0])
                else:
                    continue
                if best is None or cand[:2] < best[:2]:
                    best = cand
            start, _pr, e, from_av, item = best
            if from_av:
                i_ = item[1]
                if avail[e][0] == item:
                    heapq.heappop(avail[e])
                else:
                    avail[e].remove(item)
                    heapq.heapify(avail[e])
            else:
                i_ = item[1]
                heapq.heappop(future[e])
            o = ops[i_]
            if e == "pe":
                if o.mode != pe_mode[0]:
                    start += 150.0
                pe_mode[0] = o.mode
            elif e == "act" and o.mode is not None:
                if o.mode != act_set[0]:
                    start += 1300.0
                act_set[0] = o.mode
            free[e] = start + o.dur
            self.start_t[i_] = start
            fin[i_] = start + o.dur + o.xfer
            order.append(i_)
            for s_ in succ[i_]:
                lat = (40.0 if (ops[s_].eng == e and e == "pe" and not o.is_dma) else (150.0 if ops[s_].eng == e else 400.0)) * _WI["lat"]
                if fin[i_] + lat > ready_t[s_]:
                    ready_t[s_] = fin[i_] + lat
                npred[s_] -= 1
                if npred[s_] == 0:
                    heapq.heappush(future[ops[s_].eng], (ready_t[s_], s_))
        self.est_makespan = max(fin) if fin else 0.0
        return order

    def finalize(self):
        self._build_dag()
        order = self._list_schedule()
        self.order = order
        ops = self.ops
        dma_count = {}
        slot_last = {}
        extra = {}
        seqc = {}
        for i_ in order:
            op = ops[i_]
            seqc[op.eng] = seqc.get(op.eng, 0) + 1
            op.seq = seqc[op.eng]
            if op.is_dma:
                n_ = dma_count.get(op.eng, 0)
                dma_count[op.eng] = n_ + 1
                k = self.dma_slots[op.eng]
                op.slot = (op.eng, n_ % k)
                op.val = 16 * (n_ // k + 1)
                prev = slot_last.get(op.slot)
                if prev is not None:
                    extra[i_] = prev
                slot_last[op.slot] = i_
        known = {}
        vc = {}
        self.waits = {}
        for i_ in order:
            op = ops[i_]
            K = known.setdefault(op.eng, {})
            deps = list(op.deps)
            if i_ in extra:
                deps.append(extra[i_])
            cand = []
            for d in deps:
                p = ops[d]
                if (not p.is_dma) and p.eng == "pe" and op.eng == "pe" and not op.is_dma:
                    continue
                cand.append(d)
            cand.sort(key=lambda d: -ops[d].seq)
            w = []
            for d in cand:
                p = ops[d]
                key = ("dma", p.slot) if p.is_dma else ("eng", p.eng)
                val = p.val if p.is_dma else p.seq
                if K.get(key, -1) >= val:
                    continue
                w.append(d)
                p.marked = True
                for kk, vv in vc[d].items():
                    if K.get(kk, -1) < vv:
                        K[kk] = vv
            self.waits[i_] = w
            v = dict(K)
            if op.is_dma:
                v[("dma", op.slot)] = op.val
            else:
                if v.get(("eng", op.eng), -1) < op.seq and op.eng == "pe":
                    pass
                v[("eng", op.eng)] = max(v.get(("eng", op.eng), -1), op.seq)
            vc[i_] = v
        cnt = {}
        for i_ in order:
            op = ops[i_]
            if not op.is_dma and op.marked:
                cnt[op.eng] = cnt.get(op.eng, 0) + 1
                op.val = cnt[op.eng]
        self.counts = cnt

    def emit(self, stack):
        nc = self.nc
        sems = {}
        for e in COMPUTE:
            sems[("eng", e)] = stack.enter_context(nc.semaphore("s_" + e))
        for e, k in self.dma_slots.items():
            for i in range(k):
                sems[("dma", (e, i))] = stack.enter_context(nc.semaphore("d_%s%d" % (e, i)))
        block = stack.enter_context(nc.Block())
        ops = self.ops
        order = self.order
        waits = self.waits

        def run(engname, eng):
            for i_ in order:
                op = ops[i_]
                if op.eng != engname:
                    continue
                for d in waits[i_]:
                    p = ops[d]
                    s = sems[("dma", p.slot)] if p.is_dma else sems[("eng", p.eng)]
                    eng.wait_ge(s, p.val)
                ins = op.fn(eng)
                if op.is_dma:
                    ins.then_inc(sems[("dma", op.slot)], 16)
                elif op.marked:
                    ins.then_inc(sems[("eng", op.eng)], 1)
            last = {}
            for i_ in order:
                op = ops[i_]
                if op.is_dma and op.eng == engname:
                    last[op.slot] = op.val
            for slot, v in last.items():
                eng.wait_ge(sems[("dma", slot)], v)

        @block.sync
        def _(e):
            run("sp", e)

        @block.tensor
        def _(e):
            run("pe", e)

        @block.scalar
        def _(e):
            run("act", e)

        @block.vector
        def _(e):
            run("dve", e)

        @block.gpsimd
        def _(e):
            run("pool", e)


D = 1024
T = 2048
NS = 16
HALF = 1024
TW = HALF + NS
RW = 512
RPROJ = 1792
HW_ = 512
DFF = 2816
INP = 5888
CDEC = -0.6065306597126334
GN_EPS = 64e-5
RMS_EPS = 1e-6

PF = {}
_c = 0
for _n, _w in (("mu", 14), ("w0", 4), ("a0", 4), ("k_k", 4), ("k_a", 4), ("r_k", 4), ("ln_g", 4), ("ln_b", 4),
               ("lb0", 4), ("lb1", 4), ("hng", 4)):
    PF[_n] = _c
    _c += _w
PF_N = _c
DV = {"omu": 0, "omka": 14, "lb": 18, "omlb": 22}
DV_N = 29

CO = {"ident": 0, "blk64": 128, "ones": 256, "maskNM": 384, "maskL": 512, "maskI": 576, "cmask": 640}
CO_N = 640 + 1024 + 256


def make_consts():
    c = np.zeros((128, CO_N), np.float32)
    p = np.arange(128)
    c[:, 0:128] = np.eye(128, dtype=np.float32)
    c[:, 128:256] = (p[:, None] // 64 == p[None, :] // 64).astype(np.float32)
    c[:, 256:384] = 1.0
    s = (p % 64)[:, None]
    t = np.arange(64)[None, :]
    c[:, 384:448] = (s < t)
    c[:, 448:512] = (s <= t)
    c[:, 512:576] = (s > t)
    c[:, 576:640] = (s == t)
    tt = np.arange(1024)
    c[:, 640:1664] = (tt % 64 != 0).astype(np.float32)[None, :]
    c[:, 1664:1920] = np.eye(16, dtype=np.float32).reshape(-1)[None, :]
    return c


class Arena:
    def __init__(self, tile, nwords):
        self.t = tile
        self.n = nwords
        self.top = 0

    def alloc(self, dtype, shape):
        free = 1
        for s in shape[1:]:
            free *= s
        words = free if dtype == F32 else (free + 1) // 2
        words = (words + 7) // 8 * 8
        off = self.top
        self.top += words
        assert self.top <= self.n, ("arena overflow", self.top, self.n)
        v = self.t[0:shape[0], off:off + words]
        if dtype == BF16:
            v = v.bitcast(BF16)
        v = v[:, 0:free]
        if len(shape) > 2:
            names = ["d%d" % i for i in range(len(shape) - 1)]
            pat = "p (" + " ".join(names) + ") -> p " + " ".join(names)
            kw = {names[i]: shape[i + 1] for i in range(len(names))}
            v = v.rearrange(pat, **kw)
        return v


ARENA_WORDS = 32560


def build_program(dbg=None, passes=(0, 1), stop_after=None):
    nc = bass.Bass("TRN2", target_bir_lowering=False)

    def din(name, shape):
        return nc.dram_tensor(name, list(shape), F32, kind="ExternalInput").ap()

    def dout(name, shape):
        return nc.dram_tensor(name, list(shape), F32, kind="ExternalOutput").ap()

    x_p = din("x_p", [T, D])
    x_s = din("x_s", [NS, D])
    wkv_s = din("wkv_s", [NS * 8, 4096])
    shift_s = din("shift_s", [NS, RPROJ])
    hgrn_s = din("hgrn_s", [NS, 4, 128, 128])
    w_in = din("w_in", [D, INP])
    w2a2 = din("w2a2", [128, 512])
    g2 = din("g2", [128, 512])
    w_up_a = din("w_up_a", [RW, D])
    w_up_b = din("w_up_b", [HW_, D])
    w_out = din("w_out", [D, D])
    w_fg = din("w_fg", [D, DFF])
    w_fu = din("w_fu", [D, DFF])
    w_fd = din("w_fd", [DFF, D])
    pfm_d = din("pfm", [128, PF_N])
    consts_d = din("consts", [128, CO_N])
    g_mix = din("g_mix", [D])
    g_ffn = din("g_ffn", [D])
    g_fin = din("g_fin", [D])

    y_p = dout("y_p", [T, D])
    y_s = dout("y_s", [NS, D])
    o_wkv_p = dout("o_wkv_p", [8, 64, 64])
    o_shift_p = dout("o_shift_p", [RPROJ])
    o_hgrn_p = dout("o_hgrn_p", [4, 128, 128])
    o_wkv_s = dout("o_wkv_s", [NS * 8, 4096])
    o_shift_s = dout("o_shift_s", [NS, RPROJ])
    o_hgrn_s = dout("o_hgrn_s", [NS, 4, 128, 128])
    dbg_out = {}
    if dbg:
        for k, shp in dbg.items():
            dbg_out[k] = dout("dbg_" + k, shp)
    scr_a = nc.dram_tensor("scr_a", [NS, 6, 512], F32).ap()
    scr_y = nc.dram_tensor("scr_y", [NS, 512], F32).ap()

    st = ExitStack()
    with st:
        def sb(name, shape, dt):
            return st.enter_context(nc.sbuf_tensor(name, list(shape), dt))

        arena_t = sb("arena", [128, ARENA_WORDS], F32)
        hT = sb("hT", [128, 8, TW], BF16)
        oa = sb("oa", [128, 4, TW], BF16)
        ob = sb("ob", [128, 4, TW], BF16)
        NWB = 4
        wbuf = [sb("wbuf%d" % i, [128, 8, 512], BF16) for i in range(NWB)]
        consts = sb("consts_sb", [128, 384], F32)
        ident_bf = sb("ident_bf", [128, 128], BF16)
        masks_bf = sb("masks_bf", [128, 256], BF16)
        pfm = sb("pfm_sb", [128, PF_N], F32)
        dv = sb("dv", [128, DV_N], F32)
        lw_w = sb("lw_w", [128, 512], BF16)
        g2_w = sb("g2_w", [128, 512], BF16)
        carry = sb("carry", [128, 14], F32)
        shiftS = sb("shiftS", [128, 14, NS], F32)
        sprevT = sb("sprevT", [128, 14, NS], F32)
        H0f = sb("H0f", [128, 4, 64], F32)
        H0bd = sb("H0bd", [128, 4, 128], BF16)
        S0f = sb("S0f", [128, 4, 128], F32)
        S0b = sb("S0b", [128, 4, 128], BF16)
        WC = sb("WC", [128, 4, 16], F32)
        DC = sb("DC", [128, 4, 16], F32)
        stat = sb("stat", [128, 64], F32)
        ftmp = sb("ftmp", [128, 2, 512], F32)
        ps = st.enter_context(nc.psum_tensor("ps", [128, 8, 512], F32))

        ident = consts[:, 0:128]
        blk64 = consts[:, 128:256]
        ones = consts[:, 256:384]
        maskNM_bf = masks_bf[:, 0:128]
        maskL_bf = masks_bf[:, 128:192]
        maskI_bf = masks_bf[:, 192:256]

        S = Sched(nc)
        A = S.add
        bank_ctr = [0]

        def nb(n=1):
            b = bank_ctr[0]
            if n > 1:
                b = (b + n - 1) // n * n
            if b + n > 8:
                b = 0
            bank_ctr[0] = (b + n) % 8
            return b

        def nbp(par):
            b = bank_ctr[0]
            if b % 2 != par:
                b = (b + 1) % 8
            bank_ctr[0] = (b + 1) % 8
            return b

        par_ctr = [0]
        seq_ctr = [0, 0]

        par_nb = [6]

        def nb_par(n=1):
            m = par_nb[0]
            if n == 2:
                par_ctr[0] = (par_ctr[0] + 1) // 2 * 2
            b = par_ctr[0] % m
            par_ctr[0] = (par_ctr[0] + n) % m
            return b

        smp_ctr = [0]

        def nb_smp():
            smp_ctr[0] += 1
            return 4 + smp_ctr[0] % 2

        def nb_seq(par):
            return 6 + par

        def PB(b, n=1):
            return ["ps%d" % (b + i) for i in range(n)]

        def pf(name, j=0):
            c = PF[name] + j
            return pfm[:, c:c + 1]

        def dvc(name, j=0):
            c = DV[name] + j
            return dv[:, c:c + 1]

        wctr = [0]

        def load_w(src_ap, shape_view):
            i = wctr[0] % NWB
            wctr[0] += 1
            a, b = shape_view
            view = wbuf[i][:, :, :].rearrange("p a b -> p (a b)")[:, 0:a * b].rearrange("p (a b) -> p a b", a=a)
            rn = "wbuf%d" % i
            S.dma("pool", lambda e, view=view, src_ap=src_ap: e.dma_start(out=view, in_=src_ap), writes=[rn])
            return view, rn

        def dump(name, ap_sb, res):
            if dbg and name in dbg_out:
                S.dma("pool", lambda e: e.dma_start(out=dbg_out[name], in_=ap_sb), reads=res)

        S.dma("sp", lambda e: e.dma_start(out=consts[:], in_=consts_d[:, 0:384]), writes=["consts"])
        S.dma("pool", lambda e: e.dma_start(out=masks_bf[:], in_=consts_d[:, 384:640]))
        S.dma("sp", lambda e: e.dma_start(out=pfm[:], in_=pfm_d), writes=["pfm"])
        S.dma("pool", lambda e: e.dma_start(out=lw_w[:], in_=w2a2), writes=["lw_w"])
        S.dma("pool", lambda e: e.dma_start(out=g2_w[:], in_=g2), writes=["g2_w"])
        A("dve", lambda e: e.tensor_copy(out=ident_bf[:], in_=ident), ["consts"], ["ident_bf"])
        A("dve", lambda e: e.memset(carry[:], 0.0), [], ["carry"])
        A("dve", lambda e: e.memset(H0f[:], 0.0), [], ["H0f"])
        A("dve", lambda e: e.memset(H0bd[:], 0.0), [], ["H0b"])
        A("dve", lambda e: e.memset(S0f[:], 0.0), [], ["S0f"])
        A("dve", lambda e: e.memset(S0b[:], 0.0), [], ["S0b"])
        A("dve", lambda e: e.memset(sprevT[:], 0.0), [], ["sprevT"])
        A("dve", lambda e: e.tensor_scalar(out=dv[:, 0:14], in0=pfm[:, PF["mu"]:PF["mu"] + 14], scalar1=-1.0, scalar2=1.0,
                                            op0=ALU.mult, op1=ALU.add), ["pfm"], ["dv"])
        A("dve", lambda e: e.tensor_scalar(out=dv[:, 14:18], in0=pfm[:, PF["k_a"]:PF["k_a"] + 4], scalar1=-1.0, scalar2=1.0,
                                            op0=ALU.mult, op1=ALU.add), ["pfm"], ["dv"])
        A("dve", lambda e: e.tensor_tensor(out=dv[:, 18:22], in0=pfm[:, PF["lb0"]:PF["lb0"] + 4],
                                            in1=pfm[:, PF["lb1"]:PF["lb1"] + 4], op=ALU.subtract), ["pfm"], ["dv"])
        A("act", lambda e: e.activation(out=dv[:, 18:22], in_=dv[:, 18:22], func=AF.Sigmoid), ["dv"], ["dv"])
        A("dve", lambda e: e.tensor_scalar(out=dv[:, 22:26], in0=dv[:, 18:22], scalar1=-1.0, scalar2=1.0,
                                            op0=ALU.mult, op1=ALU.add), ["dv"], ["dv"])

        eps24, epsgn, epsrms = dv[:, 26:27], dv[:, 27:28], dv[:, 28:29]
        A("dve", lambda e: e.memset(dv[:, 26:27], 1e-24))
        A("dve", lambda e: e.memset(dv[:, 27:28], GN_EPS))
        A("dve", lambda e: e.memset(dv[:, 28:29], RMS_EPS))
        S.front = len(S.ops)
        w_in_v = w_in.rearrange("(kc p) n -> p kc n", p=128)

        for pas in passes:
            import os as _os
            _skip = _os.environ.get("KSKIP", "")
            nsamp = NS if (pas == 0 and "nosamp" not in _skip) else 0
            W = HALF + nsamp
            blocks = [(0, 512), (512, 1024)] + ([(1024, 1040)] if nsamp else [])
            t0 = pas * HALF
            S.barrier()
            ar_ = Arena(arena_t, ARENA_WORDS)

            def norm_phase(tag, x_tok, xs_tok, g_dram, arena, out_cb, src_loaded, xsres):
                gB = arena.alloc(F32, [128, 1024])
                junk = arena.alloc(F32, [128, 1024])
                h_tok = arena.alloc(BF16, [128, 8, 1024])
                hs_tok = arena.alloc(BF16, [NS, 1024])
                R = tag
                S.dma("sp", lambda e: e.dma_start(out=gB, in_=g_dram.partition_broadcast(128)), writes=[R + "gB"])
                A("dve", lambda e: e.memset(stat[:, 0:32], 0.0), [], ["stat"])
                for tt in range(8):
                    A("act", lambda e, tt=tt: e.activation(out=junk, in_=x_tok[:, tt, :], func=AF.Square,
                                                           accum_out=stat[:, tt:tt + 1]),
                      [src_loaded(tt)], [R + "junk", "stat"])
                if nsamp:
                    A("act", lambda e: e.activation(out=junk[0:NS, :], in_=xs_tok, func=AF.Square,
                                                    accum_out=stat[0:NS, 8:9]), [xsres], [R + "junk", "stat"])
                A("dve", lambda e: e.tensor_scalar(out=stat[:, 16:25], in0=stat[:, 0:9], scalar1=1.0 / D, scalar2=RMS_EPS,
                                                    op0=ALU.mult, op1=ALU.add), ["stat"], ["stat"])
                A("act", lambda e: e.activation(out=stat[:, 16:25], in_=stat[:, 16:25], func=AF.Ln), ["stat"], ["stat"])
                A("act", lambda e: e.activation(out=stat[:, 16:25], in_=stat[:, 16:25], func=AF.Exp, scale=-0.5), ["stat"], ["stat"])
                out_cb(gB, h_tok, hs_tok, R)

            x_tok = ar_.alloc(F32, [128, 8, 1024])
            xs_tok = ar_.alloc(F32, [NS, 1024])
            for tt in range(8):
                S.dma("sp", lambda e, tt=tt: e.dma_start(out=x_tok[:, tt, :], in_=x_p[t0 + tt * 128:t0 + (tt + 1) * 128, :]),
                      writes=["P0x%d" % tt])
            if nsamp:
                S.dma("sp", lambda e: e.dma_start(out=xs_tok, in_=x_s), writes=["P0xs"])

            def to_hT(gB, h_tok, hs_tok, R, x_tok_, xs_tok_, xres, xsres):
                for tt in range(8):
                    A("dve", lambda e, tt=tt: e.scalar_tensor_tensor(out=h_tok[:, tt, :], in0=x_tok_[:, tt, :],
                                                                     scalar=stat[:, 16 + tt:17 + tt], in1=gB,
                                                                     op0=ALU.mult, op1=ALU.mult),
                      [xres(tt), "stat", R + "gB"], [R + "htok%d" % tt])
                    b = nb()
                    pT = ps[:, b, :].bitcast(BF16).rearrange("p (c t) -> p c t", c=8)
                    for dc in range(8):
                        A("pe", lambda e, tt=tt, dc=dc, pT=pT: e.transpose(out=pT[:, dc, :], in_=h_tok[:, tt, dc * 128:(dc + 1) * 128],
                                                                           identity=ident_bf[:]),
                          [R + "htok%d" % tt, "ident_bf"], PB(b))
                    eng = "act" if tt % 2 == 0 else "dve"
                    if eng == "act":
                        A("act", lambda e, tt=tt, pT=pT: e.copy(out=hT[:, :, tt * 128:(tt + 1) * 128], in_=pT), PB(b), ["hT"])
                    else:
                        A("dve", lambda e, tt=tt, pT=pT: e.tensor_copy(out=hT[:, :, tt * 128:(tt + 1) * 128], in_=pT), PB(b), ["hT"])
                if nsamp:
                    A("dve", lambda e: e.scalar_tensor_tensor(out=hs_tok, in0=xs_tok_, scalar=stat[0:NS, 24:25], in1=gB[0:NS, :],
                                                              op0=ALU.mult, op1=ALU.mult), [xsres, "stat", R + "gB"], [R + "hstok"])
                    b = nb()
                    pT = ps[:, b, :].bitcast(BF16).rearrange("p (c t) -> p c t", c=8)
                    for dc in range(8):
                        A("pe", lambda e, dc=dc, pT=pT: e.transpose(out=pT[:, dc, 0:NS], in_=hs_tok[:, dc * 128:(dc + 1) * 128],
                                                                    identity=ident_bf[0:NS, 0:NS]),
                          [R + "hstok", "ident_bf"], PB(b))
                    A("act", lambda e, pT=pT: e.copy(out=hT[:, :, HALF:HALF + NS], in_=pT[:, :, 0:NS]), PB(b), ["hT"])

            norm_phase("P0", x_tok, xs_tok, g_mix, ar_,
                       lambda gB, h_tok, hs_tok, R: to_hT(gB, h_tok, hs_tok, R, x_tok, xs_tok, lambda tt: "P0x%d" % tt, "P0xs"),
                       lambda tt: "P0x%d" % tt, "P0xs")
            if dbg and "hT" in dbg_out and pas == 0:
                dump("hT", hT[:], ["hT"])
            if stop_after == "P0":
                continue

            def proj_fm(wview, wres, ncol0, kcs, act_tile, act_res, evac):
                for (c0, c1) in blocks:
                    b = nb()
                    for kc in range(kcs):
                        A("pe", lambda e, kc=kc, b=b, c0=c0, c1=c1: e.matmul(ps[:, b, 0:c1 - c0], lhsT=wview[:, kc, ncol0:ncol0 + 128],
                                                                              rhs=act_tile[:, kc, c0:c1], start=(kc == 0),
                                                                              stop=(kc == kcs - 1)),
                          [wres, act_res], PB(b))
                    evac(b, c0, c1)

            S.tag = "p%d Rprep" % pas
            S.barrier()
            ar_ = Arena(arena_t, ARENA_WORDS)
            sig = ar_.alloc(F32, [128, 4, TW])
            g_bf = ar_.alloc(BF16, [128, 4, TW])
            ar_t = ar_.alloc(BF16, [128, 4, 16, 2, 64])
            bT = ar_.alloc(BF16, [128, 4, HALF])
            kT = ar_.alloc(BF16, [128, 4, HALF])
            vT = ar_.alloc(BF16, [128, 4, TW])
            bonus = ar_.alloc(BF16, [128, 4, TW])
            smp = ar_.alloc(F32, [128, 6, 4, NS])
            mark = ar_.top
            a_bf = ar_.alloc(BF16, [128, 4, TW])
            kpr = ar_.alloc(BF16, [128, 4, TW])
            praw = [ar_.alloc(F32, [128, 1048]) for _ in range(1)]
            NT = 7
            tmp = [ar_.alloc(F32, [128, TW]) for _ in range(NT)]
            cmask = ar_.alloc(BF16, [128, HALF])
            lwb = ar_.alloc(BF16, [128, TW])
            import os as _os
            _skip = _os.environ.get("KSKIP", "")
            if "cmask" not in _skip:
                S.dma("pool", lambda e: e.dma_start(out=cmask, in_=consts_d[:, 640:1664]), writes=["cmask"])

            if nsamp and "shiftT" not in _skip:
                sh_tok = tmp[0][0:NS, :]
                sh_tok2 = tmp[1][0:NS, :]
                S.dma("sp", lambda e: e.dma_start(out=sh_tok[:, 0:1024], in_=shift_s[:, 0:1024]), writes=["tmp0"])
                S.dma("sp", lambda e: e.dma_start(out=sh_tok2[:, 0:768], in_=shift_s[:, 1024:1792]), writes=["tmp1"])
                b = nb()
                for c in range(14):
                    src = sh_tok[:, c * 128:(c + 1) * 128] if c < 8 else sh_tok2[:, (c - 8) * 128:(c - 7) * 128]
                    A("pe", lambda e, c=c, src=src, b=b: e.transpose(out=ps[:, b, c * NS:(c + 1) * NS], in_=src, identity=ident[0:NS, 0:NS]),
                      ["tmp0", "tmp1", "consts"], PB(b))
                A("dve", lambda e, b=b: e.tensor_copy(out=sprevT[:, :, :], in_=ps[:, b, 0:14 * NS].rearrange("p (c n) -> p c n", c=14)),
                  PB(b), ["sprevT"])

            if stop_after == "shiftT":
                continue
            hv = [(0, 512), (512, W)]
            pctr = [0]

            def rwkv_chunk(c, wview, wres, ncol0, out_ap=None):
                pr = praw[0]
                t1 = tmp[6]
                A("dve", lambda e: e.tensor_copy(out=pr[:, 0:1], in_=carry[:, c:c + 1]))

                def evac(b, c0, c1):
                    A("act", lambda e: e.copy(out=pr[:, 1 + c0:1 + c1], in_=ps[:, b, 0:c1 - c0]))
                    A("act", lambda e: e.activation(out=t1[:, c0:c1], in_=ps[:, b, 0:c1 - c0], func=AF.Copy, scale=dvc("omu", c)))
                proj_fm(wview, wres, ncol0, 8, hT, "hT", evac)
                A("act", lambda e: e.copy(out=carry[:, c:c + 1], in_=pr[:, HALF:HALF + 1]))
                if nsamp:
                    A("act", lambda e: e.copy(out=shiftS[:, c, :], in_=pr[:, 1 + HALF:1 + HALF + NS]))
                out_ap_ = tmp[0] if out_ap is None else out_ap
                for (a_, b_) in hv:
                    b2 = min(b_, HALF)
                    A("dve", lambda e, a_=a_, b2=b2: e.scalar_tensor_tensor(out=out_ap_[:, a_:b2], in0=pr[:, a_:b2], scalar=pf("mu", c),
                                                                            in1=t1[:, a_:b2], op0=ALU.mult, op1=ALU.add))
                if nsamp:
                    A("dve", lambda e: e.scalar_tensor_tensor(out=out_ap_[:, HALF:HALF + NS], in0=sprevT[:, c, :], scalar=pf("mu", c),
                                                              in1=t1[:, HALF:HALF + NS], op0=ALU.mult, op1=ALU.add))
                return out_ap_

            wv, wr = load_w(w_in_v[:, :, 1536:1792], (8, 256))
            psm = rwkv_chunk(12, wv, wr, 0)
            for (a_, b_) in hv:
                A("act", lambda e, a_=a_, b_=b_: e.activation(out=lwb[0:64, a_:b_], in_=psm[0:64, a_:b_], func=AF.Tanh))
                A("act", lambda e, a_=a_, b_=b_: e.copy(out=lwb[64:128, a_:b_], in_=psm[64:128, a_:b_]))
            for j in range(4):
                for (c0, c1) in blocks:
                    b2_ = nb(2)
                    b = b2_
                    A("pe", lambda e, j=j, b=b, c0=c0, c1=c1: e.matmul(ps[:, b, 0:c1 - c0], lhsT=lw_w[0:64, j * 128:(j + 1) * 128],
                                                                        rhs=lwb[0:64, c0:c1], start=True, stop=True))
                    A("act", lambda e, j=j, b=b, c0=c0, c1=c1: e.activation(out=sig[:, j, c0:c1], in_=ps[:, b, 0:c1 - c0], func=AF.Sigmoid,
                                                                             bias=pf("w0", j)))
                    b = b2_ + 1
                    A("pe", lambda e, j=j, b=b, c0=c0, c1=c1: e.matmul(ps[:, b, 0:c1 - c0], lhsT=lw_w[64:128, j * 128:(j + 1) * 128],
                                                                        rhs=lwb[64:128, c0:c1], start=True, stop=True))
                    A("act", lambda e, j=j, b=b, c0=c0, c1=c1: e.activation(out=a_bf[:, j, c0:c1], in_=ps[:, b, 0:c1 - c0], func=AF.Sigmoid,
                                                                             bias=pf("a0", j)))
            psm = rwkv_chunk(13, wv, wr, 128)
            for (a_, b_) in hv:
                A("act", lambda e, a_=a_, b_=b_: e.activation(out=lwb[:, a_:b_], in_=psm[:, a_:b_], func=AF.Sigmoid))
            for j in range(4):
                for (c0, c1) in blocks:
                    b = nb()
                    A("pe", lambda e, j=j, b=b, c0=c0, c1=c1: e.matmul(ps[:, b, 0:c1 - c0], lhsT=g2_w[:, j * 128:(j + 1) * 128],
                                                                        rhs=lwb[:, c0:c1], start=True, stop=True))
                    A("dve", lambda e, j=j, b=b, c0=c0, c1=c1: e.tensor_copy(out=g_bf[:, j, c0:c1], in_=ps[:, b, 0:c1 - c0]))
            if dbg and pas == 0:
                dump("sig", sig, ["sig"])

            wv, wr = load_w(w_in_v[:, :, 1024:1536], (8, 512))
            for j in range(4):
                rwkv_chunk(8 + j, wv, wr, j * 128, out_ap=vT[:, j, :])
                if nsamp:
                    A("act", lambda e, j=j: e.copy(out=smp[:, 3, j, :], in_=vT[:, j, HALF:HALF + NS]))

            def fp32_blocksum(src_ap, src_res, mat, evac):
                for (c0, c1) in blocks:
                    b = nb()
                    A("pe", lambda e, b=b, c0=c0, c1=c1: e.matmul(ps[:, b, 0:c1 - c0], lhsT=mat, rhs=src_ap[:, c0:c1], start=True, stop=True))
                    evac(b, c0, c1)

            def cumsum_decay(j, cs_i):
                cs = tmp[cs_i]
                for (a_, b_) in hv:
                    b2 = min(b_, HALF)
                    A("dve", lambda e, a_=a_, b2=b2: e.tensor_tensor_scan(out=cs[:, a_:b2], data0=cmask[:, a_:b2], data1=sig[:, j, a_:b2], initial=0.0,
                                                                          op0=ALU.mult, op1=ALU.add))
                if nsamp:
                    A("dve", lambda e: e.tensor_copy(out=cs[:, HALF:HALF + NS], in_=sig[:, j, HALF:HALF + NS]))
                return cs

            def cview(ap2, a_, b2):
                return ap2[:, a_:b2].rearrange("p (c t) -> p c t", t=64)

            wv, wr = load_w(w_in_v[:, :, 512:1024], (8, 512))
            for j in range(4):
                k_ap = rwkv_chunk(4 + j, wv, wr, j * 128)
                kkr, rs, cs, en, de, ka = tmp[1], tmp[2], tmp[3], tmp[4], tmp[5], tmp[2]
                for (a_, b_) in hv:
                    A("act", lambda e, j=j, a_=a_, b_=b_: e.activation(out=kkr[:, a_:b_], in_=k_ap[:, a_:b_], func=AF.Copy, scale=pf("k_k", j)))
                    A("act", lambda e, a_=a_, b_=b_: e.activation(out=rs[:, a_:b_], in_=kkr[:, a_:b_], func=AF.Square))

                def ev_ss(b, c0, c1):
                    A("act", lambda e: e.activation(out=de[:, c0:c1], in_=ps[:, b, 0:c1 - c0], func=AF.Ln, bias=eps24[:, 0:1]))
                fp32_blocksum(rs, "tmp2", blk64, ev_ss)
                cumsum_decay(j, 3)
                for (a_, b_) in hv:
                    b2 = min(b_, HALF)
                    c_lo, c_hi = a_ // 64, b2 // 64
                    A("act", lambda e, a_=a_, b_=b_: e.activation(out=de[:, a_:b_], in_=de[:, a_:b_], func=AF.Exp, scale=-0.5))
                    A("dve", lambda e, a_=a_, b_=b_: e.tensor_tensor(out=kkr[:, a_:b_], in0=kkr[:, a_:b_], in1=de[:, a_:b_], op=ALU.mult))
                    A("act", lambda e, j=j, a_=a_, b2=b2, c_lo=c_lo, c_hi=c_hi: e.activation(out=WC[:, j, c_lo:c_hi], in_=cview(cs, a_, b2)[:, :, 63],
                                                                                             func=AF.Exp, scale=CDEC))
                    A("act", lambda e, a_=a_, b2=b2: e.activation(out=en[:, a_:b2], in_=cs[:, a_:b2], func=AF.Exp, scale=-CDEC))
                    if nsamp and b_ > HALF:
                        A("act", lambda e, j=j: e.activation(out=smp[:, 1, j, :], in_=cs[:, HALF:HALF + NS], func=AF.Exp, scale=CDEC))
                        A("act", lambda e, j=j: e.activation(out=smp[:, 4, j, :], in_=kkr[:, HALF:HALF + NS], func=AF.Copy, scale=-1.0))
                    A("dve", lambda e, j=j, a_=a_, b2=b2: e.tensor_tensor(out=de[:, a_:b2], in0=cs[:, a_:b2], in1=sig[:, j, a_:b2], op=ALU.subtract))
                    A("act", lambda e, a_=a_, b2=b2: e.activation(out=de[:, a_:b2], in_=de[:, a_:b2], func=AF.Exp, scale=CDEC))
                    A("dve", lambda e, j=j, a_=a_, b2=b2, c_lo=c_lo, c_hi=c_hi: e.scalar_tensor_tensor(
                        out=ar_t[:, j, c_lo:c_hi, 0, :], in0=cview(kkr, a_, b2), scalar=-1.0, in1=cview(de, a_, b2), op0=ALU.mult, op1=ALU.mult))
                    A("dve", lambda e, j=j, a_=a_, b_=b_: e.tensor_tensor(out=ka[:, a_:b_], in0=kkr[:, a_:b_], in1=a_bf[:, j, a_:b_], op=ALU.mult))
                    if nsamp and b_ > HALF:
                        A("act", lambda e, j=j: e.copy(out=smp[:, 5, j, :], in_=ka[:, HALF:HALF + NS]))
                    A("dve", lambda e, j=j, a_=a_, b2=b2: e.tensor_tensor(out=bT[:, j, a_:b2], in0=ka[:, a_:b2], in1=en[:, a_:b2], op=ALU.mult))
                    A("dve", lambda e, j=j, a_=a_, b_=b_: e.tensor_scalar(out=kkr[:, a_:b_], in0=a_bf[:, j, a_:b_], scalar1=pf("k_a", j),
                                                                          scalar2=dvc("omka", j), op0=ALU.mult, op1=ALU.add))
                    A("dve", lambda e, a_=a_, b_=b_: e.tensor_tensor(out=kkr[:, a_:b_], in0=kkr[:, a_:b_], in1=k_ap[:, a_:b_], op=ALU.mult))
                    A("dve", lambda e, j=j, a_=a_, b2=b2: e.tensor_tensor(out=kT[:, j, a_:b2], in0=kkr[:, a_:b2], in1=en[:, a_:b2], op=ALU.mult))
                    A("act", lambda e, j=j, a_=a_, b_=b_: e.activation(out=kpr[:, j, a_:b_], in_=kkr[:, a_:b_], func=AF.Copy, scale=pf("r_k", j)))
                    if nsamp and b_ > HALF:
                        A("act", lambda e, j=j: e.copy(out=smp[:, 2, j, :], in_=kkr[:, HALF:HALF + NS]))

            wv, wr = load_w(w_in_v[:, :, 0:512], (8, 512))
            for j in range(4):
                r_ap = rwkv_chunk(j, wv, wr, j * 128)
                cs = cumsum_decay(j, 3)
                rk = tmp[1]
                for (a_, b_) in hv:
                    b2 = min(b_, HALF)
                    c_lo, c_hi = a_ // 64, b2 // 64
                    A("act", lambda e, a_=a_, b2=b2: e.activation(out=cs[:, a_:b2], in_=cs[:, a_:b2], func=AF.Exp, scale=CDEC))
                    A("dve", lambda e, j=j, a_=a_, b2=b2, c_lo=c_lo, c_hi=c_hi: e.tensor_tensor(out=ar_t[:, j, c_lo:c_hi, 1, :], in0=cview(r_ap, a_, b2),
                                                                                                 in1=cview(cs, a_, b2), op=ALU.mult))
                    A("dve", lambda e, j=j, a_=a_, b_=b_: e.tensor_tensor(out=rk[:, a_:b_], in0=r_ap[:, a_:b_], in1=kpr[:, j, a_:b_], op=ALU.mult))
                if nsamp:
                    A("act", lambda e, j=j: e.copy(out=smp[:, 0, j, :], in_=r_ap[:, HALF:HALF + NS]))

                def ev_bon(b, c0, c1, j=j):
                    A("dve", lambda e: e.tensor_tensor(out=bonus[:, j, c0:c1], in0=ps[:, b, 0:c1 - c0], in1=vT[:, j, c0:c1], op=ALU.mult))
                fp32_blocksum(rk, "tmp1", blk64, ev_bon)
            if dbg and pas == 0:
                dump("ar", ar_t, ["ar"])
                dump("bT", bT, ["bT"])
                dump("kT", kT, ["kT"])
                dump("vT", vT, ["vT"])
                dump("bonus", bonus, ["bonus"])
            if stop_after == "Rprep":
                continue

            yT = sig
            if nsamp:
                ar_.top = mark
                L1v = ar_.alloc(F32, [128, 6, 64])
                saL = ar_.alloc(F32, [128, 64])
                yL1 = ar_.alloc(F32, [128, 64])
                rs_mark = ar_.top
                wf = [wbuf[i_][:, :, :].rearrange("p a b -> p (a b)").bitcast(F32) for i_ in range(4)]
                S_h = [wf[0].rearrange("p (v k) -> p v k", k=64), wf[1].rearrange("p (v k) -> p v k", k=64)]
                T_h = [wf[2].rearrange("p (v k) -> p v k", k=64), wf[3].rearrange("p (v k) -> p v k", k=64)]
                tok6h = [wf[2][0:NS, 0:1536].rearrange("p (v n) -> p v n", v=3), wf[3][0:NS, 0:1536].rearrange("p (v n) -> p v n", v=3)]
                ytok = wf[2][0:NS, 1536:2048]
                shtok = wf[3][0:NS, 0:2048]
                def emit_rsample():
                    for hf in range(2):
                        S.dma("sp", lambda e, hf=hf: e.dma_start(out=S_h[hf].rearrange("p v k -> p (v k)"), in_=wkv_s[:, hf * 2048:(hf + 1) * 2048]))
                    for vec in range(6):
                        b = nb_par()
                        for j in range(4):
                            A("pe", lambda e, vec=vec, j=j, b=b: e.transpose(out=ps[0:NS, b, j * 128:(j + 1) * 128], in_=smp[:, vec, j, :], identity=ident))
                        dst = tok6h[vec // 3][:, vec % 3, :]
                        if vec % 2 == 0:
                            A("act", lambda e, dst=dst, b=b: e.copy(out=dst, in_=ps[0:NS, b, :]))
                        else:
                            A("dve", lambda e, dst=dst, b=b: e.tensor_copy(out=dst, in_=ps[0:NS, b, :]))
                    for hf in range(2):
                        S.dma("sp", lambda e, hf=hf: e.dma_start(out=scr_a[:, hf * 3:(hf + 1) * 3, :], in_=tok6h[hf]))
                    for vec in range(6):
                        S.dma("sp", lambda e, vec=vec: e.dma_start(out=L1v[:, vec, :], in_=scr_a[:, vec, :].rearrange("b (h n) -> b h n", h=8)))

                    def bc_v(vec):
                        return L1v[:, vec, :].unsqueeze(1).broadcast_to([128, 32, 64])

                    def bc_k(ap2, hf):
                        return ap2[:, hf * 32:(hf + 1) * 32].unsqueeze(2).broadcast_to([128, 32, 64])
                    for hf in range(2):
                        Sx, Tx = S_h[hf], T_h[hf]
                        vs = slice(hf * 32, (hf + 1) * 32)
                        A("dve", lambda e, Sx=Sx, Tx=Tx: e.tensor_tensor(out=Tx, in0=Sx, in1=bc_v(4), op=ALU.mult))
                        A("dve", lambda e, Tx=Tx, vs=vs: e.tensor_reduce(out=saL[:, vs], in_=Tx, axis=AX.X, op=ALU.add))
                        A("dve", lambda e, Sx=Sx: e.tensor_tensor(out=Sx, in0=Sx, in1=bc_v(1), op=ALU.mult))
                        A("dve", lambda e, Tx=Tx, hf=hf: e.tensor_tensor(out=Tx, in0=bc_k(saL, hf), in1=bc_v(5), op=ALU.mult))
                        A("dve", lambda e, Sx=Sx, Tx=Tx: e.tensor_tensor(out=Sx, in0=Sx, in1=Tx, op=ALU.add))
                        A("dve", lambda e, Tx=Tx, hf=hf: e.tensor_tensor(out=Tx, in0=bc_k(L1v[:, 3, :], hf), in1=bc_v(2), op=ALU.mult))
                        A("dve", lambda e, Sx=Sx, Tx=Tx: e.tensor_tensor(out=Sx, in0=Sx, in1=Tx, op=ALU.add))
                        S.dma("sp", lambda e, Sx=Sx, hf=hf: e.dma_start(out=o_wkv_s[:, hf * 2048:(hf + 1) * 2048], in_=Sx.rearrange("p v k -> p (v k)")))
                        A("dve", lambda e, Sx=Sx, Tx=Tx: e.tensor_tensor(out=Tx, in0=Sx, in1=bc_v(0), op=ALU.mult))
                        A("dve", lambda e, Tx=Tx, vs=vs: e.tensor_reduce(out=yL1[:, vs], in_=Tx, axis=AX.X, op=ALU.add))
                    S.dma("sp", lambda e: e.dma_start(out=scr_y.rearrange("b (h n) -> (b h) n", h=8), in_=yL1))
                    S.dma("sp", lambda e: e.dma_start(out=ytok, in_=scr_y))
                    b = nb_par()
                    for j in range(4):
                        A("pe", lambda e, j=j, b=b: e.transpose(out=ps[:, b, j * NS:(j + 1) * NS], in_=ytok[:, j * 128:(j + 1) * 128],
                                                                identity=ident[0:NS, 0:NS]))
                    A("act", lambda e, b=b: e.copy(out=yT[:, :, HALF:HALF + NS], in_=ps[:, b, 0:4 * NS].rearrange("p (j n) -> p j n", j=4)))
                    for g_ in range(4):
                        cs_ = list(range(g_ * 4, min(14, g_ * 4 + 4)))
                        b = nb_par()
                        for ci, c in enumerate(cs_):
                            A("pe", lambda e, ci=ci, c=c, b=b: e.transpose(out=ps[0:NS, b, ci * 128:(ci + 1) * 128], in_=shiftS[:, c, :], identity=ident))
                        n_ = len(cs_) * 128
                        A("act", lambda e, g_=g_, b=b, n_=n_: e.copy(out=shtok[:, g_ * 512:g_ * 512 + n_], in_=ps[0:NS, b, 0:n_]))
                    S.dma("sp", lambda e: e.dma_start(out=o_shift_s, in_=shtok[:, 0:RPROJ]))

            if stop_after == "Rsample":
                continue
            ar_.top = rs_mark if nsamp else mark
            bk_tok = [ar_.alloc(BF16, [128, 2, 512]) for _ in range(2)]
            v_tok = [ar_.alloc(BF16, [128, 512]) for _ in range(2)]
            NM_sb = [ar_.alloc(BF16, [128, 8, 2, 128]) for _ in range(2)]
            P_sb2 = [[ar_.alloc(BF16, [128, 8, 64]) for _ in range(2)] for _ in range(2)]
            Tt_sb2 = [[ar_.alloc(BF16, [128, 8, 64]) for _ in range(1)] for _ in range(2)]
            QT_sb2 = [[ar_.alloc(BF16, [128, 8, 2, 64]) for _ in range(2)] for _ in range(2)]
            X_sb = [ar_.alloc(BF16, [128, 8, 64]) for _ in range(2)]
            U_sb = [ar_.alloc(BF16, [128, 8, 64]) for _ in range(2)]
            XV_sb = [ar_.alloc(F32, [128, 8, 64]) for _ in range(2)]
            Hs = ar_.alloc(F32, [128, 4, 64])
            gtmp = [ar_.alloc(F32, [128, TW]) for _ in range(3)]

            for i in range(8):
                q = i % 2
                P_sb, QT_sb, Tt_sb = P_sb2[q], QT_sb2[q], Tt_sb2[q]
                S.tag = "p%d Rpar%d" % (pas, i)
                b1 = nb_par()
                b2 = nb_par()
                pT1 = ps[:, b1, :].bitcast(BF16).rearrange("p (v n) -> p v n", v=2)
                pT2 = ps[:, b2, :].bitcast(BF16)
                for j in range(4):
                    A("pe", lambda e, i=i, j=j, pT1=pT1: e.transpose(out=pT1[:, 0, j * 128:(j + 1) * 128], in_=bT[:, j, i * 128:(i + 1) * 128],
                                                                     identity=ident_bf[:]), ["bT", "ident_bf"], PB(b1))
                    A("pe", lambda e, i=i, j=j, pT1=pT1: e.transpose(out=pT1[:, 1, j * 128:(j + 1) * 128], in_=kT[:, j, i * 128:(i + 1) * 128],
                                                                     identity=ident_bf[:]), ["kT", "ident_bf"], PB(b1))
                    A("pe", lambda e, i=i, j=j, pT2=pT2: e.transpose(out=pT2[:, j * 128:(j + 1) * 128], in_=vT[:, j, i * 128:(i + 1) * 128],
                                                                     identity=ident_bf[:]), ["vT", "ident_bf"], PB(b2))
                A("act", lambda e, q=q, pT1=pT1: e.copy(out=bk_tok[q][:], in_=pT1), PB(b1), ["bk_tok%d" % q])
                A("dve", lambda e, q=q, pT2=pT2: e.tensor_copy(out=v_tok[q][:], in_=pT2[:, 0:512]), PB(b2), ["v_tok%d" % q])
                if stop_after == "c1":
                    break
                for hg in range(2):
                    b = nb_par(2)
                    for hh4 in range(4):
                        h = hg * 4 + hh4
                        j, hh = h // 2, h % 2
                        for e_ in range(2):
                            c = 2 * i + e_
                            for x, src in ((0, bT), (1, kT)):
                                bb, oo = b + hh, ((hh4 // 2) * 2 + x) * 128
                                A("pe", lambda e, src=src, j=j, hh=hh, c=c, e_=e_, bb=bb, oo=oo: e.matmul(
                                    ps[e_ * 64:(e_ + 1) * 64, bb, oo:oo + 128], lhsT=src[hh * 64:(hh + 1) * 64, j, c * 64:(c + 1) * 64],
                                    rhs=ar_t[hh * 64:(hh + 1) * 64, j, c, :, :], start=True, stop=True),
                                  ["bT", "kT", "ar"], PB(b, 2))
                    for hh in range(2):
                        nmv = NM_sb[q][:, hg * 4 + hh:(hg + 1) * 4:2, :, :]
                        A("dve", lambda e, nmv=nmv, b=b, hh=hh: e.tensor_tensor(out=nmv, in0=ps[:, b + hh, :].rearrange("p (jj x n) -> p jj x n", jj=2, x=2),
                                                                                 in1=maskNM_bf.unsqueeze(1).unsqueeze(1).broadcast_to([128, 2, 2, 128]), op=ALU.mult))
                if stop_after == "c2":
                    break
                b = nb_par(2)
                for h in range(8):
                    j, hh = h // 2, h % 2
                    for e_ in range(2):
                        c = 2 * i + e_
                        A("pe", lambda e, j=j, hh=hh, c=c, e_=e_, h=h, b=b: e.matmul(
                            ps[e_ * 64:(e_ + 1) * 64, b + hh, j * 64:(j + 1) * 64], lhsT=ar_t[hh * 64:(hh + 1) * 64, j, c, 0, :],
                            rhs=bT[hh * 64:(hh + 1) * 64, j, c * 64:(c + 1) * 64], start=True, stop=True))
                for hh in range(2):
                    pv = P_sb[0][:, hh:8:2, :]
                    A("dve", lambda e, pv=pv, b=b, hh=hh: e.tensor_tensor(out=pv, in0=ps[:, b + hh, 0:256].rearrange("p (j s) -> p j s", j=4),
                                                                           in1=maskL_bf.unsqueeze(1).broadcast_to([128, 4, 64]), op=ALU.mult))
                if stop_after == "c3":
                    break
                Q0 = NM_sb[q][:, :, 0, 0:64]
                A("dve", lambda e, q=q: e.tensor_tensor(out=QT_sb[1][:, :, 1, :], in0=NM_sb[q][:, :, 0, 0:64],
                                                          in1=maskI_bf.unsqueeze(1).broadcast_to([128, 8, 64]), op=ALU.add))
                ev_ctr = [0]
                import os as _os3
                EVK = int(_os3.environ.get("EVK", "3"))

                def evac_half(bank, e_, dst_ap, shape_pat, **kw):
                    sl = slice(e_ * 64, (e_ + 1) * 64)
                    src = ps[sl, bank, :].rearrange(shape_pat, **kw)
                    ev_ctr[0] += 1
                    if ev_ctr[0] % EVK != 0:
                        A("act", lambda e: e.copy(out=dst_ap, in_=src))
                    else:
                        A("dve", lambda e: e.tensor_copy(out=dst_ap, in_=src))

                def evac2(bk, dst):
                    for e_ in range(2):
                        sl = slice(e_ * 64, (e_ + 1) * 64)
                        evac_half(bk + e_, e_, dst[sl, :, :], "p (h s) -> p h s", h=8)

                bA = nb_par(2)
                bB = nb_par(2)
                for h in range(8):
                    for e_ in range(2):
                        sl = slice(e_ * 64, (e_ + 1) * 64)
                        A("pe", lambda e, h=h, sl=sl, e_=e_, bA=bA: e.matmul(ps[sl, bA + e_, h * 64:(h + 1) * 64], lhsT=Q0[sl, h, :], rhs=P_sb[0][sl, h, :],
                                                                             start=True, stop=True))
                        A("pe", lambda e, h=h, sl=sl, e_=e_, bB=bB: e.matmul(ps[sl, bB + e_, h * 64:(h + 1) * 64], lhsT=P_sb[0][sl, h, :], rhs=Q0[sl, h, :],
                                                                             start=True, stop=True))
                evac2(bA, P_sb[1])
                for e_ in range(2):
                    sl = slice(e_ * 64, (e_ + 1) * 64)
                    evac_half(bB + e_, e_, QT_sb[1][sl, :, 0, :], "p (h s) -> p h s", h=8)
                Tc = None
                for lev in range(1, 6):
                    pi = lev % 2
                    Pc = P_sb[pi]
                    QTc = QT_sb[pi]
                    QTn = QT_sb[1 - pi]
                    last = (lev == 5)
                    if not last:
                        bA = nb_par(2)
                        for h in range(8):
                            for e_ in range(2):
                                sl = slice(e_ * 64, (e_ + 1) * 64)
                                A("pe", lambda e, h=h, sl=sl, e_=e_, bA=bA, QTc=QTc, Pc=Pc: e.matmul(ps[sl, bA + e_, h * 64:(h + 1) * 64], lhsT=QTc[sl, h, 0, :],
                                                                                                      rhs=Pc[sl, h, :], start=True, stop=True))
                    if not last:
                        for hg in range(2):
                            bB = nb_par(2)
                            for h4 in range(4):
                                h = hg * 4 + h4
                                for e_ in range(2):
                                    sl = slice(e_ * 64, (e_ + 1) * 64)
                                    A("pe", lambda e, h=h, h4=h4, sl=sl, e_=e_, bB=bB, QTc=QTc, Pc=Pc: e.matmul(
                                        ps[sl, bB + e_, h4 * 128:(h4 + 1) * 128], lhsT=Pc[sl, h, :], rhs=QTc[sl, h, :, :], start=True, stop=True))
                            for e_ in range(2):
                                sl = slice(e_ * 64, (e_ + 1) * 64)
                                src4 = ps[sl, bB + e_, :].rearrange("p (h x s) -> p h x s", h=4, x=2)
                                hsl = slice(hg * 4, (hg + 1) * 4)
                                A("act", lambda e, sl=sl, src4=src4, hsl=hsl, QTn=QTn: e.copy(out=QTn[sl, hsl, 0, :], in_=src4[:, :, 0, :]))
                                A("dve", lambda e, sl=sl, src4=src4, hsl=hsl, QTn=QTn, QTc=QTc: e.tensor_tensor(out=QTn[sl, hsl, 1, :], in0=src4[:, :, 1, :],
                                                                                                                 in1=QTc[sl, hsl, 1, :], op=ALU.add))
                        evac2(bA, P_sb[1 - pi])
                    else:
                        bB = nb_par(2)
                        for h in range(8):
                            for e_ in range(2):
                                sl = slice(e_ * 64, (e_ + 1) * 64)
                                A("pe", lambda e, h=h, sl=sl, e_=e_, bB=bB, QTc=QTc, Pc=Pc: e.matmul(
                                    ps[sl, bB + e_, h * 64:(h + 1) * 64], lhsT=Pc[sl, h, :], rhs=QTc[sl, h, 1, :], start=True, stop=True))
                        Tc = Tt_sb[0]
                        for e_ in range(2):
                            sl = slice(e_ * 64, (e_ + 1) * 64)
                            A("dve", lambda e, sl=sl, e_=e_, bB=bB, QTc=QTc, Tc=Tc: e.tensor_tensor(out=Tc[sl, :, :], in0=ps[sl, bB + e_, :].rearrange("p (h s) -> p h s", h=8),
                                                                                                     in1=QTc[sl, :, 1, :], op=ALU.add))
                bXV = nb_par(2)
                for h in range(8):
                    for e_ in range(2):
                        sl = slice(e_ * 64, (e_ + 1) * 64)
                        A("pe", lambda e, h=h, sl=sl, e_=e_, q=q, bXV=bXV: e.matmul(ps[sl, bXV + e_, h * 64:(h + 1) * 64], lhsT=NM_sb[q][sl, h, 1, 0:64],
                                                                                    rhs=v_tok[q][sl, h * 64:(h + 1) * 64], start=True, stop=True))
                evac2(bXV, XV_sb[q])
                if stop_after == "c4":
                    break
                S.tag = "p%d Rseq%d" % (pas, i)
                for e_ in range(2):
                    c = 2 * i + e_
                    sl = slice(e_ * 64, (e_ + 1) * 64)
                    xq = c % 2
                    bX = nb_seq(e_)
                    for j in range(4):
                        A("pe", lambda e, j=j, sl=sl, c=c, bX=bX: e.matmul(ps[sl, bX, j * 128:(j + 1) * 128], lhsT=ar_t[:, j, c, 0, :],
                                                                           rhs=H0bd[:, j, :], start=True, stop=True))
                    A("dve", lambda e, sl=sl, xq=xq, bX=bX, q=q: e.tensor_tensor(out=X_sb[xq][sl, :, :], in0=ps[sl, bX, :].rearrange("p (h s) -> p h s", h=8),
                                                                                  in1=XV_sb[q][sl, :, :], op=ALU.add))
                    bU = nb_seq(e_)
                    for h in range(8):
                        A("pe", lambda e, h=h, sl=sl, xq=xq, bU=bU, Tc=Tc: e.matmul(ps[sl, bU, h * 64:(h + 1) * 64], lhsT=Tc[sl, h, :],
                                                                                    rhs=X_sb[xq][sl, h, :], start=True, stop=True),
                          [], PB(bU))
                    A("dve", lambda e, sl=sl, xq=xq, bU=bU: e.tensor_copy(out=U_sb[xq][sl, :, :], in_=ps[sl, bU, :].rearrange("p (h s) -> p h s", h=8)),
                      PB(bU), ["U%d" % xq])
                    bY1 = nb_seq(1 - e_)
                    for j in range(4):
                        A("pe", lambda e, j=j, c=c, bY1=bY1: e.matmul(ps[:, bY1, j * 64:(j + 1) * 64], lhsT=H0bd[:, j, :],
                                                                      rhs=ar_t[:, j, c, 1, :], start=True, stop=True))
                    A("act", lambda e, c=c, bY1=bY1: e.copy(out=yT[:, :, c * 64:(c + 1) * 64], in_=ps[:, bY1, 0:256].rearrange("p (j t) -> p j t", j=4)))
                    bY = nb_seq(e_)
                    for h in range(8):
                        j, hh = h // 2, h % 2
                        hs = slice(hh * 64, (hh + 1) * 64)
                        A("pe", lambda e, h=h, j=j, hs=hs, sl=sl, xq=xq, q=q, bY=bY: e.matmul(ps[hs, bY, j * 64:(j + 1) * 64], lhsT=U_sb[xq][sl, h, :],
                                                                                              rhs=NM_sb[q][sl, h, 0, 64:128], start=True, stop=False))
                        A("pe", lambda e, h=h, j=j, hs=hs, sl=sl, q=q, bY=bY: e.matmul(ps[hs, bY, j * 64:(j + 1) * 64], lhsT=v_tok[q][sl, h * 64:(h + 1) * 64],
                                                                                       rhs=NM_sb[q][sl, h, 1, 64:128], start=False, stop=True))
                    A("dve", lambda e, c=c, bY=bY: e.tensor_tensor(out=yT[:, :, c * 64:(c + 1) * 64], in0=ps[:, bY, 0:256].rearrange("p (j t) -> p j t", j=4),
                                                                    in1=yT[:, :, c * 64:(c + 1) * 64], op=ALU.add))
                    bG = nb_seq(e_)
                    for h in range(8):
                        j, hh = h // 2, h % 2
                        hs = slice(hh * 64, (hh + 1) * 64)
                        A("pe", lambda e, h=h, j=j, hs=hs, sl=sl, xq=xq, q=q, bG=bG: e.matmul(ps[hs, bG, j * 64:(j + 1) * 64], lhsT=bk_tok[q][sl, 0, h * 64:(h + 1) * 64],
                                                                                              rhs=U_sb[xq][sl, h, :], start=True, stop=False),
                          ["bk_tok%d" % q, "U%d" % xq], PB(bG))
                        A("pe", lambda e, h=h, j=j, hs=hs, sl=sl, q=q, bG=bG: e.matmul(ps[hs, bG, j * 64:(j + 1) * 64], lhsT=bk_tok[q][sl, 1, h * 64:(h + 1) * 64],
                                                                                       rhs=v_tok[q][sl, h * 64:(h + 1) * 64], start=False, stop=True),
                          ["bk_tok%d" % q, "v_tok%d" % q], PB(bG))
                    A("dve", lambda e, bG=bG: e.tensor_tensor(out=Hs[:], in0=ps[:, bG, 0:256].rearrange("p (j v) -> p j v", j=4), in1=H0f[:], op=ALU.add),
                      PB(bG) + ["H0f"], ["Hs"])
                    A("dve", lambda e, c=c: e.tensor_tensor(out=H0f[:], in0=Hs[:], in1=WC[:, :, c:c + 1].broadcast_to([128, 4, 64]), op=ALU.mult),
                      ["Hs", "WC"], ["H0f"])
                    A("act", lambda e: e.copy(out=H0bd[0:64, :, 0:64], in_=H0f[0:64, :, :]), ["H0f"], ["H0b"])
                    A("act", lambda e: e.copy(out=H0bd[64:128, :, 64:128], in_=H0f[64:128, :, :]), ["H0f"], ["H0b"])
            if stop_after in ("c1", "c2", "c3", "c4"):
                continue
            if dbg and pas == 0:
                dump("yT", yT, ["yT"])
            if stop_after == "Rchunk":
                continue
            if nsamp:
                S.tag = "p%d Rsample" % pas
                emit_rsample()
            S.tag = "p%d Rpost" % pas
            for j in range(4):
                yj = yT[:, j, :]
                yc, sq_, rs_ = gtmp[0], gtmp[1], gtmp[2]

                def ev_mean(b, c0, c1, j=j):
                    A("dve", lambda e: e.scalar_tensor_tensor(out=yc[:, c0:c1], in0=ps[:, b, 0:c1 - c0], scalar=-1.0 / 64, in1=yT[:, j, c0:c1],
                                                              op0=ALU.mult, op1=ALU.add), PB(b) + ["yT"], ["gtmp0"])
                fp32_blocksum(yj, "yT", blk64, ev_mean)
                A("act", lambda e: e.activation(out=sq_[:, 0:W], in_=yc[:, 0:W], func=AF.Square), ["gtmp0"], ["gtmp1"])

                def ev_var(b, c0, c1):
                    A("act", lambda e: e.activation(out=rs_[:, c0:c1], in_=ps[:, b, 0:c1 - c0], func=AF.Ln, scale=1.0 / 64, bias=epsgn[:, 0:1]))
                fp32_blocksum(sq_, "gtmp1", blk64, ev_var)
                A("act", lambda e: e.activation(out=rs_[:, 0:W], in_=rs_[:, 0:W], func=AF.Exp, scale=-0.5), ["gtmp2"], ["gtmp2"])
                A("dve", lambda e: e.tensor_tensor(out=yc[:, 0:W], in0=yc[:, 0:W], in1=rs_[:, 0:W], op=ALU.mult), ["gtmp0", "gtmp2"], ["gtmp0"])
                A("dve", lambda e, j=j: e.tensor_scalar(out=yc[:, 0:W], in0=yc[:, 0:W], scalar1=pf("ln_g", j), scalar2=pf("ln_b", j),
                                                        op0=ALU.mult, op1=ALU.add), ["gtmp0", "pfm"], ["gtmp0"])
                A("dve", lambda e, j=j: e.tensor_tensor(out=yc[:, 0:W], in0=yc[:, 0:W], in1=bonus[:, j, 0:W], op=ALU.add), ["gtmp0", "bonus"], ["gtmp0"])
                A("dve", lambda e, j=j: e.tensor_tensor(out=oa[:, j, 0:W], in0=yc[:, 0:W], in1=g_bf[:, j, 0:W], op=ALU.mult), ["gtmp0", "g_bf"], ["oa"])
            if dbg and pas == 0:
                dump("oa", oa[:], ["oa"])
            if pas == passes[-1]:
                wst = gtmp[0][0:64, 0:512].rearrange("p (j n) -> p j n", j=4)
                b = nb()
                for j in range(4):
                    A("pe", lambda e, j=j, b=b: e.transpose(out=ps[0:64, b, j * 128:(j + 1) * 128], in_=H0f[:, j, :], identity=ident),
                      ["H0f", "consts"], PB(b))
                A("act", lambda e, b=b: e.copy(out=wst, in_=ps[0:64, b, :].rearrange("p (j n) -> p j n", j=4)), PB(b), ["gtmp0"])
                S.dma("sp", lambda e: e.dma_start(out=o_wkv_p.rearrange("(j hh) v k -> v j hh k", hh=2),
                                                   in_=wst.rearrange("p j (hh k) -> p j hh k", hh=2)), reads=["gtmp0"])
                b = nb()
                A("pe", lambda e, b=b: e.transpose(out=ps[0:14, b, 0:128], in_=carry[:, :], identity=ident), ["carry", "consts"], PB(b))
                A("act", lambda e, b=b: e.copy(out=gtmp[1][0:14, 0:128], in_=ps[0:14, b, 0:128]), PB(b), ["gtmp1"])
                S.dma("sp", lambda e: e.dma_start(out=o_shift_p.rearrange("(c p) -> c p", p=128), in_=gtmp[1][0:14, 0:128]), reads=["gtmp1"])
            if stop_after == "Rpost":
                continue

            S.tag = "p%d H" % pas
            S.barrier()
            ar_ = Arena(arena_t, ARENA_WORDS)
            Eb = ar_.alloc(F32, [128, 4, HALF])
            qT = ar_.alloc(BF16, [128, 4, HALF])
            hkT = ar_.alloc(BF16, [128, 4, HALF])
            hvT = ar_.alloc(BF16, [128, 4, TW])
            sgo = ar_.alloc(BF16, [128, 4, TW])
            oT = ar_.alloc(F32, [128, 4, TW])
            smpH = ar_.alloc(F32, [128, 4, 4, NS])
            hmark = ar_.top
            htmp = [ar_.alloc(F32, [128, TW]) for _ in range(8)]
            hset = [htmp[0:4], htmp[4:8]]
            cmaskH = ar_.alloc(BF16, [128, HALF])
            S.dma("pool", lambda e: e.dma_start(out=cmaskH, in_=consts_d[:, 640:1664]), writes=["cmaskH"])
            HB = RPROJ
            wv, wr = load_w(w_in_v[:, :, HB + 512:HB + 1024], (8, 512))
            hvh = [(0, 512), (512, W)]
            for h in range(4):
                T0, T1_, T2, T3 = hset[h % 2]

                def ev_f(b, c0, c1, T0=T0):
                    A("act", lambda e: e.activation(out=T0[:, c0:c1], in_=ps[:, b, 0:c1 - c0], func=AF.Sigmoid))
                proj_fm(wv, wr, h * 128, 8, hT, "hT", ev_f)
                for (a_, b_) in hvh:
                    b2 = min(b_, HALF)
                    A("dve", lambda e, h=h, a_=a_, b_=b_: e.tensor_scalar(out=T0[:, a_:b_], in0=T0[:, a_:b_], scalar1=dvc("omlb", h), scalar2=dvc("lb", h),
                                                                          op0=ALU.mult, op1=ALU.add))
                    A("dve", lambda e, a_=a_, b_=b_: e.tensor_scalar(out=T1_[:, a_:b_], in0=T0[:, a_:b_], scalar1=-1.0, scalar2=1.0, op0=ALU.mult, op1=ALU.add))
                    A("act", lambda e, a_=a_, b2=b2: e.activation(out=T2[:, a_:b2], in_=T0[:, a_:b2], func=AF.Ln))
                    A("dve", lambda e, a_=a_, b2=b2: e.tensor_tensor_scan(out=T3[:, a_:b2], data0=cmaskH[:, a_:b2], data1=T2[:, a_:b2], initial=0.0,
                                                                          op0=ALU.mult, op1=ALU.add))
                    A("act", lambda e, h=h, a_=a_, b2=b2: e.activation(out=Eb[:, h, a_:b2], in_=T3[:, a_:b2], func=AF.Exp))
                    A("act", lambda e, h=h, a_=a_, b2=b2: e.activation(out=DC[:, h, a_ // 64:b2 // 64], in_=T3[:, a_:b2].rearrange("p (c t) -> p c t", t=64)[:, :, 63],
                                                                       func=AF.Exp))
                    A("act", lambda e, a_=a_, b2=b2: e.activation(out=T2[:, a_:b2], in_=T3[:, a_:b2], func=AF.Exp, scale=-1.0))
                    A("dve", lambda e, h=h, a_=a_, b2=b2: e.tensor_tensor(out=hkT[:, h, a_:b2], in0=T1_[:, a_:b2], in1=T2[:, a_:b2], op=ALU.mult))
                if nsamp:
                    A("act", lambda e, h=h: e.copy(out=smpH[:, 1, h, :], in_=T0[:, HALF:HALF + NS]))
                    A("act", lambda e, h=h: e.copy(out=smpH[:, 2, h, :], in_=T1_[:, HALF:HALF + NS]))
            wv, wr = load_w(w_in_v[:, :, HB:HB + 512], (8, 512))
            for h in range(4):
                T0 = hset[h % 2][0]

                def ev_q(b, c0, c1, T0=T0, h=h):
                    A("act", lambda e: e.activation(out=T0[:, c0:c1], in_=ps[:, b, 0:c1 - c0], func=AF.Silu))
                    if c0 < HALF:
                        A("dve", lambda e: e.tensor_tensor(out=qT[:, h, c0:c1], in0=T0[:, c0:c1], in1=Eb[:, h, c0:c1], op=ALU.mult))
                proj_fm(wv, wr, h * 128, 8, hT, "hT", ev_q)
                if nsamp:
                    A("act", lambda e, h=h: e.copy(out=smpH[:, 0, h, :], in_=T0[:, HALF:HALF + NS]))
            wv, wr = load_w(w_in_v[:, :, HB + 1024:HB + 1536], (8, 512))
            for h in range(4):
                def ev_i(b, c0, c1, h=h):
                    A("dve", lambda e: e.tensor_copy(out=hvT[:, h, c0:c1], in_=ps[:, b, 0:c1 - c0]), PB(b), ["hvT"])
                    if c0 >= HALF:
                        A("dve", lambda e: e.tensor_copy(out=smpH[:, 3, h, :], in_=ps[:, b, 0:NS]), PB(b), ["smpH"])
                proj_fm(wv, wr, h * 128, 8, hT, "hT", ev_i)
            wv, wr = load_w(w_in_v[:, :, HB + 1536:HB + 2048], (8, 512))
            for h in range(4):
                def ev_og(b, c0, c1, h=h):
                    A("act", lambda e: e.activation(out=sgo[:, h, c0:c1], in_=ps[:, b, 0:c1 - c0], func=AF.Sigmoid), PB(b), ["sgo"])
                proj_fm(wv, wr, h * 128, 8, hT, "hT", ev_og)

            if nsamp:
                S.barrier()
                ar_.top = hmark
                S_s = ar_.alloc(F32, [128, NS, 4, 128])
                ktok_s = ar_.alloc(F32, [NS, 512])
                vtok_s = ar_.alloc(F32, [NS, 512])
                vm = [ar_.alloc(F32, [NS, 512]) for _ in range(2)]
                tS_l = [ar_.alloc(F32, [128, 4, 128]) for _ in range(3)]
                q_bf = ar_.alloc(BF16, [128, 4, NS])
                Sb_l = [ar_.alloc(BF16, [128, 4, 128]) for _ in range(3)]
                S.dma("sp", lambda e: e.dma_start(out=S_s, in_=hgrn_s.rearrange("b h k v -> k b h v")), writes=["S_s"])
                for vec, dst, dn in ((2, ktok_s, "ktok_s"), (3, vtok_s, "vtok_s")):
                    b = nb_smp()
                    for h in range(4):
                        A("pe", lambda e, vec=vec, h=h, b=b: e.transpose(out=ps[0:NS, b, h * 128:(h + 1) * 128], in_=smpH[:, vec, h, :], identity=ident),
                          ["smpH", "consts"], PB(b))
                    A("act", lambda e, dst=dst, b=b: e.copy(out=dst, in_=ps[0:NS, b, :]), PB(b), [dn])
                for bi in range(NS):
                    vq = bi % 2
                    tS = tS_l[bi % 3]
                    A("dve", lambda e, bi=bi, vq=vq: e.tensor_scalar(out=vm[vq], in0=vtok_s, scalar1=ident[0:NS, bi:bi + 1], scalar2=None, op0=ALU.mult),
                      ["vtok_s", "consts"], ["vm%d" % vq])
                    b = nb_smp()
                    for h in range(4):
                        A("pe", lambda e, h=h, vq=vq, b=b: e.matmul(ps[:, b, h * 128:(h + 1) * 128], lhsT=ktok_s[:, h * 128:(h + 1) * 128],
                                                                    rhs=vm[vq][:, h * 128:(h + 1) * 128], start=True, stop=True),
                          ["ktok_s", "vm%d" % vq], PB(b))
                    A("dve", lambda e, bi=bi: e.tensor_tensor(out=tS, in0=S_s[:, bi, :, :],
                                                               in1=smpH[:, 1, :, bi:bi + 1].broadcast_to([128, 4, 128]), op=ALU.mult),
                      ["S_s", "smpH"], ["tS"])
                    A("dve", lambda e, bi=bi, b=b: e.tensor_tensor(out=S_s[:, bi, :, :], in0=ps[:, b, :].rearrange("p (h v) -> p h v", h=4), in1=tS,
                                                                    op=ALU.add), PB(b) + ["tS"], ["S_s"])
                S.dma("sp", lambda e: e.dma_start(out=o_hgrn_s.rearrange("b h k v -> k b h v"), in_=S_s), reads=["S_s"])
                A("act", lambda e: e.copy(out=q_bf, in_=smpH[:, 0, :, :]))
                bO_ = nb_smp()
                for bi in range(NS):
                    Sb = Sb_l[bi % 3]
                    A("act", lambda e, bi=bi, Sb=Sb: e.copy(out=Sb, in_=S_s[:, bi, :, :]))
                    for h in range(4):
                        A("pe", lambda e, h=h, bi=bi, Sb=Sb, bO_=bO_: e.matmul(ps[:, bO_, h * NS + bi:h * NS + bi + 1], lhsT=Sb[:, h, :],
                                                                                rhs=q_bf[:, h, bi:bi + 1], start=True, stop=True))
                A("act", lambda e, bO_=bO_: e.copy(out=oT[:, :, HALF:HALF + NS], in_=ps[:, bO_, 0:4 * NS].rearrange("p (h n) -> p h n", h=4)))

            if not nsamp:
                ar_.top = hmark
            par_nb[0] = 4 if nsamp else 6
            par_ctr[0] = 0
            hk_tok = [ar_.alloc(BF16, [128, 512]) for _ in range(2)]
            hv_tok = [ar_.alloc(BF16, [128, 512]) for _ in range(2)]
            PT_sb = [ar_.alloc(BF16, [128, 4, 64]) for _ in range(2)]
            Ss = ar_.alloc(F32, [128, 4, 128])
            ar_.top = hmark
            htmp = [ar_.alloc(F32, [128, TW]) for _ in range(2)]
            for i in range(8):
                q = i % 2
                b1 = nb_par()
                pTk = ps[:, b1, :].bitcast(BF16).rearrange("p (v n) -> p v n", v=2)
                for h in range(4):
                    A("pe", lambda e, i=i, h=h, pTk=pTk: e.transpose(out=pTk[:, 0, h * 128:(h + 1) * 128], in_=hkT[:, h, i * 128:(i + 1) * 128],
                                                                     identity=ident_bf[:]), ["hkT", "ident_bf"], PB(b1))
                    A("pe", lambda e, i=i, h=h, pTk=pTk: e.transpose(out=pTk[:, 1, h * 128:(h + 1) * 128], in_=hvT[:, h, i * 128:(i + 1) * 128],
                                                                     identity=ident_bf[:]), ["hvT", "ident_bf"], PB(b1))
                A("act", lambda e, q=q, pTk=pTk: e.copy(out=hk_tok[q], in_=pTk[:, 0, :]), PB(b1), ["hk_tok%d" % q])
                A("dve", lambda e, q=q, pTk=pTk: e.tensor_copy(out=hv_tok[q], in_=pTk[:, 1, :]), PB(b1), ["hv_tok%d" % q])
                bS = nb_par()
                for h in range(4):
                    for e_ in range(2):
                        c = 2 * i + e_
                        A("pe", lambda e, h=h, e_=e_, c=c, bS=bS: e.matmul(ps[e_ * 64:(e_ + 1) * 64, bS, h * 64:(h + 1) * 64], lhsT=hkT[:, h, c * 64:(c + 1) * 64],
                                                                           rhs=qT[:, h, c * 64:(c + 1) * 64], start=True, stop=True), ["hkT", "qT"], PB(bS))
                A("dve", lambda e, q=q, bS=bS: e.tensor_tensor(out=PT_sb[q], in0=ps[:, bS, 0:256].rearrange("p (h t) -> p h t", h=4),
                                                                in1=masks_bf[:, 64:128].unsqueeze(1).broadcast_to([128, 4, 64]), op=ALU.mult),
                  PB(bS) + ["consts"], ["PT%d" % q])
                bO = nb_par()
                for e_ in range(2):
                    c = 2 * i + e_
                    sl = slice(e_ * 64, (e_ + 1) * 64)
                    for h in range(4):
                        oo = h * 128 + e_ * 64
                        A("pe", lambda e, h=h, c=c, oo=oo, bO=bO: e.matmul(ps[:, bO, oo:oo + 64], lhsT=S0b[:, h, :], rhs=qT[:, h, c * 64:(c + 1) * 64],
                                                                           start=True, stop=False), ["S0b", "qT"], PB(bO))
                        A("pe", lambda e, h=h, sl=sl, q=q, oo=oo, bO=bO: e.matmul(ps[:, bO, oo:oo + 64], lhsT=hv_tok[q][sl, h * 128:(h + 1) * 128],
                                                                                  rhs=PT_sb[q][sl, h, :], start=False, stop=True),
                          ["hv_tok%d" % q, "PT%d" % q], PB(bO))
                    bG = nb_seq(e_)
                    for h in range(4):
                        A("pe", lambda e, h=h, sl=sl, q=q, bG=bG: e.matmul(ps[:, bG, h * 128:(h + 1) * 128], lhsT=hk_tok[q][sl, h * 128:(h + 1) * 128],
                                                                           rhs=hv_tok[q][sl, h * 128:(h + 1) * 128], start=True, stop=True),
                          ["hk_tok%d" % q, "hv_tok%d" % q], PB(bG))
                    A("dve", lambda e, bG=bG: e.tensor_tensor(out=Ss, in0=ps[:, bG, :].rearrange("p (h v) -> p h v", h=4), in1=S0f[:], op=ALU.add),
                      PB(bG) + ["S0f"], ["Ss"])
                    A("dve", lambda e, c=c: e.tensor_tensor(out=S0f[:], in0=Ss, in1=DC[:, :, c:c + 1].broadcast_to([128, 4, 128]), op=ALU.mult),
                      ["Ss", "DC"], ["S0f"])
                    A("act", lambda e: e.copy(out=S0b[:], in_=S0f[:]), ["S0f"], ["S0b"])
                A("act", lambda e, i=i, bO=bO: e.copy(out=oT[:, :, i * 128:(i + 1) * 128], in_=ps[:, bO, :].rearrange("p (h t) -> p h t", h=4)),
                  PB(bO), ["oT"])
            par_nb[0] = 6
            if dbg and pas == 0:
                dump("oT", oT, ["oT"])
            for h in range(4):
                A("act", lambda e, h=h: e.activation(out=htmp[0][:, 0:W], in_=oT[:, h, 0:W], func=AF.Square), ["oT"], ["htmp0"])

                def ev_ms(b, c0, c1):
                    A("act", lambda e: e.activation(out=htmp[1][:, c0:c1], in_=ps[:, b, 0:c1 - c0], func=AF.Ln, scale=1.0 / 128, bias=epsrms[:, 0:1]))
                fp32_blocksum(htmp[0], "htmp0", ones, ev_ms)
                A("act", lambda e: e.activation(out=htmp[1][:, 0:W], in_=htmp[1][:, 0:W], func=AF.Exp, scale=-0.5), ["htmp1"], ["htmp1"])
                A("dve", lambda e, h=h: e.tensor_tensor(out=htmp[0][:, 0:W], in0=oT[:, h, 0:W], in1=htmp[1][:, 0:W], op=ALU.mult), ["oT", "htmp1"], ["htmp0"])
                A("dve", lambda e, h=h: e.scalar_tensor_tensor(out=ob[:, h, 0:W], in0=htmp[0][:, 0:W], scalar=pf("hng", h), in1=sgo[:, h, 0:W],
                                                               op0=ALU.mult, op1=ALU.mult), ["htmp0", "pfm", "sgo"], ["ob"])
            if dbg and pas == 0:
                dump("ob", ob[:], ["ob"])
            if pas == passes[-1]:
                S.dma("sp", lambda e: e.dma_start(out=o_hgrn_p.rearrange("h k v -> k h v"), in_=S0f[:]), reads=["S0f"])
            if stop_after == "H":
                continue

            S.barrier()
            ar_ = Arena(arena_t, ARENA_WORDS)
            x_tok = ar_.alloc(F32, [128, 8, 1024])
            xs_tok = ar_.alloc(F32, [NS, 1024])
            mergedT = ar_.alloc(BF16, [128, 8, TW])
            gm = [ar_.alloc(F32, [128, 512]) for _ in range(4)]
            GB = RPROJ + 2048
            w_upa_v = w_up_a.rearrange("(kc p) n -> p kc n", p=128)
            w_upb_v = w_up_b.rearrange("(kc p) n -> p kc n", p=128)
            for dcg in range(2):
                wga, wgar = load_w(w_in_v[:, :, GB + dcg * 512:GB + (dcg + 1) * 512], (8, 512))
                wgb, wgbr = load_w(w_in_v[:, :, GB + 1024 + dcg * 512:GB + 1024 + (dcg + 1) * 512], (8, 512))
                wu, wur = load_w(w_upa_v[:, :, dcg * 512:(dcg + 1) * 512], (4, 512))
                iu_ = (wctr[0] - 1) % NWB
                wub_view = wbuf[iu_][:, 4:8, :]
                S.dma("pool", lambda e, wub_view=wub_view, dcg=dcg: e.dma_start(out=wub_view, in_=w_upb_v[:, :, dcg * 512:(dcg + 1) * 512]), writes=[wur])
                for dc in range(4):
                    n0 = dc * 128
                    for (c0, c1) in blocks:
                        w_ = c1 - c0
                        b1, b2, b3, b4 = nb(), nb(), nb(), nb()
                        for kc in range(8):
                            A("pe", lambda e, kc=kc, b1=b1, c0=c0, c1=c1, n0=n0, wga=wga: e.matmul(ps[:, b1, 0:c1 - c0], lhsT=wga[:, kc, n0:n0 + 128],
                                                                                                   rhs=hT[:, kc, c0:c1], start=(kc == 0), stop=(kc == 7)),
                              [wgar, "hT"], PB(b1))
                        for kc in range(8):
                            A("pe", lambda e, kc=kc, b2=b2, c0=c0, c1=c1, n0=n0, wgb=wgb: e.matmul(ps[:, b2, 0:c1 - c0], lhsT=wgb[:, kc, n0:n0 + 128],
                                                                                                   rhs=hT[:, kc, c0:c1], start=(kc == 0), stop=(kc == 7)),
                              [wgbr, "hT"], PB(b2))
                        for kc in range(4):
                            A("pe", lambda e, kc=kc, b3=b3, c0=c0, c1=c1, n0=n0, wu=wu: e.matmul(ps[:, b3, 0:c1 - c0], lhsT=wu[:, kc, n0:n0 + 128],
                                                                                                 rhs=oa[:, kc, c0:c1], start=(kc == 0), stop=(kc == 3)),
                              [wur, "oa"], PB(b3))
                        for kc in range(4):
                            A("pe", lambda e, kc=kc, b4=b4, c0=c0, c1=c1, n0=n0, wub_view=wub_view: e.matmul(ps[:, b4, 0:c1 - c0], lhsT=wub_view[:, kc, n0:n0 + 128],
                                                                                                             rhs=ob[:, kc, c0:c1], start=(kc == 0), stop=(kc == 3)),
                              [wur, "ob"], PB(b4))
                        A("act", lambda e, b1=b1, w_=w_: e.activation(out=gm[0][:, 0:w_], in_=ps[:, b1, 0:w_], func=AF.Sigmoid), PB(b1), ["gm0"])
                        A("act", lambda e, b2=b2, w_=w_: e.activation(out=gm[1][:, 0:w_], in_=ps[:, b2, 0:w_], func=AF.Sigmoid), PB(b2), ["gm1"])
                        A("dve", lambda e, b3=b3, w_=w_: e.tensor_tensor(out=gm[2][:, 0:w_], in0=ps[:, b3, 0:w_], in1=gm[0][:, 0:w_], op=ALU.mult),
                          PB(b3) + ["gm0"], ["gm2"])
                        A("dve", lambda e, b4=b4, w_=w_: e.tensor_tensor(out=gm[3][:, 0:w_], in0=ps[:, b4, 0:w_], in1=gm[1][:, 0:w_], op=ALU.mult),
                          PB(b4) + ["gm1"], ["gm3"])
                        A("dve", lambda e, dcg=dcg, dc=dc, c0=c0, c1=c1, w_=w_: e.tensor_tensor(out=mergedT[:, dcg * 4 + dc, c0:c1], in0=gm[2][:, 0:w_],
                                                                                                 in1=gm[3][:, 0:w_], op=ALU.add),
                          ["gm2", "gm3"], ["mergedT"])
            if dbg and pas == 0:
                dump("mergedT", mergedT, ["mergedT"])
            if stop_after == "G":
                continue

            S.barrier()
            ar_.top = 0
            x_tok = ar_.alloc(F32, [128, 8, 1024])
            xs_tok = ar_.alloc(F32, [NS, 1024])
            mergedT = ar_.alloc(BF16, [128, 8, TW])
            for tt in range(8):
                S.dma("sp", lambda e, tt=tt: e.dma_start(out=x_tok[:, tt, :], in_=x_p[t0 + tt * 128:t0 + (tt + 1) * 128, :]),
                      writes=["x%d" % tt])
            if nsamp:
                S.dma("sp", lambda e: e.dma_start(out=xs_tok, in_=x_s), writes=["xs"])
            w_out_v = w_out.rearrange("(kc p) n -> p kc n", p=128)
            wos = [load_w(w_out_v[:, :, half * 512:(half + 1) * 512], (8, 512))[0] for half in range(2)]
            for tt in range(8):
                for half in range(2):
                    wo = wos[half]
                    b = nb()
                    for kc in range(8):
                        A("pe", lambda e, kc=kc, tt=tt, b=b, wo=wo: e.matmul(ps[:, b, :], lhsT=mergedT[:, kc, tt * 128:(tt + 1) * 128], rhs=wo[:, kc, :],
                                                                             start=(kc == 0), stop=(kc == 7)))
                    A("dve", lambda e, tt=tt, half=half, b=b: e.tensor_tensor(out=x_tok[:, tt, half * 512:(half + 1) * 512], in0=ps[:, b, :],
                                                                               in1=x_tok[:, tt, half * 512:(half + 1) * 512], op=ALU.add))
            if nsamp:
                for half in range(2):
                    wo = wos[half]
                    b = nb()
                    for kc in range(8):
                        A("pe", lambda e, kc=kc, b=b, wo=wo: e.matmul(ps[0:NS, b, :], lhsT=mergedT[:, kc, HALF:HALF + NS], rhs=wo[:, kc, :],
                                                                      start=(kc == 0), stop=(kc == 7)))
                    A("dve", lambda e, half=half, b=b: e.tensor_tensor(out=xs_tok[:, half * 512:(half + 1) * 512], in0=ps[0:NS, b, :],
                                                                        in1=xs_tok[:, half * 512:(half + 1) * 512], op=ALU.add))
            norm_phase("O", x_tok, xs_tok, g_ffn, ar_,
                       lambda gB, h_tok, hs_tok, R: to_hT(gB, h_tok, hs_tok, R, x_tok, xs_tok, lambda tt: "x%d" % tt, "xs"),
                       lambda tt: "x%d" % tt, "xs")
            if dbg and pas == 0:
                dump("hT2", hT[:], ["hT"])
            if stop_after == "O":
                continue

            S.barrier()
            ar_.top = 0
            x_tok = ar_.alloc(F32, [128, 8, 1024])
            xs_tok = ar_.alloc(F32, [NS, 1024])
            actT = ar_.alloc(BF16, [128, 22, TW])
            wdn = ar_.alloc(BF16, [128, 22, 1024])
            gBf = ftmp[:, :, :].rearrange("p a b -> p (a b)")
            junkD = ar_.alloc(BF16, [128, 1024])
            w_fd_v = w_fd.rearrange("(fc p) n -> p fc n", p=128)
            w_fg_v = w_fg.rearrange("(kc p) n -> p kc n", p=128)
            w_fu_v = w_fu.rearrange("(kc p) n -> p kc n", p=128)
            for fg in range(11):
                wg, wgr = load_w(w_fg_v[:, :, fg * 256:(fg + 1) * 256], (8, 256))
                wu, wur = load_w(w_fu_v[:, :, fg * 256:(fg + 1) * 256], (8, 256))
                if fg in (2, 4, 6, 8):
                    k_ = (fg - 2) // 2
                    lo, hi = (0, 6, 12, 18)[k_], (6, 12, 18, 22)[k_]
                    S.dma("pool", lambda e, lo=lo, hi=hi: e.dma_start(out=wdn[:, lo:hi, :], in_=w_fd_v[:, lo:hi, :]), writes=["wdn%d" % k_])
                for f2 in range(2):
                    fc = fg * 2 + f2
                    n0 = f2 * 128
                    for (c0, c1) in blocks:
                        w_ = c1 - c0
                        b1, b2 = nb(), nb()
                        for kc in range(8):
                            A("pe", lambda e, kc=kc, b1=b1, c0=c0, c1=c1, n0=n0, wg=wg: e.matmul(ps[:, b1, 0:c1 - c0], lhsT=wg[:, kc, n0:n0 + 128],
                                                                                                 rhs=hT[:, kc, c0:c1], start=(kc == 0), stop=(kc == 7)),
                              [wgr, "hT"], PB(b1))
                        for kc in range(8):
                            A("pe", lambda e, kc=kc, b2=b2, c0=c0, c1=c1, n0=n0, wu=wu: e.matmul(ps[:, b2, 0:c1 - c0], lhsT=wu[:, kc, n0:n0 + 128],
                                                                                                 rhs=hT[:, kc, c0:c1], start=(kc == 0), stop=(kc == 7)),
                              [wur, "hT"], PB(b2))
                        fq = (fc + (c0 // 512)) % 2
                        A("act", lambda e, b1=b1, w_=w_, fq=fq: e.activation(out=ftmp[:, fq, 0:w_], in_=ps[:, b1, 0:w_], func=AF.Silu), PB(b1), ["ftmp%d" % fq])
                        A("dve", lambda e, b2=b2, w_=w_, fq=fq, fc=fc, c0=c0, c1=c1: e.tensor_tensor(out=actT[:, fc, c0:c1], in0=ps[:, b2, 0:w_],
                                                                                                      in1=ftmp[:, fq, 0:w_], op=ALU.mult),
                          PB(b2) + ["ftmp%d" % fq], ["actT"])
            if stop_after == "F":
                continue

            S.barrier()
            S.dma("sp", lambda e: e.dma_start(out=gBf, in_=g_fin.partition_broadcast(128)), writes=["gBf"])
            A("dve", lambda e: e.memset(stat[:, 0:32], 0.0), [], ["stat"])
            wres_all = ["wdn0", "wdn1", "wdn2", "wdn3"]
            for tt in range(8):
                for half in range(2):
                    b = nb()
                    for fc in range(22):
                        A("pe", lambda e, fc=fc, tt=tt, half=half, b=b: e.matmul(ps[:, b, :], lhsT=actT[:, fc, tt * 128:(tt + 1) * 128],
                                                                                 rhs=wdn[:, fc, half * 512:(half + 1) * 512], start=(fc == 0), stop=(fc == 21)),
                          ["actT"] + wres_all, PB(b))
                    A("dve", lambda e, tt=tt, half=half, b=b: e.tensor_tensor(out=x_tok[:, tt, half * 512:(half + 1) * 512], in0=ps[:, b, :],
                                                                               in1=x_tok[:, tt, half * 512:(half + 1) * 512], op=ALU.add),
                      PB(b) + ["x%d" % tt], ["x%d" % tt])
                A("act", lambda e, tt=tt: e.activation(out=junkD, in_=x_tok[:, tt, :], func=AF.Square,
                                                       accum_out=stat[:, tt:tt + 1]), ["x%d" % tt, "stat"], ["ftmp0", "ftmp1", "stat%d" % tt])
                A("dve", lambda e, tt=tt: e.tensor_scalar(out=stat[:, 16 + tt:17 + tt], in0=stat[:, tt:tt + 1], scalar1=1.0 / D, scalar2=RMS_EPS,
                                                          op0=ALU.mult, op1=ALU.add), ["stat%d" % tt, "stat"], ["stat%d" % tt])
                A("act", lambda e, tt=tt: e.activation(out=stat[:, 16 + tt:17 + tt], in_=stat[:, 16 + tt:17 + tt], func=AF.Ln), ["stat%d" % tt], ["stat%d" % tt])
                A("act", lambda e, tt=tt: e.activation(out=stat[:, 16 + tt:17 + tt], in_=stat[:, 16 + tt:17 + tt], func=AF.Exp, scale=-0.5),
                  ["stat%d" % tt], ["stat%d" % tt])
                A("dve", lambda e, tt=tt: e.scalar_tensor_tensor(out=x_tok[:, tt, :], in0=x_tok[:, tt, :], scalar=stat[:, 16 + tt:17 + tt], in1=gBf,
                                                                 op0=ALU.mult, op1=ALU.mult), ["x%d" % tt, "stat%d" % tt, "gBf"], ["x%d" % tt])
                S.dma("sp", lambda e, tt=tt: e.dma_start(out=y_p[t0 + tt * 128:t0 + (tt + 1) * 128, :], in_=x_tok[:, tt, :]), reads=["x%d" % tt])
            if nsamp:
                for half in range(2):
                    b = nb()
                    for fc in range(22):
                        A("pe", lambda e, fc=fc, half=half, b=b: e.matmul(ps[0:NS, b, :], lhsT=actT[:, fc, HALF:HALF + NS],
                                                                          rhs=wdn[:, fc, half * 512:(half + 1) * 512], start=(fc == 0), stop=(fc == 21)),
                          ["actT"] + wres_all, PB(b))
                    A("dve", lambda e, half=half, b=b: e.tensor_tensor(out=xs_tok[:, half * 512:(half + 1) * 512], in0=ps[0:NS, b, :],
                                                                        in1=xs_tok[:, half * 512:(half + 1) * 512], op=ALU.add), PB(b) + ["xs"], ["xs"])
                A("act", lambda e: e.activation(out=junkD[0:NS, :], in_=xs_tok, func=AF.Square,
                                                accum_out=stat[0:NS, 8:9]), ["xs", "stat"], ["ftmp0", "ftmp1", "stat8"])
                A("dve", lambda e: e.tensor_scalar(out=stat[0:NS, 24:25], in0=stat[0:NS, 8:9], scalar1=1.0 / D, scalar2=RMS_EPS,
                                                    op0=ALU.mult, op1=ALU.add), ["stat8", "stat"], ["stat8"])
                A("act", lambda e: e.activation(out=stat[0:NS, 24:25], in_=stat[0:NS, 24:25], func=AF.Ln), ["stat8"], ["stat8"])
                A("act", lambda e: e.activation(out=stat[0:NS, 24:25], in_=stat[0:NS, 24:25], func=AF.Exp, scale=-0.5), ["stat8"], ["stat8"])
                A("dve", lambda e: e.scalar_tensor_tensor(out=xs_tok, in0=xs_tok, scalar=stat[0:NS, 24:25], in1=gBf[0:NS, :],
                                                          op0=ALU.mult, op1=ALU.mult), ["xs", "stat8", "gBf"], ["xs"])
                S.dma("sp", lambda e: e.dma_start(out=y_s, in_=xs_tok), reads=["xs"])
        S.finalize()
        S.emit(st)
    return nc


DBG_EXTRA = {"oa": [128, 4, 1040], "oT": [128, 4, 1040], "ob": [128, 4, 1040], "mergedT": [128, 8, 1040], "hT2": [128, 8, 1040]}


def _host_maps(inp, ncores=8):
    f = np.ascontiguousarray

    def a32(v):
        return np.asarray(v, dtype=np.float32)

    def fm(v, ncol):
        return f(a32(v).reshape(ncol, 128).T)

    pfm = np.concatenate([fm(inp['rwkv_mu'][0], 14), fm(inp['rwkv_w0'][0], 4), fm(inp['rwkv_a0'][0], 4), fm(inp['rwkv_k_k'][0], 4),
                          fm(inp['rwkv_k_a'][0], 4), fm(a32(inp['rwkv_r_k'][0]).reshape(-1), 4), fm(inp['rwkv_ln_g'][0], 4),
                          fm(inp['rwkv_ln_b'][0], 4), fm(inp['hgrn_lb'][0], 4), fm(inp['hgrn_lb'][1], 4), fm(inp['hgrn_norm_g'][0], 4)], axis=1)
    shared = dict(w_in=f(a32(inp['w_in'][0])), w2a2=f(np.concatenate([a32(inp['rwkv_w2'][0]), a32(inp['rwkv_a2'][0])], axis=0)),
                  g2=f(a32(inp['rwkv_g2'][0])), w_up_a=f(a32(inp['w_up_a'][0])), w_up_b=f(a32(inp['w_up_b'][0])), w_out=f(a32(inp['w_out'][0])),
                  w_fg=f(a32(inp['w_ffn_gate'][0])), w_fu=f(a32(inp['w_ffn_up'][0])), w_fd=f(a32(inp['w_ffn_down'][0])), pfm=f(pfm),
                  consts=make_consts(), g_mix=f(a32(inp['norm_mix_g'][0])), g_ffn=f(a32(inp['norm_ffn_g'][0])), g_fin=f(a32(inp['norm_final_g'])))
    maps = []
    for c in range(ncores):
        m = dict(shared)
        m['x_p'] = f(a32(inp['x_prompt'][c]))
        m['x_s'] = f(a32(inp['x_sample'][c * NS:(c + 1) * NS, 0]))
        m['wkv_s'] = f(a32(inp['state_rwkv_wkv'][0, c * NS:(c + 1) * NS]).reshape(NS * 8, 4096))
        m['shift_s'] = f(a32(inp['state_rwkv_shift'][0, c * NS:(c + 1) * NS]))
        m['hgrn_s'] = f(a32(inp['state_hgrn'][0, c * NS:(c + 1) * NS]))
        maps.append(m)
    return maps


_NC_CACHE = {}


def kernel(**inputs):
    ncores = 8
    if "nc" not in _NC_CACHE:
        _NC_CACHE["nc"] = build_program()
    nc = _NC_CACHE["nc"]
    maps = _host_maps(inputs, ncores)
    res = run_bass_kernel_spmd(nc, maps, core_ids=list(range(ncores)))
    r = res.results
    y_p = np.stack([r[c]["y_p"] for c in range(ncores)]).astype(np.float32)
    y_s = np.concatenate([r[c]["y_s"] for c in range(ncores)], axis=0).reshape(128, 1, D).astype(np.float32)
    wkv_p = np.stack([r[c]["o_wkv_p"] for c in range(ncores)])[None].astype(np.float32)
    shift_p = np.stack([r[c]["o_shift_p"] for c in range(ncores)])[None].astype(np.float32)
    hgrn_p = np.stack([r[c]["o_hgrn_p"] for c in range(ncores)])[None].astype(np.float32)
    wkv_s = np.concatenate([r[c]["o_wkv_s"].reshape(NS, 8, 64, 64) for c in range(ncores)], axis=0)[None].astype(np.float32)
    shift_s = np.concatenate([r[c]["o_shift_s"] for c in range(ncores)], axis=0)[None].astype(np.float32)
    hgrn_s = np.concatenate([r[c]["o_hgrn_s"] for c in range(ncores)], axis=0)[None].astype(np.float32)
    return (y_p, y_s, wkv_p, shift_p, hgrn_p, wkv_s, shift_s, hgrn_s)
```

```python
from contextlib import ExitStack
import numpy as np
import concourse.bass as bass
import concourse.mybir as mybir
from concourse.bass_utils import run_bass_kernel_spmd

F32 = mybir.dt.float32
BF16 = mybir.dt.bfloat16
AF = mybir.ActivationFunctionType
ALU = mybir.AluOpType
AX = mybir.AxisListType

COMPUTE = ("pe", "act", "dve", "pool")


class Op:
    __slots__ = ("idx", "eng", "fn", "call", "is_dma", "deps", "val", "marked", "slot", "acc", "dur", "xfer", "seq", "mode")

    def __init__(self, idx, eng, fn, call, is_dma):
        self.idx = idx
        self.eng = eng
        self.fn = fn
        self.call = call
        self.is_dma = is_dma
        self.deps = ()
        self.val = None
        self.marked = False
        self.slot = None
        self.acc = None
        self.dur = 100.0
        self.xfer = 0.0
        self.seq = 0
        self.mode = None


class _Rec:
    def __init__(self):
        self.call = None

    def __getattr__(self, name):
        def f(*a, **k):
            self.call = (name, a, k)
            return self
        return f


_WRITE_KW = ("out", "accum_out", "ap")
_ESZ = {}


def _esz(dt):
    if dt not in _ESZ:
        _ESZ[dt] = 4 if dt == F32 else 2
    return _ESZ[dt]


def _ap_intervals(ap):
    sp = str(ap.space)
    esz = _esz(ap.dtype)
    dims = list(ap.ap)
    if sp == "DRAM":
        base = ap.offset
        p0, p1 = 0, 1
        fd = dims
    else:
        pstride, pcnt = dims[0]
        p0 = ap.offset // pstride
        base = ap.offset % pstride
        p1 = p0 + pcnt
        fd = dims[1:]
    fd = sorted([(st_, c) for (st_, c) in fd if c > 1 and st_ != 0])
    run = 1
    k = 0
    while k < len(fd) and fd[k][0] == run:
        run *= fd[k][1]
        k += 1
    outer = fd[k:]
    n_outer = 1
    for st_, c in outer:
        n_outer *= c
    if n_outer <= 32:
        offs = [0]
        for st_, c in outer:
            offs = [o + i * st_ for o in offs for i in range(c)]
        iv = [((base + o) * esz, (base + o + run) * esz) for o in offs]
    else:
        ext = sum((c - 1) * st_ for st_, c in outer) + run
        iv = [(base * esz, (base + ext) * esz)]
    return sp, ap.name, p0, p1, iv


def _call_accesses(call):
    name, a, k = call
    items = []
    if name == "matmul":
        items.append((a[0] if a else k.get("out"), True))
        for kw in ("lhsT", "rhs"):
            items.append((k[kw], False))
    elif name == "memset":
        items.append((a[0] if a else k.get("ap"), True))
    else:
        for kw, v in k.items():
            if hasattr(v, "ap") and hasattr(v, "space"):
                items.append((v, kw in _WRITE_KW))
        for v in a:
            if hasattr(v, "ap") and hasattr(v, "space"):
                items.append((v, False))
    acc = []
    for ap, is_w in items:
        if ap is None:
            continue
        sp, nm, p0, p1, iv = _ap_intervals(ap)
        if sp == "PSUM":
            banks = set()
            for b0, b1 in iv:
                for bk in range(b0 // 2048, (b1 - 1) // 2048 + 1):
                    banks.add(bk)
            for bk in banks:
                acc.append((("ps", bk), 0, 128, 0, 1, is_w))
        else:
            for b0, b1 in iv:
                acc.append(((sp, nm), p0, p1, b0, b1, is_w))
    return acc


def _free_elems(ap):
    n = 1
    for s_ in ap.shape[1:]:
        n *= s_
    return n


_WI = {"fp32": 1.0, "dve": 1.0, "act": 1.0, "pe_small": 1.0, "pe_big": 1.0, "lat": 1.0, "pool": 1.0}


def _est(op):
    _est0(op)
    name = op.call[0]
    if op.is_dma:
        return
    if name == "matmul":
        k = op.call[2]
        if k["lhsT"].dtype == F32:
            op.dur *= _WI["fp32"]
        elif _free_elems(k["rhs"]) >= 512:
            op.dur *= _WI["pe_big"]
        else:
            op.dur *= _WI["pe_small"]
    elif op.eng in ("dve", "act", "pool"):
        op.dur *= _WI[op.eng]


def _est0(op):
    name, a, k = op.call
    if op.is_dma:
        ap = k.get("out")
        nbytes = 1
        for s_ in ap.shape:
            nbytes *= s_
        nbytes *= _esz(ap.dtype)
        op.dur = 1200.0 if op.eng == "pool" else 150.0
        op.xfer = 2000.0 + nbytes / 150.0
        return
    def _rnd(v):
        return 32 if v <= 32 else (64 if v <= 64 else 128)
    if name == "matmul":
        n = _free_elems(k["rhs"])
        f = 4.0 if k["lhsT"].dtype == F32 else 1.0
        op.dur = (16.0 + max(n, 48) * 0.42) * f
        op.mode = (_rnd(k["lhsT"].shape[0]), _rnd(_free_elems(k["lhsT"])), f)
    elif name == "transpose":
        op.dur = 70.0
        op.mode = ("T", _rnd(k["in_"].shape[0]), _rnd(_free_elems(k["in_"])))
    elif op.eng == "act":
        op.dur = 120.0 + _free_elems(k["out"]) * 0.85
        fn_ = k.get("func")
        if fn_ in (AF.Exp, AF.Ln):
            op.mode = "A"
        elif fn_ in (AF.Sigmoid, AF.Tanh):
            op.mode = "B"
        elif fn_ == AF.Silu:
            op.mode = "C"
    else:
        o = k.get("out") if "out" in k else (a[0] if a else None)
        n = _free_elems(o) if o is not None else 64
        if op.eng == "pool":
            op.dur = 100.0 + n * 2.2
        else:
            op.dur = 70.0 + n * 1.17


class Sched:
    def __init__(self, nc, dma_slots=None, reorder=True):
        self.nc = nc
        self.ops = []
        self.dma_slots = dma_slots or {"sp": 8, "act": 2, "pool": 6, "dve": 2, "pe": 2}
        self.reorder = reorder
        self.tag = ""
        self.tags = []

    def _mk(self, eng, fn, is_dma):
        rec = _Rec()
        fn(rec)
        name, a, k = rec.call
        op = Op(len(self.ops), eng, (lambda e: getattr(e, name)(*a, **k)), (name, a, k), is_dma)
        self.ops.append(op)
        self.tags.append(self.tag)
        return op

    def add(self, eng, fn, reads=(), writes=()):
        return self._mk(eng, fn, False)

    def dma(self, eng, fn, reads=(), writes=()):
        return self._mk(eng, fn, True)

    def barrier(self):
        pass

    def _build_dag(self):
        wlog = {}
        rlog = {}
        for op in self.ops:
            acc = _call_accesses(op.call)
            _est(op)
            deps = set()
            for key, p0, p1, b0, b1, is_w in acc:
                for (q0, q1, c0, c1, oi, oe) in wlog.get(key, ()):
                    if q0 < p1 and p0 < q1 and c0 < b1 and b0 < c1:
                        deps.add(oi)
                if is_w or key[0] == "ps":
                    for (q0, q1, c0, c1, oi, oe) in rlog.get(key, ()):
                        if q0 < p1 and p0 < q1 and c0 < b1 and b0 < c1:
                            if is_w or oe != op.eng:
                                deps.add(oi)
            for key, p0, p1, b0, b1, is_w in acc:
                if is_w:
                    for lg in (wlog, rlog):
                        l_ = lg.get(key)
                        if l_:
                            lg[key] = [e_ for e_ in l_ if not (p0 <= e_[0] and e_[1] <= p1 and b0 <= e_[2] and e_[3] <= b1)]
                    wlog.setdefault(key, []).append((p0, p1, b0, b1, op.idx, op.eng))
                else:
                    rlog.setdefault(key, []).append((p0, p1, b0, b1, op.idx, op.eng))
            deps.discard(op.idx)
            op.deps = tuple(sorted(deps))

    def _list_schedule(self):
        import heapq
        ops = self.ops
        n = len(ops)
        if not self.reorder:
            return list(range(n))
        npred = [len(o.deps) for o in ops]
        succ = [[] for _ in range(n)]
        for o in ops:
            for d in o.deps:
                succ[d].append(o.idx)
        ready_t = [0.0] * n
        fin = [0.0] * n
        import os as _os2
        PRI = _os2.environ.get("KPRI", "5")
        prio = list(range(n))
        if PRI != "idx":
            cp = [0.0] * n
            for i_ in range(n - 1, -1, -1):
                m = 0.0
                for s_ in succ[i_]:
                    if cp[s_] > m:
                        m = cp[s_]
                cp[i_] = m + ops[i_].dur + ops[i_].xfer + 200.0
            w_ = float(PRI)
            rank = sorted(range(n), key=lambda i_: (i_ - w_ * cp[i_] / 100.0))
            for r_, i_ in enumerate(rank):
                prio[i_] = r_
            for i_ in range(min(getattr(self, "front", 0), n)):
                prio[i_] = i_ - n
        engs = ("pe", "act", "dve", "pool", "sp")
        future = {e: [] for e in engs}
        avail = {e: [] for e in engs}
        free = {e: 0.0 for e in engs}
        for o in ops:
            if npred[o.idx] == 0:
                heapq.heappush(future[o.eng], (0.0, o.idx))
        order = []
        self.start_t = {}
        pe_mode = [None]
        act_set = [None]
        WINDOW = 3000
        done_upto = 0
        sched = [False] * n
        while len(order) < n:
            best = None
            for e in engs:
                fu, av = future[e], avail[e]
                while fu and fu[0][0] <= free[e]:
                    t_, i_ = heapq.heappop(fu)
                    heapq.heappush(av, (prio[i_], i_))
                if av:
                    pick = av[0]
                    if e == "pe" and len(av) > 1 and ops[pick[1]].mode != pe_mode[0]:
                        best_same = None
                        for i2 in av:
                            if ops[i2[1]].mode == pe_mode[0] and (best_same is None or i2 < best_same):
                                best_same = i2
                        if best_same is not None and best_same[0] - pick[0] < 400:
                            pick = best_same
                    if e == "act" and len(av) > 1 and ops[pick[1]].mode not in (None, act_set[0]):
                        best_same = None
                        for i2 in av:
                            if ops[i2[1]].mode in (None, act_set[0]) and (best_same is None or i2 < best_same):
                                best_same = i2
                        if best_same is not None and best_same[0] - pick[0] < 300:
                            pick = best_same
                    cand = (free[e], pick[0], e, True, pick)
                elif fu:
                    cand = (fu[0][0], prio[fu[0][1]], e, False, fu[0])
                else:
                    continue
                if best is None or cand[:2] < best[:2]:
                    best = cand
            start, _pr, e, from_av, item = best
            if from_av:
                i_ = item[1]
                if avail[e][0] == item:
                    heapq.heappop(avail[e])
                else:
                    avail[e].remove(item)
                    heapq.heapify(avail[e])
            else:
                i_ = item[1]
                heapq.heappop(future[e])
            o = ops[i_]
            if e == "pe":
                if o.mode != pe_mode[0]:
                    start += 150.0
                pe_mode[0] = o.mode
            elif e == "act" and o.mode is not None:
                if o.mode != act_set[0]:
                    start += 1300.0
                act_set[0] = o.mode
            free[e] = start + o.dur
            self.start_t[i_] = start
            fin[i_] = start + o.dur + o.xfer
            order.append(i_)
            for s_ in succ[i_]:
                lat = (40.0 if (ops[s_].eng == e and e == "pe" and not o.is_dma) else (150.0 if ops[s_].eng == e else 400.0)) * _WI["lat"]
                if fin[i_] + lat > ready_t[s_]:
                    ready_t[s_] = fin[i_] + lat
                npred[s_] -= 1
                if npred[s_] == 0:
                    heapq.heappush(future[ops[s_].eng], (ready_t[s_], s_))
        self.est_makespan = max(fin) if fin else 0.0
        return order

    def finalize(self):
        self._build_dag()
        order = self._list_schedule()
        self.order = order
        ops = self.ops
        dma_count = {}
        slot_last = {}
        extra = {}
        seqc = {}
        for i_ in order:
            op = ops[i_]
            seqc[op.eng] = seqc.get(op.eng, 0) + 1
            op.seq = seqc[op.eng]
            if op.is_dma:
                n_ = dma_count.get(op.eng, 0)
                dma_count[op.eng] = n_ + 1
                k = self.dma_slots[op.eng]
                op.slot = (op.eng, n_ % k)
                op.val = 16 * (n_ // k + 1)
                prev = slot_last.get(op.slot)
                if prev is not None:
                    extra[i_] = prev
                slot_last[op.slot] = i_
        known = {}
        vc = {}
        self.waits = {}
        for i_ in order:
            op = ops[i_]
            K = known.setdefault(op.eng, {})
            deps = list(op.deps)
            if i_ in extra:
                deps.append(extra[i_])
            cand = []
            for d in deps:
                p = ops[d]
                if (not p.is_dma) and p.eng == "pe" and op.eng == "pe" and not op.is_dma:
                    continue
                cand.append(d)
            cand.sort(key=lambda d: -ops[d].seq)
            w = []
            for d in cand:
                p = ops[d]
                key = ("dma", p.slot) if p.is_dma else ("eng", p.eng)
                val = p.val if p.is_dma else p.seq
                if K.get(key, -1) >= val:
                    continue
                w.append(d)
                p.marked = True
                for kk, vv in vc[d].items():
                    if K.get(kk, -1) < vv:
                        K[kk] = vv
            self.waits[i_] = w
            v = dict(K)
            if op.is_dma:
                v[("dma", op.slot)] = op.val
            else:
                if v.get(("eng", op.eng), -1) < op.seq and op.eng == "pe":
                    pass
                v[("eng", op.eng)] = max(v.get(("eng", op.eng), -1), op.seq)
            vc[i_] = v
        cnt = {}
        for i_ in order:
            op = ops[i_]
            if not op.is_dma and op.marked:
                cnt[op.eng] = cnt.get(op.eng, 0) + 1
                op.val = cnt[op.eng]
        self.counts = cnt

    def emit(self, stack):
        nc = self.nc
        sems = {}
        for e in COMPUTE:
            sems[("eng", e)] = stack.enter_context(nc.semaphore("s_" + e))
        for e, k in self.dma_slots.items():
            for i in range(k):
                sems[("dma", (e, i))] = stack.enter_context(nc.semaphore("d_%s%d" % (e, i)))
        block = stack.enter_context(nc.Block())
        ops = self.ops
        order = self.order
        waits = self.waits

        def run(engname, eng):
            for i_ in order:
                op = ops[i_]
                if op.eng != engname:
                    continue
                for d in waits[i_]:
                    p = ops[d]
                    s = sems[("dma", p.slot)] if p.is_dma else sems[("eng", p.eng)]
                    eng.wait_ge(s, p.val)
                ins = op.fn(eng)
                if op.is_dma:
                    ins.then_inc(sems[("dma", op.slot)], 16)
                elif op.marked:
                    ins.then_inc(sems[("eng", op.eng)], 1)
            last = {}
            for i_ in order:
                op = ops[i_]
                if op.is_dma and op.eng == engname:
                    last[op.slot] = op.val
            for slot, v in last.items():
                eng.wait_ge(sems[("dma", slot)], v)

        @block.sync
        def _(e):
            run("sp", e)

        @block.tensor
        def _(e):
            run("pe", e)

        @block.scalar
        def _(e):
            run("act", e)

        @block.vector
        def _(e):
            run("dve", e)

        @block.gpsimd
        def _(e):
            run("pool", e)


D = 1024
T = 2048
NS = 16
HALF = 1024
TW = HALF + NS
RW = 512
RPROJ = 1792
HW_ = 512
DFF = 2816
INP = 5888
CDEC = -0.6065306597126334
GN_EPS = 64e-5
RMS_EPS = 1e-6

PF = {}
_c = 0
for _n, _w in (("mu", 14), ("w0", 4), ("a0", 4), ("k_k", 4), ("k_a", 4), ("r_k", 4), ("ln_g", 4), ("ln_b", 4),
               ("lb0", 4), ("lb1", 4), ("hng", 4)):
    PF[_n] = _c
    _c += _w
PF_N = _c
DV = {"omu": 0, "omka": 14, "lb": 18, "omlb": 22}
DV_N = 29

CO = {"ident": 0, "blk64": 128, "ones": 256, "maskNM": 384, "maskL": 512, "maskI": 576, "cmask": 640}
CO_N = 640 + 1024 + 256


def make_consts():
    c = np.zeros((128, CO_N), np.float32)
    p = np.arange(128)
    c[:, 0:128] = np.eye(128, dtype=np.float32)
    c[:, 128:256] = (p[:, None] // 64 == p[None, :] // 64).astype(np.float32)
    c[:, 256:384] = 1.0
    s = (p % 64)[:, None]
    t = np.arange(64)[None, :]
    c[:, 384:448] = (s < t)
    c[:, 448:512] = (s <= t)
    c[:, 512:576] = (s > t)
    c[:, 576:640] = (s == t)
    tt = np.arange(1024)
    c[:, 640:1664] = (tt % 64 != 0).astype(np.float32)[None, :]
    c[:, 1664:1920] = np.eye(16, dtype=np.float32).reshape(-1)[None, :]
    return c


class Arena:
    def __init__(self, tile, nwords):
        self.t = tile
        self.n = nwords
        self.top = 0

    def alloc(self, dtype, shape):
        free = 1
        for s in shape[1:]:
            free *= s
        words = free if dtype == F32 else (free + 1) // 2
        words = (words + 7) // 8 * 8
        off = self.top
        self.top += words
        assert self.top <= self.n, ("arena overflow", self.top, self.n)
        v = self.t[0:shape[0], off:off + words]
        if dtype == BF16:
            v = v.bitcast(BF16)
        v = v[:, 0:free]
        if len(shape) > 2:
            names = ["d%d" % i for i in range(len(shape) - 1)]
            pat = "p (" + " ".join(names) + ") -> p " + " ".join(names)
            kw = {names[i]: shape[i + 1] for i in range(len(names))}
            v = v.rearrange(pat, **kw)
        return v


ARENA_WORDS = 32560


def build_program(dbg=None, passes=(0, 1), stop_after=None):
    nc = bass.Bass("TRN2", target_bir_lowering=False)

    def din(name, shape):
        return nc.dram_tensor(name, list(shape), F32, kind="ExternalInput").ap()

    def dout(name, shape):
        return nc.dram_tensor(name, list(shape), F32, kind="ExternalOutput").ap()

    x_p = din("x_p", [T, D])
    x_s = din("x_s", [NS, D])
    wkv_s = din("wkv_s", [NS * 8, 4096])
    shift_s = din("shift_s", [NS, RPROJ])
    hgrn_s = din("hgrn_s", [NS, 4, 128, 128])
    w_in = din("w_in", [D, INP])
    w2a2 = din("w2a2", [128, 512])
    g2 = din("g2", [128, 512])
    w_up_a = din("w_up_a", [RW, D])
    w_up_b = din("w_up_b", [HW_, D])
    w_out = din("w_out", [D, D])
    w_fg = din("w_fg", [D, DFF])
    w_fu = din("w_fu", [D, DFF])
    w_fd = din("w_fd", [DFF, D])
    pfm_d = din("pfm", [128, PF_N])
    consts_d = din("consts", [128, CO_N])
    g_mix = din("g_mix", [D])
    g_ffn = din("g_ffn", [D])
    g_fin = din("g_fin", [D])

    y_p = dout("y_p", [T, D])
    y_s = dout("y_s", [NS, D])
    o_wkv_p = dout("o_wkv_p", [8, 64, 64])
    o_shift_p = dout("o_shift_p", [RPROJ])
    o_hgrn_p = dout("o_hgrn_p", [4, 128, 128])
    o_wkv_s = dout("o_wkv_s", [NS * 8, 4096])
    o_shift_s = dout("o_shift_s", [NS, RPROJ])
    o_hgrn_s = dout("o_hgrn_s", [NS, 4, 128, 128])
    dbg_out = {}
    if dbg:
        for k, shp in dbg.items():
            dbg_out[k] = dout("dbg_" + k, shp)
    scr_a = nc.dram_tensor("scr_a", [NS, 6, 512], F32).ap()
    scr_y = nc.dram_tensor("scr_y", [NS, 512], F32).ap()

    st = ExitStack()
    with st:
        def sb(name, shape, dt):
            return st.enter_context(nc.sbuf_tensor(name, list(shape), dt))

        arena_t = sb("arena", [128, ARENA_WORDS], F32)
        hT = sb("hT", [128, 8, TW], BF16)
        oa = sb("oa", [128, 4, TW], BF16)
        ob = sb("ob", [128, 4, TW], BF16)
        NWB = 4
        wbuf = [sb("wbuf%d" % i, [128, 8, 512], BF16) for i in range(NWB)]
        consts = sb("consts_sb", [128, 384], F32)
        ident_bf = sb("ident_bf", [128, 128], BF16)
        masks_bf = sb("masks_bf", [128, 256], BF16)
        pfm = sb("pfm_sb", [128, PF_N], F32)
        dv = sb("dv", [128, DV_N], F32)
        lw_w = sb("lw_w", [128, 512], BF16)
        g2_w = sb("g2_w", [128, 512], BF16)
        carry = sb("carry", [128, 14], F32)
        shiftS = sb("shiftS", [128, 14, NS], F32)
        sprevT = sb("sprevT", [128, 14, NS], F32)
        H0f = sb("H0f", [128, 4, 64], F32)
        H0bd = sb("H0bd", [128, 4, 128], BF16)
        S0f = sb("S0f", [128, 4, 128], F32)
        S0b = sb("S0b", [128, 4, 128], BF16)
        WC = sb("WC", [128, 4, 16], F32)
        DC = sb("DC", [128, 4, 16], F32)
        stat = sb("stat", [128, 64], F32)
        ftmp = sb("ftmp", [128, 2, 512], F32)
        ps = st.enter_context(nc.psum_tensor("ps", [128, 8, 512], F32))

        ident = consts[:, 0:128]
        blk64 = consts[:, 128:256]
        ones = consts[:, 256:384]
        maskNM_bf = masks_bf[:, 0:128]
        maskL_bf = masks_bf[:, 128:192]
        maskI_bf = masks_bf[:, 192:256]

        S = Sched(nc)
        A = S.add
        bank_ctr = [0]

        def nb(n=1):
            b = bank_ctr[0]
            if n > 1:
                b = (b + n - 1) // n * n
            if b + n > 8:
                b = 0
            bank_ctr[0] = (b + n) % 8
            return b

        def nbp(par):
            b = bank_ctr[0]
            if b % 2 != par:
                b = (b + 1) % 8
            bank_ctr[0] = (b + 1) % 8
            return b

        par_ctr = [0]
        seq_ctr = [0, 0]

        par_nb = [6]

        def nb_par(n=1):
            m = par_nb[0]
            if n == 2:
                par_ctr[0] = (par_ctr[0] + 1) // 2 * 2
            b = par_ctr[0] % m
            par_ctr[0] = (par_ctr[0] + n) % m
            return b

        smp_ctr = [0]

        def nb_smp():
            smp_ctr[0] += 1
            return 4 + smp_ctr[0] % 2

        def nb_seq(par):
            return 6 + par

        def PB(b, n=1):
            return ["ps%d" % (b + i) for i in range(n)]

        def pf(name, j=0):
            c = PF[name] + j
            return pfm[:, c:c + 1]

        def dvc(name, j=0):
            c = DV[name] + j
            return dv[:, c:c + 1]

        wctr = [0]

        def load_w(src_ap, shape_view):
            i = wctr[0] % NWB
            wctr[0] += 1
            a, b = shape_view
            view = wbuf[i][:, :, :].rearrange("p a b -> p (a b)")[:, 0:a * b].rearrange("p (a b) -> p a b", a=a)
            rn = "wbuf%d" % i
            S.dma("pool", lambda e, view=view, src_ap=src_ap: e.dma_start(out=view, in_=src_ap), writes=[rn])
            return view, rn

        def dump(name, ap_sb, res):
            if dbg and name in dbg_out:
                S.dma("pool", lambda e: e.dma_start(out=dbg_out[name], in_=ap_sb), reads=res)

        S.dma("sp", lambda e: e.dma_start(out=consts[:], in_=consts_d[:, 0:384]), writes=["consts"])
        S.dma("pool", lambda e: e.dma_start(out=masks_bf[:], in_=consts_d[:, 384:640]))
        S.dma("sp", lambda e: e.dma_start(out=pfm[:], in_=pfm_d), writes=["pfm"])
        S.dma("pool", lambda e: e.dma_start(out=lw_w[:], in_=w2a2), writes=["lw_w"])
        S.dma("pool", lambda e: e.dma_start(out=g2_w[:], in_=g2), writes=["g2_w"])
        A("dve", lambda e: e.tensor_copy(out=ident_bf[:], in_=ident), ["consts"], ["ident_bf"])
        A("dve", lambda e: e.memset(carry[:], 0.0), [], ["carry"])
        A("dve", lambda e: e.memset(H0f[:], 0.0), [], ["H0f"])
        A("dve", lambda e: e.memset(H0bd[:], 0.0), [], ["H0b"])
        A("dve", lambda e: e.memset(S0f[:], 0.0), [], ["S0f"])
        A("dve", lambda e: e.memset(S0b[:], 0.0), [], ["S0b"])
        A("dve", lambda e: e.memset(sprevT[:], 0.0), [], ["sprevT"])
        A("dve", lambda e: e.tensor_scalar(out=dv[:, 0:14], in0=pfm[:, PF["mu"]:PF["mu"] + 14], scalar1=-1.0, scalar2=1.0,
                                            op0=ALU.mult, op1=ALU.add), ["pfm"], ["dv"])
        A("dve", lambda e: e.tensor_scalar(out=dv[:, 14:18], in0=pfm[:, PF["k_a"]:PF["k_a"] + 4], scalar1=-1.0, scalar2=1.0,
                                            op0=ALU.mult, op1=ALU.add), ["pfm"], ["dv"])
        A("dve", lambda e: e.tensor_tensor(out=dv[:, 18:22], in0=pfm[:, PF["lb0"]:PF["lb0"] + 4],
                                            in1=pfm[:, PF["lb1"]:PF["lb1"] + 4], op=ALU.subtract), ["pfm"], ["dv"])
        A("act", lambda e: e.activation(out=dv[:, 18:22], in_=dv[:, 18:22], func=AF.Sigmoid), ["dv"], ["dv"])
        A("dve", lambda e: e.tensor_scalar(out=dv[:, 22:26], in0=dv[:, 18:22], scalar1=-1.0, scalar2=1.0,
                                            op0=ALU.mult, op1=ALU.add), ["dv"], ["dv"])

        eps24, epsgn, epsrms = dv[:, 26:27], dv[:, 27:28], dv[:, 28:29]
        A("dve", lambda e: e.memset(dv[:, 26:27], 1e-24))
        A("dve", lambda e: e.memset(dv[:, 27:28], GN_EPS))
        A("dve", lambda e: e.memset(dv[:, 28:29], RMS_EPS))
        S.front = len(S.ops)
        w_in_v = w_in.rearrange("(kc p) n -> p kc n", p=128)

        for pas in passes:
            import os as _os
            _skip = _os.environ.get("KSKIP", "")
            nsamp = NS if (pas == 0 and "nosamp" not in _skip) else 0
            W = HALF + nsamp
            blocks = [(0, 512), (512, 1024)] + ([(1024, 1040)] if nsamp else [])
            t0 = pas * HALF
            S.barrier()
            ar_ = Arena(arena_t, ARENA_WORDS)

            def norm_phase(tag, x_tok, xs_tok, g_dram, arena, out_cb, src_loaded, xsres):
                gB = arena.alloc(F32, [128, 1024])
                junk = arena.alloc(F32, [128, 1024])
                h_tok = arena.alloc(BF16, [128, 8, 1024])
                hs_tok = arena.alloc(BF16, [NS, 1024])
                R = tag
                S.dma("sp", lambda e: e.dma_start(out=gB, in_=g_dram.partition_broadcast(128)), writes=[R + "gB"])
                A("dve", lambda e: e.memset(stat[:, 0:32], 0.0), [], ["stat"])
                for tt in range(8):
                    A("act", lambda e, tt=tt: e.activation(out=junk, in_=x_tok[:, tt, :], func=AF.Square, accum_out=stat[:, tt:tt + 1]))
                    A("dve", lambda e, tt=tt: e.tensor_scalar(out=stat[:, 16 + tt:17 + tt], in0=stat[:, tt:tt + 1], scalar1=1.0 / D, scalar2=RMS_EPS,
                                                              op0=ALU.mult, op1=ALU.add))
                    A("act", lambda e, tt=tt: e.activation(out=stat[:, 16 + tt:17 + tt], in_=stat[:, 16 + tt:17 + tt], func=AF.Ln))
                    A("act", lambda e, tt=tt: e.activation(out=stat[:, 16 + tt:17 + tt], in_=stat[:, 16 + tt:17 + tt], func=AF.Exp, scale=-0.5))
                if nsamp:
                    A("act", lambda e: e.activation(out=junk[0:NS, :], in_=xs_tok, func=AF.Square, accum_out=stat[0:NS, 8:9]))
                    A("dve", lambda e: e.tensor_scalar(out=stat[0:NS, 24:25], in0=stat[0:NS, 8:9], scalar1=1.0 / D, scalar2=RMS_EPS,
                                                        op0=ALU.mult, op1=ALU.add))
                    A("act", lambda e: e.activation(out=stat[0:NS, 24:25], in_=stat[0:NS, 24:25], func=AF.Ln))
                    A("act", lambda e: e.activation(out=stat[0:NS, 24:25], in_=stat[0:NS, 24:25], func=AF.Exp, scale=-0.5))
                out_cb(gB, h_tok, hs_tok, R)

            x_tok = ar_.alloc(F32, [128, 8, 1024])
            xs_tok = ar_.alloc(F32, [NS, 1024])
            for tt in range(8):
                S.dma("sp", lambda e, tt=tt: e.dma_start(out=x_tok[:, tt, :], in_=x_p[t0 + tt * 128:t0 + (tt + 1) * 128, :]),
                      writes=["P0x%d" % tt])
            if nsamp:
                S.dma("sp", lambda e: e.dma_start(out=xs_tok, in_=x_s), writes=["P0xs"])

            def to_hT(gB, h_tok, hs_tok, R, x_tok_, xs_tok_, xres, xsres):
                for tt in range(8):
                    A("dve", lambda e, tt=tt: e.scalar_tensor_tensor(out=h_tok[:, tt, :], in0=x_tok_[:, tt, :],
                                                                     scalar=stat[:, 16 + tt:17 + tt], in1=gB,
                                                                     op0=ALU.mult, op1=ALU.mult),
                      [xres(tt), "stat", R + "gB"], [R + "htok%d" % tt])
                    b = nb()
                    pT = ps[:, b, :].bitcast(BF16).rearrange("p (c t) -> p c t", c=8)
                    for dc in range(8):
                        A("pe", lambda e, tt=tt, dc=dc, pT=pT: e.transpose(out=pT[:, dc, :], in_=h_tok[:, tt, dc * 128:(dc + 1) * 128],
                                                                           identity=ident_bf[:]),
                          [R + "htok%d" % tt, "ident_bf"], PB(b))
                    eng = "act" if tt % 2 == 0 else "dve"
                    if eng == "act":
                        A("act", lambda e, tt=tt, pT=pT: e.copy(out=hT[:, :, tt * 128:(tt + 1) * 128], in_=pT), PB(b), ["hT"])
                    else:
                        A("dve", lambda e, tt=tt, pT=pT: e.tensor_copy(out=hT[:, :, tt * 128:(tt + 1) * 128], in_=pT), PB(b), ["hT"])
                if nsamp:
                    A("dve", lambda e: e.scalar_tensor_tensor(out=hs_tok, in0=xs_tok_, scalar=stat[0:NS, 24:25], in1=gB[0:NS, :],
                                                              op0=ALU.mult, op1=ALU.mult), [xsres, "stat", R + "gB"], [R + "hstok"])
                    b = nb()
                    pT = ps[:, b, :].bitcast(BF16).rearrange("p (c t) -> p c t", c=8)
                    for dc in range(8):
                        A("pe", lambda e, dc=dc, pT=pT: e.transpose(out=pT[:, dc, 0:NS], in_=hs_tok[:, dc * 128:(dc + 1) * 128],
                                                                    identity=ident_bf[0:NS, 0:NS]),
                          [R + "hstok", "ident_bf"], PB(b))
                    A("act", lambda e, pT=pT: e.copy(out=hT[:, :, HALF:HALF + NS], in_=pT[:, :, 0:NS]), PB(b), ["hT"])

            norm_phase("P0", x_tok, xs_tok, g_mix, ar_,
                       lambda gB, h_tok, hs_tok, R: to_hT(gB, h_tok, hs_tok, R, x_tok, xs_tok, lambda tt: "P0x%d" % tt, "P0xs"),
                       lambda tt: "P0x%d" % tt, "P0xs")
            if dbg and "hT" in dbg_out and pas == 0:
                dump("hT", hT[:], ["hT"])
            if stop_after == "P0":
                continue

            def proj_fm(wview, wres, ncol0, kcs, act_tile, act_res, evac):
                for (c0, c1) in blocks:
                    b = nb()
                    for kc in range(kcs):
                        A("pe", lambda e, kc=kc, b=b, c0=c0, c1=c1: e.matmul(ps[:, b, 0:c1 - c0], lhsT=wview[:, kc, ncol0:ncol0 + 128],
                                                                              rhs=act_tile[:, kc, c0:c1], start=(kc == 0),
                                                                              stop=(kc == kcs - 1)),
                          [wres, act_res], PB(b))
                    evac(b, c0, c1)

            S.tag = "p%d Rprep" % pas
            S.barrier()
            ar_ = Arena(arena_t, ARENA_WORDS)
            sig = ar_.alloc(F32, [128, 4, TW])
            g_bf = ar_.alloc(BF16, [128, 4, TW])
            ar_t = ar_.alloc(BF16, [128, 4, 16, 2, 64])
            bT = ar_.alloc(BF16, [128, 4, HALF])
            kT = ar_.alloc(BF16, [128, 4, HALF])
            vT = ar_.alloc(BF16, [128, 4, TW])
            bonus = ar_.alloc(BF16, [128, 4, TW])
            smp = ar_.alloc(F32, [128, 6, 4, NS])
            mark = ar_.top
            a_bf = ar_.alloc(BF16, [128, 4, TW])
            kpr = ar_.alloc(BF16, [128, 4, TW])
            praw = [ar_.alloc(F32, [128, 1048]) for _ in range(1)]
            NT = 7
            tmp = [ar_.alloc(F32, [128, TW]) for _ in range(NT)]
            cmask = ar_.alloc(BF16, [128, HALF])
            lwb = ar_.alloc(BF16, [128, TW])
            import os as _os
            _skip = _os.environ.get("KSKIP", "")
            if "cmask" not in _skip:
                S.dma("pool", lambda e: e.dma_start(out=cmask, in_=consts_d[:, 640:1664]), writes=["cmask"])

            if nsamp and "shiftT" not in _skip:
                sh_tok = tmp[0][0:NS, :]
                sh_tok2 = tmp[1][0:NS, :]
                S.dma("sp", lambda e: e.dma_start(out=sh_tok[:, 0:1024], in_=shift_s[:, 0:1024]), writes=["tmp0"])
                S.dma("sp", lambda e: e.dma_start(out=sh_tok2[:, 0:768], in_=shift_s[:, 1024:1792]), writes=["tmp1"])
                b = nb()
                for c in range(14):
                    src = sh_tok[:, c * 128:(c + 1) * 128] if c < 8 else sh_tok2[:, (c - 8) * 128:(c - 7) * 128]
                    A("pe", lambda e, c=c, src=src, b=b: e.transpose(out=ps[:, b, c * NS:(c + 1) * NS], in_=src, identity=ident[0:NS, 0:NS]),
                      ["tmp0", "tmp1", "consts"], PB(b))
                A("dve", lambda e, b=b: e.tensor_copy(out=sprevT[:, :, :], in_=ps[:, b, 0:14 * NS].rearrange("p (c n) -> p c n", c=14)),
                  PB(b), ["sprevT"])

            if stop_after == "shiftT":
                continue
            hv = [(0, 512), (512, W)]
            pctr = [0]

            def rwkv_chunk(c, wview, wres, ncol0, out_ap=None):
                pr = praw[0]
                t1 = tmp[6]
                A("dve", lambda e: e.tensor_copy(out=pr[:, 0:1], in_=carry[:, c:c + 1]))

                def evac(b, c0, c1):
                    A("act", lambda e: e.copy(out=pr[:, 1 + c0:1 + c1], in_=ps[:, b, 0:c1 - c0]))
                    A("act", lambda e: e.activation(out=t1[:, c0:c1], in_=ps[:, b, 0:c1 - c0], func=AF.Copy, scale=dvc("omu", c)))
                proj_fm(wview, wres, ncol0, 8, hT, "hT", evac)
                A("act", lambda e: e.copy(out=carry[:, c:c + 1], in_=pr[:, HALF:HALF + 1]))
                if nsamp:
                    A("act", lambda e: e.copy(out=shiftS[:, c, :], in_=pr[:, 1 + HALF:1 + HALF + NS]))
                out_ap_ = tmp[0] if out_ap is None else out_ap
                for (a_, b_) in hv:
                    b2 = min(b_, HALF)
                    A("dve", lambda e, a_=a_, b2=b2: e.scalar_tensor_tensor(out=out_ap_[:, a_:b2], in0=pr[:, a_:b2], scalar=pf("mu", c),
                                                                            in1=t1[:, a_:b2], op0=ALU.mult, op1=ALU.add))
                if nsamp:
                    A("dve", lambda e: e.scalar_tensor_tensor(out=out_ap_[:, HALF:HALF + NS], in0=sprevT[:, c, :], scalar=pf("mu", c),
                                                              in1=t1[:, HALF:HALF + NS], op0=ALU.mult, op1=ALU.add))
                return out_ap_

            wv, wr = load_w(w_in_v[:, :, 1536:1792], (8, 256))
            psm = rwkv_chunk(12, wv, wr, 0)
            for (a_, b_) in hv:
                A("act", lambda e, a_=a_, b_=b_: e.activation(out=lwb[0:64, a_:b_], in_=psm[0:64, a_:b_], func=AF.Tanh))
                A("act", lambda e, a_=a_, b_=b_: e.copy(out=lwb[64:128, a_:b_], in_=psm[64:128, a_:b_]))
            for j in range(4):
                for (c0, c1) in blocks:
                    b2_ = nb(2)
                    b = b2_
                    A("pe", lambda e, j=j, b=b, c0=c0, c1=c1: e.matmul(ps[:, b, 0:c1 - c0], lhsT=lw_w[0:64, j * 128:(j + 1) * 128],
                                                                        rhs=lwb[0:64, c0:c1], start=True, stop=True))
                    A("act", lambda e, j=j, b=b, c0=c0, c1=c1: e.activation(out=sig[:, j, c0:c1], in_=ps[:, b, 0:c1 - c0], func=AF.Sigmoid,
                                                                             bias=pf("w0", j)))
                    b = b2_ + 1
                    A("pe", lambda e, j=j, b=b, c0=c0, c1=c1: e.matmul(ps[:, b, 0:c1 - c0], lhsT=lw_w[64:128, j * 128:(j + 1) * 128],
                                                                        rhs=lwb[64:128, c0:c1], start=True, stop=True))
                    A("act", lambda e, j=j, b=b, c0=c0, c1=c1: e.activation(out=a_bf[:, j, c0:c1], in_=ps[:, b, 0:c1 - c0], func=AF.Sigmoid,
                                                                             bias=pf("a0", j)))
            psm = rwkv_chunk(13, wv, wr, 128)
            for (a_, b_) in hv:
                A("act", lambda e, a_=a_, b_=b_: e.activation(out=lwb[:, a_:b_], in_=psm[:, a_:b_], func=AF.Sigmoid))
            for j in range(4):
                for (c0, c1) in blocks:
                    b = nb()
                    A("pe", lambda e, j=j, b=b, c0=c0, c1=c1: e.matmul(ps[:, b, 0:c1 - c0], lhsT=g2_w[:, j * 128:(j + 1) * 128],
                                                                        rhs=lwb[:, c0:c1], start=True, stop=True))
                    A("dve", lambda e, j=j, b=b, c0=c0, c1=c1: e.tensor_copy(out=g_bf[:, j, c0:c1], in_=ps[:, b, 0:c1 - c0]))
            if dbg and pas == 0:
                dump("sig", sig, ["sig"])

            wv, wr = load_w(w_in_v[:, :, 1024:1536], (8, 512))
            for j in range(4):
                rwkv_chunk(8 + j, wv, wr, j * 128, out_ap=vT[:, j, :])
                if nsamp:
                    A("act", lambda e, j=j: e.copy(out=smp[:, 3, j, :], in_=vT[:, j, HALF:HALF + NS]))

            def fp32_blocksum(src_ap, src_res, mat, evac):
                for (c0, c1) in blocks:
                    b = nb()
                    A("pe", lambda e, b=b, c0=c0, c1=c1: e.matmul(ps[:, b, 0:c1 - c0], lhsT=mat, rhs=src_ap[:, c0:c1], start=True, stop=True))
                    evac(b, c0, c1)

            def cumsum_decay(j, cs_i):
                cs = tmp[cs_i]
                for (a_, b_) in hv:
                    b2 = min(b_, HALF)
                    A("dve", lambda e, a_=a_, b2=b2: e.tensor_tensor_scan(out=cs[:, a_:b2], data0=cmask[:, a_:b2], data1=sig[:, j, a_:b2], initial=0.0,
                                                                          op0=ALU.mult, op1=ALU.add))
                if nsamp:
                    A("dve", lambda e: e.tensor_copy(out=cs[:, HALF:HALF + NS], in_=sig[:, j, HALF:HALF + NS]))
                return cs

            def cview(ap2, a_, b2):
                return ap2[:, a_:b2].rearrange("p (c t) -> p c t", t=64)

            wv, wr = load_w(w_in_v[:, :, 512:1024], (8, 512))
            for j in range(4):
                k_ap = rwkv_chunk(4 + j, wv, wr, j * 128)
                kkr, rs, cs, en, de, ka = tmp[1], tmp[2], tmp[3], tmp[4], tmp[5], tmp[2]
                for (a_, b_) in hv:
                    A("act", lambda e, j=j, a_=a_, b_=b_: e.activation(out=kkr[:, a_:b_], in_=k_ap[:, a_:b_], func=AF.Copy, scale=pf("k_k", j)))
                    A("act", lambda e, a_=a_, b_=b_: e.activation(out=rs[:, a_:b_], in_=kkr[:, a_:b_], func=AF.Square))

                def ev_ss(b, c0, c1):
                    A("act", lambda e: e.activation(out=de[:, c0:c1], in_=ps[:, b, 0:c1 - c0], func=AF.Ln, bias=eps24[:, 0:1]))
                fp32_blocksum(rs, "tmp2", blk64, ev_ss)
                cumsum_decay(j, 3)
                for (a_, b_) in hv:
                    b2 = min(b_, HALF)
                    c_lo, c_hi = a_ // 64, b2 // 64
                    A("act", lambda e, a_=a_, b_=b_: e.activation(out=de[:, a_:b_], in_=de[:, a_:b_], func=AF.Exp, scale=-0.5))
                    A("dve", lambda e, a_=a_, b_=b_: e.tensor_tensor(out=kkr[:, a_:b_], in0=kkr[:, a_:b_], in1=de[:, a_:b_], op=ALU.mult))
                    A("act", lambda e, j=j, a_=a_, b2=b2, c_lo=c_lo, c_hi=c_hi: e.activation(out=WC[:, j, c_lo:c_hi], in_=cview(cs, a_, b2)[:, :, 63],
                                                                                             func=AF.Exp, scale=CDEC))
                    A("act", lambda e, a_=a_, b2=b2: e.activation(out=en[:, a_:b2], in_=cs[:, a_:b2], func=AF.Exp, scale=-CDEC))
                    if nsamp and b_ > HALF:
                        A("act", lambda e, j=j: e.activation(out=smp[:, 1, j, :], in_=cs[:, HALF:HALF + NS], func=AF.Exp, scale=CDEC))
                        A("act", lambda e, j=j: e.activation(out=smp[:, 4, j, :], in_=kkr[:, HALF:HALF + NS], func=AF.Copy, scale=-1.0))
                    A("dve", lambda e, j=j, a_=a_, b2=b2: e.tensor_tensor(out=de[:, a_:b2], in0=cs[:, a_:b2], in1=sig[:, j, a_:b2], op=ALU.subtract))
                    A("act", lambda e, a_=a_, b2=b2: e.activation(out=de[:, a_:b2], in_=de[:, a_:b2], func=AF.Exp, scale=CDEC))
                    A("dve", lambda e, j=j, a_=a_, b2=b2, c_lo=c_lo, c_hi=c_hi: e.scalar_tensor_tensor(
                        out=ar_t[:, j, c_lo:c_hi, 0, :], in0=cview(kkr, a_, b2), scalar=-1.0, in1=cview(de, a_, b2), op0=ALU.mult, op1=ALU.mult))
                    A("dve", lambda e, j=j, a_=a_, b_=b_: e.tensor_tensor(out=ka[:, a_:b_], in0=kkr[:, a_:b_], in1=a_bf[:, j, a_:b_], op=ALU.mult))
                    if nsamp and b_ > HALF:
                        A("act", lambda e, j=j: e.copy(out=smp[:, 5, j, :], in_=ka[:, HALF:HALF + NS]))
                    A("dve", lambda e, j=j, a_=a_, b2=b2: e.tensor_tensor(out=bT[:, j, a_:b2], in0=ka[:, a_:b2], in1=en[:, a_:b2], op=ALU.mult))
                    A("dve", lambda e, j=j, a_=a_, b_=b_: e.tensor_scalar(out=kkr[:, a_:b_], in0=a_bf[:, j, a_:b_], scalar1=pf("k_a", j),
                                                                          scalar2=dvc("omka", j), op0=ALU.mult, op1=ALU.add))
                    A("dve", lambda e, a_=a_, b_=b_: e.tensor_tensor(out=kkr[:, a_:b_], in0=kkr[:, a_:b_], in1=k_ap[:, a_:b_], op=ALU.mult))
                    A("dve", lambda e, j=j, a_=a_, b2=b2: e.tensor_tensor(out=kT[:, j, a_:b2], in0=kkr[:, a_:b2], in1=en[:, a_:b2], op=ALU.mult))
                    A("act", lambda e, j=j, a_=a_, b_=b_: e.activation(out=kpr[:, j, a_:b_], in_=kkr[:, a_:b_], func=AF.Copy, scale=pf("r_k", j)))
                    if nsamp and b_ > HALF:
                        A("act", lambda e, j=j: e.copy(out=smp[:, 2, j, :], in_=kkr[:, HALF:HALF + NS]))

            wv, wr = load_w(w_in_v[:, :, 0:512], (8, 512))
            for j in range(4):
                r_ap = rwkv_chunk(j, wv, wr, j * 128)
                cs = cumsum_decay(j, 3)
                rk = tmp[1]
                for (a_, b_) in hv:
                    b2 = min(b_, HALF)
                    c_lo, c_hi = a_ // 64, b2 // 64
                    A("act", lambda e, a_=a_, b2=b2: e.activation(out=cs[:, a_:b2], in_=cs[:, a_:b2], func=AF.Exp, scale=CDEC))
                    A("dve", lambda e, j=j, a_=a_, b2=b2, c_lo=c_lo, c_hi=c_hi: e.tensor_tensor(out=ar_t[:, j, c_lo:c_hi, 1, :], in0=cview(r_ap, a_, b2),
                                                                                                 in1=cview(cs, a_, b2), op=ALU.mult))
                    A("dve", lambda e, j=j, a_=a_, b_=b_: e.tensor_tensor(out=rk[:, a_:b_], in0=r_ap[:, a_:b_], in1=kpr[:, j, a_:b_], op=ALU.mult))
                if nsamp:
                    A("act", lambda e, j=j: e.copy(out=smp[:, 0, j, :], in_=r_ap[:, HALF:HALF + NS]))

                def ev_bon(b, c0, c1, j=j):
                    A("dve", lambda e: e.tensor_tensor(out=bonus[:, j, c0:c1], in0=ps[:, b, 0:c1 - c0], in1=vT[:, j, c0:c1], op=ALU.mult))
                fp32_blocksum(rk, "tmp1", blk64, ev_bon)
            if dbg and pas == 0:
                dump("ar", ar_t, ["ar"])
                dump("bT", bT, ["bT"])
                dump("kT", kT, ["kT"])
                dump("vT", vT, ["vT"])
                dump("bonus", bonus, ["bonus"])
            if stop_after == "Rprep":
                continue

            yT = sig
            if nsamp:
                ar_.top = mark
                L1v = ar_.alloc(F32, [128, 6, 64])
                saL = ar_.alloc(F32, [128, 64])
                yL1 = ar_.alloc(F32, [128, 64])
                rs_mark = ar_.top
                wf = [wbuf[i_][:, :, :].rearrange("p a b -> p (a b)").bitcast(F32) for i_ in range(4)]
                S_h = [wf[0].rearrange("p (v k) -> p v k", k=64), wf[1].rearrange("p (v k) -> p v k", k=64)]
                T_h = [wf[2].rearrange("p (v k) -> p v k", k=64), wf[3].rearrange("p (v k) -> p v k", k=64)]
                tok6h = [wf[2][0:NS, 0:1536].rearrange("p (v n) -> p v n", v=3), wf[3][0:NS, 0:1536].rearrange("p (v n) -> p v n", v=3)]
                ytok = wf[2][0:NS, 1536:2048]
                shtok = wf[3][0:NS, 0:2048]
                def emit_rsample():
                    for hf in range(2):
                        S.dma("sp", lambda e, hf=hf: e.dma_start(out=S_h[hf].rearrange("p v k -> p (v k)"), in_=wkv_s[:, hf * 2048:(hf + 1) * 2048]))
                    for vec in range(6):
                        b = nb_par()
                        for j in range(4):
                            A("pe", lambda e, vec=vec, j=j, b=b: e.transpose(out=ps[0:NS, b, j * 128:(j + 1) * 128], in_=smp[:, vec, j, :], identity=ident))
                        dst = tok6h[vec // 3][:, vec % 3, :]
                        if vec % 2 == 0:
                            A("act", lambda e, dst=dst, b=b: e.copy(out=dst, in_=ps[0:NS, b, :]))
                        else:
                            A("dve", lambda e, dst=dst, b=b: e.tensor_copy(out=dst, in_=ps[0:NS, b, :]))
                    for hf in range(2):
                        S.dma("sp", lambda e, hf=hf: e.dma_start(out=scr_a[:, hf * 3:(hf + 1) * 3, :], in_=tok6h[hf]))
                    for vec in range(6):
                        S.dma("sp", lambda e, vec=vec: e.dma_start(out=L1v[:, vec, :], in_=scr_a[:, vec, :].rearrange("b (h n) -> b h n", h=8)))

                    def bc_v(vec):
                        return L1v[:, vec, :].unsqueeze(1).broadcast_to([128, 32, 64])

                    def bc_k(ap2, hf):
                        return ap2[:, hf * 32:(hf + 1) * 32].unsqueeze(2).broadcast_to([128, 32, 64])
                    for hf in range(2):
                        Sx, Tx = S_h[hf], T_h[hf]
                        vs = slice(hf * 32, (hf + 1) * 32)
                        A("dve", lambda e, Sx=Sx, Tx=Tx: e.tensor_tensor(out=Tx, in0=Sx, in1=bc_v(4), op=ALU.mult))
                        A("dve", lambda e, Tx=Tx, vs=vs: e.tensor_reduce(out=saL[:, vs], in_=Tx, axis=AX.X, op=ALU.add))
                        A("dve", lambda e, Sx=Sx: e.tensor_tensor(out=Sx, in0=Sx, in1=bc_v(1), op=ALU.mult))
                        A("dve", lambda e, Tx=Tx, hf=hf: e.tensor_tensor(out=Tx, in0=bc_k(saL, hf), in1=bc_v(5), op=ALU.mult))
                        A("dve", lambda e, Sx=Sx, Tx=Tx: e.tensor_tensor(out=Sx, in0=Sx, in1=Tx, op=ALU.add))
                        A("dve", lambda e, Tx=Tx, hf=hf: e.tensor_tensor(out=Tx, in0=bc_k(L1v[:, 3, :], hf), in1=bc_v(2), op=ALU.mult))
                        A("dve", lambda e, Sx=Sx, Tx=Tx: e.tensor_tensor(out=Sx, in0=Sx, in1=Tx, op=ALU.add))
                        S.dma("sp", lambda e, Sx=Sx, hf=hf: e.dma_start(out=o_wkv_s[:, hf * 2048:(hf + 1) * 2048], in_=Sx.rearrange("p v k -> p (v k)")))
                        A("dve", lambda e, Sx=Sx, Tx=Tx: e.tensor_tensor(out=Tx, in0=Sx, in1=bc_v(0), op=ALU.mult))
                        A("dve", lambda e, Tx=Tx, vs=vs: e.tensor_reduce(out=yL1[:, vs], in_=Tx, axis=AX.X, op=ALU.add))
                    S.dma("sp", lambda e: e.dma_start(out=scr_y.rearrange("b (h n) -> (b h) n", h=8), in_=yL1))
                    S.dma("sp", lambda e: e.dma_start(out=ytok, in_=scr_y))
                    b = nb_par()
                    for j in range(4):
                        A("pe", lambda e, j=j, b=b: e.transpose(out=ps[:, b, j * NS:(j + 1) * NS], in_=ytok[:, j * 128:(j + 1) * 128],
                                                                identity=ident[0:NS, 0:NS]))
                    A("act", lambda e, b=b: e.copy(out=yT[:, :, HALF:HALF + NS], in_=ps[:, b, 0:4 * NS].rearrange("p (j n) -> p j n", j=4)))
                    for g_ in range(4):
                        cs_ = list(range(g_ * 4, min(14, g_ * 4 + 4)))
                        b = nb_par()
                        for ci, c in enumerate(cs_):
                            A("pe", lambda e, ci=ci, c=c, b=b: e.transpose(out=ps[0:NS, b, ci * 128:(ci + 1) * 128], in_=shiftS[:, c, :], identity=ident))
                        n_ = len(cs_) * 128
                        A("act", lambda e, g_=g_, b=b, n_=n_: e.copy(out=shtok[:, g_ * 512:g_ * 512 + n_], in_=ps[0:NS, b, 0:n_]))
                    S.dma("sp", lambda e: e.dma_start(out=o_shift_s, in_=shtok[:, 0:RPROJ]))

            if stop_after == "Rsample":
                continue
            ar_.top = rs_mark if nsamp else mark
            bk_tok = [ar_.alloc(BF16, [128, 2, 512]) for _ in range(2)]
            v_tok = [ar_.alloc(BF16, [128, 512]) for _ in range(2)]
            NM_sb = [ar_.alloc(BF16, [128, 8, 2, 128]) for _ in range(2)]
            P_sb2 = [[ar_.alloc(BF16, [128, 8, 64]) for _ in range(2)] for _ in range(2)]
            Tt_sb2 = [[ar_.alloc(BF16, [128, 8, 64]) for _ in range(1)] for _ in range(2)]
            QT_sb2 = [[ar_.alloc(BF16, [128, 8, 2, 64]) for _ in range(2)] for _ in range(2)]
            X_sb = [ar_.alloc(BF16, [128, 8, 64]) for _ in range(2)]
            U_sb = [ar_.alloc(BF16, [128, 8, 64]) for _ in range(2)]
            XV_sb = [ar_.alloc(F32, [128, 8, 64]) for _ in range(2)]
            Hs = ar_.alloc(F32, [128, 4, 64])
            gtmp = [ar_.alloc(F32, [128, TW]) for _ in range(3)]

            for i in range(8):
                q = i % 2
                P_sb, QT_sb, Tt_sb = P_sb2[q], QT_sb2[q], Tt_sb2[q]
                S.tag = "p%d Rpar%d" % (pas, i)
                b1 = nb_par()
                b2 = nb_par()
                pT1 = ps[:, b1, :].bitcast(BF16).rearrange("p (v n) -> p v n", v=2)
                pT2 = ps[:, b2, :].bitcast(BF16)
                for j in range(4):
                    A("pe", lambda e, i=i, j=j, pT1=pT1: e.transpose(out=pT1[:, 0, j * 128:(j + 1) * 128], in_=bT[:, j, i * 128:(i + 1) * 128],
                                                                     identity=ident_bf[:]), ["bT", "ident_bf"], PB(b1))
                    A("pe", lambda e, i=i, j=j, pT1=pT1: e.transpose(out=pT1[:, 1, j * 128:(j + 1) * 128], in_=kT[:, j, i * 128:(i + 1) * 128],
                                                                     identity=ident_bf[:]), ["kT", "ident_bf"], PB(b1))
                    A("pe", lambda e, i=i, j=j, pT2=pT2: e.transpose(out=pT2[:, j * 128:(j + 1) * 128], in_=vT[:, j, i * 128:(i + 1) * 128],
                                                                     identity=ident_bf[:]), ["vT", "ident_bf"], PB(b2))
                A("act", lambda e, q=q, pT1=pT1: e.copy(out=bk_tok[q][:], in_=pT1), PB(b1), ["bk_tok%d" % q])
                A("dve", lambda e, q=q, pT2=pT2: e.tensor_copy(out=v_tok[q][:], in_=pT2[:, 0:512]), PB(b2), ["v_tok%d" % q])
                if stop_after == "c1":
                    break
                for hg in range(2):
                    b = nb_par(2)
                    for hh4 in range(4):
                        h = hg * 4 + hh4
                        j, hh = h // 2, h % 2
                        for e_ in range(2):
                            c = 2 * i + e_
                            for x, src in ((0, bT), (1, kT)):
                                bb, oo = b + hh, ((hh4 // 2) * 2 + x) * 128
                                A("pe", lambda e, src=src, j=j, hh=hh, c=c, e_=e_, bb=bb, oo=oo: e.matmul(
                                    ps[e_ * 64:(e_ + 1) * 64, bb, oo:oo + 128], lhsT=src[hh * 64:(hh + 1) * 64, j, c * 64:(c + 1) * 64],
                                    rhs=ar_t[hh * 64:(hh + 1) * 64, j, c, :, :], start=True, stop=True),
                                  ["bT", "kT", "ar"], PB(b, 2))
                    for hh in range(2):
                        nmv = NM_sb[q][:, hg * 4 + hh:(hg + 1) * 4:2, :, :]
                        A("dve", lambda e, nmv=nmv, b=b, hh=hh: e.tensor_tensor(out=nmv, in0=ps[:, b + hh, :].rearrange("p (jj x n) -> p jj x n", jj=2, x=2),
                                                                                 in1=maskNM_bf.unsqueeze(1).unsqueeze(1).broadcast_to([128, 2, 2, 128]), op=ALU.mult))
                if stop_after == "c2":
                    break
                b = nb_par(2)
                for h in range(8):
                    j, hh = h // 2, h % 2
                    for e_ in range(2):
                        c = 2 * i + e_
                        A("pe", lambda e, j=j, hh=hh, c=c, e_=e_, h=h, b=b: e.matmul(
                            ps[e_ * 64:(e_ + 1) * 64, b + hh, j * 64:(j + 1) * 64], lhsT=ar_t[hh * 64:(hh + 1) * 64, j, c, 0, :],
                            rhs=bT[hh * 64:(hh + 1) * 64, j, c * 64:(c + 1) * 64], start=True, stop=True))
                for hh in range(2):
                    pv = P_sb[0][:, hh:8:2, :]
                    A("dve", lambda e, pv=pv, b=b, hh=hh: e.tensor_tensor(out=pv, in0=ps[:, b + hh, 0:256].rearrange("p (j s) -> p j s", j=4),
                                                                           in1=maskL_bf.unsqueeze(1).broadcast_to([128, 4, 64]), op=ALU.mult))
                if stop_after == "c3":
                    break
                Q0 = NM_sb[q][:, :, 0, 0:64]
                A("dve", lambda e, q=q: e.tensor_tensor(out=QT_sb[1][:, :, 1, :], in0=NM_sb[q][:, :, 0, 0:64],
                                                          in1=maskI_bf.unsqueeze(1).broadcast_to([128, 8, 64]), op=ALU.add))
                ev_ctr = [0]
                import os as _os3
                EVK = int(_os3.environ.get("EVK", "3"))

                def evac_half(bank, e_, dst_ap, shape_pat, **kw):
                    sl = slice(e_ * 64, (e_ + 1) * 64)
                    src = ps[sl, bank, :].rearrange(shape_pat, **kw)
                    ev_ctr[0] += 1
                    if ev_ctr[0] % EVK != 0:
                        A("act", lambda e: e.copy(out=dst_ap, in_=src))
                    else:
                        A("dve", lambda e: e.tensor_copy(out=dst_ap, in_=src))

                def evac2(bk, dst):
                    for e_ in range(2):
                        sl = slice(e_ * 64, (e_ + 1) * 64)
                        evac_half(bk + e_, e_, dst[sl, :, :], "p (h s) -> p h s", h=8)

                bA = nb_par(2)
                bB = nb_par(2)
                for h in range(8):
                    for e_ in range(2):
                        sl = slice(e_ * 64, (e_ + 1) * 64)
                        A("pe", lambda e, h=h, sl=sl, e_=e_, bA=bA: e.matmul(ps[sl, bA + e_, h * 64:(h + 1) * 64], lhsT=Q0[sl, h, :], rhs=P_sb[0][sl, h, :],
                                                                             start=True, stop=True))
                        A("pe", lambda e, h=h, sl=sl, e_=e_, bB=bB: e.matmul(ps[sl, bB + e_, h * 64:(h + 1) * 64], lhsT=P_sb[0][sl, h, :], rhs=Q0[sl, h, :],
                                                                             start=True, stop=True))
                evac2(bA, P_sb[1])
                for e_ in range(2):
                    sl = slice(e_ * 64, (e_ + 1) * 64)
                    evac_half(bB + e_, e_, QT_sb[1][sl, :, 0, :], "p (h s) -> p h s", h=8)
                Tc = None
                for lev in range(1, 6):
                    pi = lev % 2
                    Pc = P_sb[pi]
                    QTc = QT_sb[pi]
                    QTn = QT_sb[1 - pi]
                    last = (lev == 5)
                    if not last:
                        bA = nb_par(2)
                        for h in range(8):
                            for e_ in range(2):
                                sl = slice(e_ * 64, (e_ + 1) * 64)
                                A("pe", lambda e, h=h, sl=sl, e_=e_, bA=bA, QTc=QTc, Pc=Pc: e.matmul(ps[sl, bA + e_, h * 64:(h + 1) * 64], lhsT=QTc[sl, h, 0, :],
                                                                                                      rhs=Pc[sl, h, :], start=True, stop=True))
                    if not last:
                        for hg in range(2):
                            bB = nb_par(2)
                            for h4 in range(4):
                                h = hg * 4 + h4
                                for e_ in range(2):
                                    sl = slice(e_ * 64, (e_ + 1) * 64)
                                    A("pe", lambda e, h=h, h4=h4, sl=sl, e_=e_, bB=bB, QTc=QTc, Pc=Pc: e.matmul(
                                        ps[sl, bB + e_, h4 * 128:(h4 + 1) * 128], lhsT=Pc[sl, h, :], rhs=QTc[sl, h, :, :], start=True, stop=True))
                            for e_ in range(2):
                                sl = slice(e_ * 64, (e_ + 1) * 64)
                                src4 = ps[sl, bB + e_, :].rearrange("p (h x s) -> p h x s", h=4, x=2)
                                hsl = slice(hg * 4, (hg + 1) * 4)
                                A("act", lambda e, sl=sl, src4=src4, hsl=hsl, QTn=QTn: e.copy(out=QTn[sl, hsl, 0, :], in_=src4[:, :, 0, :]))
                                A("dve", lambda e, sl=sl, src4=src4, hsl=hsl, QTn=QTn, QTc=QTc: e.tensor_tensor(out=QTn[sl, hsl, 1, :], in0=src4[:, :, 1, :],
                                                                                                                 in1=QTc[sl, hsl, 1, :], op=ALU.add))
                        evac2(bA, P_sb[1 - pi])
                    else:
                        bB = nb_par(2)
                        for h in range(8):
                            for e_ in range(2):
                                sl = slice(e_ * 64, (e_ + 1) * 64)
                                A("pe", lambda e, h=h, sl=sl, e_=e_, bB=bB, QTc=QTc, Pc=Pc: e.matmul(
                                    ps[sl, bB + e_, h * 64:(h + 1) * 64], lhsT=Pc[sl, h, :], rhs=QTc[sl, h, 1, :], start=True, stop=True))
                        Tc = Tt_sb[0]
                        for e_ in range(2):
                            sl = slice(e_ * 64, (e_ + 1) * 64)
                            A("dve", lambda e, sl=sl, e_=e_, bB=bB, QTc=QTc, Tc=Tc: e.tensor_tensor(out=Tc[sl, :, :], in0=ps[sl, bB + e_, :].rearrange("p (h s) -> p h s", h=8),
                                                                                                     in1=QTc[sl, :, 1, :], op=ALU.add))
                bXV = nb_par(2)
                for h in range(8):
                    for e_ in range(2):
                        sl = slice(e_ * 64, (e_ + 1) * 64)
                        A("pe", lambda e, h=h, sl=sl, e_=e_, q=q, bXV=bXV: e.matmul(ps[sl, bXV + e_, h * 64:(h + 1) * 64], lhsT=NM_sb[q][sl, h, 1, 0:64],
                                                                                    rhs=v_tok[q][sl, h * 64:(h + 1) * 64], start=True, stop=True))
                evac2(bXV, XV_sb[q])
                if stop_after == "c4":
                    break
                S.tag = "p%d Rseq%d" % (pas, i)
                for e_ in range(2):
                    c = 2 * i + e_
                    sl = slice(e_ * 64, (e_ + 1) * 64)
                    xq = c % 2
                    bX = nb_seq(e_)
                    for j in range(4):
                        A("pe", lambda e, j=j, sl=sl, c=c, bX=bX: e.matmul(ps[sl, bX, j * 128:(j + 1) * 128], lhsT=ar_t[:, j, c, 0, :],
                                                                           rhs=H0bd[:, j, :], start=True, stop=True))
                    A("dve", lambda e, sl=sl, xq=xq, bX=bX, q=q: e.tensor_tensor(out=X_sb[xq][sl, :, :], in0=ps[sl, bX, :].rearrange("p (h s) -> p h s", h=8),
                                                                                  in1=XV_sb[q][sl, :, :], op=ALU.add))
                    bU = nb_seq(e_)
                    for h in range(8):
                        A("pe", lambda e, h=h, sl=sl, xq=xq, bU=bU, Tc=Tc: e.matmul(ps[sl, bU, h * 64:(h + 1) * 64], lhsT=Tc[sl, h, :],
                                                                                    rhs=X_sb[xq][sl, h, :], start=True, stop=True),
                          [], PB(bU))
                    A("dve", lambda e, sl=sl, xq=xq, bU=bU: e.tensor_copy(out=U_sb[xq][sl, :, :], in_=ps[sl, bU, :].rearrange("p (h s) -> p h s", h=8)),
                      PB(bU), ["U%d" % xq])
                    bY1 = nb_seq(1 - e_)
                    for j in range(4):
                        A("pe", lambda e, j=j, c=c, bY1=bY1: e.matmul(ps[:, bY1, j * 64:(j + 1) * 64], lhsT=H0bd[:, j, :],
                                                                      rhs=ar_t[:, j, c, 1, :], start=True, stop=True))
                    A("act", lambda e, c=c, bY1=bY1: e.copy(out=yT[:, :, c * 64:(c + 1) * 64], in_=ps[:, bY1, 0:256].rearrange("p (j t) -> p j t", j=4)))
                    bY = nb_seq(e_)
                    for h in range(8):
                        j, hh = h // 2, h % 2
                        hs = slice(hh * 64, (hh + 1) * 64)
                        A("pe", lambda e, h=h, j=j, hs=hs, sl=sl, xq=xq, q=q, bY=bY: e.matmul(ps[hs, bY, j * 64:(j + 1) * 64], lhsT=U_sb[xq][sl, h, :],
                                                                                              rhs=NM_sb[q][sl, h, 0, 64:128], start=True, stop=False))
                        A("pe", lambda e, h=h, j=j, hs=hs, sl=sl, q=q, bY=bY: e.matmul(ps[hs, bY, j * 64:(j + 1) * 64], lhsT=v_tok[q][sl, h * 64:(h + 1) * 64],
                                                                                       rhs=NM_sb[q][sl, h, 1, 64:128], start=False, stop=True))
                    A("dve", lambda e, c=c, bY=bY: e.tensor_tensor(out=yT[:, :, c * 64:(c + 1) * 64], in0=ps[:, bY, 0:256].rearrange("p (j t) -> p j t", j=4),
                                                                    in1=yT[:, :, c * 64:(c + 1) * 64], op=ALU.add))
                    bG = nb_seq(e_)
                    for h in range(8):
                        j, hh = h // 2, h % 2
                        hs = slice(hh * 64, (hh + 1) * 64)
                        A("pe", lambda e, h=h, j=j, hs=hs, sl=sl, xq=xq, q=q, bG=bG: e.matmul(ps[hs, bG, j * 64:(j + 1) * 64], lhsT=bk_tok[q][sl, 0, h * 64:(h + 1) * 64],
                                                                                              rhs=U_sb[xq][sl, h, :], start=True, stop=False),
                          ["bk_tok%d" % q, "U%d" % xq], PB(bG))
                        A("pe", lambda e, h=h, j=j, hs=hs, sl=sl, q=q, bG=bG: e.matmul(ps[hs, bG, j * 64:(j + 1) * 64], lhsT=bk_tok[q][sl, 1, h * 64:(h + 1) * 64],
                                                                                       rhs=v_tok[q][sl, h * 64:(h + 1) * 64], start=False, stop=True),
                          ["bk_tok%d" % q, "v_tok%d" % q], PB(bG))
                    A("dve", lambda e, bG=bG: e.tensor_tensor(out=Hs[:], in0=ps[:, bG, 0:256].rearrange("p (j v) -> p j v", j=4), in1=H0f[:], op=ALU.add),
                      PB(bG) + ["H0f"], ["Hs"])
                    A("dve", lambda e, c=c: e.tensor_tensor(out=H0f[:], in0=Hs[:], in1=WC[:, :, c:c + 1].broadcast_to([128, 4, 64]), op=ALU.mult),
                      ["Hs", "WC"], ["H0f"])
                    A("act", lambda e: e.copy(out=H0bd[0:64, :, 0:64], in_=H0f[0:64, :, :]), ["H0f"], ["H0b"])
                    A("act", lambda e: e.copy(out=H0bd[64:128, :, 64:128], in_=H0f[64:128, :, :]), ["H0f"], ["H0b"])
            if stop_after in ("c1", "c2", "c3", "c4"):
                continue
            if dbg and pas == 0:
                dump("yT", yT, ["yT"])
            if stop_after == "Rchunk":
                continue
            if nsamp:
                S.tag = "p%d Rsample" % pas
                emit_rsample()
            S.tag = "p%d Rpost" % pas
            for j in range(4):
                yj = yT[:, j, :]
                yc, sq_, rs_ = gtmp[0], gtmp[1], gtmp[2]

                def ev_mean(b, c0, c1, j=j):
                    A("dve", lambda e: e.scalar_tensor_tensor(out=yc[:, c0:c1], in0=ps[:, b, 0:c1 - c0], scalar=-1.0 / 64, in1=yT[:, j, c0:c1],
                                                              op0=ALU.mult, op1=ALU.add), PB(b) + ["yT"], ["gtmp0"])
                fp32_blocksum(yj, "yT", blk64, ev_mean)
                A("act", lambda e: e.activation(out=sq_[:, 0:W], in_=yc[:, 0:W], func=AF.Square), ["gtmp0"], ["gtmp1"])

                def ev_var(b, c0, c1):
                    A("act", lambda e: e.activation(out=rs_[:, c0:c1], in_=ps[:, b, 0:c1 - c0], func=AF.Ln, scale=1.0 / 64, bias=epsgn[:, 0:1]))
                fp32_blocksum(sq_, "gtmp1", blk64, ev_var)
                A("act", lambda e: e.activation(out=rs_[:, 0:W], in_=rs_[:, 0:W], func=AF.Exp, scale=-0.5), ["gtmp2"], ["gtmp2"])
                A("dve", lambda e: e.tensor_tensor(out=yc[:, 0:W], in0=yc[:, 0:W], in1=rs_[:, 0:W], op=ALU.mult), ["gtmp0", "gtmp2"], ["gtmp0"])
                A("dve", lambda e, j=j: e.tensor_scalar(out=yc[:, 0:W], in0=yc[:, 0:W], scalar1=pf("ln_g", j), scalar2=pf("ln_b", j),
                                                        op0=ALU.mult, op1=ALU.add), ["gtmp0", "pfm"], ["gtmp0"])
                A("dve", lambda e, j=j: e.tensor_tensor(out=yc[:, 0:W], in0=yc[:, 0:W], in1=bonus[:, j, 0:W], op=ALU.add), ["gtmp0", "bonus"], ["gtmp0"])
                A("dve", lambda e, j=j: e.tensor_tensor(out=oa[:, j, 0:W], in0=yc[:, 0:W], in1=g_bf[:, j, 0:W], op=ALU.mult), ["gtmp0", "g_bf"], ["oa"])
            if dbg and pas == 0:
                dump("oa", oa[:], ["oa"])
            if pas == passes[-1]:
                wst = gtmp[0][0:64, 0:512].rearrange("p (j n) -> p j n", j=4)
                b = nb()
                for j in range(4):
                    A("pe", lambda e, j=j, b=b: e.transpose(out=ps[0:64, b, j * 128:(j + 1) * 128], in_=H0f[:, j, :], identity=ident),
                      ["H0f", "consts"], PB(b))
                A("act", lambda e, b=b: e.copy(out=wst, in_=ps[0:64, b, :].rearrange("p (j n) -> p j n", j=4)), PB(b), ["gtmp0"])
                S.dma("sp", lambda e: e.dma_start(out=o_wkv_p.rearrange("(j hh) v k -> v j hh k", hh=2),
                                                   in_=wst.rearrange("p j (hh k) -> p j hh k", hh=2)), reads=["gtmp0"])
                b = nb()
                A("pe", lambda e, b=b: e.transpose(out=ps[0:14, b, 0:128], in_=carry[:, :], identity=ident), ["carry", "consts"], PB(b))
                A("act", lambda e, b=b: e.copy(out=gtmp[1][0:14, 0:128], in_=ps[0:14, b, 0:128]), PB(b), ["gtmp1"])
                S.dma("sp", lambda e: e.dma_start(out=o_shift_p.rearrange("(c p) -> c p", p=128), in_=gtmp[1][0:14, 0:128]), reads=["gtmp1"])
            if stop_after == "Rpost":
                continue

            S.tag = "p%d H" % pas
            S.barrier()
            ar_ = Arena(arena_t, ARENA_WORDS)
            Eb = ar_.alloc(F32, [128, 4, HALF])
            qT = ar_.alloc(BF16, [128, 4, HALF])
            hkT = ar_.alloc(BF16, [128, 4, HALF])
            hvT = ar_.alloc(BF16, [128, 4, TW])
            sgo = ar_.alloc(BF16, [128, 4, TW])
            oT = ar_.alloc(F32, [128, 4, TW])
            smpH = ar_.alloc(F32, [128, 4, 4, NS])
            hmark = ar_.top
            htmp = [ar_.alloc(F32, [128, TW]) for _ in range(8)]
            hset = [htmp[0:4], htmp[4:8]]
            cmaskH = ar_.alloc(BF16, [128, HALF])
            S.dma("pool", lambda e: e.dma_start(out=cmaskH, in_=consts_d[:, 640:1664]), writes=["cmaskH"])
            HB = RPROJ
            wv, wr = load_w(w_in_v[:, :, HB + 512:HB + 1024], (8, 512))
            hvh = [(0, 512), (512, W)]
            for h in range(4):
                T0, T1_, T2, T3 = hset[h % 2]

                def ev_f(b, c0, c1, T0=T0):
                    A("act", lambda e: e.activation(out=T0[:, c0:c1], in_=ps[:, b, 0:c1 - c0], func=AF.Sigmoid))
                proj_fm(wv, wr, h * 128, 8, hT, "hT", ev_f)
                for (a_, b_) in hvh:
                    b2 = min(b_, HALF)
                    A("dve", lambda e, h=h, a_=a_, b_=b_: e.tensor_scalar(out=T0[:, a_:b_], in0=T0[:, a_:b_], scalar1=dvc("omlb", h), scalar2=dvc("lb", h),
                                                                          op0=ALU.mult, op1=ALU.add))
                    A("dve", lambda e, a_=a_, b_=b_: e.tensor_scalar(out=T1_[:, a_:b_], in0=T0[:, a_:b_], scalar1=-1.0, scalar2=1.0, op0=ALU.mult, op1=ALU.add))
                    A("act", lambda e, a_=a_, b2=b2: e.activation(out=T2[:, a_:b2], in_=T0[:, a_:b2], func=AF.Ln))
                    A("dve", lambda e, a_=a_, b2=b2: e.tensor_tensor_scan(out=T3[:, a_:b2], data0=cmaskH[:, a_:b2], data1=T2[:, a_:b2], initial=0.0,
                                                                          op0=ALU.mult, op1=ALU.add))
                    A("act", lambda e, h=h, a_=a_, b2=b2: e.activation(out=Eb[:, h, a_:b2], in_=T3[:, a_:b2], func=AF.Exp))
                    A("act", lambda e, h=h, a_=a_, b2=b2: e.activation(out=DC[:, h, a_ // 64:b2 // 64], in_=T3[:, a_:b2].rearrange("p (c t) -> p c t", t=64)[:, :, 63],
                                                                       func=AF.Exp))
                    A("act", lambda e, a_=a_, b2=b2: e.activation(out=T2[:, a_:b2], in_=T3[:, a_:b2], func=AF.Exp, scale=-1.0))
                    A("dve", lambda e, h=h, a_=a_, b2=b2: e.tensor_tensor(out=hkT[:, h, a_:b2], in0=T1_[:, a_:b2], in1=T2[:, a_:b2], op=ALU.mult))
                if nsamp:
                    A("act", lambda e, h=h: e.copy(out=smpH[:, 1, h, :], in_=T0[:, HALF:HALF + NS]))
                    A("act", lambda e, h=h: e.copy(out=smpH[:, 2, h, :], in_=T1_[:, HALF:HALF + NS]))
            wv, wr = load_w(w_in_v[:, :, HB:HB + 512], (8, 512))
            for h in range(4):
                T0 = hset[h % 2][0]

                def ev_q(b, c0, c1, T0=T0, h=h):
                    A("act", lambda e: e.activation(out=T0[:, c0:c1], in_=ps[:, b, 0:c1 - c0], func=AF.Silu))
                    if c0 < HALF:
                        A("dve", lambda e: e.tensor_tensor(out=qT[:, h, c0:c1], in0=T0[:, c0:c1], in1=Eb[:, h, c0:c1], op=ALU.mult))
                proj_fm(wv, wr, h * 128, 8, hT, "hT", ev_q)
                if nsamp:
                    A("act", lambda e, h=h: e.copy(out=smpH[:, 0, h, :], in_=T0[:, HALF:HALF + NS]))
            wv, wr = load_w(w_in_v[:, :, HB + 1024:HB + 1536], (8, 512))
            for h in range(4):
                def ev_i(b, c0, c1, h=h):
                    A("dve", lambda e: e.tensor_copy(out=hvT[:, h, c0:c1], in_=ps[:, b, 0:c1 - c0]), PB(b), ["hvT"])
                    if c0 >= HALF:
                        A("dve", lambda e: e.tensor_copy(out=smpH[:, 3, h, :], in_=ps[:, b, 0:NS]), PB(b), ["smpH"])
                proj_fm(wv, wr, h * 128, 8, hT, "hT", ev_i)
            wv, wr = load_w(w_in_v[:, :, HB + 1536:HB + 2048], (8, 512))
            for h in range(4):
                def ev_og(b, c0, c1, h=h):
                    A("act", lambda e: e.activation(out=sgo[:, h, c0:c1], in_=ps[:, b, 0:c1 - c0], func=AF.Sigmoid), PB(b), ["sgo"])
                proj_fm(wv, wr, h * 128, 8, hT, "hT", ev_og)

            if nsamp:
                S.barrier()
                ar_.top = hmark
                S_s = ar_.alloc(F32, [128, NS, 4, 128])
                ktok_s = ar_.alloc(F32, [NS, 512])
                vtok_s = ar_.alloc(F32, [NS, 512])
                vm = [ar_.alloc(F32, [NS, 512]) for _ in range(2)]
                tS_l = [ar_.alloc(F32, [128, 4, 128]) for _ in range(3)]
                q_bf = ar_.alloc(BF16, [128, 4, NS])
                Sb_l = [ar_.alloc(BF16, [128, 4, 128]) for _ in range(3)]
                S.dma("sp", lambda e: e.dma_start(out=S_s, in_=hgrn_s.rearrange("b h k v -> k b h v")), writes=["S_s"])
                for vec, dst, dn in ((2, ktok_s, "ktok_s"), (3, vtok_s, "vtok_s")):
                    b = nb_smp()
                    for h in range(4):
                        A("pe", lambda e, vec=vec, h=h, b=b: e.transpose(out=ps[0:NS, b, h * 128:(h + 1) * 128], in_=smpH[:, vec, h, :], identity=ident),
                          ["smpH", "consts"], PB(b))
                    A("act", lambda e, dst=dst, b=b: e.copy(out=dst, in_=ps[0:NS, b, :]), PB(b), [dn])
                for bi in range(NS):
                    vq = bi % 2
                    tS = tS_l[bi % 3]
                    A("dve", lambda e, bi=bi, vq=vq: e.tensor_scalar(out=vm[vq], in0=vtok_s, scalar1=ident[0:NS, bi:bi + 1], scalar2=None, op0=ALU.mult),
                      ["vtok_s", "consts"], ["vm%d" % vq])
                    b = nb_smp()
                    for h in range(4):
                        A("pe", lambda e, h=h, vq=vq, b=b: e.matmul(ps[:, b, h * 128:(h + 1) * 128], lhsT=ktok_s[:, h * 128:(h + 1) * 128],
                                                                    rhs=vm[vq][:, h * 128:(h + 1) * 128], start=True, stop=True),
                          ["ktok_s", "vm%d" % vq], PB(b))
                    A("dve", lambda e, bi=bi: e.tensor_tensor(out=tS, in0=S_s[:, bi, :, :],
                                                               in1=smpH[:, 1, :, bi:bi + 1].broadcast_to([128, 4, 128]), op=ALU.mult),
                      ["S_s", "smpH"], ["tS"])
                    A("dve", lambda e, bi=bi, b=b: e.tensor_tensor(out=S_s[:, bi, :, :], in0=ps[:, b, :].rearrange("p (h v) -> p h v", h=4), in1=tS,
                                                                    op=ALU.add), PB(b) + ["tS"], ["S_s"])
                S.dma("sp", lambda e: e.dma_start(out=o_hgrn_s.rearrange("b h k v -> k b h v"), in_=S_s), reads=["S_s"])
                A("act", lambda e: e.copy(out=q_bf, in_=smpH[:, 0, :, :]))
                bO_ = nb_smp()
                for bi in range(NS):
                    Sb = Sb_l[bi % 3]
                    A("act", lambda e, bi=bi, Sb=Sb: e.copy(out=Sb, in_=S_s[:, bi, :, :]))
                    for h in range(4):
                        A("pe", lambda e, h=h, bi=bi, Sb=Sb, bO_=bO_: e.matmul(ps[:, bO_, h * NS + bi:h * NS + bi + 1], lhsT=Sb[:, h, :],
                                                                                rhs=q_bf[:, h, bi:bi + 1], start=True, stop=True))
                A("act", lambda e, bO_=bO_: e.copy(out=oT[:, :, HALF:HALF + NS], in_=ps[:, bO_, 0:4 * NS].rearrange("p (h n) -> p h n", h=4)))

            if not nsamp:
                ar_.top = hmark
            par_nb[0] = 4 if nsamp else 6
            par_ctr[0] = 0
            hk_tok = [ar_.alloc(BF16, [128, 512]) for _ in range(2)]
            hv_tok = [ar_.alloc(BF16, [128, 512]) for _ in range(2)]
            PT_sb = [ar_.alloc(BF16, [128, 4, 64]) for _ in range(2)]
            Ss = ar_.alloc(F32, [128, 4, 128])
            ar_.top = hmark
            htmp = [ar_.alloc(F32, [128, TW]) for _ in range(2)]
            for i in range(8):
                q = i % 2
                b1 = nb_par()
                pTk = ps[:, b1, :].bitcast(BF16).rearrange("p (v n) -> p v n", v=2)
                for h in range(4):
                    A("pe", lambda e, i=i, h=h, pTk=pTk: e.transpose(out=pTk[:, 0, h * 128:(h + 1) * 128], in_=hkT[:, h, i * 128:(i + 1) * 128],
                                                                     identity=ident_bf[:]), ["hkT", "ident_bf"], PB(b1))
                    A("pe", lambda e, i=i, h=h, pTk=pTk: e.transpose(out=pTk[:, 1, h * 128:(h + 1) * 128], in_=hvT[:, h, i * 128:(i + 1) * 128],
                                                                     identity=ident_bf[:]), ["hvT", "ident_bf"], PB(b1))
                A("act", lambda e, q=q, pTk=pTk: e.copy(out=hk_tok[q], in_=pTk[:, 0, :]), PB(b1), ["hk_tok%d" % q])
                A("dve", lambda e, q=q, pTk=pTk: e.tensor_copy(out=hv_tok[q], in_=pTk[:, 1, :]), PB(b1), ["hv_tok%d" % q])
                bS = nb_par()
                for h in range(4):
                    for e_ in range(2):
                        c = 2 * i + e_
                        A("pe", lambda e, h=h, e_=e_, c=c, bS=bS: e.matmul(ps[e_ * 64:(e_ + 1) * 64, bS, h * 64:(h + 1) * 64], lhsT=hkT[:, h, c * 64:(c + 1) * 64],
                                                                           rhs=qT[:, h, c * 64:(c + 1) * 64], start=True, stop=True), ["hkT", "qT"], PB(bS))
                A("dve", lambda e, q=q, bS=bS: e.tensor_tensor(out=PT_sb[q], in0=ps[:, bS, 0:256].rearrange("p (h t) -> p h t", h=4),
                                                                in1=masks_bf[:, 64:128].unsqueeze(1).broadcast_to([128, 4, 64]), op=ALU.mult),
                  PB(bS) + ["consts"], ["PT%d" % q])
                bO = nb_par()
                for e_ in range(2):
                    c = 2 * i + e_
                    sl = slice(e_ * 64, (e_ + 1) * 64)
                    for h in range(4):
                        oo = h * 128 + e_ * 64
                        A("pe", lambda e, h=h, c=c, oo=oo, bO=bO: e.matmul(ps[:, bO, oo:oo + 64], lhsT=S0b[:, h, :], rhs=qT[:, h, c * 64:(c + 1) * 64],
                                                                           start=True, stop=False), ["S0b", "qT"], PB(bO))
                        A("pe", lambda e, h=h, sl=sl, q=q, oo=oo, bO=bO: e.matmul(ps[:, bO, oo:oo + 64], lhsT=hv_tok[q][sl, h * 128:(h + 1) * 128],
                                                                                  rhs=PT_sb[q][sl, h, :], start=False, stop=True),
                          ["hv_tok%d" % q, "PT%d" % q], PB(bO))
                    bG = nb_seq(e_)
                    for h in range(4):
                        A("pe", lambda e, h=h, sl=sl, q=q, bG=bG: e.matmul(ps[:, bG, h * 128:(h + 1) * 128], lhsT=hk_tok[q][sl, h * 128:(h + 1) * 128],
                                                                           rhs=hv_tok[q][sl, h * 128:(h + 1) * 128], start=True, stop=True),
                          ["hk_tok%d" % q, "hv_tok%d" % q], PB(bG))
                    A("dve", lambda e, bG=bG: e.tensor_tensor(out=Ss, in0=ps[:, bG, :].rearrange("p (h v) -> p h v", h=4), in1=S0f[:], op=ALU.add),
                      PB(bG) + ["S0f"], ["Ss"])
                    A("dve", lambda e, c=c: e.tensor_tensor(out=S0f[:], in0=Ss, in1=DC[:, :, c:c + 1].broadcast_to([128, 4, 128]), op=ALU.mult),
                      ["Ss", "DC"], ["S0f"])
                    A("act", lambda e: e.copy(out=S0b[:], in_=S0f[:]), ["S0f"], ["S0b"])
                A("act", lambda e, i=i, bO=bO: e.copy(out=oT[:, :, i * 128:(i + 1) * 128], in_=ps[:, bO, :].rearrange("p (h t) -> p h t", h=4)),
                  PB(bO), ["oT"])
            par_nb[0] = 6
            if dbg and pas == 0:
                dump("oT", oT, ["oT"])
            for h in range(4):
                A("act", lambda e, h=h: e.activation(out=htmp[0][:, 0:W], in_=oT[:, h, 0:W], func=AF.Square), ["oT"], ["htmp0"])

                def ev_ms(b, c0, c1):
                    A("act", lambda e: e.activation(out=htmp[1][:, c0:c1], in_=ps[:, b, 0:c1 - c0], func=AF.Ln, scale=1.0 / 128, bias=epsrms[:, 0:1]))
                fp32_blocksum(htmp[0], "htmp0", ones, ev_ms)
                A("act", lambda e: e.activation(out=htmp[1][:, 0:W], in_=htmp[1][:, 0:W], func=AF.Exp, scale=-0.5), ["htmp1"], ["htmp1"])
                A("dve", lambda e, h=h: e.tensor_tensor(out=htmp[0][:, 0:W], in0=oT[:, h, 0:W], in1=htmp[1][:, 0:W], op=ALU.mult), ["oT", "htmp1"], ["htmp0"])
                A("dve", lambda e, h=h: e.scalar_tensor_tensor(out=ob[:, h, 0:W], in0=htmp[0][:, 0:W], scalar=pf("hng", h), in1=sgo[:, h, 0:W],
                                                               op0=ALU.mult, op1=ALU.mult), ["htmp0", "pfm", "sgo"], ["ob"])
            if dbg and pas == 0:
                dump("ob", ob[:], ["ob"])
            if pas == passes[-1]:
                S.dma("sp", lambda e: e.dma_start(out=o_hgrn_p.rearrange("h k v -> k h v"), in_=S0f[:]), reads=["S0f"])
            if stop_after == "H":
                continue

            S.barrier()
            ar_ = Arena(arena_t, ARENA_WORDS)
            x_tok = ar_.alloc(F32, [128, 8, 1024])
            xs_tok = ar_.alloc(F32, [NS, 1024])
            mergedT = ar_.alloc(BF16, [128, 8, TW])
            gm = [ar_.alloc(F32, [128, 512]) for _ in range(4)]
            GB = RPROJ + 2048
            w_upa_v = w_up_a.rearrange("(kc p) n -> p kc n", p=128)
            w_upb_v = w_up_b.rearrange("(kc p) n -> p kc n", p=128)
            for dcg in range(2):
                wga, wgar = load_w(w_in_v[:, :, GB + dcg * 512:GB + (dcg + 1) * 512], (8, 512))
                wgb, wgbr = load_w(w_in_v[:, :, GB + 1024 + dcg * 512:GB + 1024 + (dcg + 1) * 512], (8, 512))
                wu, wur = load_w(w_upa_v[:, :, dcg * 512:(dcg + 1) * 512], (4, 512))
                iu_ = (wctr[0] - 1) % NWB
                wub_view = wbuf[iu_][:, 4:8, :]
                S.dma("pool", lambda e, wub_view=wub_view, dcg=dcg: e.dma_start(out=wub_view, in_=w_upb_v[:, :, dcg * 512:(dcg + 1) * 512]), writes=[wur])
                for dc in range(4):
                    n0 = dc * 128
                    for (c0, c1) in blocks:
                        w_ = c1 - c0
                        b1, b2, b3, b4 = nb(), nb(), nb(), nb()
                        for kc in range(8):
                            A("pe", lambda e, kc=kc, b1=b1, c0=c0, c1=c1, n0=n0, wga=wga: e.matmul(ps[:, b1, 0:c1 - c0], lhsT=wga[:, kc, n0:n0 + 128],
                                                                                                   rhs=hT[:, kc, c0:c1], start=(kc == 0), stop=(kc == 7)),
                              [wgar, "hT"], PB(b1))
                        for kc in range(8):
                            A("pe", lambda e, kc=kc, b2=b2, c0=c0, c1=c1, n0=n0, wgb=wgb: e.matmul(ps[:, b2, 0:c1 - c0], lhsT=wgb[:, kc, n0:n0 + 128],
                                                                                                   rhs=hT[:, kc, c0:c1], start=(kc == 0), stop=(kc == 7)),
                              [wgbr, "hT"], PB(b2))
                        for kc in range(4):
                            A("pe", lambda e, kc=kc, b3=b3, c0=c0, c1=c1, n0=n0, wu=wu: e.matmul(ps[:, b3, 0:c1 - c0], lhsT=wu[:, kc, n0:n0 + 128],
                                                                                                 rhs=oa[:, kc, c0:c1], start=(kc == 0), stop=(kc == 3)),
                              [wur, "oa"], PB(b3))
                        for kc in range(4):
                            A("pe", lambda e, kc=kc, b4=b4, c0=c0, c1=c1, n0=n0, wub_view=wub_view: e.matmul(ps[:, b4, 0:c1 - c0], lhsT=wub_view[:, kc, n0:n0 + 128],
                                                                                                             rhs=ob[:, kc, c0:c1], start=(kc == 0), stop=(kc == 3)),
                              [wur, "ob"], PB(b4))
                        A("act", lambda e, b1=b1, w_=w_: e.activation(out=gm[0][:, 0:w_], in_=ps[:, b1, 0:w_], func=AF.Sigmoid), PB(b1), ["gm0"])
                        A("act", lambda e, b2=b2, w_=w_: e.activation(out=gm[1][:, 0:w_], in_=ps[:, b2, 0:w_], func=AF.Sigmoid), PB(b2), ["gm1"])
                        A("dve", lambda e, b3=b3, w_=w_: e.tensor_tensor(out=gm[2][:, 0:w_], in0=ps[:, b3, 0:w_], in1=gm[0][:, 0:w_], op=ALU.mult),
                          PB(b3) + ["gm0"], ["gm2"])
                        A("dve", lambda e, b4=b4, w_=w_: e.tensor_tensor(out=gm[3][:, 0:w_], in0=ps[:, b4, 0:w_], in1=gm[1][:, 0:w_], op=ALU.mult),
                          PB(b4) + ["gm1"], ["gm3"])
                        A("dve", lambda e, dcg=dcg, dc=dc, c0=c0, c1=c1, w_=w_: e.tensor_tensor(out=mergedT[:, dcg * 4 + dc, c0:c1], in0=gm[2][:, 0:w_],
                                                                                                 in1=gm[3][:, 0:w_], op=ALU.add),
                          ["gm2", "gm3"], ["mergedT"])
            if dbg and pas == 0:
                dump("mergedT", mergedT, ["mergedT"])
            if stop_after == "G":
                continue

            S.barrier()
            ar_.top = 0
            x_tok = ar_.alloc(F32, [128, 8, 1024])
            xs_tok = ar_.alloc(F32, [NS, 1024])
            mergedT = ar_.alloc(BF16, [128, 8, TW])
            for tt in range(8):
                S.dma("sp", lambda e, tt=tt: e.dma_start(out=x_tok[:, tt, :], in_=x_p[t0 + tt * 128:t0 + (tt + 1) * 128, :]),
                      writes=["x%d" % tt])
            if nsamp:
                S.dma("sp", lambda e: e.dma_start(out=xs_tok, in_=x_s), writes=["xs"])
            w_out_v = w_out.rearrange("(kc p) n -> p kc n", p=128)
            wos = [load_w(w_out_v[:, :, half * 512:(half + 1) * 512], (8, 512))[0] for half in range(2)]
            for tt in range(8):
                for half in range(2):
                    wo = wos[half]
                    b = nb()
                    for kc in range(8):
                        A("pe", lambda e, kc=kc, tt=tt, b=b, wo=wo: e.matmul(ps[:, b, :], lhsT=mergedT[:, kc, tt * 128:(tt + 1) * 128], rhs=wo[:, kc, :],
                                                                             start=(kc == 0), stop=(kc == 7)))
                    A("dve", lambda e, tt=tt, half=half, b=b: e.tensor_tensor(out=x_tok[:, tt, half * 512:(half + 1) * 512], in0=ps[:, b, :],
                                                                               in1=x_tok[:, tt, half * 512:(half + 1) * 512], op=ALU.add))
            if nsamp:
                for half in range(2):
                    wo = wos[half]
                    b = nb()
                    for kc in range(8):
                        A("pe", lambda e, kc=kc, b=b, wo=wo: e.matmul(ps[0:NS, b, :], lhsT=mergedT[:, kc, HALF:HALF + NS], rhs=wo[:, kc, :],
                                                                      start=(kc == 0), stop=(kc == 7)))
                    A("dve", lambda e, half=half, b=b: e.tensor_tensor(out=xs_tok[:, half * 512:(half + 1) * 512], in0=ps[0:NS, b, :],
                                                                        in1=xs_tok[:, half * 512:(half + 1) * 512], op=ALU.add))
            norm_phase("O", x_tok, xs_tok, g_ffn, ar_,
                       lambda gB, h_tok, hs_tok, R: to_hT(gB, h_tok, hs_tok, R, x_tok, xs_tok, lambda tt: "x%d" % tt, "xs"),
                       lambda tt: "x%d" % tt, "xs")
            if dbg and pas == 0:
                dump("hT2", hT[:], ["hT"])
            if stop_after == "O":
                continue

            S.barrier()
            ar_.top = 0
            x_tok = ar_.alloc(F32, [128, 8, 1024])
            xs_tok = ar_.alloc(F32, [NS, 1024])
            actT = ar_.alloc(BF16, [128, 22, TW])
            wdn = ar_.alloc(BF16, [128, 22, 1024])
            gBf = ftmp[:, :, :].rearrange("p a b -> p (a b)")
            junkD = ar_.alloc(BF16, [128, 1024])
            w_fd_v = w_fd.rearrange("(fc p) n -> p fc n", p=128)
            w_fg_v = w_fg.rearrange("(kc p) n -> p kc n", p=128)
            w_fu_v = w_fu.rearrange("(kc p) n -> p kc n", p=128)
            for fg in range(11):
                wg, wgr = load_w(w_fg_v[:, :, fg * 256:(fg + 1) * 256], (8, 256))
                wu, wur = load_w(w_fu_v[:, :, fg * 256:(fg + 1) * 256], (8, 256))
                if fg in (2, 4, 6, 8):
                    k_ = (fg - 2) // 2
                    lo, hi = (0, 6, 12, 18)[k_], (6, 12, 18, 22)[k_]
                    S.dma("pool", lambda e, lo=lo, hi=hi: e.dma_start(out=wdn[:, lo:hi, :], in_=w_fd_v[:, lo:hi, :]), writes=["wdn%d" % k_])
                for f2 in range(2):
                    fc = fg * 2 + f2
                    n0 = f2 * 128
                    for (c0, c1) in blocks:
                        w_ = c1 - c0
                        b1, b2 = nb(), nb()
                        for kc in range(8):
                            A("pe", lambda e, kc=kc, b1=b1, c0=c0, c1=c1, n0=n0, wg=wg: e.matmul(ps[:, b1, 0:c1 - c0], lhsT=wg[:, kc, n0:n0 + 128],
                                                                                                 rhs=hT[:, kc, c0:c1], start=(kc == 0), stop=(kc == 7)),
                              [wgr, "hT"], PB(b1))
                        for kc in range(8):
                            A("pe", lambda e, kc=kc, b2=b2, c0=c0, c1=c1, n0=n0, wu=wu: e.matmul(ps[:, b2, 0:c1 - c0], lhsT=wu[:, kc, n0:n0 + 128],
                                                                                                 rhs=hT[:, kc, c0:c1], start=(kc == 0), stop=(kc == 7)),
                              [wur, "hT"], PB(b2))
                        fq = (fc + (c0 // 512)) % 2
                        A("act", lambda e, b1=b1, w_=w_, fq=fq: e.activation(out=ftmp[:, fq, 0:w_], in_=ps[:, b1, 0:w_], func=AF.Silu), PB(b1), ["ftmp%d" % fq])
                        A("dve", lambda e, b2=b2, w_=w_, fq=fq, fc=fc, c0=c0, c1=c1: e.tensor_tensor(out=actT[:, fc, c0:c1], in0=ps[:, b2, 0:w_],
                                                                                                      in1=ftmp[:, fq, 0:w_], op=ALU.mult),
                          PB(b2) + ["ftmp%d" % fq], ["actT"])
            if stop_after == "F":
                continue

            S.barrier()
            S.dma("sp", lambda e: e.dma_start(out=gBf, in_=g_fin.partition_broadcast(128)), writes=["gBf"])
            A("dve", lambda e: e.memset(stat[:, 0:32], 0.0), [], ["stat"])
            wres_all = ["wdn0", "wdn1", "wdn2", "wdn3"]
            for tt in range(8):
                for half in range(2):
                    b = nb()
                    for fc in range(22):
                        A("pe", lambda e, fc=fc, tt=tt, half=half, b=b: e.matmul(ps[:, b, :], lhsT=actT[:, fc, tt * 128:(tt + 1) * 128],
                                                                                 rhs=wdn[:, fc, half * 512:(half + 1) * 512], start=(fc == 0), stop=(fc == 21)),
                          ["actT"] + wres_all, PB(b))
                    A("dve", lambda e, tt=tt, half=half, b=b: e.tensor_tensor(out=x_tok[:, tt, half * 512:(half + 1) * 512], in0=ps[:, b, :],
                                                                               in1=x_tok[:, tt, half * 512:(half + 1) * 512], op=ALU.add),
                      PB(b) + ["x%d" % tt], ["x%d" % tt])
                A("act", lambda e, tt=tt: e.activation(out=junkD, in_=x_tok[:, tt, :], func=AF.Square,
                                                       accum_out=stat[:, tt:tt + 1]), ["x%d" % tt, "stat"], ["ftmp0", "ftmp1", "stat%d" % tt])
                A("dve", lambda e, tt=tt: e.tensor_scalar(out=stat[:, 16 + tt:17 + tt], in0=stat[:, tt:tt + 1], scalar1=1.0 / D, scalar2=RMS_EPS,
                                                          op0=ALU.mult, op1=ALU.add), ["stat%d" % tt, "stat"], ["stat%d" % tt])
                A("act", lambda e, tt=tt: e.activation(out=stat[:, 16 + tt:17 + tt], in_=stat[:, 16 + tt:17 + tt], func=AF.Ln), ["stat%d" % tt], ["stat%d" % tt])
                A("act", lambda e, tt=tt: e.activation(out=stat[:, 16 + tt:17 + tt], in_=stat[:, 16 + tt:17 + tt], func=AF.Exp, scale=-0.5),
                  ["stat%d" % tt], ["stat%d" % tt])
                A("dve", lambda e, tt=tt: e.scalar_tensor_tensor(out=x_tok[:, tt, :], in0=x_tok[:, tt, :], scalar=stat[:, 16 + tt:17 + tt], in1=gBf,
                                                                 op0=ALU.mult, op1=ALU.mult), ["x%d" % tt, "stat%d" % tt, "gBf"], ["x%d" % tt])
                S.dma("sp", lambda e, tt=tt: e.dma_start(out=y_p[t0 + tt * 128:t0 + (tt + 1) * 128, :], in_=x_tok[:, tt, :]), reads=["x%d" % tt])
            if nsamp:
                for half in range(2):
                    b = nb()
                    for fc in range(22):
                        A("pe", lambda e, fc=fc, half=half, b=b: e.matmul(ps[0:NS, b, :], lhsT=actT[:, fc, HALF:HALF + NS],
                                                                          rhs=wdn[:, fc, half * 512:(half + 1) * 512], start=(fc == 0), stop=(fc == 21)),
                          ["actT"] + wres_all, PB(b))
                    A("dve", lambda e, half=half, b=b: e.tensor_tensor(out=xs_tok[:, half * 512:(half + 1) * 512], in0=ps[0:NS, b, :],
                                                                        in1=xs_tok[:, half * 512:(half + 1) * 512], op=ALU.add), PB(b) + ["xs"], ["xs"])
                A("act", lambda e: e.activation(out=junkD[0:NS, :], in_=xs_tok, func=AF.Square,
                                                accum_out=stat[0:NS, 8:9]), ["xs", "stat"], ["ftmp0", "ftmp1", "stat8"])
                A("dve", lambda e: e.tensor_scalar(out=stat[0:NS, 24:25], in0=stat[0:NS, 8:9], scalar1=1.0 / D, scalar2=RMS_EPS,
                                                    op0=ALU.mult, op1=ALU.add), ["stat8", "stat"], ["stat8"])
                A("act", lambda e: e.activation(out=stat[0:NS, 24:25], in_=stat[0:NS, 24:25], func=AF.Ln), ["stat8"], ["stat8"])
                A("act", lambda e: e.activation(out=stat[0:NS, 24:25], in_=stat[0:NS, 24:25], func=AF.Exp, scale=-0.5), ["stat8"], ["stat8"])
                A("dve", lambda e: e.scalar_tensor_tensor(out=xs_tok, in0=xs_tok, scalar=stat[0:NS, 24:25], in1=gBf[0:NS, :],
                                                          op0=ALU.mult, op1=ALU.mult), ["xs", "stat8", "gBf"], ["xs"])
                S.dma("sp", lambda e: e.dma_start(out=y_s, in_=xs_tok), reads=["xs"])
        S.finalize()
        S.emit(st)
    return nc


DBG_EXTRA = {"oa": [128, 4, 1040], "oT": [128, 4, 1040], "ob": [128, 4, 1040], "mergedT": [128, 8, 1040], "hT2": [128, 8, 1040]}


def _host_maps(inp, ncores=8):
    f = np.ascontiguousarray

    def a32(v):
        return np.asarray(v, dtype=np.float32)

    def fm(v, ncol):
        return f(a32(v).reshape(ncol, 128).T)

    pfm = np.concatenate([fm(inp['rwkv_mu'][0], 14), fm(inp['rwkv_w0'][0], 4), fm(inp['rwkv_a0'][0], 4), fm(inp['rwkv_k_k'][0], 4),
                          fm(inp['rwkv_k_a'][0], 4), fm(a32(inp['rwkv_r_k'][0]).reshape(-1), 4), fm(inp['rwkv_ln_g'][0], 4),
                          fm(inp['rwkv_ln_b'][0], 4), fm(inp['hgrn_lb'][0], 4), fm(inp['hgrn_lb'][1], 4), fm(inp['hgrn_norm_g'][0], 4)], axis=1)
    shared = dict(w_in=f(a32(inp['w_in'][0])), w2a2=f(np.concatenate([a32(inp['rwkv_w2'][0]), a32(inp['rwkv_a2'][0])], axis=0)),
                  g2=f(a32(inp['rwkv_g2'][0])), w_up_a=f(a32(inp['w_up_a'][0])), w_up_b=f(a32(inp['w_up_b'][0])), w_out=f(a32(inp['w_out'][0])),
                  w_fg=f(a32(inp['w_ffn_gate'][0])), w_fu=f(a32(inp['w_ffn_up'][0])), w_fd=f(a32(inp['w_ffn_down'][0])), pfm=f(pfm),
                  consts=make_consts(), g_mix=f(a32(inp['norm_mix_g'][0])), g_ffn=f(a32(inp['norm_ffn_g'][0])), g_fin=f(a32(inp['norm_final_g'])))
    maps = []
    for c in range(ncores):
        m = dict(shared)
        m['x_p'] = f(a32(inp['x_prompt'][c]))
        m['x_s'] = f(a32(inp['x_sample'][c * NS:(c + 1) * NS, 0]))
        m['wkv_s'] = f(a32(inp['state_rwkv_wkv'][0, c * NS:(c + 1) * NS]).reshape(NS * 8, 4096))
        m['shift_s'] = f(a32(inp['state_rwkv_shift'][0, c * NS:(c + 1) * NS]))
        m['hgrn_s'] = f(a32(inp['state_hgrn'][0, c * NS:(c + 1) * NS]))
        maps.append(m)
    return maps


_NC_CACHE = {}


def kernel(**inputs):
    ncores = 8
    if "nc" not in _NC_CACHE:
        _NC_CACHE["nc"] = build_program()
    nc = _NC_CACHE["nc"]
    maps = _host_maps(inputs, ncores)
    res = run_bass_kernel_spmd(nc, maps, core_ids=list(range(ncores)))
    r = res.results
    y_p = np.stack([r[c]["y_p"] for c in range(ncores)]).astype(np.float32)
    y_s = np.concatenate([r[c]["y_s"] for c in range(ncores)], axis=0).reshape(128, 1, D).astype(np.float32)
    wkv_p = np.stack([r[c]["o_wkv_p"] for c in range(ncores)])[None].astype(np.float32)
    shift_p = np.stack([r[c]["o_shift_p"] for c in range(ncores)])[None].astype(np.float32)
    hgrn_p = np.stack([r[c]["o_hgrn_p"] for c in range(ncores)])[None].astype(np.float32)
    wkv_s = np.concatenate([r[c]["o_wkv_s"].reshape(NS, 8, 64, 64) for c in range(ncores)], axis=0)[None].astype(np.float32)
    shift_s = np.concatenate([r[c]["o_shift_s"] for c in range(ncores)], axis=0)[None].astype(np.float32)
    hgrn_s = np.concatenate([r[c]["o_hgrn_s"] for c in range(ncores)], axis=0)[None].astype(np.float32)
    return (y_p, y_s, wkv_p, shift_p, hgrn_p, wkv_s, shift_s, hgrn_s)
```

```python
from contextlib import ExitStack
import numpy as np
import concourse.bass as bass
import concourse.mybir as mybir
from concourse.bass_utils import run_bass_kernel_spmd

F32 = mybir.dt.float32
BF16 = mybir.dt.bfloat16
AF = mybir.ActivationFunctionType
ALU = mybir.AluOpType
AX = mybir.AxisListType

COMPUTE = ("pe", "act", "dve", "pool")


class Op:
    __slots__ = ("idx", "eng", "fn", "call", "is_dma", "deps", "val", "marked", "slot", "acc", "dur", "xfer", "seq", "mode")

    def __init__(self, idx, eng, fn, call, is_dma):
        self.idx = idx
        self.eng = eng
        self.fn = fn
        self.call = call
        self.is_dma = is_dma
        self.deps = ()
        self.val = None
        self.marked = False
        self.slot = None
        self.acc = None
        self.dur = 100.0
        self.xfer = 0.0
        self.seq = 0
        self.mode = None


class _Rec:
    def __init__(self):
        self.call = None

    def __getattr__(self, name):
        def f(*a, **k):
            self.call = (name, a, k)
            return self
        return f


_WRITE_KW = ("out", "accum_out", "ap")
_ESZ = {}


def _esz(dt):
    if dt not in _ESZ:
        _ESZ[dt] = 4 if dt == F32 else 2
    return _ESZ[dt]


def _ap_intervals(ap):
    sp = str(ap.space)
    esz = _esz(ap.dtype)
    dims = list(ap.ap)
    if sp == "DRAM":
        base = ap.offset
        p0, p1 = 0, 1
        fd = dims
    else:
        pstride, pcnt = dims[0]
        p0 = ap.offset // pstride
        base = ap.offset % pstride
        p1 = p0 + pcnt
        fd = dims[1:]
    fd = sorted([(st_, c) for (st_, c) in fd if c > 1 and st_ != 0])
    run = 1
    k = 0
    while k < len(fd) and fd[k][0] == run:
        run *= fd[k][1]
        k += 1
    outer = fd[k:]
    n_outer = 1
    for st_, c in outer:
        n_outer *= c
    if n_outer <= 32:
        offs = [0]
        for st_, c in outer:
            offs = [o + i * st_ for o in offs for i in range(c)]
        iv = [((base + o) * esz, (base + o + run) * esz) for o in offs]
    else:
        ext = sum((c - 1) * st_ for st_, c in outer) + run
        iv = [(base * esz, (base + ext) * esz)]
    return sp, ap.name, p0, p1, iv


def _call_accesses(call):
    name, a, k = call
    items = []
    if name == "matmul":
        items.append((a[0] if a else k.get("out"), True))
        for kw in ("lhsT", "rhs"):
            items.append((k[kw], False))
    elif name == "memset":
        items.append((a[0] if a else k.get("ap"), True))
    else:
        for kw, v in k.items():
            if hasattr(v, "ap") and hasattr(v, "space"):
                items.append((v, kw in _WRITE_KW))
        for v in a:
            if hasattr(v, "ap") and hasattr(v, "space"):
                items.append((v, False))
    acc = []
    for ap, is_w in items:
        if ap is None:
            continue
        sp, nm, p0, p1, iv = _ap_intervals(ap)
        if sp == "PSUM":
            banks = set()
            for b0, b1 in iv:
                for bk in range(b0 // 2048, (b1 - 1) // 2048 + 1):
                    banks.add(bk)
            for bk in banks:
                acc.append((("ps", bk), 0, 128, 0, 1, is_w))
        else:
            for b0, b1 in iv:
                acc.append(((sp, nm), p0, p1, b0, b1, is_w))
    return acc


def _free_elems(ap):
    n = 1
    for s_ in ap.shape[1:]:
        n *= s_
    return n


_WI = {"fp32": 1.0, "dve": 1.0, "act": 1.0, "pe_small": 1.0, "pe_big": 1.0, "lat": 1.0, "pool": 1.0}


def _est(op):
    _est0(op)
    name = op.call[0]
    if op.is_dma:
        return
    if name == "matmul":
        k = op.call[2]
        if k["lhsT"].dtype == F32:
            op.dur *= _WI["fp32"]
        elif _free_elems(k["rhs"]) >= 512:
            op.dur *= _WI["pe_big"]
        else:
            op.dur *= _WI["pe_small"]
    elif op.eng in ("dve", "act", "pool"):
        op.dur *= _WI[op.eng]


def _est0(op):
    name, a, k = op.call
    if op.is_dma:
        ap = k.get("out")
        nbytes = 1
        for s_ in ap.shape:
            nbytes *= s_
        nbytes *= _esz(ap.dtype)
        op.dur = 1200.0 if op.eng == "pool" else 150.0
        op.xfer = 2000.0 + nbytes / 150.0
        return
    def _rnd(v):
        return 32 if v <= 32 else (64 if v <= 64 else 128)
    if name == "matmul":
        n = _free_elems(k["rhs"])
        f = 4.0 if k["lhsT"].dtype == F32 else 1.0
        op.dur = (16.0 + max(n, 48) * 0.42) * f
        op.mode = (_rnd(k["lhsT"].shape[0]), _rnd(_free_elems(k["lhsT"])), f)
    elif name == "transpose":
        op.dur = 70.0
        op.mode = ("T", _rnd(k["in_"].shape[0]), _rnd(_free_elems(k["in_"])))
    elif op.eng == "act":
        op.dur = 120.0 + _free_elems(k["out"]) * 0.85
        fn_ = k.get("func")
        if fn_ in (AF.Exp, AF.Ln):
            op.mode = "A"
        elif fn_ in (AF.Sigmoid, AF.Tanh):
            op.mode = "B"
        elif fn_ == AF.Silu:
            op.mode = "C"
    else:
        o = k.get("out") if "out" in k else (a[0] if a else None)
        n = _free_elems(o) if o is not None else 64
        if op.eng == "pool":
            op.dur = 100.0 + n * 2.2
        else:
            op.dur = 70.0 + n * 1.17


class Sched:
    def __init__(self, nc, dma_slots=None, reorder=True):
        self.nc = nc
        self.ops = []
        self.dma_slots = dma_slots or {"sp": 8, "act": 2, "pool": 6, "dve": 2, "pe": 2}
        self.reorder = reorder
        self.tag = ""
        self.tags = []

    def _mk(self, eng, fn, is_dma):
        rec = _Rec()
        fn(rec)
        name, a, k = rec.call
        op = Op(len(self.ops), eng, (lambda e: getattr(e, name)(*a, **k)), (name, a, k), is_dma)
        self.ops.append(op)
        self.tags.append(self.tag)
        return op

    def add(self, eng, fn, reads=(), writes=()):
        return self._mk(eng, fn, False)

    def dma(self, eng, fn, reads=(), writes=()):
        return self._mk(eng, fn, True)

    def barrier(self):
        pass

    def _build_dag(self):
        wlog = {}
        rlog = {}
        for op in self.ops:
            acc = _call_accesses(op.call)
            _est(op)
            deps = set()
            for key, p0, p1, b0, b1, is_w in acc:
                for (q0, q1, c0, c1, oi, oe) in wlog.get(key, ()):
                    if q0 < p1 and p0 < q1 and c0 < b1 and b0 < c1:
                        deps.add(oi)
                if is_w or key[0] == "ps":
                    for (q0, q1, c0, c1, oi, oe) in rlog.get(key, ()):
                        if q0 < p1 and p0 < q1 and c0 < b1 and b0 < c1:
                            if is_w or oe != op.eng:
                                deps.add(oi)
            for key, p0, p1, b0, b1, is_w in acc:
                if is_w:
                    for lg in (wlog, rlog):
                        l_ = lg.get(key)
                        if l_:
                            lg[key] = [e_ for e_ in l_ if not (p0 <= e_[0] and e_[1] <= p1 and b0 <= e_[2] and e_[3] <= b1)]
                    wlog.setdefault(key, []).append((p0, p1, b0, b1, op.idx, op.eng))
                else:
                    rlog.setdefault(key, []).append((p0, p1, b0, b1, op.idx, op.eng))
            deps.discard(op.idx)
            op.deps = tuple(sorted(deps))

    def _list_schedule(self):
        import heapq
        ops = self.ops
        n = len(ops)
        if not self.reorder:
            return list(range(n))
        npred = [len(o.deps) for o in ops]
        succ = [[] for _ in range(n)]
        for o in ops:
            for d in o.deps:
                succ[d].append(o.idx)
        ready_t = [0.0] * n
        fin = [0.0] * n
        import os as _os2
        PRI = _os2.environ.get("KPRI", "5")
        prio = list(range(n))
        if PRI != "idx":
            cp = [0.0] * n
            for i_ in range(n - 1, -1, -1):
                m = 0.0
                for s_ in succ[i_]:
                    if cp[s_] > m:
                        m = cp[s_]
                cp[i_] = m + ops[i_].dur + ops[i_].xfer + 200.0
            w_ = float(PRI)
            rank = sorted(range(n), key=lambda i_: (i_ - w_ * cp[i_] / 100.0))
            for r_, i_ in enumerate(rank):
                prio[i_] = r_
            for i_ in range(min(getattr(self, "front", 0), n)):
                prio[i_] = i_ - n
        engs = ("pe", "act", "dve", "pool", "sp")
        future = {e: [] for e in engs}
        avail = {e: [] for e in engs}
        free = {e: 0.0 for e in engs}
        for o in ops:
            if npred[o.idx] == 0:
                heapq.heappush(future[o.eng], (0.0, o.idx))
        order = []
        self.start_t = {}
        pe_mode = [None]
        act_set = [None]
        WINDOW = 3000
        done_upto = 0
        sched = [False] * n
        while len(order) < n:
            best = None
            for e in engs:
                fu, av = future[e], avail[e]
                while fu and fu[0][0] <= free[e]:
                    t_, i_ = heapq.heappop(fu)
                    heapq.heappush(av, (prio[i_], i_))
                if av:
                    pick = av[0]
                    if e == "pe" and len(av) > 1 and ops[pick[1]].mode != pe_mode[0]:
                        best_same = None
                        for i2 in av:
                            if ops[i2[1]].mode == pe_mode[0] and (best_same is None or i2 < best_same):
                                best_same = i2
                        if best_same is not None and best_same[0] - pick[0] < 400:
                            pick = best_same
                    if e == "act" and len(av) > 1 and ops[pick[1]].mode not in (None, act_set[0]):
                        best_same = None
                        for i2 in av:
                            if ops[i2[1]].mode in (None, act_set[0]) and (best_same is None or i2 < best_same):
                                best_same = i2
                        if best_same is not None and best_same[0] - pick[0] < 300:
                            pick = best_same
                    cand = (free[e], pick[0], e, True, pick)
                elif fu:
                    cand = (fu[0][0], prio[fu[0][1]], e, False, fu[0])
                else:
                    continue
                if best is None or cand[:2] < best[:2]:
                    best = cand
            start, _pr, e, from_av, item = best
            if from_av:
                i_ = item[1]
                if avail[e][0] == item:
                    heapq.heappop(avail[e])
                else:
                    avail[e].remove(item)
                    heapq.heapify(avail[e])
            else:
                i_ = item[1]
                heapq.heappop(future[e])
            o = ops[i_]
            if e == "pe":
                if o.mode != pe_mode[0]:
                    start += 150.0
                pe_mode[0] = o.mode
            elif e == "act" and o.mode is not None:
                if o.mode != act_set[0]:
                    start += 1300.0
                act_set[0] = o.mode
            free[e] = start + o.dur
            self.start_t[i_] = start
            fin[i_] = start + o.dur + o.xfer
            order.append(i_)
            for s_ in succ[i_]:
                lat = (40.0 if (ops[s_].eng == e and e == "pe" and not o.is_dma) else (150.0 if ops[s_].eng == e else 400.0)) * _WI["lat"]
                if fin[i_] + lat > ready_t[s_]:
                    ready_t[s_] = fin[i_] + lat
                npred[s_] -= 1
                if npred[s_] == 0:
                    heapq.heappush(future[ops[s_].eng], (ready_t[s_], s_))
        self.est_makespan = max(fin) if fin else 0.0
        return order

    def finalize(self):
        self._build_dag()
        order = self._list_schedule()
        self.order = order
        ops = self.ops
        dma_count = {}
        slot_last = {}
        extra = {}
        seqc = {}
        for i_ in order:
            op = ops[i_]
            seqc[op.eng] = seqc.get(op.eng, 0) + 1
            op.seq = seqc[op.eng]
            if op.is_dma:
                n_ = dma_count.get(op.eng, 0)
                dma_count[op.eng] = n_ + 1
                k = self.dma_slots[op.eng]
                op.slot = (op.eng, n_ % k)
                op.val = 16 * (n_ // k + 1)
                prev = slot_last.get(op.slot)
                if prev is not None:
                    extra[i_] = prev
                slot_last[op.slot] = i_
        known = {}
        vc = {}
        self.waits = {}
        for i_ in order:
            op = ops[i_]
            K = known.setdefault(op.eng, {})
            deps = list(op.deps)
            if i_ in extra:
                deps.append(extra[i_])
            cand = []
            for d in deps:
                p = ops[d]
                if (not p.is_dma) and p.eng == "pe" and op.eng == "pe" and not op.is_dma:
                    continue
                cand.append(d)
            cand.sort(key=lambda d: -ops[d].seq)
            w = []
            for d in cand:
                p = ops[d]
                key = ("dma", p.slot) if p.is_dma else ("eng", p.eng)
                val = p.val if p.is_dma else p.seq
                if K.get(key, -1) >= val:
                    continue
                w.append(d)
                p.marked = True
                for kk, vv in vc[d].items():
                    if K.get(kk, -1) < vv:
                        K[kk] = vv
            self.waits[i_] = w
            v = dict(K)
            if op.is_dma:
                v[("dma", op.slot)] = op.val
            else:
                if v.get(("eng", op.eng), -1) < op.seq and op.eng == "pe":
                    pass
                v[("eng", op.eng)] = max(v.get(("eng", op.eng), -1), op.seq)
            vc[i_] = v
        cnt = {}
        for i_ in order:
            op = ops[i_]
            if not op.is_dma and op.marked:
                cnt[op.eng] = cnt.get(op.eng, 0) + 1
                op.val = cnt[op.eng]
        self.counts = cnt

    def emit(self, stack):
        nc = self.nc
        sems = {}
        for e in COMPUTE:
            sems[("eng", e)] = stack.enter_context(nc.semaphore("s_" + e))
        for e, k in self.dma_slots.items():
            for i in range(k):
                sems[("dma", (e, i))] = stack.enter_context(nc.semaphore("d_%s%d" % (e, i)))
        block = stack.enter_context(nc.Block())
        ops = self.ops
        order = self.order
        waits = self.waits

        def run(engname, eng):
            for i_ in order:
                op = ops[i_]
                if op.eng != engname:
                    continue
                for d in waits[i_]:
                    p = ops[d]
                    s = sems[("dma", p.slot)] if p.is_dma else sems[("eng", p.eng)]
                    eng.wait_ge(s, p.val)
                ins = op.fn(eng)
                if op.is_dma:
                    ins.then_inc(sems[("dma", op.slot)], 16)
                elif op.marked:
                    ins.then_inc(sems[("eng", op.eng)], 1)
            last = {}
            for i_ in order:
                op = ops[i_]
                if op.is_dma and op.eng == engname:
                    last[op.slot] = op.val
            for slot, v in last.items():
                eng.wait_ge(sems[("dma", slot)], v)

        @block.sync
        def _(e):
            run("sp", e)

        @block.tensor
        def _(e):
            run("pe", e)

        @block.scalar
        def _(e):
            run("act", e)

        @block.vector
        def _(e):
            run("dve", e)

        @block.gpsimd
        def _(e):
            run("pool", e)


D = 1024
T = 2048
NS = 16
HALF = 1024
TW = HALF + NS
RW = 512
RPROJ = 1792
HW_ = 512
DFF = 2816
INP = 5888
CDEC = -0.6065306597126334
GN_EPS = 64e-5
RMS_EPS = 1e-6

PF = {}
_c = 0
for _n, _w in (("mu", 14), ("w0", 4), ("a0", 4), ("k_k", 4), ("k_a", 4), ("r_k", 4), ("ln_g", 4), ("ln_b", 4),
               ("lb0", 4), ("lb1", 4), ("hng", 4)):
    PF[_n] = _c
    _c += _w
PF_N = _c
DV = {"omu": 0, "omka": 14, "lb": 18, "omlb": 22}
DV_N = 29

CO = {"ident": 0, "blk64": 128, "ones": 256, "maskNM": 384, "maskL": 512, "maskI": 576, "cmask": 640}
CO_N = 640 + 1024 + 256


def make_consts():
    c = np.zeros((128, CO_N), np.float32)
    p = np.arange(128)
    c[:, 0:128] = np.eye(128, dtype=np.float32)
    c[:, 128:256] = (p[:, None] // 64 == p[None, :] // 64).astype(np.float32)
    c[:, 256:384] = 1.0
    s = (p % 64)[:, None]
    t = np.arange(64)[None, :]
    c[:, 384:448] = (s < t)
    c[:, 448:512] = (s <= t)
    c[:, 512:576] = (s > t)
    c[:, 576:640] = (s == t)
    tt = np.arange(1024)
    c[:, 640:1664] = (tt % 64 != 0).astype(np.float32)[None, :]
    c[:, 1664:1920] = np.eye(16, dtype=np.float32).reshape(-1)[None, :]
    return c


class Arena:
    def __init__(self, tile, nwords):
        self.t = tile
        self.n = nwords
        self.top = 0

    def alloc(self, dtype, shape):
        free = 1
        for s in shape[1:]:
            free *= s
        words = free if dtype == F32 else (free + 1) // 2
        words = (words + 7) // 8 * 8
        off = self.top
        self.top += words
        assert self.top <= self.n, ("arena overflow", self.top, self.n)
        v = self.t[0:shape[0], off:off + words]
        if dtype == BF16:
            v = v.bitcast(BF16)
        v = v[:, 0:free]
        if len(shape) > 2:
            names = ["d%d" % i for i in range(len(shape) - 1)]
            pat = "p (" + " ".join(names) + ") -> p " + " ".join(names)
            kw = {names[i]: shape[i + 1] for i in range(len(names))}
            v = v.rearrange(pat, **kw)
        return v


ARENA_WORDS = 32560


def build_program(dbg=None, passes=(0, 1), stop_after=None):
    nc = bass.Bass("TRN2", target_bir_lowering=False)

    def din(name, shape):
        return nc.dram_tensor(name, list(shape), F32, kind="ExternalInput").ap()

    def dout(name, shape):
        return nc.dram_tensor(name, list(shape), F32, kind="ExternalOutput").ap()

    x_p = din("x_p", [T, D])
    x_s = din("x_s", [NS, D])
    wkv_s = din("wkv_s", [NS * 8, 4096])
    shift_s = din("shift_s", [NS, RPROJ])
    hgrn_s = din("hgrn_s", [NS, 4, 128, 128])
    w_in = din("w_in", [D, INP])
    w2a2 = din("w2a2", [128, 512])
    g2 = din("g2", [128, 512])
    w_up_a = din("w_up_a", [RW, D])
    w_up_b = din("w_up_b", [HW_, D])
    w_out = din("w_out", [D, D])
    w_fg = din("w_fg", [D, DFF])
    w_fu = din("w_fu", [D, DFF])
    w_fd = din("w_fd", [DFF, D])
    pfm_d = din("pfm", [128, PF_N])
    consts_d = din("consts", [128, CO_N])
    g_mix = din("g_mix", [D])
    g_ffn = din("g_ffn", [D])
    g_fin = din("g_fin", [D])

    y_p = dout("y_p", [T, D])
    y_s = dout("y_s", [NS, D])
    o_wkv_p = dout("o_wkv_p", [8, 64, 64])
    o_shift_p = dout("o_shift_p", [RPROJ])
    o_hgrn_p = dout("o_hgrn_p", [4, 128, 128])
    o_wkv_s = dout("o_wkv_s", [NS * 8, 4096])
    o_shift_s = dout("o_shift_s", [NS, RPROJ])
    o_hgrn_s = dout("o_hgrn_s", [NS, 4, 128, 128])
    dbg_out = {}
    if dbg:
        for k, shp in dbg.items():
            dbg_out[k] = dout("dbg_" + k, shp)
    scr_a = nc.dram_tensor("scr_a", [NS, 6, 512], F32).ap()
    scr_y = nc.dram_tensor("scr_y", [NS, 512], F32).ap()

    st = ExitStack()
    with st:
        def sb(name, shape, dt):
            return st.enter_context(nc.sbuf_tensor(name, list(shape), dt))

        arena_t = sb("arena", [128, ARENA_WORDS], F32)
        hT = sb("hT", [128, 8, TW], BF16)
        oa = sb("oa", [128, 4, TW], BF16)
        ob = sb("ob", [128, 4, TW], BF16)
        NWB = 4
        wbuf = [sb("wbuf%d" % i, [128, 8, 512], BF16) for i in range(NWB)]
        consts = sb("consts_sb", [128, 384], F32)
        ident_bf = sb("ident_bf", [128, 128], BF16)
        masks_bf = sb("masks_bf", [128, 256], BF16)
        pfm = sb("pfm_sb", [128, PF_N], F32)
        dv = sb("dv", [128, DV_N], F32)
        lw_w = sb("lw_w", [128, 512], BF16)
        g2_w = sb("g2_w", [128, 512], BF16)
        carry = sb("carry", [128, 14], F32)
        shiftS = sb("shiftS", [128, 14, NS], F32)
        sprevT = sb("sprevT", [128, 14, NS], F32)
        H0f = sb("H0f", [128, 4, 64], F32)
        H0bd = sb("H0bd", [128, 4, 128], BF16)
        S0f = sb("S0f", [128, 4, 128], F32)
        S0b = sb("S0b", [128, 4, 128], BF16)
        WC = sb("WC", [128, 4, 16], F32)
        DC = sb("DC", [128, 4, 16], F32)
        stat = sb("stat", [128, 64], F32)
        ftmp = sb("ftmp", [128, 2, 512], F32)
        ps = st.enter_context(nc.psum_tensor("ps", [128, 8, 512], F32))

        ident = consts[:, 0:128]
        blk64 = consts[:, 128:256]
        ones = consts[:, 256:384]
        maskNM_bf = masks_bf[:, 0:128]
        maskL_bf = masks_bf[:, 128:192]
        maskI_bf = masks_bf[:, 192:256]

        S = Sched(nc)
        A = S.add
        bank_ctr = [0]

        def nb(n=1):
            b = bank_ctr[0]
            if n > 1:
                b = (b + n - 1) // n * n
            if b + n > 8:
                b = 0
            bank_ctr[0] = (b + n) % 8
            return b

        def nbp(par):
            b = bank_ctr[0]
            if b % 2 != par:
                b = (b + 1) % 8
            bank_ctr[0] = (b + 1) % 8
            return b

        par_ctr = [0]
        seq_ctr = [0, 0]

        par_nb = [6]

        def nb_par(n=1):
            m = par_nb[0]
            if n == 2:
                par_ctr[0] = (par_ctr[0] + 1) // 2 * 2
            b = par_ctr[0] % m
            par_ctr[0] = (par_ctr[0] + n) % m
            return b

        smp_ctr = [0]

        def nb_smp():
            smp_ctr[0] += 1
            return 4 + smp_ctr[0] % 2

        def nb_seq(par):
            return 6 + par

        def PB(b, n=1):
            return ["ps%d" % (b + i) for i in range(n)]

        def pf(name, j=0):
            c = PF[name] + j
            return pfm[:, c:c + 1]

        def dvc(name, j=0):
            c = DV[name] + j
            return dv[:, c:c + 1]

        wctr = [0]

        def load_w(src_ap, shape_view):
            i = wctr[0] % NWB
            wctr[0] += 1
            a, b = shape_view
            view = wbuf[i][:, :, :].rearrange("p a b -> p (a b)")[:, 0:a * b].rearrange("p (a b) -> p a b", a=a)
            rn = "wbuf%d" % i
            S.dma("pool", lambda e, view=view, src_ap=src_ap: e.dma_start(out=view, in_=src_ap), writes=[rn])
            return view, rn

        def dump(name, ap_sb, res):
            if dbg and name in dbg_out:
                S.dma("pool", lambda e: e.dma_start(out=dbg_out[name], in_=ap_sb), reads=res)

        S.dma("sp", lambda e: e.dma_start(out=consts[:], in_=consts_d[:, 0:384]), writes=["consts"])
        S.dma("pool", lambda e: e.dma_start(out=masks_bf[:], in_=consts_d[:, 384:640]))
        S.dma("sp", lambda e: e.dma_start(out=pfm[:], in_=pfm_d), writes=["pfm"])
        S.dma("pool", lambda e: e.dma_start(out=lw_w[:], in_=w2a2), writes=["lw_w"])
        S.dma("pool", lambda e: e.dma_start(out=g2_w[:], in_=g2), writes=["g2_w"])
        A("dve", lambda e: e.tensor_copy(out=ident_bf[:], in_=ident), ["consts"], ["ident_bf"])
        A("dve", lambda e: e.memset(carry[:], 0.0), [], ["carry"])
        A("dve", lambda e: e.memset(H0f[:], 0.0), [], ["H0f"])
        A("dve", lambda e: e.memset(H0bd[:], 0.0), [], ["H0b"])
        A("dve", lambda e: e.memset(S0f[:], 0.0), [], ["S0f"])
        A("dve", lambda e: e.memset(S0b[:], 0.0), [], ["S0b"])
        A("dve", lambda e: e.memset(sprevT[:], 0.0), [], ["sprevT"])
        A("dve", lambda e: e.tensor_scalar(out=dv[:, 0:14], in0=pfm[:, PF["mu"]:PF["mu"] + 14], scalar1=-1.0, scalar2=1.0,
                                            op0=ALU.mult, op1=ALU.add), ["pfm"], ["dv"])
        A("dve", lambda e: e.tensor_scalar(out=dv[:, 14:18], in0=pfm[:, PF["k_a"]:PF["k_a"] + 4], scalar1=-1.0, scalar2=1.0,
                                            op0=ALU.mult, op1=ALU.add), ["pfm"], ["dv"])
        A("dve", lambda e: e.tensor_tensor(out=dv[:, 18:22], in0=pfm[:, PF["lb0"]:PF["lb0"] + 4],
                                            in1=pfm[:, PF["lb1"]:PF["lb1"] + 4], op=ALU.subtract), ["pfm"], ["dv"])
        A("act", lambda e: e.activation(out=dv[:, 18:22], in_=dv[:, 18:22], func=AF.Sigmoid), ["dv"], ["dv"])
        A("dve", lambda e: e.tensor_scalar(out=dv[:, 22:26], in0=dv[:, 18:22], scalar1=-1.0, scalar2=1.0,
                                            op0=ALU.mult, op1=ALU.add), ["dv"], ["dv"])

        eps24, epsgn, epsrms = dv[:, 26:27], dv[:, 27:28], dv[:, 28:29]
        A("dve", lambda e: e.memset(dv[:, 26:27], 1e-24))
        A("dve", lambda e: e.memset(dv[:, 27:28], GN_EPS))
        A("dve", lambda e: e.memset(dv[:, 28:29], RMS_EPS))
        S.front = len(S.ops)
        w_in_v = w_in.rearrange("(kc p) n -> p kc n", p=128)

        for pas in passes:
            import os as _os
            _skip = _os.environ.get("KSKIP", "")
            nsamp = NS if (pas == 0 and "nosamp" not in _skip) else 0
            W = HALF + nsamp
            blocks = [(0, 512), (512, 1024)] + ([(1024, 1040)] if nsamp else [])
            t0 = pas * HALF
            S.barrier()
            ar_ = Arena(arena_t, ARENA_WORDS)

            def norm_phase(tag, x_tok, xs_tok, g_dram, arena, out_cb, src_loaded, xsres):
                gB = arena.alloc(F32, [128, 1024])
                junk = arena.alloc(F32, [128, 1024])
                h_tok = arena.alloc(BF16, [128, 8, 1024])
                hs_tok = arena.alloc(BF16, [NS, 1024])
                R = tag
                S.dma("sp", lambda e: e.dma_start(out=gB, in_=g_dram.partition_broadcast(128)), writes=[R + "gB"])
                A("dve", lambda e: e.memset(stat[:, 0:32], 0.0), [], ["stat"])
                for tt in range(8):
                    A("act", lambda e, tt=tt: e.activation(out=junk, in_=x_tok[:, tt, :], func=AF.Square, accum_out=stat[:, tt:tt + 1]))
                    A("dve", lambda e, tt=tt: e.tensor_scalar(out=stat[:, 16 + tt:17 + tt], in0=stat[:, tt:tt + 1], scalar1=1.0 / D, scalar2=RMS_EPS,
                                                              op0=ALU.mult, op1=ALU.add))
                    A("act", lambda e, tt=tt: e.activation(out=stat[:, 16 + tt:17 + tt], in_=stat[:, 16 + tt:17 + tt], func=AF.Ln))
                    A("act", lambda e, tt=tt: e.activation(out=stat[:, 16 + tt:17 + tt], in_=stat[:, 16 + tt:17 + tt], func=AF.Exp, scale=-0.5))
                if nsamp:
                    A("act", lambda e: e.activation(out=junk[0:NS, :], in_=xs_tok, func=AF.Square, accum_out=stat[0:NS, 8:9]))
                    A("dve", lambda e: e.tensor_scalar(out=stat[0:NS, 24:25], in0=stat[0:NS, 8:9], scalar1=1.0 / D, scalar2=RMS_EPS,
                                                        op0=ALU.mult, op1=ALU.add))
                    A("act", lambda e: e.activation(out=stat[0:NS, 24:25], in_=stat[0:NS, 24:25], func=AF.Ln))
                    A("act", lambda e: e.activation(out=stat[0:NS, 24:25], in_=stat[0:NS, 24:25], func=AF.Exp, scale=-0.5))
                out_cb(gB, h_tok, hs_tok, R)

            x_tok = ar_.alloc(F32, [128, 8, 1024])
            xs_tok = ar_.alloc(F32, [NS, 1024])
            for tt in range(8):
                S.dma("sp", lambda e, tt=tt: e.dma_start(out=x_tok[:, tt, :], in_=x_p[t0 + tt * 128:t0 + (tt + 1) * 128, :]),
                      writes=["P0x%d" % tt])
            if nsamp:
                S.dma("sp", lambda e: e.dma_start(out=xs_tok, in_=x_s), writes=["P0xs"])

            def to_hT(gB, h_tok, hs_tok, R, x_tok_, xs_tok_, xres, xsres):
                for tt in range(8):
                    A("dve", lambda e, tt=tt: e.scalar_tensor_tensor(out=h_tok[:, tt, :], in0=x_tok_[:, tt, :],
                                                                     scalar=stat[:, 16 + tt:17 + tt], in1=gB,
                                                                     op0=ALU.mult, op1=ALU.mult),
                      [xres(tt), "stat", R + "gB"], [R + "htok%d" % tt])
                    b = nb()
                    pT = ps[:, b, :].bitcast(BF16).rearrange("p (c t) -> p c t", c=8)
                    for dc in range(8):
                        A("pe", lambda e, tt=tt, dc=dc, pT=pT: e.transpose(out=pT[:, dc, :], in_=h_tok[:, tt, dc * 128:(dc + 1) * 128],
                                                                           identity=ident_bf[:]),
                          [R + "htok%d" % tt, "ident_bf"], PB(b))
                    eng = "act" if tt % 2 == 0 else "dve"
                    if eng == "act":
                        A("act", lambda e, tt=tt, pT=pT: e.copy(out=hT[:, :, tt * 128:(tt + 1) * 128], in_=pT), PB(b), ["hT"])
                    else:
                        A("dve", lambda e, tt=tt, pT=pT: e.tensor_copy(out=hT[:, :, tt * 128:(tt + 1) * 128], in_=pT), PB(b), ["hT"])
                if nsamp:
                    A("dve", lambda e: e.scalar_tensor_tensor(out=hs_tok, in0=xs_tok_, scalar=stat[0:NS, 24:25], in1=gB[0:NS, :],
                                                              op0=ALU.mult, op1=ALU.mult), [xsres, "stat", R + "gB"], [R + "hstok"])
                    b = nb()
                    pT = ps[:, b, :].bitcast(BF16).rearrange("p (c t) -> p c t", c=8)
                    for dc in range(8):
                        A("pe", lambda e, dc=dc, pT=pT: e.transpose(out=pT[:, dc, 0:NS], in_=hs_tok[:, dc * 128:(dc + 1) * 128],
                                                                    identity=ident_bf[0:NS, 0:NS]),
                          [R + "hstok", "ident_bf"], PB(b))
                    A("act", lambda e, pT=pT: e.copy(out=hT[:, :, HALF:HALF + NS], in_=pT[:, :, 0:NS]), PB(b), ["hT"])

            norm_phase("P0", x_tok, xs_tok, g_mix, ar_,
                       lambda gB, h_tok, hs_tok, R: to_hT(gB, h_tok, hs_tok, R, x_tok, xs_tok, lambda tt: "P0x%d" % tt, "P0xs"),
                       lambda tt: "P0x%d" % tt, "P0xs")
            if dbg and "hT" in dbg_out and pas == 0:
                dump("hT", hT[:], ["hT"])
            if stop_after == "P0":
                continue

            def proj_fm(wview, wres, ncol0, kcs, act_tile, act_res, evac):
                for (c0, c1) in blocks:
                    b = nb()
                    for kc in range(kcs):
                        A("pe", lambda e, kc=kc, b=b, c0=c0, c1=c1: e.matmul(ps[:, b, 0:c1 - c0], lhsT=wview[:, kc, ncol0:ncol0 + 128],
                                                                              rhs=act_tile[:, kc, c0:c1], start=(kc == 0),
                                                                              stop=(kc == kcs - 1)),
                          [wres, act_res], PB(b))
                    evac(b, c0, c1)

            S.tag = "p%d Rprep" % pas
            S.barrier()
            ar_ = Arena(arena_t, ARENA_WORDS)
            sig = ar_.alloc(F32, [128, 4, TW])
            g_bf = ar_.alloc(BF16, [128, 4, TW])
            ar_t = ar_.alloc(BF16, [128, 4, 16, 2, 64])
            bT = ar_.alloc(BF16, [128, 4, HALF])
            kT = ar_.alloc(BF16, [128, 4, HALF])
            vT = ar_.alloc(BF16, [128, 4, TW])
            bonus = ar_.alloc(BF16, [128, 4, TW])
            smp = ar_.alloc(F32, [128, 6, 4, NS])
            mark = ar_.top
            a_bf = ar_.alloc(BF16, [128, 4, TW])
            kpr = ar_.alloc(BF16, [128, 4, TW])
            praw = [ar_.alloc(F32, [128, 1048]) for _ in range(1)]
            NT = 7
            tmp = [ar_.alloc(F32, [128, TW]) for _ in range(NT)]
            cmask = ar_.alloc(BF16, [128, HALF])
            lwb = ar_.alloc(BF16, [128, TW])
            import os as _os
            _skip = _os.environ.get("KSKIP", "")
            if "cmask" not in _skip:
                S.dma("pool", lambda e: e.dma_start(out=cmask, in_=consts_d[:, 640:1664]), writes=["cmask"])

            if nsamp and "shiftT" not in _skip:
                sh_tok = tmp[0][0:NS, :]
                sh_tok2 = tmp[1][0:NS, :]
                S.dma("sp", lambda e: e.dma_start(out=sh_tok[:, 0:1024], in_=shift_s[:, 0:1024]), writes=["tmp0"])
                S.dma("sp", lambda e: e.dma_start(out=sh_tok2[:, 0:768], in_=shift_s[:, 1024:1792]), writes=["tmp1"])
                b = nb()
                for c in range(14):
                    src = sh_tok[:, c * 128:(c + 1) * 128] if c < 8 else sh_tok2[:, (c - 8) * 128:(c - 7) * 128]
                    A("pe", lambda e, c=c, src=src, b=b: e.transpose(out=ps[:, b, c * NS:(c + 1) * NS], in_=src, identity=ident[0:NS, 0:NS]),
                      ["tmp0", "tmp1", "consts"], PB(b))
                A("dve", lambda e, b=b: e.tensor_copy(out=sprevT[:, :, :], in_=ps[:, b, 0:14 * NS].rearrange("p (c n) -> p c n", c=14)),
                  PB(b), ["sprevT"])

            if stop_after == "shiftT":
                continue
            hv = [(0, 512), (512, W)]
            pctr = [0]

            def rwkv_chunk(c, wview, wres, ncol0, out_ap=None):
                pr = praw[0]
                t1 = tmp[6]
                A("dve", lambda e: e.tensor_copy(out=pr[:, 0:1], in_=carry[:, c:c + 1]))

                def evac(b, c0, c1):
                    A("act", lambda e: e.copy(out=pr[:, 1 + c0:1 + c1], in_=ps[:, b, 0:c1 - c0]))
                    A("act", lambda e: e.activation(out=t1[:, c0:c1], in_=ps[:, b, 0:c1 - c0], func=AF.Copy, scale=dvc("omu", c)))
                proj_fm(wview, wres, ncol0, 8, hT, "hT", evac)
                A("act", lambda e: e.copy(out=carry[:, c:c + 1], in_=pr[:, HALF:HALF + 1]))
                if nsamp:
                    A("act", lambda e: e.copy(out=shiftS[:, c, :], in_=pr[:, 1 + HALF:1 + HALF + NS]))
                out_ap_ = tmp[0] if out_ap is None else out_ap
                for (a_, b_) in hv:
                    b2 = min(b_, HALF)
                    A("dve", lambda e, a_=a_, b2=b2: e.scalar_tensor_tensor(out=out_ap_[:, a_:b2], in0=pr[:, a_:b2], scalar=pf("mu", c),
                                                                            in1=t1[:, a_:b2], op0=ALU.mult, op1=ALU.add))
                if nsamp:
                    A("dve", lambda e: e.scalar_tensor_tensor(out=out_ap_[:, HALF:HALF + NS], in0=sprevT[:, c, :], scalar=pf("mu", c),
                                                              in1=t1[:, HALF:HALF + NS], op0=ALU.mult, op1=ALU.add))
                return out_ap_

            wv, wr = load_w(w_in_v[:, :, 1536:1792], (8, 256))
            psm = rwkv_chunk(12, wv, wr, 0)
            for (a_, b_) in hv:
                A("act", lambda e, a_=a_, b_=b_: e.activation(out=lwb[0:64, a_:b_], in_=psm[0:64, a_:b_], func=AF.Tanh))
                A("act", lambda e, a_=a_, b_=b_: e.copy(out=lwb[64:128, a_:b_], in_=psm[64:128, a_:b_]))
            for j in range(4):
                for (c0, c1) in blocks:
                    b2_ = nb(2)
                    b = b2_
                    A("pe", lambda e, j=j, b=b, c0=c0, c1=c1: e.matmul(ps[:, b, 0:c1 - c0], lhsT=lw_w[0:64, j * 128:(j + 1) * 128],
                                                                        rhs=lwb[0:64, c0:c1], start=True, stop=True))
                    A("act", lambda e, j=j, b=b, c0=c0, c1=c1: e.activation(out=sig[:, j, c0:c1], in_=ps[:, b, 0:c1 - c0], func=AF.Sigmoid,
                                                                             bias=pf("w0", j)))
                    b = b2_ + 1
                    A("pe", lambda e, j=j, b=b, c0=c0, c1=c1: e.matmul(ps[:, b, 0:c1 - c0], lhsT=lw_w[64:128, j * 128:(j + 1) * 128],
                                                                        rhs=lwb[64:128, c0:c1], start=True, stop=True))
                    A("act", lambda e, j=j, b=b, c0=c0, c1=c1: e.activation(out=a_bf[:, j, c0:c1], in_=ps[:, b, 0:c1 - c0], func=AF.Sigmoid,
                                                                             bias=pf("a0", j)))
            psm = rwkv_chunk(13, wv, wr, 128)
            for (a_, b_) in hv:
                A("act", lambda e, a_=a_, b_=b_: e.activation(out=lwb[:, a_:b_], in_=psm[:, a_:b_], func=AF.Sigmoid))
            for j in range(4):
                for (c0, c1) in blocks:
                    b = nb()
                    A("pe", lambda e, j=j, b=b, c0=c0, c1=c1: e.matmul(ps[:, b, 0:c1 - c0], lhsT=g2_w[:, j * 128:(j + 1) * 128],
                                                                        rhs=lwb[:, c0:c1], start=True, stop=True))
                    A("dve", lambda e, j=j, b=b, c0=c0, c1=c1: e.tensor_copy(out=g_bf[:, j, c0:c1], in_=ps[:, b, 0:c1 - c0]))
            if dbg and pas == 0:
                dump("sig", sig, ["sig"])

            wv, wr = load_w(w_in_v[:, :, 1024:1536], (8, 512))
            for j in range(4):
                rwkv_chunk(8 + j, wv, wr, j * 128, out_ap=vT[:, j, :])
                if nsamp:
                    A("act", lambda e, j=j: e.copy(out=smp[:, 3, j, :], in_=vT[:, j, HALF:HALF + NS]))

            def fp32_blocksum(src_ap, src_res, mat, evac):
                for (c0, c1) in blocks:
                    b = nb()
                    A("pe", lambda e, b=b, c0=c0, c1=c1: e.matmul(ps[:, b, 0:c1 - c0], lhsT=mat, rhs=src_ap[:, c0:c1], start=True, stop=True))
                    evac(b, c0, c1)

            def cumsum_decay(j, cs_i):
                cs = tmp[cs_i]
                for (a_, b_) in hv:
                    b2 = min(b_, HALF)
                    A("dve", lambda e, a_=a_, b2=b2: e.tensor_tensor_scan(out=cs[:, a_:b2], data0=cmask[:, a_:b2], data1=sig[:, j, a_:b2], initial=0.0,
                                                                          op0=ALU.mult, op1=ALU.add))
                if nsamp:
                    A("dve", lambda e: e.tensor_copy(out=cs[:, HALF:HALF + NS], in_=sig[:, j, HALF:HALF + NS]))
                return cs

            def cview(ap2, a_, b2):
                return ap2[:, a_:b2].rearrange("p (c t) -> p c t", t=64)

            wv, wr = load_w(w_in_v[:, :, 512:1024], (8, 512))
            for j in range(4):
                k_ap = rwkv_chunk(4 + j, wv, wr, j * 128)
                kkr, rs, cs, en, de, ka = tmp[1], tmp[2], tmp[3], tmp[4], tmp[5], tmp[2]
                for (a_, b_) in hv:
                    A("act", lambda e, j=j, a_=a_, b_=b_: e.activation(out=kkr[:, a_:b_], in_=k_ap[:, a_:b_], func=AF.Copy, scale=pf("k_k", j)))
                    A("act", lambda e, a_=a_, b_=b_: e.activation(out=rs[:, a_:b_], in_=kkr[:, a_:b_], func=AF.Square))

                def ev_ss(b, c0, c1):
                    A("act", lambda e: e.activation(out=de[:, c0:c1], in_=ps[:, b, 0:c1 - c0], func=AF.Ln, bias=eps24[:, 0:1]))
                fp32_blocksum(rs, "tmp2", blk64, ev_ss)
                cumsum_decay(j, 3)
                for (a_, b_) in hv:
                    b2 = min(b_, HALF)
                    c_lo, c_hi = a_ // 64, b2 // 64
                    A("act", lambda e, a_=a_, b_=b_: e.activation(out=de[:, a_:b_], in_=de[:, a_:b_], func=AF.Exp, scale=-0.5))
                    A("dve", lambda e, a_=a_, b_=b_: e.tensor_tensor(out=kkr[:, a_:b_], in0=kkr[:, a_:b_], in1=de[:, a_:b_], op=ALU.mult))
                    A("act", lambda e, j=j, a_=a_, b2=b2, c_lo=c_lo, c_hi=c_hi: e.activation(out=WC[:, j, c_lo:c_hi], in_=cview(cs, a_, b2)[:, :, 63],
                                                                                             func=AF.Exp, scale=CDEC))
                    A("act", lambda e, a_=a_, b2=b2: e.activation(out=en[:, a_:b2], in_=cs[:, a_:b2], func=AF.Exp, scale=-CDEC))
                    if nsamp and b_ > HALF:
                        A("act", lambda e, j=j: e.activation(out=smp[:, 1, j, :], in_=cs[:, HALF:HALF + NS], func=AF.Exp, scale=CDEC))
                        A("act", lambda e, j=j: e.activation(out=smp[:, 4, j, :], in_=kkr[:, HALF:HALF + NS], func=AF.Copy, scale=-1.0))
                    A("dve", lambda e, j=j, a_=a_, b2=b2: e.tensor_tensor(out=de[:, a_:b2], in0=cs[:, a_:b2], in1=sig[:, j, a_:b2], op=ALU.subtract))
                    A("act", lambda e, a_=a_, b2=b2: e.activation(out=de[:, a_:b2], in_=de[:, a_:b2], func=AF.Exp, scale=CDEC))
                    A("dve", lambda e, j=j, a_=a_, b2=b2, c_lo=c_lo, c_hi=c_hi: e.scalar_tensor_tensor(
                        out=ar_t[:, j, c_lo:c_hi, 0, :], in0=cview(kkr, a_, b2), scalar=-1.0, in1=cview(de, a_, b2), op0=ALU.mult, op1=ALU.mult))
                    A("dve", lambda e, j=j, a_=a_, b_=b_: e.tensor_tensor(out=ka[:, a_:b_], in0=kkr[:, a_:b_], in1=a_bf[:, j, a_:b_], op=ALU.mult))
                    if nsamp and b_ > HALF:
                        A("act", lambda e, j=j: e.copy(out=smp[:, 5, j, :], in_=ka[:, HALF:HALF + NS]))
                    A("dve", lambda e, j=j, a_=a_, b2=b2: e.tensor_tensor(out=bT[:, j, a_:b2], in0=ka[:, a_:b2], in1=en[:, a_:b2], op=ALU.mult))
                    A("dve", lambda e, j=j, a_=a_, b_=b_: e.tensor_scalar(out=kkr[:, a_:b_], in0=a_bf[:, j, a_:b_], scalar1=pf("k_a", j),
                                                                          scalar2=dvc("omka", j), op0=ALU.mult, op1=ALU.add))
                    A("dve", lambda e, a_=a_, b_=b_: e.tensor_tensor(out=kkr[:, a_:b_], in0=kkr[:, a_:b_], in1=k_ap[:, a_:b_], op=ALU.mult))
                    A("dve", lambda e, j=j, a_=a_, b2=b2: e.tensor_tensor(out=kT[:, j, a_:b2], in0=kkr[:, a_:b2], in1=en[:, a_:b2], op=ALU.mult))
                    A("act", lambda e, j=j, a_=a_, b_=b_: e.activation(out=kpr[:, j, a_:b_], in_=kkr[:, a_:b_], func=AF.Copy, scale=pf("r_k", j)))
                    if nsamp and b_ > HALF:
                        A("act", lambda e, j=j: e.copy(out=smp[:, 2, j, :], in_=kkr[:, HALF:HALF + NS]))

            wv, wr = load_w(w_in_v[:, :, 0:512], (8, 512))
            for j in range(4):
                r_ap = rwkv_chunk(j, wv, wr, j * 128)
                cs = cumsum_decay(j, 3)
                rk = tmp[1]
                for (a_, b_) in hv:
                    b2 = min(b_, HALF)
                    c_lo, c_hi = a_ // 64, b2 // 64
                    A("act", lambda e, a_=a_, b2=b2: e.activation(out=cs[:, a_:b2], in_=cs[:, a_:b2], func=AF.Exp, scale=CDEC))
                    A("dve", lambda e, j=j, a_=a_, b2=b2, c_lo=c_lo, c_hi=c_hi: e.tensor_tensor(out=ar_t[:, j, c_lo:c_hi, 1, :], in0=cview(r_ap, a_, b2),
                                                                                                 in1=cview(cs, a_, b2), op=ALU.mult))
                    A("dve", lambda e, j=j, a_=a_, b_=b_: e.tensor_tensor(out=rk[:, a_:b_], in0=r_ap[:, a_:b_], in1=kpr[:, j, a_:b_], op=ALU.mult))
                if nsamp:
                    A("act", lambda e, j=j: e.copy(out=smp[:, 0, j, :], in_=r_ap[:, HALF:HALF + NS]))

                def ev_bon(b, c0, c1, j=j):
                    A("dve", lambda e: e.tensor_tensor(out=bonus[:, j, c0:c1], in0=ps[:, b, 0:c1 - c0], in1=vT[:, j, c0:c1], op=ALU.mult))
                fp32_blocksum(rk, "tmp1", blk64, ev_bon)
            if dbg and pas == 0:
                dump("ar", ar_t, ["ar"])
                dump("bT", bT, ["bT"])
                dump("kT", kT, ["kT"])
                dump("vT", vT, ["vT"])
                dump("bonus", bonus, ["bonus"])
            if stop_after == "Rprep":
                continue

            yT = sig
            if nsamp:
                ar_.top = mark
                L1v = ar_.alloc(F32, [128, 6, 64])
                saL = ar_.alloc(F32, [128, 64])
                yL1 = ar_.alloc(F32, [128, 64])
                rs_mark = ar_.top
                wf = [wbuf[i_][:, :, :].rearrange("p a b -> p (a b)").bitcast(F32) for i_ in range(4)]
                S_h = [wf[0].rearrange("p (v k) -> p v k", k=64), wf[1].rearrange("p (v k) -> p v k", k=64)]
                T_h = [wf[2].rearrange("p (v k) -> p v k", k=64), wf[3].rearrange("p (v k) -> p v k", k=64)]
                tok6h = [wf[2][0:NS, 0:1536].rearrange("p (v n) -> p v n", v=3), wf[3][0:NS, 0:1536].rearrange("p (v n) -> p v n", v=3)]
                ytok = wf[2][0:NS, 1536:2048]
                shtok = wf[3][0:NS, 0:2048]
                def emit_rsample():
                    for hf in range(2):
                        S.dma("sp", lambda e, hf=hf: e.dma_start(out=S_h[hf].rearrange("p v k -> p (v k)"), in_=wkv_s[:, hf * 2048:(hf + 1) * 2048]))
                    for vec in range(6):
                        b = nb_par()
                        for j in range(4):
                            A("pe", lambda e, vec=vec, j=j, b=b: e.transpose(out=ps[0:NS, b, j * 128:(j + 1) * 128], in_=smp[:, vec, j, :], identity=ident))
                        dst = tok6h[vec // 3][:, vec % 3, :]
                        if vec % 2 == 0:
                            A("act", lambda e, dst=dst, b=b: e.copy(out=dst, in_=ps[0:NS, b, :]))
                        else:
                            A("dve", lambda e, dst=dst, b=b: e.tensor_copy(out=dst, in_=ps[0:NS, b, :]))
                    for hf in range(2):
                        S.dma("sp", lambda e, hf=hf: e.dma_start(out=scr_a[:, hf * 3:(hf + 1) * 3, :], in_=tok6h[hf]))
                    for vec in range(6):
                        S.dma("sp", lambda e, vec=vec: e.dma_start(out=L1v[:, vec, :], in_=scr_a[:, vec, :].rearrange("b (h n) -> b h n", h=8)))

                    def bc_v(vec):
                        return L1v[:, vec, :].unsqueeze(1).broadcast_to([128, 32, 64])

                    def bc_k(ap2, hf):
                        return ap2[:, hf * 32:(hf + 1) * 32].unsqueeze(2).broadcast_to([128, 32, 64])
                    for hf in range(2):
                        Sx, Tx = S_h[hf], T_h[hf]
                        vs = slice(hf * 32, (hf + 1) * 32)
                        A("dve", lambda e, Sx=Sx, Tx=Tx: e.tensor_tensor(out=Tx, in0=Sx, in1=bc_v(4), op=ALU.mult))
                        A("dve", lambda e, Tx=Tx, vs=vs: e.tensor_reduce(out=saL[:, vs], in_=Tx, axis=AX.X, op=ALU.add))
                        A("dve", lambda e, Sx=Sx: e.tensor_tensor(out=Sx, in0=Sx, in1=bc_v(1), op=ALU.mult))
                        A("dve", lambda e, Tx=Tx, hf=hf: e.tensor_tensor(out=Tx, in0=bc_k(saL, hf), in1=bc_v(5), op=ALU.mult))
                        A("dve", lambda e, Sx=Sx, Tx=Tx: e.tensor_tensor(out=Sx, in0=Sx, in1=Tx, op=ALU.add))
                        A("dve", lambda e, Tx=Tx, hf=hf: e.tensor_tensor(out=Tx, in0=bc_k(L1v[:, 3, :], hf), in1=bc_v(2), op=ALU.mult))
                        A("dve", lambda e, Sx=Sx, Tx=Tx: e.tensor_tensor(out=Sx, in0=Sx, in1=Tx, op=ALU.add))
                        S.dma("sp", lambda e, Sx=Sx, hf=hf: e.dma_start(out=o_wkv_s[:, hf * 2048:(hf + 1) * 2048], in_=Sx.rearrange("p v k -> p (v k)")))
                        A("dve", lambda e, Sx=Sx, Tx=Tx: e.tensor_tensor(out=Tx, in0=Sx, in1=bc_v(0), op=ALU.mult))
                        A("dve", lambda e, Tx=Tx, vs=vs: e.tensor_reduce(out=yL1[:, vs], in_=Tx, axis=AX.X, op=ALU.add))
                    S.dma("sp", lambda e: e.dma_start(out=scr_y.rearrange("b (h n) -> (b h) n", h=8), in_=yL1))
                    S.dma("sp", lambda e: e.dma_start(out=ytok, in_=scr_y))
                    b = nb_par()
                    for j in range(4):
                        A("pe", lambda e, j=j, b=b: e.transpose(out=ps[:, b, j * NS:(j + 1) * NS], in_=ytok[:, j * 128:(j + 1) * 128],
                                                                identity=ident[0:NS, 0:NS]))
                    A("act", lambda e, b=b: e.copy(out=yT[:, :, HALF:HALF + NS], in_=ps[:, b, 0:4 * NS].rearrange("p (j n) -> p j n", j=4)))
                    for g_ in range(4):
                        cs_ = list(range(g_ * 4, min(14, g_ * 4 + 4)))
                        b = nb_par()
                        for ci, c in enumerate(cs_):
                            A("pe", lambda e, ci=ci, c=c, b=b: e.transpose(out=ps[0:NS, b, ci * 128:(ci + 1) * 128], in_=shiftS[:, c, :], identity=ident))
                        n_ = len(cs_) * 128
                        A("act", lambda e, g_=g_, b=b, n_=n_: e.copy(out=shtok[:, g_ * 512:g_ * 512 + n_], in_=ps[0:NS, b, 0:n_]))
                    S.dma("sp", lambda e: e.dma_start(out=o_shift_s, in_=shtok[:, 0:RPROJ]))

            if stop_after == "Rsample":
                continue
            ar_.top = rs_mark if nsamp else mark
            bk_tok = [ar_.alloc(BF16, [128, 2, 512]) for _ in range(2)]
            v_tok = [ar_.alloc(BF16, [128, 512]) for _ in range(2)]
            NM_sb = [ar_.alloc(BF16, [128, 8, 2, 128]) for _ in range(2)]
            P_sb2 = [[ar_.alloc(BF16, [128, 8, 64]) for _ in range(2)] for _ in range(2)]
            Tt_sb2 = [[ar_.alloc(BF16, [128, 8, 64]) for _ in range(1)] for _ in range(2)]
            QT_sb2 = [[ar_.alloc(BF16, [128, 8, 2, 64]) for _ in range(2)] for _ in range(2)]
            X_sb = [ar_.alloc(BF16, [128, 8, 64]) for _ in range(2)]
            U_sb = [ar_.alloc(BF16, [128, 8, 64]) for _ in range(2)]
            XV_sb = [ar_.alloc(F32, [128, 8, 64]) for _ in range(2)]
            Hs = ar_.alloc(F32, [128, 4, 64])
            gtmp = [ar_.alloc(F32, [128, TW]) for _ in range(3)]

            for i in range(8):
                q = i % 2
                P_sb, QT_sb, Tt_sb = P_sb2[q], QT_sb2[q], Tt_sb2[q]
                S.tag = "p%d Rpar%d" % (pas, i)
                b1 = nb_par()
                b2 = nb_par()
                pT1 = ps[:, b1, :].bitcast(BF16).rearrange("p (v n) -> p v n", v=2)
                pT2 = ps[:, b2, :].bitcast(BF16)
                for j in range(4):
                    A("pe", lambda e, i=i, j=j, pT1=pT1: e.transpose(out=pT1[:, 0, j * 128:(j + 1) * 128], in_=bT[:, j, i * 128:(i + 1) * 128],
                                                                     identity=ident_bf[:]), ["bT", "ident_bf"], PB(b1))
                    A("pe", lambda e, i=i, j=j, pT1=pT1: e.transpose(out=pT1[:, 1, j * 128:(j + 1) * 128], in_=kT[:, j, i * 128:(i + 1) * 128],
                                                                     identity=ident_bf[:]), ["kT", "ident_bf"], PB(b1))
                    A("pe", lambda e, i=i, j=j, pT2=pT2: e.transpose(out=pT2[:, j * 128:(j + 1) * 128], in_=vT[:, j, i * 128:(i + 1) * 128],
                                                                     identity=ident_bf[:]), ["vT", "ident_bf"], PB(b2))
                A("act", lambda e, q=q, pT1=pT1: e.copy(out=bk_tok[q][:], in_=pT1), PB(b1), ["bk_tok%d" % q])
                A("dve", lambda e, q=q, pT2=pT2: e.tensor_copy(out=v_tok[q][:], in_=pT2[:, 0:512]), PB(b2), ["v_tok%d" % q])
                if stop_after == "c1":
                    break
                for hg in range(2):
                    b = nb_par(2)
                    for hh4 in range(4):
                        h = hg * 4 + hh4
                        j, hh = h // 2, h % 2
                        for e_ in range(2):
                            c = 2 * i + e_
                            for x, src in ((0, bT), (1, kT)):
                                bb, oo = b + hh, ((hh4 // 2) * 2 + x) * 128
                                A("pe", lambda e, src=src, j=j, hh=hh, c=c, e_=e_, bb=bb, oo=oo: e.matmul(
                                    ps[e_ * 64:(e_ + 1) * 64, bb, oo:oo + 128], lhsT=src[hh * 64:(hh + 1) * 64, j, c * 64:(c + 1) * 64],
                                    rhs=ar_t[hh * 64:(hh + 1) * 64, j, c, :, :], start=True, stop=True),
                                  ["bT", "kT", "ar"], PB(b, 2))
                    for hh in range(2):
                        nmv = NM_sb[q][:, hg * 4 + hh:(hg + 1) * 4:2, :, :]
                        A("dve", lambda e, nmv=nmv, b=b, hh=hh: e.tensor_tensor(out=nmv, in0=ps[:, b + hh, :].rearrange("p (jj x n) -> p jj x n", jj=2, x=2),
                                                                                 in1=maskNM_bf.unsqueeze(1).unsqueeze(1).broadcast_to([128, 2, 2, 128]), op=ALU.mult))
                if stop_after == "c2":
                    break
                b = nb_par(2)
                for h in range(8):
                    j, hh = h // 2, h % 2
                    for e_ in range(2):
                        c = 2 * i + e_
                        A("pe", lambda e, j=j, hh=hh, c=c, e_=e_, h=h, b=b: e.matmul(
                            ps[e_ * 64:(e_ + 1) * 64, b + hh, j * 64:(j + 1) * 64], lhsT=ar_t[hh * 64:(hh + 1) * 64, j, c, 0, :],
                            rhs=bT[hh * 64:(hh + 1) * 64, j, c * 64:(c + 1) * 64], start=True, stop=True))
                for hh in range(2):
                    pv = P_sb[0][:, hh:8:2, :]
                    A("dve", lambda e, pv=pv, b=b, hh=hh: e.tensor_tensor(out=pv, in0=ps[:, b + hh, 0:256].rearrange("p (j s) -> p j s", j=4),
                                                                           in1=maskL_bf.unsqueeze(1).broadcast_to([128, 4, 64]), op=ALU.mult))
                if stop_after == "c3":
                    break
                Q0 = NM_sb[q][:, :, 0, 0:64]
                A("dve", lambda e, q=q: e.tensor_tensor(out=QT_sb[1][:, :, 1, :], in0=NM_sb[q][:, :, 0, 0:64],
                                                          in1=maskI_bf.unsqueeze(1).broadcast_to([128, 8, 64]), op=ALU.add))
                ev_ctr = [0]
                import os as _os3
                EVK = int(_os3.environ.get("EVK", "3"))

                def evac_half(bank, e_, dst_ap, shape_pat, **kw):
                    sl = slice(e_ * 64, (e_ + 1) * 64)
                    src = ps[sl, bank, :].rearrange(shape_pat, **kw)
                    ev_ctr[0] += 1
                    if ev_ctr[0] % EVK != 0:
                        A("act", lambda e: e.copy(out=dst_ap, in_=src))
                    else:
                        A("dve", lambda e: e.tensor_copy(out=dst_ap, in_=src))

                def evac2(bk, dst):
                    for e_ in range(2):
                        sl = slice(e_ * 64, (e_ + 1) * 64)
                        evac_half(bk + e_, e_, dst[sl, :, :], "p (h s) -> p h s", h=8)

                bA = nb_par(2)
                bB = nb_par(2)
                for h in range(8):
                    for e_ in range(2):
                        sl = slice(e_ * 64, (e_ + 1) * 64)
                        A("pe", lambda e, h=h, sl=sl, e_=e_, bA=bA: e.matmul(ps[sl, bA + e_, h * 64:(h + 1) * 64], lhsT=Q0[sl, h, :], rhs=P_sb[0][sl, h, :],
                                                                             start=True, stop=True))
                        A("pe", lambda e, h=h, sl=sl, e_=e_, bB=bB: e.matmul(ps[sl, bB + e_, h * 64:(h + 1) * 64], lhsT=P_sb[0][sl, h, :], rhs=Q0[sl, h, :],
                                                                             start=True, stop=True))
                evac2(bA, P_sb[1])
                for e_ in range(2):
                    sl = slice(e_ * 64, (e_ + 1) * 64)
                    evac_half(bB + e_, e_, QT_sb[1][sl, :, 0, :], "p (h s) -> p h s", h=8)
                Tc = None
                for lev in range(1, 6):
                    pi = lev % 2
                    Pc = P_sb[pi]
                    QTc = QT_sb[pi]
                    QTn = QT_sb[1 - pi]
                    last = (lev == 5)
                    if not last:
                        bA = nb_par(2)
                        for h in range(8):
                            for e_ in range(2):
                                sl = slice(e_ * 64, (e_ + 1) * 64)
                                A("pe", lambda e, h=h, sl=sl, e_=e_, bA=bA, QTc=QTc, Pc=Pc: e.matmul(ps[sl, bA + e_, h * 64:(h + 1) * 64], lhsT=QTc[sl, h, 0, :],
                                                                                                      rhs=Pc[sl, h, :], start=True, stop=True))
                    if not last:
                        for hg in range(2):
                            bB = nb_par(2)
                            for h4 in range(4):
                                h = hg * 4 + h4
                                for e_ in range(2):
                                    sl = slice(e_ * 64, (e_ + 1) * 64)
                                    A("pe", lambda e, h=h, h4=h4, sl=sl, e_=e_, bB=bB, QTc=QTc, Pc=Pc: e.matmul(
                                        ps[sl, bB + e_, h4 * 128:(h4 + 1) * 128], lhsT=Pc[sl, h, :], rhs=QTc[sl, h, :, :], start=True, stop=True))
                            for e_ in range(2):
                                sl = slice(e_ * 64, (e_ + 1) * 64)
                                src4 = ps[sl, bB + e_, :].rearrange("p (h x s) -> p h x s", h=4, x=2)
                                hsl = slice(hg * 4, (hg + 1) * 4)
                                A("act", lambda e, sl=sl, src4=src4, hsl=hsl, QTn=QTn: e.copy(out=QTn[sl, hsl, 0, :], in_=src4[:, :, 0, :]))
                                A("dve", lambda e, sl=sl, src4=src4, hsl=hsl, QTn=QTn, QTc=QTc: e.tensor_tensor(out=QTn[sl, hsl, 1, :], in0=src4[:, :, 1, :],
                                                                                                                 in1=QTc[sl, hsl, 1, :], op=ALU.add))
                        evac2(bA, P_sb[1 - pi])
                    else:
                        bB = nb_par(2)
                        for h in range(8):
                            for e_ in range(2):
                                sl = slice(e_ * 64, (e_ + 1) * 64)
                                A("pe", lambda e, h=h, sl=sl, e_=e_, bB=bB, QTc=QTc, Pc=Pc: e.matmul(
                                    ps[sl, bB + e_, h * 64:(h + 1) * 64], lhsT=Pc[sl, h, :], rhs=QTc[sl, h, 1, :], start=True, stop=True))
                        Tc = Tt_sb[0]
                        for e_ in range(2):
                            sl = slice(e_ * 64, (e_ + 1) * 64)
                            A("dve", lambda e, sl=sl, e_=e_, bB=bB, QTc=QTc, Tc=Tc: e.tensor_tensor(out=Tc[sl, :, :], in0=ps[sl, bB + e_, :].rearrange("p (h s) -> p h s", h=8),
                                                                                                     in1=QTc[sl, :, 1, :], op=ALU.add))
                bXV = nb_par(2)
                for h in range(8):
                    for e_ in range(2):
                        sl = slice(e_ * 64, (e_ + 1) * 64)
                        A("pe", lambda e, h=h, sl=sl, e_=e_, q=q, bXV=bXV: e.matmul(ps[sl, bXV + e_, h * 64:(h + 1) * 64], lhsT=NM_sb[q][sl, h, 1, 0:64],
                                                                                    rhs=v_tok[q][sl, h * 64:(h + 1) * 64], start=True, stop=True))
                evac2(bXV, XV_sb[q])
                if stop_after == "c4":
                    break
                S.tag = "p%d Rseq%d" % (pas, i)
                for e_ in range(2):
                    c = 2 * i + e_
                    sl = slice(e_ * 64, (e_ + 1) * 64)
                    xq = c % 2
                    bX = nb_seq(e_)
                    for j in range(4):
                        A("pe", lambda e, j=j, sl=sl, c=c, bX=bX: e.matmul(ps[sl, bX, j * 128:(j + 1) * 128], lhsT=ar_t[:, j, c, 0, :],
                                                                           rhs=H0bd[:, j, :], start=True, stop=True))
                    A("dve", lambda e, sl=sl, xq=xq, bX=bX, q=q: e.tensor_tensor(out=X_sb[xq][sl, :, :], in0=ps[sl, bX, :].rearrange("p (h s) -> p h s", h=8),
                                                                                  in1=XV_sb[q][sl, :, :], op=ALU.add))
                    bU = nb_seq(e_)
                    for h in range(8):
                        A("pe", lambda e, h=h, sl=sl, xq=xq, bU=bU, Tc=Tc: e.matmul(ps[sl, bU, h * 64:(h + 1) * 64], lhsT=Tc[sl, h, :],
                                                                                    rhs=X_sb[xq][sl, h, :], start=True, stop=True),
                          [], PB(bU))
                    A("dve", lambda e, sl=sl, xq=xq, bU=bU: e.tensor_copy(out=U_sb[xq][sl, :, :], in_=ps[sl, bU, :].rearrange("p (h s) -> p h s", h=8)),
                      PB(bU), ["U%d" % xq])
                    bY1 = nb_seq(1 - e_)
                    for j in range(4):
                        A("pe", lambda e, j=j, c=c, bY1=bY1: e.matmul(ps[:, bY1, j * 64:(j + 1) * 64], lhsT=H0bd[:, j, :],
                                                                      rhs=ar_t[:, j, c, 1, :], start=True, stop=True))
                    A("act", lambda e, c=c, bY1=bY1: e.copy(out=yT[:, :, c * 64:(c + 1) * 64], in_=ps[:, bY1, 0:256].rearrange("p (j t) -> p j t", j=4)))
                    bY = nb_seq(e_)
                    for h in range(8):
                        j, hh = h // 2, h % 2
                        hs = slice(hh * 64, (hh + 1) * 64)
                        A("pe", lambda e, h=h, j=j, hs=hs, sl=sl, xq=xq, q=q, bY=bY: e.matmul(ps[hs, bY, j * 64:(j + 1) * 64], lhsT=U_sb[xq][sl, h, :],
                                                                                              rhs=NM_sb[q][sl, h, 0, 64:128], start=True, stop=False))
                        A("pe", lambda e, h=h, j=j, hs=hs, sl=sl, q=q, bY=bY: e.matmul(ps[hs, bY, j * 64:(j + 1) * 64], lhsT=v_tok[q][sl, h * 64:(h + 1) * 64],
                                                                                       rhs=NM_sb[q][sl, h, 1, 64:128], start=False, stop=True))
                    A("dve", lambda e, c=c, bY=bY: e.tensor_tensor(out=yT[:, :, c * 64:(c + 1) * 64], in0=ps[:, bY, 0:256].rearrange("p (j t) -> p j t", j=4),
                                                                    in1=yT[:, :, c * 64:(c + 1) * 64], op=ALU.add))
                    bG = nb_seq(e_)
                    for h in range(8):
                        j, hh = h // 2, h % 2
                        hs = slice(hh * 64, (hh + 1) * 64)
                        A("pe", lambda e, h=h, j=j, hs=hs, sl=sl, xq=xq, q=q, bG=bG: e.matmul(ps[hs, bG, j * 64:(j + 1) * 64], lhsT=bk_tok[q][sl, 0, h * 64:(h + 1) * 64],
                                                                                              rhs=U_sb[xq][sl, h, :], start=True, stop=False),
                          ["bk_tok%d" % q, "U%d" % xq], PB(bG))
                        A("pe", lambda e, h=h, j=j, hs=hs, sl=sl, q=q, bG=bG: e.matmul(ps[hs, bG, j * 64:(j + 1) * 64], lhsT=bk_tok[q][sl, 1, h * 64:(h + 1) * 64],
                                                                                       rhs=v_tok[q][sl, h * 64:(h + 1) * 64], start=False, stop=True),
                          ["bk_tok%d" % q, "v_tok%d" % q], PB(bG))
                    A("dve", lambda e, bG=bG: e.tensor_tensor(out=Hs[:], in0=ps[:, bG, 0:256].rearrange("p (j v) -> p j v", j=4), in1=H0f[:], op=ALU.add),
                      PB(bG) + ["H0f"], ["Hs"])
                    A("dve", lambda e, c=c: e.tensor_tensor(out=H0f[:], in0=Hs[:], in1=WC[:, :, c:c + 1].broadcast_to([128, 4, 64]), op=ALU.mult),
                      ["Hs", "WC"], ["H0f"])
                    A("act", lambda e: e.copy(out=H0bd[0:64, :, 0:64], in_=H0f[0:64, :, :]), ["H0f"], ["H0b"])
                    A("act", lambda e: e.copy(out=H0bd[64:128, :, 64:128], in_=H0f[64:128, :, :]), ["H0f"], ["H0b"])
            if stop_after in ("c1", "c2", "c3", "c4"):
                continue
            if dbg and pas == 0:
                dump("yT", yT, ["yT"])
            if stop_after == "Rchunk":
                continue
            if nsamp:
                S.tag = "p%d Rsample" % pas
                emit_rsample()
            S.tag = "p%d Rpost" % pas
            _top_save = ar_.top
            ar_.top = rs_mark if nsamp else mark
            gtmp_b = [ar_.alloc(F32, [128, TW]) for _ in range(3)]
            ar_.top = _top_save
            for j in range(4):
                yj = yT[:, j, :]
                yc, sq_, rs_ = (gtmp[0], gtmp[1], gtmp[2]) if j % 2 == 0 else (gtmp_b[0], gtmp_b[1], gtmp_b[2])

                def ev_mean(b, c0, c1, j=j):
                    A("dve", lambda e: e.scalar_tensor_tensor(out=yc[:, c0:c1], in0=ps[:, b, 0:c1 - c0], scalar=-1.0 / 64, in1=yT[:, j, c0:c1],
                                                              op0=ALU.mult, op1=ALU.add), PB(b) + ["yT"], ["gtmp0"])
                fp32_blocksum(yj, "yT", blk64, ev_mean)
                A("act", lambda e: e.activation(out=sq_[:, 0:W], in_=yc[:, 0:W], func=AF.Square), ["gtmp0"], ["gtmp1"])

                def ev_var(b, c0, c1):
                    A("act", lambda e: e.activation(out=rs_[:, c0:c1], in_=ps[:, b, 0:c1 - c0], func=AF.Ln, scale=1.0 / 64, bias=epsgn[:, 0:1]))
                fp32_blocksum(sq_, "gtmp1", blk64, ev_var)
                A("act", lambda e: e.activation(out=rs_[:, 0:W], in_=rs_[:, 0:W], func=AF.Exp, scale=-0.5), ["gtmp2"], ["gtmp2"])
                A("dve", lambda e: e.tensor_tensor(out=yc[:, 0:W], in0=yc[:, 0:W], in1=rs_[:, 0:W], op=ALU.mult), ["gtmp0", "gtmp2"], ["gtmp0"])
                A("dve", lambda e, j=j: e.tensor_scalar(out=yc[:, 0:W], in0=yc[:, 0:W], scalar1=pf("ln_g", j), scalar2=pf("ln_b", j),
                                                        op0=ALU.mult, op1=ALU.add), ["gtmp0", "pfm"], ["gtmp0"])
                A("dve", lambda e, j=j: e.tensor_tensor(out=yc[:, 0:W], in0=yc[:, 0:W], in1=bonus[:, j, 0:W], op=ALU.add), ["gtmp0", "bonus"], ["gtmp0"])
                A("dve", lambda e, j=j: e.tensor_tensor(out=oa[:, j, 0:W], in0=yc[:, 0:W], in1=g_bf[:, j, 0:W], op=ALU.mult), ["gtmp0", "g_bf"], ["oa"])
            if dbg and pas == 0:
                dump("oa", oa[:], ["oa"])
            if pas == passes[-1]:
                wst = gtmp[0][0:64, 0:512].rearrange("p (j n) -> p j n", j=4)
                b = nb()
                for j in range(4):
                    A("pe", lambda e, j=j, b=b: e.transpose(out=ps[0:64, b, j * 128:(j + 1) * 128], in_=H0f[:, j, :], identity=ident),
                      ["H0f", "consts"], PB(b))
                A("act", lambda e, b=b: e.copy(out=wst, in_=ps[0:64, b, :].rearrange("p (j n) -> p j n", j=4)), PB(b), ["gtmp0"])
                S.dma("sp", lambda e: e.dma_start(out=o_wkv_p.rearrange("(j hh) v k -> v j hh k", hh=2),
                                                   in_=wst.rearrange("p j (hh k) -> p j hh k", hh=2)), reads=["gtmp0"])
                b = nb()
                A("pe", lambda e, b=b: e.transpose(out=ps[0:14, b, 0:128], in_=carry[:, :], identity=ident), ["carry", "consts"], PB(b))
                A("act", lambda e, b=b: e.copy(out=gtmp[1][0:14, 0:128], in_=ps[0:14, b, 0:128]), PB(b), ["gtmp1"])
                S.dma("sp", lambda e: e.dma_start(out=o_shift_p.rearrange("(c p) -> c p", p=128), in_=gtmp[1][0:14, 0:128]), reads=["gtmp1"])
            if stop_after == "Rpost":
                continue

            S.tag = "p%d H" % pas
            S.barrier()
            ar_ = Arena(arena_t, ARENA_WORDS)
            Eb = ar_.alloc(F32, [128, 4, HALF])
            qT = ar_.alloc(BF16, [128, 4, HALF])
            hkT = ar_.alloc(BF16, [128, 4, HALF])
            hvT = ar_.alloc(BF16, [128, 4, TW])
            sgo = ar_.alloc(BF16, [128, 4, TW])
            oT = ar_.alloc(F32, [128, 4, TW])
            smpH = ar_.alloc(F32, [128, 4, 4, NS])
            hmark = ar_.top
            htmp = [ar_.alloc(F32, [128, TW]) for _ in range(8)]
            hset = [htmp[0:4], htmp[4:8]]
            cmaskH = ar_.alloc(BF16, [128, HALF])
            S.dma("pool", lambda e: e.dma_start(out=cmaskH, in_=consts_d[:, 640:1664]), writes=["cmaskH"])
            HB = RPROJ
            wv, wr = load_w(w_in_v[:, :, HB + 512:HB + 1024], (8, 512))
            hvh = [(0, 512), (512, W)]
            for h in range(4):
                T0, T1_, T2, T3 = hset[h % 2]

                def ev_f(b, c0, c1, T0=T0):
                    A("act", lambda e: e.activation(out=T0[:, c0:c1], in_=ps[:, b, 0:c1 - c0], func=AF.Sigmoid))
                proj_fm(wv, wr, h * 128, 8, hT, "hT", ev_f)
                for (a_, b_) in hvh:
                    b2 = min(b_, HALF)
                    A("dve", lambda e, h=h, a_=a_, b_=b_: e.tensor_scalar(out=T0[:, a_:b_], in0=T0[:, a_:b_], scalar1=dvc("omlb", h), scalar2=dvc("lb", h),
                                                                          op0=ALU.mult, op1=ALU.add))
                    A("dve", lambda e, a_=a_, b_=b_: e.tensor_scalar(out=T1_[:, a_:b_], in0=T0[:, a_:b_], scalar1=-1.0, scalar2=1.0, op0=ALU.mult, op1=ALU.add))
                    A("act", lambda e, a_=a_, b2=b2: e.activation(out=T2[:, a_:b2], in_=T0[:, a_:b2], func=AF.Ln))
                    A("dve", lambda e, a_=a_, b2=b2: e.tensor_tensor_scan(out=T3[:, a_:b2], data0=cmaskH[:, a_:b2], data1=T2[:, a_:b2], initial=0.0,
                                                                          op0=ALU.mult, op1=ALU.add))
                    A("act", lambda e, h=h, a_=a_, b2=b2: e.activation(out=Eb[:, h, a_:b2], in_=T3[:, a_:b2], func=AF.Exp))
                    A("act", lambda e, h=h, a_=a_, b2=b2: e.activation(out=DC[:, h, a_ // 64:b2 // 64], in_=T3[:, a_:b2].rearrange("p (c t) -> p c t", t=64)[:, :, 63],
                                                                       func=AF.Exp))
                    A("act", lambda e, a_=a_, b2=b2: e.activation(out=T2[:, a_:b2], in_=T3[:, a_:b2], func=AF.Exp, scale=-1.0))
                    A("dve", lambda e, h=h, a_=a_, b2=b2: e.tensor_tensor(out=hkT[:, h, a_:b2], in0=T1_[:, a_:b2], in1=T2[:, a_:b2], op=ALU.mult))
                if nsamp:
                    A("act", lambda e, h=h: e.copy(out=smpH[:, 1, h, :], in_=T0[:, HALF:HALF + NS]))
                    A("act", lambda e, h=h: e.copy(out=smpH[:, 2, h, :], in_=T1_[:, HALF:HALF + NS]))
            wv, wr = load_w(w_in_v[:, :, HB:HB + 512], (8, 512))
            for h in range(4):
                T0 = hset[h % 2][0]

                def ev_q(b, c0, c1, T0=T0, h=h):
                    A("act", lambda e: e.activation(out=T0[:, c0:c1], in_=ps[:, b, 0:c1 - c0], func=AF.Silu))
                    if c0 < HALF:
                        A("dve", lambda e: e.tensor_tensor(out=qT[:, h, c0:c1], in0=T0[:, c0:c1], in1=Eb[:, h, c0:c1], op=ALU.mult))
                proj_fm(wv, wr, h * 128, 8, hT, "hT", ev_q)
                if nsamp:
                    A("act", lambda e, h=h: e.copy(out=smpH[:, 0, h, :], in_=T0[:, HALF:HALF + NS]))
            wv, wr = load_w(w_in_v[:, :, HB + 1024:HB + 1536], (8, 512))
            for h in range(4):
                def ev_i(b, c0, c1, h=h):
                    A("dve", lambda e: e.tensor_copy(out=hvT[:, h, c0:c1], in_=ps[:, b, 0:c1 - c0]), PB(b), ["hvT"])
                    if c0 >= HALF:
                        A("dve", lambda e: e.tensor_copy(out=smpH[:, 3, h, :], in_=ps[:, b, 0:NS]), PB(b), ["smpH"])
                proj_fm(wv, wr, h * 128, 8, hT, "hT", ev_i)
            wv, wr = load_w(w_in_v[:, :, HB + 1536:HB + 2048], (8, 512))
            for h in range(4):
                def ev_og(b, c0, c1, h=h):
                    A("act", lambda e: e.activation(out=sgo[:, h, c0:c1], in_=ps[:, b, 0:c1 - c0], func=AF.Sigmoid), PB(b), ["sgo"])
                proj_fm(wv, wr, h * 128, 8, hT, "hT", ev_og)

            if nsamp:
                S.barrier()
                ar_.top = hmark
                S_s = ar_.alloc(F32, [128, NS, 4, 128])
                ktok_s = ar_.alloc(F32, [NS, 512])
                vtok_s = ar_.alloc(F32, [NS, 512])
                vm = [ar_.alloc(F32, [NS, 512]) for _ in range(2)]
                tS_l = [ar_.alloc(F32, [128, 4, 128]) for _ in range(3)]
                q_bf = ar_.alloc(BF16, [128, 4, NS])
                Sb_l = [ar_.alloc(BF16, [128, 4, 128]) for _ in range(3)]
                S.dma("sp", lambda e: e.dma_start(out=S_s, in_=hgrn_s.rearrange("b h k v -> k b h v")), writes=["S_s"])
                for vec, dst, dn in ((2, ktok_s, "ktok_s"), (3, vtok_s, "vtok_s")):
                    b = nb_smp()
                    for h in range(4):
                        A("pe", lambda e, vec=vec, h=h, b=b: e.transpose(out=ps[0:NS, b, h * 128:(h + 1) * 128], in_=smpH[:, vec, h, :], identity=ident),
                          ["smpH", "consts"], PB(b))
                    A("act", lambda e, dst=dst, b=b: e.copy(out=dst, in_=ps[0:NS, b, :]), PB(b), [dn])
                for bi in range(NS):
                    vq = bi % 2
                    tS = tS_l[bi % 3]
                    A("dve", lambda e, bi=bi, vq=vq: e.tensor_scalar(out=vm[vq], in0=vtok_s, scalar1=ident[0:NS, bi:bi + 1], scalar2=None, op0=ALU.mult),
                      ["vtok_s", "consts"], ["vm%d" % vq])
                    b = nb_smp()
                    for h in range(4):
                        A("pe", lambda e, h=h, vq=vq, b=b: e.matmul(ps[:, b, h * 128:(h + 1) * 128], lhsT=ktok_s[:, h * 128:(h + 1) * 128],
                                                                    rhs=vm[vq][:, h * 128:(h + 1) * 128], start=True, stop=True),
                          ["ktok_s", "vm%d" % vq], PB(b))
                    A("dve", lambda e, bi=bi: e.tensor_tensor(out=tS, in0=S_s[:, bi, :, :],
                                                               in1=smpH[:, 1, :, bi:bi + 1].broadcast_to([128, 4, 128]), op=ALU.mult),
                      ["S_s", "smpH"], ["tS"])
                    A("dve", lambda e, bi=bi, b=b: e.tensor_tensor(out=S_s[:, bi, :, :], in0=ps[:, b, :].rearrange("p (h v) -> p h v", h=4), in1=tS,
                                                                    op=ALU.add), PB(b) + ["tS"], ["S_s"])
                S.dma("sp", lambda e: e.dma_start(out=o_hgrn_s.rearrange("b h k v -> k b h v"), in_=S_s), reads=["S_s"])
                A("act", lambda e: e.copy(out=q_bf, in_=smpH[:, 0, :, :]))
                bO_ = nb_smp()
                for bi in range(NS):
                    Sb = Sb_l[bi % 3]
                    A("act", lambda e, bi=bi, Sb=Sb: e.copy(out=Sb, in_=S_s[:, bi, :, :]))
                    for h in range(4):
                        A("pe", lambda e, h=h, bi=bi, Sb=Sb, bO_=bO_: e.matmul(ps[:, bO_, h * NS + bi:h * NS + bi + 1], lhsT=Sb[:, h, :],
                                                                                rhs=q_bf[:, h, bi:bi + 1], start=True, stop=True))
                A("act", lambda e, bO_=bO_: e.copy(out=oT[:, :, HALF:HALF + NS], in_=ps[:, bO_, 0:4 * NS].rearrange("p (h n) -> p h n", h=4)))

            if not nsamp:
                ar_.top = hmark
            par_nb[0] = 4 if nsamp else 6
            par_ctr[0] = 0
            hk_tok = [ar_.alloc(BF16, [128, 512]) for _ in range(2)]
            hv_tok = [ar_.alloc(BF16, [128, 512]) for _ in range(2)]
            PT_sb = [ar_.alloc(BF16, [128, 4, 64]) for _ in range(2)]
            Ss = ar_.alloc(F32, [128, 4, 128])
            ar_.top = hmark
            htmp4 = [ar_.alloc(F32, [128, TW]) for _ in range(4)]
            htmp = htmp4[0:2]
            for i in range(8):
                q = i % 2
                b1 = nb_par()
                pTk = ps[:, b1, :].bitcast(BF16).rearrange("p (v n) -> p v n", v=2)
                for h in range(4):
                    A("pe", lambda e, i=i, h=h, pTk=pTk: e.transpose(out=pTk[:, 0, h * 128:(h + 1) * 128], in_=hkT[:, h, i * 128:(i + 1) * 128],
                                                                     identity=ident_bf[:]), ["hkT", "ident_bf"], PB(b1))
                    A("pe", lambda e, i=i, h=h, pTk=pTk: e.transpose(out=pTk[:, 1, h * 128:(h + 1) * 128], in_=hvT[:, h, i * 128:(i + 1) * 128],
                                                                     identity=ident_bf[:]), ["hvT", "ident_bf"], PB(b1))
                A("act", lambda e, q=q, pTk=pTk: e.copy(out=hk_tok[q], in_=pTk[:, 0, :]), PB(b1), ["hk_tok%d" % q])
                A("dve", lambda e, q=q, pTk=pTk: e.tensor_copy(out=hv_tok[q], in_=pTk[:, 1, :]), PB(b1), ["hv_tok%d" % q])
                bS = nb_par()
                for h in range(4):
                    for e_ in range(2):
                        c = 2 * i + e_
                        A("pe", lambda e, h=h, e_=e_, c=c, bS=bS: e.matmul(ps[e_ * 64:(e_ + 1) * 64, bS, h * 64:(h + 1) * 64], lhsT=hkT[:, h, c * 64:(c + 1) * 64],
                                                                           rhs=qT[:, h, c * 64:(c + 1) * 64], start=True, stop=True), ["hkT", "qT"], PB(bS))
                A("dve", lambda e, q=q, bS=bS: e.tensor_tensor(out=PT_sb[q], in0=ps[:, bS, 0:256].rearrange("p (h t) -> p h t", h=4),
                                                                in1=masks_bf[:, 64:128].unsqueeze(1).broadcast_to([128, 4, 64]), op=ALU.mult),
                  PB(bS) + ["consts"], ["PT%d" % q])
                bO = nb_par()
                for e_ in range(2):
                    c = 2 * i + e_
                    sl = slice(e_ * 64, (e_ + 1) * 64)
                    for h in range(4):
                        oo = h * 128 + e_ * 64
                        A("pe", lambda e, h=h, c=c, oo=oo, bO=bO: e.matmul(ps[:, bO, oo:oo + 64], lhsT=S0b[:, h, :], rhs=qT[:, h, c * 64:(c + 1) * 64],
                                                                           start=True, stop=False), ["S0b", "qT"], PB(bO))
                        A("pe", lambda e, h=h, sl=sl, q=q, oo=oo, bO=bO: e.matmul(ps[:, bO, oo:oo + 64], lhsT=hv_tok[q][sl, h * 128:(h + 1) * 128],
                                                                                  rhs=PT_sb[q][sl, h, :], start=False, stop=True),
                          ["hv_tok%d" % q, "PT%d" % q], PB(bO))
                    bG = nb_seq(e_)
                    for h in range(4):
                        A("pe", lambda e, h=h, sl=sl, q=q, bG=bG: e.matmul(ps[:, bG, h * 128:(h + 1) * 128], lhsT=hk_tok[q][sl, h * 128:(h + 1) * 128],
                                                                           rhs=hv_tok[q][sl, h * 128:(h + 1) * 128], start=True, stop=True),
                          ["hk_tok%d" % q, "hv_tok%d" % q], PB(bG))
                    A("dve", lambda e, bG=bG: e.tensor_tensor(out=Ss, in0=ps[:, bG, :].rearrange("p (h v) -> p h v", h=4), in1=S0f[:], op=ALU.add),
                      PB(bG) + ["S0f"], ["Ss"])
                    A("dve", lambda e, c=c: e.tensor_tensor(out=S0f[:], in0=Ss, in1=DC[:, :, c:c + 1].broadcast_to([128, 4, 128]), op=ALU.mult),
                      ["Ss", "DC"], ["S0f"])
                    A("act", lambda e: e.copy(out=S0b[:], in_=S0f[:]), ["S0f"], ["S0b"])
                A("act", lambda e, i=i, bO=bO: e.copy(out=oT[:, :, i * 128:(i + 1) * 128], in_=ps[:, bO, :].rearrange("p (h t) -> p h t", h=4)),
                  PB(bO), ["oT"])
            par_nb[0] = 6
            if dbg and pas == 0:
                dump("oT", oT, ["oT"])
            for h in range(4):
                htmp = htmp4[0:2] if h % 2 == 0 else htmp4[2:4]
                A("act", lambda e, h=h: e.activation(out=htmp[0][:, 0:W], in_=oT[:, h, 0:W], func=AF.Square), ["oT"], ["htmp0"])

                def ev_ms(b, c0, c1):
                    A("act", lambda e: e.activation(out=htmp[1][:, c0:c1], in_=ps[:, b, 0:c1 - c0], func=AF.Ln, scale=1.0 / 128, bias=epsrms[:, 0:1]))
                fp32_blocksum(htmp[0], "htmp0", ones, ev_ms)
                A("act", lambda e: e.activation(out=htmp[1][:, 0:W], in_=htmp[1][:, 0:W], func=AF.Exp, scale=-0.5), ["htmp1"], ["htmp1"])
                A("dve", lambda e, h=h: e.tensor_tensor(out=htmp[0][:, 0:W], in0=oT[:, h, 0:W], in1=htmp[1][:, 0:W], op=ALU.mult), ["oT", "htmp1"], ["htmp0"])
                A("dve", lambda e, h=h: e.scalar_tensor_tensor(out=ob[:, h, 0:W], in0=htmp[0][:, 0:W], scalar=pf("hng", h), in1=sgo[:, h, 0:W],
                                                               op0=ALU.mult, op1=ALU.mult), ["htmp0", "pfm", "sgo"], ["ob"])
            if dbg and pas == 0:
                dump("ob", ob[:], ["ob"])
            if pas == passes[-1]:
                S.dma("sp", lambda e: e.dma_start(out=o_hgrn_p.rearrange("h k v -> k h v"), in_=S0f[:]), reads=["S0f"])
            if stop_after == "H":
                continue

            S.barrier()
            ar_ = Arena(arena_t, ARENA_WORDS)
            x_tok = ar_.alloc(F32, [128, 8, 1024])
            xs_tok = ar_.alloc(F32, [NS, 1024])
            mergedT = ar_.alloc(BF16, [128, 8, TW])
            gm = [ar_.alloc(F32, [128, 512]) for _ in range(4)]
            GB = RPROJ + 2048
            w_upa_v = w_up_a.rearrange("(kc p) n -> p kc n", p=128)
            w_upb_v = w_up_b.rearrange("(kc p) n -> p kc n", p=128)
            for dcg in range(2):
                wga, wgar = load_w(w_in_v[:, :, GB + dcg * 512:GB + (dcg + 1) * 512], (8, 512))
                wgb, wgbr = load_w(w_in_v[:, :, GB + 1024 + dcg * 512:GB + 1024 + (dcg + 1) * 512], (8, 512))
                wu, wur = load_w(w_upa_v[:, :, dcg * 512:(dcg + 1) * 512], (4, 512))
                iu_ = (wctr[0] - 1) % NWB
                wub_view = wbuf[iu_][:, 4:8, :]
                S.dma("pool", lambda e, wub_view=wub_view, dcg=dcg: e.dma_start(out=wub_view, in_=w_upb_v[:, :, dcg * 512:(dcg + 1) * 512]), writes=[wur])
                for dc in range(4):
                    n0 = dc * 128
                    for (c0, c1) in blocks:
                        w_ = c1 - c0
                        b1, b2, b3, b4 = nb(), nb(), nb(), nb()
                        for kc in range(8):
                            A("pe", lambda e, kc=kc, b1=b1, c0=c0, c1=c1, n0=n0, wga=wga: e.matmul(ps[:, b1, 0:c1 - c0], lhsT=wga[:, kc, n0:n0 + 128],
                                                                                                   rhs=hT[:, kc, c0:c1], start=(kc == 0), stop=(kc == 7)),
                              [wgar, "hT"], PB(b1))
                        for kc in range(8):
                            A("pe", lambda e, kc=kc, b2=b2, c0=c0, c1=c1, n0=n0, wgb=wgb: e.matmul(ps[:, b2, 0:c1 - c0], lhsT=wgb[:, kc, n0:n0 + 128],
                                                                                                   rhs=hT[:, kc, c0:c1], start=(kc == 0), stop=(kc == 7)),
                              [wgbr, "hT"], PB(b2))
                        for kc in range(4):
                            A("pe", lambda e, kc=kc, b3=b3, c0=c0, c1=c1, n0=n0, wu=wu: e.matmul(ps[:, b3, 0:c1 - c0], lhsT=wu[:, kc, n0:n0 + 128],
                                                                                                 rhs=oa[:, kc, c0:c1], start=(kc == 0), stop=(kc == 3)),
                              [wur, "oa"], PB(b3))
                        for kc in range(4):
                            A("pe", lambda e, kc=kc, b4=b4, c0=c0, c1=c1, n0=n0, wub_view=wub_view: e.matmul(ps[:, b4, 0:c1 - c0], lhsT=wub_view[:, kc, n0:n0 + 128],
                                                                                                             rhs=ob[:, kc, c0:c1], start=(kc == 0), stop=(kc == 3)),
                              [wur, "ob"], PB(b4))
                        A("act", lambda e, b1=b1, w_=w_: e.activation(out=gm[0][:, 0:w_], in_=ps[:, b1, 0:w_], func=AF.Sigmoid), PB(b1), ["gm0"])
                        A("act", lambda e, b2=b2, w_=w_: e.activation(out=gm[1][:, 0:w_], in_=ps[:, b2, 0:w_], func=AF.Sigmoid), PB(b2), ["gm1"])
                        A("dve", lambda e, b3=b3, w_=w_: e.tensor_tensor(out=gm[2][:, 0:w_], in0=ps[:, b3, 0:w_], in1=gm[0][:, 0:w_], op=ALU.mult),
                          PB(b3) + ["gm0"], ["gm2"])
                        A("dve", lambda e, b4=b4, w_=w_: e.tensor_tensor(out=gm[3][:, 0:w_], in0=ps[:, b4, 0:w_], in1=gm[1][:, 0:w_], op=ALU.mult),
                          PB(b4) + ["gm1"], ["gm3"])
                        A("dve", lambda e, dcg=dcg, dc=dc, c0=c0, c1=c1, w_=w_: e.tensor_tensor(out=mergedT[:, dcg * 4 + dc, c0:c1], in0=gm[2][:, 0:w_],
                                                                                                 in1=gm[3][:, 0:w_], op=ALU.add),
                          ["gm2", "gm3"], ["mergedT"])
            if dbg and pas == 0:
                dump("mergedT", mergedT, ["mergedT"])
            if stop_after == "G":
                continue

            S.barrier()
            ar_.top = 0
            x_tok = ar_.alloc(F32, [128, 8, 1024])
            xs_tok = ar_.alloc(F32, [NS, 1024])
            mergedT = ar_.alloc(BF16, [128, 8, TW])
            for tt in range(8):
                S.dma("sp", lambda e, tt=tt: e.dma_start(out=x_tok[:, tt, :], in_=x_p[t0 + tt * 128:t0 + (tt + 1) * 128, :]),
                      writes=["x%d" % tt])
            if nsamp:
                S.dma("sp", lambda e: e.dma_start(out=xs_tok, in_=x_s), writes=["xs"])
            w_out_v = w_out.rearrange("(kc p) n -> p kc n", p=128)
            wos = [load_w(w_out_v[:, :, half * 512:(half + 1) * 512], (8, 512))[0] for half in range(2)]
            for tt in range(8):
                for half in range(2):
                    wo = wos[half]
                    b = nb()
                    for kc in range(8):
                        A("pe", lambda e, kc=kc, tt=tt, b=b, wo=wo: e.matmul(ps[:, b, :], lhsT=mergedT[:, kc, tt * 128:(tt + 1) * 128], rhs=wo[:, kc, :],
                                                                             start=(kc == 0), stop=(kc == 7)))
                    A("dve", lambda e, tt=tt, half=half, b=b: e.tensor_tensor(out=x_tok[:, tt, half * 512:(half + 1) * 512], in0=ps[:, b, :],
                                                                               in1=x_tok[:, tt, half * 512:(half + 1) * 512], op=ALU.add))
            if nsamp:
                for half in range(2):
                    wo = wos[half]
                    b = nb()
                    for kc in range(8):
                        A("pe", lambda e, kc=kc, b=b, wo=wo: e.matmul(ps[0:NS, b, :], lhsT=mergedT[:, kc, HALF:HALF + NS], rhs=wo[:, kc, :],
                                                                      start=(kc == 0), stop=(kc == 7)))
                    A("dve", lambda e, half=half, b=b: e.tensor_tensor(out=xs_tok[:, half * 512:(half + 1) * 512], in0=ps[0:NS, b, :],
                                                                        in1=xs_tok[:, half * 512:(half + 1) * 512], op=ALU.add))
            norm_phase("O", x_tok, xs_tok, g_ffn, ar_,
                       lambda gB, h_tok, hs_tok, R: to_hT(gB, h_tok, hs_tok, R, x_tok, xs_tok, lambda tt: "x%d" % tt, "xs"),
                       lambda tt: "x%d" % tt, "xs")
            if dbg and pas == 0:
                dump("hT2", hT[:], ["hT"])
            if stop_after == "O":
                continue

            S.barrier()
            ar_.top = 0
            x_tok = ar_.alloc(F32, [128, 8, 1024])
            xs_tok = ar_.alloc(F32, [NS, 1024])
            actT = ar_.alloc(BF16, [128, 22, TW])
            wdn = ar_.alloc(BF16, [128, 22, 1024])
            gBf = ftmp[:, :, :].rearrange("p a b -> p (a b)")
            junkD = ar_.alloc(BF16, [128, 1024])
            w_fd_v = w_fd.rearrange("(fc p) n -> p fc n", p=128)
            w_fg_v = w_fg.rearrange("(kc p) n -> p kc n", p=128)
            w_fu_v = w_fu.rearrange("(kc p) n -> p kc n", p=128)
            for fg in range(11):
                wg, wgr = load_w(w_fg_v[:, :, fg * 256:(fg + 1) * 256], (8, 256))
                wu, wur = load_w(w_fu_v[:, :, fg * 256:(fg + 1) * 256], (8, 256))
                if fg in (2, 4, 6, 8):
                    k_ = (fg - 2) // 2
                    lo, hi = (0, 6, 12, 18)[k_], (6, 12, 18, 22)[k_]
                    S.dma("pool", lambda e, lo=lo, hi=hi: e.dma_start(out=wdn[:, lo:hi, :], in_=w_fd_v[:, lo:hi, :]), writes=["wdn%d" % k_])
                for f2 in range(2):
                    fc = fg * 2 + f2
                    n0 = f2 * 128
                    for (c0, c1) in blocks:
                        w_ = c1 - c0
                        b1, b2 = nb(), nb()
                        for kc in range(8):
                            A("pe", lambda e, kc=kc, b1=b1, c0=c0, c1=c1, n0=n0, wg=wg: e.matmul(ps[:, b1, 0:c1 - c0], lhsT=wg[:, kc, n0:n0 + 128],
                                                                                                 rhs=hT[:, kc, c0:c1], start=(kc == 0), stop=(kc == 7)),
                              [wgr, "hT"], PB(b1))
                        for kc in range(8):
                            A("pe", lambda e, kc=kc, b2=b2, c0=c0, c1=c1, n0=n0, wu=wu: e.matmul(ps[:, b2, 0:c1 - c0], lhsT=wu[:, kc, n0:n0 + 128],
                                                                                                 rhs=hT[:, kc, c0:c1], start=(kc == 0), stop=(kc == 7)),
                              [wur, "hT"], PB(b2))
                        fq = (fc + (c0 // 512)) % 2
                        A("act", lambda e, b1=b1, w_=w_, fq=fq: e.activation(out=ftmp[:, fq, 0:w_], in_=ps[:, b1, 0:w_], func=AF.Silu), PB(b1), ["ftmp%d" % fq])
                        A("dve", lambda e, b2=b2, w_=w_, fq=fq, fc=fc, c0=c0, c1=c1: e.tensor_tensor(out=actT[:, fc, c0:c1], in0=ps[:, b2, 0:w_],
                                                                                                      in1=ftmp[:, fq, 0:w_], op=ALU.mult),
                          PB(b2) + ["ftmp%d" % fq], ["actT"])
            if stop_after == "F":
                continue

            S.barrier()
            S.dma("sp", lambda e: e.dma_start(out=gBf, in_=g_fin.partition_broadcast(128)), writes=["gBf"])
            A("dve", lambda e: e.memset(stat[:, 0:32], 0.0), [], ["stat"])
            wres_all = ["wdn0", "wdn1", "wdn2", "wdn3"]
            for tt in range(8):
                for half in range(2):
                    b = nb()
                    for fc in range(22):
                        A("pe", lambda e, fc=fc, tt=tt, half=half, b=b: e.matmul(ps[:, b, :], lhsT=actT[:, fc, tt * 128:(tt + 1) * 128],
                                                                                 rhs=wdn[:, fc, half * 512:(half + 1) * 512], start=(fc == 0), stop=(fc == 21)),
                          ["actT"] + wres_all, PB(b))
                    A("dve", lambda e, tt=tt, half=half, b=b: e.tensor_tensor(out=x_tok[:, tt, half * 512:(half + 1) * 512], in0=ps[:, b, :],
                                                                               in1=x_tok[:, tt, half * 512:(half + 1) * 512], op=ALU.add),
                      PB(b) + ["x%d" % tt], ["x%d" % tt])
                A("act", lambda e, tt=tt: e.activation(out=junkD, in_=x_tok[:, tt, :], func=AF.Square,
                                                       accum_out=stat[:, tt:tt + 1]), ["x%d" % tt, "stat"], ["ftmp0", "ftmp1", "stat%d" % tt])
                A("dve", lambda e, tt=tt: e.tensor_scalar(out=stat[:, 16 + tt:17 + tt], in0=stat[:, tt:tt + 1], scalar1=1.0 / D, scalar2=RMS_EPS,
                                                          op0=ALU.mult, op1=ALU.add), ["stat%d" % tt, "stat"], ["stat%d" % tt])
                A("act", lambda e, tt=tt: e.activation(out=stat[:, 16 + tt:17 + tt], in_=stat[:, 16 + tt:17 + tt], func=AF.Ln), ["stat%d" % tt], ["stat%d" % tt])
                A("act", lambda e, tt=tt: e.activation(out=stat[:, 16 + tt:17 + tt], in_=stat[:, 16 + tt:17 + tt], func=AF.Exp, scale=-0.5),
                  ["stat%d" % tt], ["stat%d" % tt])
                A("dve", lambda e, tt=tt: e.scalar_tensor_tensor(out=x_tok[:, tt, :], in0=x_tok[:, tt, :], scalar=stat[:, 16 + tt:17 + tt], in1=gBf,
                                                                 op0=ALU.mult, op1=ALU.mult), ["x%d" % tt, "stat%d" % tt, "gBf"], ["x%d" % tt])
                S.dma("sp", lambda e, tt=tt: e.dma_start(out=y_p[t0 + tt * 128:t0 + (tt + 1) * 128, :], in_=x_tok[:, tt, :]), reads=["x%d" % tt])
            if nsamp:
                for half in range(2):
                    b = nb()
                    for fc in range(22):
                        A("pe", lambda e, fc=fc, half=half, b=b: e.matmul(ps[0:NS, b, :], lhsT=actT[:, fc, HALF:HALF + NS],
                                                                          rhs=wdn[:, fc, half * 512:(half + 1) * 512], start=(fc == 0), stop=(fc == 21)),
                          ["actT"] + wres_all, PB(b))
                    A("dve", lambda e, half=half, b=b: e.tensor_tensor(out=xs_tok[:, half * 512:(half + 1) * 512], in0=ps[0:NS, b, :],
                                                                        in1=xs_tok[:, half * 512:(half + 1) * 512], op=ALU.add), PB(b) + ["xs"], ["xs"])
                A("act", lambda e: e.activation(out=junkD[0:NS, :], in_=xs_tok, func=AF.Square,
                                                accum_out=stat[0:NS, 8:9]), ["xs", "stat"], ["ftmp0", "ftmp1", "stat8"])
                A("dve", lambda e: e.tensor_scalar(out=stat[0:NS, 24:25], in0=stat[0:NS, 8:9], scalar1=1.0 / D, scalar2=RMS_EPS,
                                                    op0=ALU.mult, op1=ALU.add), ["stat8", "stat"], ["stat8"])
                A("act", lambda e: e.activation(out=stat[0:NS, 24:25], in_=stat[0:NS, 24:25], func=AF.Ln), ["stat8"], ["stat8"])
                A("act", lambda e: e.activation(out=stat[0:NS, 24:25], in_=stat[0:NS, 24:25], func=AF.Exp, scale=-0.5), ["stat8"], ["stat8"])
                A("dve", lambda e: e.scalar_tensor_tensor(out=xs_tok, in0=xs_tok, scalar=stat[0:NS, 24:25], in1=gBf[0:NS, :],
                                                          op0=ALU.mult, op1=ALU.mult), ["xs", "stat8", "gBf"], ["xs"])
                S.dma("sp", lambda e: e.dma_start(out=y_s, in_=xs_tok), reads=["xs"])
        S.finalize()
        S.emit(st)
    return nc


DBG_EXTRA = {"oa": [128, 4, 1040], "oT": [128, 4, 1040], "ob": [128, 4, 1040], "mergedT": [128, 8, 1040], "hT2": [128, 8, 1040]}


def _host_maps(inp, ncores=8):
    f = np.ascontiguousarray

    def a32(v):
        return np.asarray(v, dtype=np.float32)

    def fm(v, ncol):
        return f(a32(v).reshape(ncol, 128).T)

    pfm = np.concatenate([fm(inp['rwkv_mu'][0], 14), fm(inp['rwkv_w0'][0], 4), fm(inp['rwkv_a0'][0], 4), fm(inp['rwkv_k_k'][0], 4),
                          fm(inp['rwkv_k_a'][0], 4), fm(a32(inp['rwkv_r_k'][0]).reshape(-1), 4), fm(inp['rwkv_ln_g'][0], 4),
                          fm(inp['rwkv_ln_b'][0], 4), fm(inp['hgrn_lb'][0], 4), fm(inp['hgrn_lb'][1], 4), fm(inp['hgrn_norm_g'][0], 4)], axis=1)
    shared = dict(w_in=f(a32(inp['w_in'][0])), w2a2=f(np.concatenate([a32(inp['rwkv_w2'][0]), a32(inp['rwkv_a2'][0])], axis=0)),
                  g2=f(a32(inp['rwkv_g2'][0])), w_up_a=f(a32(inp['w_up_a'][0])), w_up_b=f(a32(inp['w_up_b'][0])), w_out=f(a32(inp['w_out'][0])),
                  w_fg=f(a32(inp['w_ffn_gate'][0])), w_fu=f(a32(inp['w_ffn_up'][0])), w_fd=f(a32(inp['w_ffn_down'][0])), pfm=f(pfm),
                  consts=make_consts(), g_mix=f(a32(inp['norm_mix_g'][0])), g_ffn=f(a32(inp['norm_ffn_g'][0])), g_fin=f(a32(inp['norm_final_g'])))
    maps = []
    for c in range(ncores):
        m = dict(shared)
        m['x_p'] = f(a32(inp['x_prompt'][c]))
        m['x_s'] = f(a32(inp['x_sample'][c * NS:(c + 1) * NS, 0]))
        m['wkv_s'] = f(a32(inp['state_rwkv_wkv'][0, c * NS:(c + 1) * NS]).reshape(NS * 8, 4096))
        m['shift_s'] = f(a32(inp['state_rwkv_shift'][0, c * NS:(c + 1) * NS]))
        m['hgrn_s'] = f(a32(inp['state_hgrn'][0, c * NS:(c + 1) * NS]))
        maps.append(m)
    return maps


_NC_CACHE = {}


def kernel(**inputs):
    ncores = 8
    if "nc" not in _NC_CACHE:
        _NC_CACHE["nc"] = build_program()
    nc = _NC_CACHE["nc"]
    maps = _host_maps(inputs, ncores)
    res = run_bass_kernel_spmd(nc, maps, core_ids=list(range(ncores)))
    r = res.results
    y_p = np.stack([r[c]["y_p"] for c in range(ncores)]).astype(np.float32)
    y_s = np.concatenate([r[c]["y_s"] for c in range(ncores)], axis=0).reshape(128, 1, D).astype(np.float32)
    wkv_p = np.stack([r[c]["o_wkv_p"] for c in range(ncores)])[None].astype(np.float32)
    shift_p = np.stack([r[c]["o_shift_p"] for c in range(ncores)])[None].astype(np.float32)
    hgrn_p = np.stack([r[c]["o_hgrn_p"] for c in range(ncores)])[None].astype(np.float32)
    wkv_s = np.concatenate([r[c]["o_wkv_s"].reshape(NS, 8, 64, 64) for c in range(ncores)], axis=0)[None].astype(np.float32)
    shift_s = np.concatenate([r[c]["o_shift_s"] for c in range(ncores)], axis=0)[None].astype(np.float32)
    hgrn_s = np.concatenate([r[c]["o_hgrn_s"] for c in range(ncores)], axis=0)[None].astype(np.float32)
    return (y_p, y_s, wkv_p, shift_p, hgrn_p, wkv_s, shift_s, hgrn_s)
```
